# Optimizing a Trainium2 kernel written in Bass

```python
import math
import jax, jax.numpy as jnp
from jax import lax
import numpy as np

D_MODEL = 1024
BATCH = 8
SEQ = 4096
DEPTH = 4

N_MIXERS = 2
N_SSD_LAYERS = (DEPTH + N_MIXERS - 1) // N_MIXERS
N_LRU_LAYERS = DEPTH // N_MIXERS
SSD_EXPAND = 2
SSD_D_INNER = SSD_EXPAND * D_MODEL
SSD_HEADDIM = 64
SSD_HEADS = SSD_D_INNER // SSD_HEADDIM
SSD_GROUPS = 4
SSD_HPG = SSD_HEADS // SSD_GROUPS
SSD_D_STATE = 128
SSD_CONV = 4
SSD_CHUNK = 128
SSD_CONV_DIM = SSD_D_INNER + 2 * SSD_GROUPS * SSD_D_STATE
SSD_IN_DIM = SSD_D_INNER + SSD_CONV_DIM + SSD_HEADS
LRU_WIDTH = 1280
LRU_HEADS = 10
LRU_BLOCK = LRU_WIDTH // LRU_HEADS
LRU_CONV = 4
LRU_C = 8.0
D_FF = 4 * D_MODEL
FFN_CONV = 3
EPS = 1e-6

kernel_name = 'hybrid_ssd_rglru_convffn_trunk'


def rmsnorm(x, w):
    xf = x.astype(jnp.float32)
    y = xf * lax.rsqrt(jnp.mean(xf * xf, axis=-1, keepdims=True) + EPS)
    return (y * w.astype(jnp.float32)).astype(x.dtype)


def causal_dwconv(x, w, b):
    k = w.shape[0]
    s = x.shape[1]
    xp = jnp.pad(x, ((0, 0), (k - 1, 0), (0, 0)))
    y = b
    for j in range(k):
        y = y + xp[:, j:j + s, :] * w[j]
    return y


def ssd_chunked_scan(xh, dt, a, bm, cm):
    bsz, s, h, p = xh.shape
    nc = s // SSD_CHUNK
    xdt = (xh * dt[..., None]).reshape(bsz, nc, SSD_CHUNK, SSD_GROUPS, SSD_HPG, p)
    da = (dt * a).reshape(bsz, nc, SSD_CHUNK, SSD_GROUPS, SSD_HPG)
    bc = bm.reshape(bsz, nc, SSD_CHUNK, SSD_GROUPS, SSD_D_STATE)
    cc = cm.reshape(bsz, nc, SSD_CHUNK, SSD_GROUPS, SSD_D_STATE)
    cs = jnp.moveaxis(jnp.cumsum(da, axis=2), 2, -1)
    tri = jnp.tril(jnp.ones((SSD_CHUNK, SSD_CHUNK), dtype=bool))
    decay_in = jnp.exp(jnp.where(tri, cs[..., :, None] - cs[..., None, :], -jnp.inf))
    cb = jnp.einsum('bclgn,bcsgn->bcgls', cc, bc)
    scores = cb[:, :, :, None] * decay_in
    y_diag = jnp.einsum('bcgels,bcsgep->bclgep', scores, xdt)
    decay_out = jnp.moveaxis(jnp.exp(cs[..., -1:] - cs), -1, 2)
    states = jnp.einsum('bclgn,bclgep->bcgepn', bc, xdt * decay_out[..., None])
    tot = cs[..., -1]
    inc = jnp.moveaxis(jnp.cumsum(tot, axis=1), 1, -1)
    exc = inc - jnp.moveaxis(tot, 1, -1)
    stri = jnp.tril(jnp.ones((nc, nc), dtype=bool), -1)
    decay_chunk = jnp.exp(jnp.where(stri, exc[..., :, None] - inc[..., None, :], -jnp.inf))
    entering = jnp.einsum('bgezc,bcgepn->bzgepn', decay_chunk, states)
    state_decay = jnp.moveaxis(jnp.exp(cs), -1, 2)
    y_off = jnp.einsum('bclgn,bcgepn->bclgep', cc, entering) * state_decay[..., None]
    return (y_diag + y_off).reshape(bsz, s, h, p)


def gated_group_rmsnorm(y, z, w):
    bsz, s, d = y.shape
    g = (y * jax.nn.silu(z.astype(jnp.float32))).reshape(bsz, s, SSD_GROUPS, d // SSD_GROUPS)
    g = g * lax.rsqrt(jnp.mean(g * g, axis=-1, keepdims=True) + EPS)
    return g.reshape(bsz, s, d) * w.astype(jnp.float32)


def ssd_mixer(u, w_in, conv_w, conv_b, dt_bias, a_log, d_skip, norm_w, w_out):
    bsz, s, _ = u.shape
    proj = u @ w_in
    z = proj[..., :SSD_D_INNER]
    xbc = proj[..., SSD_D_INNER:SSD_D_INNER + SSD_CONV_DIM]
    dt_raw = proj[..., SSD_D_INNER + SSD_CONV_DIM:]
    xbc = jax.nn.silu(causal_dwconv(xbc, conv_w, conv_b)).astype(jnp.float32)
    xs = xbc[..., :SSD_D_INNER].reshape(bsz, s, SSD_HEADS, SSD_HEADDIM)
    bm = xbc[..., SSD_D_INNER:SSD_D_INNER + SSD_GROUPS * SSD_D_STATE].reshape(bsz, s, SSD_GROUPS, SSD_D_STATE)
    cm = xbc[..., SSD_D_INNER + SSD_GROUPS * SSD_D_STATE:].reshape(bsz, s, SSD_GROUPS, SSD_D_STATE)
    dt = jax.nn.softplus(dt_raw.astype(jnp.float32) + dt_bias.astype(jnp.float32))
    a = -jnp.exp(a_log.astype(jnp.float32))
    y = ssd_chunked_scan(xs, dt, a, bm, cm)
    y = y + d_skip.astype(jnp.float32)[:, None] * xs
    y = gated_group_rmsnorm(y.reshape(bsz, s, SSD_D_INNER), z, norm_w)
    return (y.astype(u.dtype) @ w_out).astype(u.dtype)


def _linear_combine(c1, c2):
    a1, b1 = c1
    a2, b2 = c2
    return a1 * a2, a2 * b1 + b2


def rglru_mixer(u, w_in, b_in, conv_w, conv_b, w_gx, b_gx, w_ga, b_ga, lam, w_out, b_out):
    bsz, s, _ = u.shape
    proj = u @ w_in + b_in
    ybr = jax.nn.gelu(proj[..., :LRU_WIDTH].astype(jnp.float32))
    xbr = causal_dwconv(proj[..., LRU_WIDTH:], conv_w, conv_b)
    xh = xbr.reshape(bsz, s, LRU_HEADS, LRU_BLOCK)
    gx = jax.nn.sigmoid((jnp.einsum('bshi,hij->bshj', xh, w_gx).reshape(bsz, s, LRU_WIDTH) + b_gx).astype(jnp.float32))
    ga = jax.nn.sigmoid((jnp.einsum('bshi,hij->bshj', xh, w_ga).reshape(bsz, s, LRU_WIDTH) + b_ga).astype(jnp.float32))
    log_a = LRU_C * ga * jax.nn.log_sigmoid(lam.astype(jnp.float32))
    a_t = jnp.exp(log_a)
    mult = jnp.sqrt(-jnp.expm1(2.0 * log_a))
    b_t = mult * gx * xbr.astype(jnp.float32)
    _, h = lax.associative_scan(_linear_combine, (a_t, b_t), axis=1)
    out = (h * ybr).astype(u.dtype) @ w_out + b_out
    return out.astype(u.dtype)


def conv_ffn(u, w_up, conv_w, conv_b, w_down):
    hcv = causal_dwconv(u @ w_up, conv_w, conv_b)
    g = hcv[..., :D_FF]
    v = hcv[..., D_FF:]
    return ((jax.nn.gelu(g.astype(jnp.float32)) * v.astype(jnp.float32)).astype(u.dtype) @ w_down).astype(u.dtype)


def setup_inputs(seed: int = 0) -> dict:
    key = jax.random.key(seed)
    ks = jax.random.split(key, 32)
    f32 = jnp.float32

    def nrm(k, shape, scale):
        return scale * jax.random.normal(k, shape, f32)

    def gain(k, shape):
        return 1.0 + 0.1 * jax.random.normal(k, shape, f32)

    na, nb = N_SSD_LAYERS, N_LRU_LAYERS
    dt0 = jnp.exp(jax.random.uniform(ks[9], (na, SSD_HEADS), f32, math.log(1e-3), math.log(1e-1)))
    lam_u = jax.random.uniform(ks[20], (nb, LRU_WIDTH), f32, 0.9, 0.999)
    return {
        'x': jax.random.normal(ks[0], (BATCH, SEQ, D_MODEL), f32),
        'norm_mix_pre': gain(ks[1], (DEPTH, D_MODEL)),
        'norm_mix_post': gain(ks[2], (DEPTH, D_MODEL)),
        'norm_ffn_pre': gain(ks[3], (DEPTH, D_MODEL)),
        'norm_ffn_post': gain(ks[4], (DEPTH, D_MODEL)),
        'ssd_w_in': nrm(ks[5], (na, D_MODEL, SSD_IN_DIM), D_MODEL ** -0.5),
        'ssd_conv_w': nrm(ks[6], (na, SSD_CONV, SSD_CONV_DIM), SSD_CONV ** -0.5),
        'ssd_conv_b': nrm(ks[7], (na, SSD_CONV_DIM), 0.01),
        'ssd_dt_bias': dt0 + jnp.log(-jnp.expm1(-dt0)),
        'ssd_a_log': jnp.log(jax.random.uniform(ks[10], (na, SSD_HEADS), f32, 1.0, 16.0)),
        'ssd_d': gain(ks[11], (na, SSD_HEADS)),
        'ssd_norm': gain(ks[12], (na, SSD_D_INNER)),
        'ssd_w_out': nrm(ks[13], (na, SSD_D_INNER, D_MODEL), SSD_D_INNER ** -0.5),
        'lru_w_in': nrm(ks[14], (nb, D_MODEL, 2 * LRU_WIDTH), D_MODEL ** -0.5),
        'lru_b_in': nrm(ks[15], (nb, 2 * LRU_WIDTH), 0.01),
        'lru_conv_w': nrm(ks[16], (nb, LRU_CONV, LRU_WIDTH), LRU_CONV ** -0.5),
        'lru_conv_b': nrm(ks[17], (nb, LRU_WIDTH), 0.01),
        'lru_w_gx': nrm(ks[18], (nb, LRU_HEADS, LRU_BLOCK, LRU_BLOCK), LRU_BLOCK ** -0.5),
        'lru_b_gx': nrm(ks[19], (nb, LRU_WIDTH), 0.01),
        'lru_w_ga': nrm(ks[21], (nb, LRU_HEADS, LRU_BLOCK, LRU_BLOCK), LRU_BLOCK ** -0.5),
        'lru_b_ga': nrm(ks[22], (nb, LRU_WIDTH), 0.01),
        'lru_lambda': jnp.log(lam_u) - jnp.log1p(-lam_u),
        'lru_w_out': nrm(ks[23], (nb, LRU_WIDTH, D_MODEL), LRU_WIDTH ** -0.5),
        'lru_b_out': nrm(ks[24], (nb, D_MODEL), 0.01),
        'ffn_w_up': nrm(ks[25], (DEPTH, D_MODEL, 2 * D_FF), D_MODEL ** -0.5),
        'ffn_conv_w': nrm(ks[26], (DEPTH, FFN_CONV, 2 * D_FF), FFN_CONV ** -0.5),
        'ffn_conv_b': nrm(ks[27], (DEPTH, 2 * D_FF), 0.01),
        'ffn_w_down': nrm(ks[28], (DEPTH, D_FF, D_MODEL), D_FF ** -0.5),
    }


def reference(x, norm_mix_pre, norm_mix_post, norm_ffn_pre, norm_ffn_post,
              ssd_w_in, ssd_conv_w, ssd_conv_b, ssd_dt_bias, ssd_a_log, ssd_d, ssd_norm, ssd_w_out,
              lru_w_in, lru_b_in, lru_conv_w, lru_conv_b, lru_w_gx, lru_b_gx, lru_w_ga, lru_b_ga,
              lru_lambda, lru_w_out, lru_b_out,
              ffn_w_up, ffn_conv_w, ffn_conv_b, ffn_w_down):
    for i in range(DEPTH):
        h = rmsnorm(x, norm_mix_pre[i])
        j = i // N_MIXERS
        if i % N_MIXERS == 0:
            m = ssd_mixer(h, ssd_w_in[j], ssd_conv_w[j], ssd_conv_b[j], ssd_dt_bias[j],
                          ssd_a_log[j], ssd_d[j], ssd_norm[j], ssd_w_out[j])
        else:
            m = rglru_mixer(h, lru_w_in[j], lru_b_in[j], lru_conv_w[j], lru_conv_b[j],
                            lru_w_gx[j], lru_b_gx[j], lru_w_ga[j], lru_b_ga[j],
                            lru_lambda[j], lru_w_out[j], lru_b_out[j])
        x = x + rmsnorm(m, norm_mix_post[i])
        h = rmsnorm(x, norm_ffn_pre[i])
        f = conv_ffn(h, ffn_w_up[i], ffn_conv_w[i], ffn_conv_b[i], ffn_w_down[i])
        x = x + rmsnorm(f, norm_ffn_post[i])
    return x
```

```python
import numpy as np
import concourse.bass as bass
import concourse.mybir as mybir
from concourse.bass_utils import run_bass_kernel_spmd

F32 = mybir.dt.float32
BF16 = mybir.dt.bfloat16
AF = mybir.ActivationFunctionType
ALU = mybir.AluOpType

D = 1024
DEPTH = 4
DFF = 4096
EPS = 1e-6

ENGS = ("pe", "act", "dve", "pool", "sp")
EPOCH = 30000
NDSEM = 8


class Prog:
    def __init__(self, nc, same_engine_sync=False):
        self.nc = nc
        self.same_engine_sync = same_engine_sync
        self.ops = {e: [] for e in ENGS}
        self.cnt = {e: 0 for e in ENGS}
        self.epoch = {e: 0 for e in ENGS}
        self.sems = {e: [] for e in ENGS}
        self.dsems = {e: [] for e in ENGS}
        self.ndma = {e: 0 for e in ENGS}
        self.known = {e: {} for e in ENGS}
        self.last_w = {}
        self.readers = {}
        self._ctx = []
        self.fence_vals = {}
        self.small_toks = set()

    def _new_sem(self, name):
        cm = self.nc.semaphore(name)
        s = cm.__enter__()
        self._ctx.append(cm)
        return s

    def _eng_sem(self, e):
        ep = self.epoch[e]
        while len(self.sems[e]) <= ep:
            self.sems[e].append(self._new_sem(f"s_{e}_{len(self.sems[e])}"))
        return self.sems[e][ep]

    def _deps(self, e, reads, writes, force_same=False):
        toks = []
        for k in reads:
            t = self.last_w.get(k)
            if t is not None:
                toks.append(t)
        for k in writes:
            t = self.last_w.get(k)
            if t is not None:
                toks.append(t)
            toks.extend(self.readers.get(k, ()))
        waits = {}
        for (sem, val, te, isdma) in toks:
            if te == e and not isdma:
                if e == "pe" or not (self.same_engine_sync or force_same or (id(sem), val) in self.small_toks):
                    continue
            sid = id(sem)
            if self.known[e].get(sid, 0) >= val:
                continue
            if sid not in waits or waits[sid][1] < val:
                waits[sid] = (sem, val)
        for sid, (sem, val) in waits.items():
            self.known[e][sid] = val
        return list(waits.values())

    def _record(self, tok, reads, writes):
        for k in writes:
            self.last_w[k] = tok
            self.readers[k] = []
        for k in reads:
            if k in writes:
                continue
            lst = self.readers.setdefault(k, [])
            lst.append(tok)
            if len(lst) > 8:
                best = {}
                for t in lst:
                    sid = id(t[0])
                    if sid not in best or best[sid][1] < t[1]:
                        best[sid] = t
                self.readers[k] = list(best.values())

    def op(self, e, fn, reads=(), writes=(), force_same=False, small=False):
        waits = self._deps(e, reads, writes, force_same)
        if self.cnt[e] >= EPOCH:
            self.epoch[e] += 1
            self.cnt[e] = 0
        sem = self._eng_sem(e)
        self.cnt[e] += 1
        tok = (sem, self.cnt[e], e, False)
        if small:
            self.small_toks.add((id(sem), self.cnt[e]))
        self.ops[e].append((waits, fn, sem, 1))
        self._record(tok, reads, writes)
        return tok

    def dma(self, q, out, in_, reads=(), writes=(), nofence=False, **kw):
        if not self.dsems[q]:
            self.dsems[q] = [self._new_sem(f"d_{q}_{i}") for i in range(NDSEM)]
        i = self.ndma[q]
        self.ndma[q] += 1
        sem = self.dsems[q][i % NDSEM]
        waits = self._deps(q, reads, writes)
        prev = 16 * (i // NDSEM)
        if prev > 0 and self.known[q].get(id(sem), 0) < prev:
            waits.append((sem, prev))
            self.known[q][id(sem)] = prev
        tok = (sem, prev + 16, q, True)
        if not nofence:
            self.fence_vals.setdefault(q, {})[i % NDSEM] = (sem, prev + 16)

        def fn(eng, out=out, in_=in_, kw=kw):
            return eng.dma_start(out=out, in_=in_, **kw)

        self.ops[q].append((waits, fn, sem, 16))
        self._record(tok, reads, writes)
        return tok

    def fence(self):
        toks = []
        for f in ENGS:
            if self.sems[f] and self.cnt[f] > 0:
                toks.append((self.sems[f][self.epoch[f]], self.cnt[f]))
            for (sem, val) in self.fence_vals.get(f, {}).values():
                toks.append((sem, val))
        for e in ENGS:
            if not self.ops[e]:
                continue
            waits = []
            own = self.sems[e][self.epoch[e]] if self.sems[e] else None
            for (sem, val) in toks:
                if sem is own:
                    continue
                if self.known[e].get(id(sem), 0) >= val:
                    continue
                waits.append((sem, val))
                self.known[e][id(sem)] = val
            if waits:
                self.ops[e].append((waits, None, None, 0))

    def finish_wait_all(self, e="sp"):
        toks = {}
        for k, t in self.last_w.items():
            sid = id(t[0])
            if sid not in toks or toks[sid][1] < t[1]:
                toks[sid] = t
        waits = []
        for sid, (sem, val, te, isdma) in toks.items():
            if self.known[e].get(sid, 0) >= val:
                continue
            waits.append((sem, val))
        self.ops[e].append((waits, None, None, 0))

    def emit(self):
        nc = self.nc
        engmap = {"pe": "tensor", "act": "scalar", "dve": "vector", "pool": "gpsimd", "sp": "sync"}
        with nc.Block() as block:
            for e in ENGS:
                ops = self.ops[e]
                if not ops:
                    continue

                def body(eng, ops=ops):
                    for (waits, fn, sem, inc) in ops:
                        for (ws, wv) in waits:
                            eng.wait_ge(ws, wv)
                        if fn is None:
                            continue
                        ins = fn(eng)
                        ins.then_inc(sem, inc)

                getattr(block, engmap[e])(body)
        for cm in reversed(self._ctx):
            cm.__exit__(None, None, None)
        self._ctx = []


class Pools:
    def __init__(self, nc):
        self.nc = nc
        self._ctx = []

    def sb(self, name, shape, dt):
        cm = self.nc.sbuf_tensor(name, list(shape), dt)
        t = cm.__enter__()
        self._ctx.append(cm)
        return t

    def ps(self, name, shape, dt):
        cm = self.nc.psum_tensor(name, list(shape), dt)
        t = cm.__enter__()
        self._ctx.append(cm)
        return t

    def close(self):
        for cm in reversed(self._ctx):
            cm.__exit__(None, None, None)
        self._ctx = []


class Arena:
    def __init__(self, pools, nf32):
        self.t = pools.sb("arena", [128, nf32], F32)
        self.n = nf32
        self.off = 0

    def reset(self):
        self.off = 0

    def alloc(self, shape, dt):
        assert shape[0] == 128
        nel = 1
        for d_ in shape[1:]:
            nel *= d_
        nf = nel if dt == F32 else (nel + 1) // 2
        nf = (nf + 15) // 16 * 16
        assert self.off + nf <= self.n, ("arena overflow", self.off, nf, self.n)
        v = self.t[:, self.off:self.off + nf]
        self.off += nf
        if dt != F32:
            v = v.bitcast(dt)
        v = v[:, 0:nel]
        if len(shape) == 3:
            v = v.rearrange("p (a b) -> p a b", a=shape[1])
        return v


class VecPack:
    def __init__(self):
        self.cols = []
        self.index = {}
        self.n = 0

    def add(self, name, vec):
        vec = np.asarray(vec, dtype=np.float32).reshape(-1)
        assert vec.size % 128 == 0, (name, vec.size)
        nch = vec.size // 128
        self.index[name] = (self.n, nch)
        self.cols.append(vec.reshape(nch, 128).T)
        self.n += nch

    def add_raw(self, name, arr):
        arr = np.asarray(arr, dtype=np.float32)
        assert arr.shape[0] == 128
        self.index[name] = (self.n, arr.shape[1])
        self.cols.append(arr)
        self.n += arr.shape[1]

    def build(self):
        return np.ascontiguousarray(np.concatenate(self.cols, axis=1))


def windows(S, n, halo):
    out = []
    t = 0
    while t < S:
        m = min(n, S - t)
        out.append((t, m))
        t += m
    return out


class Builder:
    def __init__(self, S, vec_index, nvec, layers=(0, 1, 2, 3), do_mixer=True, do_ffn=True, debug=False):
        self.S = S
        self.debug = debug
        self.dbg = {}
        self.vi = vec_index
        self.layers = layers
        self.do_mixer = do_mixer
        self.do_ffn = do_ffn
        nc = self.nc = bass.Bass("TRN2", target_bir_lowering=False)
        self.P = Pools(nc)
        self.pr = Prog(nc, same_engine_sync=True)
        P, pr = self.P, self.pr
        self.x_in = nc.dram_tensor("x", [S, D], F32, kind="ExternalInput").ap()
        self.y_out = nc.dram_tensor("y", [S, D], F32, kind="ExternalOutput").ap()
        self.vec_d = nc.dram_tensor("vecs", [128, nvec], F32, kind="ExternalInput").ap()
        self.wup_f = nc.dram_tensor("wup", [DEPTH, 16, 128, 4096], F32, kind="ExternalInput").ap()
        self.wdn_f = nc.dram_tensor("wdn", [DEPTH, 4, 128, 8192], F32, kind="ExternalInput").ap()
        self.wup_b = nc.dram_tensor("wup_b", [DEPTH, 16, 128, 4096], BF16, kind="Internal").ap()
        self.wdn_b = nc.dram_tensor("wdn_b", [DEPTH, 4, 128, 8192], BF16, kind="Internal").ap()
        self.swin_f = nc.dram_tensor("swin", [2, 10, 128, 4096], F32, kind="ExternalInput").ap()
        self.swdt_f = nc.dram_tensor("swdt", [2, 128, 256], F32, kind="ExternalInput").ap()
        self.swout_f = nc.dram_tensor("swout", [2, 8, 128, 2048], F32, kind="ExternalInput").ap()
        self.swin_b = nc.dram_tensor("swin_b", [2, 10, 128, 4096], BF16, kind="Internal").ap()
        self.swdt_b = nc.dram_tensor("swdt_b", [2, 128, 256], BF16, kind="Internal").ap()
        self.swout_b = nc.dram_tensor("swout_b", [2, 8, 128, 2048], BF16, kind="Internal").ap()
        self.snorm_d = nc.dram_tensor("snorm", [2, 128, 2048], F32, kind="ExternalInput").ap()
        self.ssm_d = nc.dram_tensor("ssm", [2, 128, 96], F32, kind="ExternalInput").ap()
        self.lwin_f = nc.dram_tensor("lwin", [2, 2560, 1024], F32, kind="ExternalInput").ap()
        self.lwgx_f = nc.dram_tensor("lwgx", [2, 128, 1280], F32, kind="ExternalInput").ap()
        self.lwga_f = nc.dram_tensor("lwga", [2, 128, 1280], F32, kind="ExternalInput").ap()
        self.lwout_f = nc.dram_tensor("lwout", [2, 128, 10240], F32, kind="ExternalInput").ap()
        self.lwin_b = nc.dram_tensor("lwin_b", [2, 2560, 1024], BF16, kind="Internal").ap()
        self.lwgx_b = nc.dram_tensor("lwgx_b", [2, 128, 1280], BF16, kind="Internal").ap()
        self.lwga_b = nc.dram_tensor("lwga_b", [2, 128, 1280], BF16, kind="Internal").ap()
        self.lwout_b = nc.dram_tensor("lwout_b", [2, 128, 10240], BF16, kind="Internal").ap()
        self.xT = [nc.dram_tensor(f"xT{i}", [D, S], F32, kind="Internal").ap() for i in range(2)]
        self.cur = 0
        self.vecs = P.sb("vecs_sb", [128, nvec], F32)
        pr.dma("sp", self.vecs[:], self.vec_d, writes=["vecs"])
        self.ident = P.sb("ident", [128, 128], F32)
        self.ones_f = P.sb("ones_f", [128, 128], F32)
        self.ones_b = P.sb("ones_b", [128, 128], BF16)
        pr.op("pool", lambda e: e.memset(self.ones_f[:], 1.0), writes=["ones_f"])
        pr.op("pool", lambda e: e.memset(self.ones_b[:], 1.0), writes=["ones_b"])
        pr.op("pool", lambda e: e.affine_select(out=self.ident[:], in_=self.ones_f[:], pattern=[[-1, 128]],
                                                compare_op=ALU.is_equal, fill=0.0, base=0, channel_multiplier=1),
              reads=["ones_f"], writes=["ident"])
        self.ident_b = P.sb("ident_b", [128, 128], BF16)
        self.triU = P.sb("triU", [128, 128], F32)
        self.triSL = P.sb("triSL", [128, 128], F32)
        pr.op("pool", lambda e: e.tensor_copy(out=self.ident_b[:], in_=self.ident[:]), reads=["ident"], writes=["ident_b"])
        pr.op("pool", lambda e: e.affine_select(out=self.triU[:], in_=self.ones_f[:], pattern=[[1, 128]],
                                                compare_op=ALU.is_ge, fill=0.0, base=0, channel_multiplier=-1),
              reads=["ones_f"], writes=["triU"])
        pr.op("pool", lambda e: e.affine_select(out=self.triSL[:], in_=self.ones_f[:], pattern=[[-1, 128]],
                                                compare_op=ALU.is_gt, fill=0.0, base=0, channel_multiplier=1),
              reads=["ones_f"], writes=["triSL"])
        self.psall = P.ps("psall", [128, 4096], F32)
        self.psb = [self.psall[:, i * 512:(i + 1) * 512] for i in range(8)]
        self.A = Arena(P, 50176)
        self.cast_jobs = []
        self.cast_done = 0

    def dump(self, name, ap, reads):
        if not self.debug:
            return
        t = self.nc.dram_tensor("dbg_" + name, list(ap.shape), ap.dtype, kind="ExternalOutput").ap()
        self.dbg[name] = t
        self.pr.dma("pool", t, ap, reads=reads, writes=[("dbg", name)])

    def vcol(self, name, c=0, n=1):
        c0, nch = self.vi[name]
        assert c + n <= nch, (name, c, n, nch)
        return self.vecs[:, c0 + c:c0 + c + n]

    def add_cast(self, dst, src, key):
        self.cast_jobs.append((dst, src, key))

    def pump_casts(self, n):
        while n > 0 and self.cast_done < len(self.cast_jobs):
            dst, src, key = self.cast_jobs[self.cast_done]
            self.pr.dma("pool", dst, src, writes=[key], nofence=True, max_dma_last_dim=4096)
            self.cast_done += 1
            n -= 1

    def pump_until(self, key):
        while self.cast_done < len(self.cast_jobs) and key not in self.pr.last_w:
            self.pump_casts(1)

    def phase_in(self):
        pr, P, S = self.pr, self.P, self.S
        pr.fence()
        self.A.reset()
        dst = self.xT[self.cur].rearrange("(c p) t -> p c t", p=128)
        xin = [self.A.alloc([128, D], F32) for i in range(2)]
        xst = [self.A.alloc([128, 8, 512], F32) for i in range(2)]
        nblk = (S + 511) // 512
        for b in range(nblk):
            t0 = b * 512
            nt = min(512, S - t0) // 128
            st = xst[b % 2]
            for tt in range(nt):
                xi = xin[(b * 4 + tt) % 2]
                kxi = ("xin", (b * 4 + tt) % 2)
                pr.dma("sp", xi[:], self.x_in[t0 + tt * 128:t0 + (tt + 1) * 128, :], writes=[kxi])
                for half in range(2):
                    bank = (tt * 2 + half) % 8
                    ps = self.psb[bank]

                    def tr(e, xi=xi, ps=ps, half=half):
                        for q in range(4):
                            c = half * 4 + q
                            i = e.transpose(ps[:, q * 128:(q + 1) * 128], xi[:, c * 128:(c + 1) * 128], self.ident[:])
                        return i
                    pr.op("pe", tr, reads=[kxi, "ident"], writes=[("ps", bank)])
                    eng = "act" if half == 0 else "dve"

                    def ev(e, st=st, ps=ps, half=half, tt=tt, eng=eng):
                        o = st[:, half * 4:(half + 1) * 4, tt * 128:(tt + 1) * 128]
                        i_ = ps[:].rearrange("p (q t) -> p q t", q=4)
                        if eng == "act":
                            return e.activation(out=o, in_=i_, func=AF.Identity)
                        return e.tensor_copy(out=o, in_=i_)
                    pr.op(eng, ev, reads=[("ps", bank)], writes=[("xst", b % 2, half, tt)])
            rk = [("xst", b % 2, h, tt) for h in range(2) for tt in range(nt)]
            pr.dma("pool", dst[:, :, t0:t0 + nt * 128], st[:, :, 0:nt * 128], reads=rk, writes=[("xT", self.cur)])

    def phase_out(self):
        pr, P, S = self.pr, self.P, self.S
        pr.fence()
        self.A.reset()
        src = self.xT[self.cur].rearrange("(c p) t -> p c t", p=128)
        xld = [self.A.alloc([128, 8, 512], F32) for i in range(2)]
        yst = [self.A.alloc([128, D], F32) for i in range(2)]
        nblk = (S + 511) // 512
        for b in range(nblk):
            t0 = b * 512
            nt = min(512, S - t0) // 128
            xl = xld[b % 2]
            kx = ("xld", b % 2)
            pr.dma("sp", xl[:, :, 0:nt * 128], src[:, :, t0:t0 + nt * 128], reads=[("xT", self.cur)], writes=[kx])
            for tt in range(nt):
                ys = yst[(b * 4 + tt) % 2]
                ky = ("yst", (b * 4 + tt) % 2)
                for half in range(2):
                    bank = (tt * 2 + half) % 8
                    ps = self.psb[bank]

                    def tr(e, xl=xl, ps=ps, half=half, tt=tt):
                        for q in range(4):
                            c = half * 4 + q
                            i = e.transpose(ps[:, q * 128:(q + 1) * 128], xl[:, c, tt * 128:(tt + 1) * 128], self.ident[:])
                        return i
                    pr.op("pe", tr, reads=[kx, "ident"], writes=[("ps", bank)])
                    eng = "act" if half == 0 else "dve"

                    def ev(e, ys=ys, ps=ps, half=half, eng=eng):
                        o = ys[:, half * 512:(half + 1) * 512]
                        if eng == "act":
                            return e.activation(out=o, in_=ps[:], func=AF.Identity)
                        return e.tensor_copy(out=o, in_=ps[:])
                    pr.op(eng, ev, reads=[("ps", bank)], writes=[(ky, half)])
                pr.dma("pool", self.y_out[t0 + tt * 128:t0 + (tt + 1) * 128, :], ys[:],
                       reads=[(ky, 0), (ky, 1)], writes=["y"])

    def prenorm(self, xblk, kx, C, wname, hT, kh, sq, ksq, rstd, krstd, stat_bank):
        pr = self.pr
        ps = self.psb[stat_bank]
        for c in range(8):
            pr.op("act", lambda e, c=c: e.activation(out=sq[:, c, 0:C], in_=xblk[:, c, 0:C], func=AF.Square),
                  reads=[kx], writes=[(ksq, c)])

        def st(e):
            for c in range(8):
                i = e.matmul(ps[:, 0:C], self.ones_b[:], sq[:, c, 0:C], start=(c == 0), stop=(c == 7))
            return i
        pr.op("pe", st, reads=[(ksq, c) for c in range(8)] + ["ones_b"], writes=[("ps", stat_bank)])
        pr.op("act", lambda e: e.activation(out=rstd[:, 0:C], in_=ps[:, 0:C], func=AF.Ln, scale=1.0 / D, bias=self.vcol("eps")),
              reads=[("ps", stat_bank), "vecs"], writes=[krstd])
        pr.op("act", lambda e: e.activation(out=rstd[:, 0:C], in_=rstd[:, 0:C], func=AF.Exp, scale=-0.5),
              reads=[krstd], writes=[krstd])
        for c in range(8):
            pr.op("dve", lambda e, c=c: e.scalar_tensor_tensor(out=hT[:, c, 0:C], in0=xblk[:, c, 0:C], scalar=self.vcol(wname, c),
                                                               in1=rstd[:, 0:C], op0=ALU.mult, op1=ALU.mult),
                  reads=[kx, krstd, "vecs"], writes=[(kh, c)])

    def phase_ffn(self, li):
        pr, P, S = self.pr, self.P, self.S
        pr.fence()
        self.A.reset()
        src = self.xT[self.cur].rearrange("(c p) t -> p c t", p=128)
        dst = self.xT[1 - self.cur].rearrange("(c p) t -> p c t", p=128)
        xblk = self.A.alloc([128, 8, 512], F32)
        sq = self.A.alloc([128, 8, 512], BF16)
        hT = self.A.alloc([128, 8, 512], BF16)
        rstd = self.A.alloc([128, 512], F32)
        rstd2 = self.A.alloc([128, 512], F32)
        gvT = self.A.alloc([128, 32, 512], BF16)
        fT = self.A.alloc([128, 8, 512], F32)
        xnew = self.A.alloc([128, 8, 512], F32)
        NWU, NWD = 3, 2
        wus = [self.A.alloc([128, 8, 512], BF16) for i in range(NWU)]
        wds = [self.A.alloc([128, 32, 256], BF16) for i in range(NWD)]
        tg = [self.A.alloc([128, 512], F32) for i in range(4)]
        tv = [self.A.alloc([128, 512], F32) for i in range(4)]
        gg = [self.A.alloc([128, 512], F32) for i in range(4)]
        nwu = 0
        nwd = 0
        npair = 0
        wins = windows(S, 464, 2)
        for wi, (t0, n) in enumerate(wins):
            C = n + 2
            kx = (f"f{li}_xblk",)
            if t0 == 0:
                pr.op("pool", lambda e: e.memset(xblk[:, :, 0:2], 0.0), writes=[(kx, "halo")])
                pr.dma("sp", xblk[:, :, 2:C], src[:, :, 0:n], reads=[("xT", self.cur)], writes=[kx])
                kxr = [kx, (kx, "halo")]
            else:
                pr.dma("sp", xblk[:, :, 0:C], src[:, :, t0 - 2:t0 + n], reads=[("xT", self.cur)], writes=[kx, (kx, "halo")])
                kxr = [kx, (kx, "halo")]
            self._prenorm_multi(xblk, kxr, C, f"nfpre{li}", hT, f"f{li}_hT", sq, f"f{li}_sq", rstd, f"f{li}_rstd", 6)
            khs = [(f"f{li}_hT", c) for c in range(8)]
            deferred, deferred_next = [], []
            for j2 in range(16):
                ws = wus[nwu % NWU]
                kw = ("wu", li, nwu % NWU)
                nwu += 1
                self.pump_until(("wup_b", li, j2))
                pr.dma("sp", ws[:], self.wup_b[li, j2].rearrange("p (k c) -> p k c", k=8),
                       reads=[("wup_b", li, j2)], writes=[kw])
                for jj in range(2):
                    j = j2 * 2 + jj
                    pb = (npair % 4) * 2
                    npair += 1
                    gps, vps = self.psb[pb], self.psb[pb + 1]

                    def mmg(e, ws=ws, gps=gps, jj=jj, C=C, off=0):
                        for k in range(8):
                            i = e.matmul(gps[:, 0:C], ws[:, k, off + jj * 128:off + (jj + 1) * 128], hT[:, k, 0:C],
                                         start=(k == 0), stop=(k == 7))
                        return i
                    pr.op("pe", mmg, reads=[kw] + khs, writes=[("ps", pb)])
                    pr.op("pe", lambda e, ws=ws, vps=vps, jj=jj, C=C: mmg(e, ws, vps, jj, C, 256),
                          reads=[kw] + khs, writes=[("ps", pb + 1)])
                    par = j % 4
                    tgt, tvt, ggt = tg[par], tv[par], gg[par]
                    ktg, ktv, kgg = ("tg", par), ("tv", par), ("gg", par)
                    jv = 32 + j
                    pr.op("act", lambda e, gps=gps, tgt=tgt, j=j, C=C, n=n: e.activation(
                        out=tgt[:, 0:n], in_=gps[:, 2:C], func=AF.Identity,
                        scale=self.vcol(f"fcw{li}_2", j), bias=self.vcol(f"fcb{li}", j)),
                        reads=[("ps", pb), "vecs"], writes=[ktg])
                    pr.op("act", lambda e, vps=vps, tvt=tvt, jv=jv, C=C, n=n: e.activation(
                        out=tvt[:, 0:n], in_=vps[:, 2:C], func=AF.Identity,
                        scale=self.vcol(f"fcw{li}_2", jv), bias=self.vcol(f"fcb{li}", jv)),
                        reads=[("ps", pb + 1), "vecs"], writes=[ktv])
                    pr.op("dve", lambda e, gps=gps, tgt=tgt, j=j, C=C, n=n: e.scalar_tensor_tensor(
                        out=tgt[:, 0:n], in0=gps[:, 1:C - 1], scalar=self.vcol(f"fcw{li}_1", j), in1=tgt[:, 0:n],
                        op0=ALU.mult, op1=ALU.add), reads=[("ps", pb), ktg, "vecs"], writes=[ktg])
                    pr.op("dve", lambda e, gps=gps, tgt=tgt, j=j, C=C, n=n: e.scalar_tensor_tensor(
                        out=tgt[:, 0:n], in0=gps[:, 0:C - 2], scalar=self.vcol(f"fcw{li}_0", j), in1=tgt[:, 0:n],
                        op0=ALU.mult, op1=ALU.add), reads=[("ps", pb), ktg, "vecs"], writes=[ktg])
                    deferred_next.append(lambda tgt=tgt, ggt=ggt, n=n, ktg=ktg, kgg=kgg: pr.op(
                        "act", lambda e: e.activation(out=ggt[:, 0:n], in_=tgt[:, 0:n], func=AF.Gelu_apprx_tanh),
                        reads=[ktg], writes=[kgg]))
                    pr.op("dve", lambda e, vps=vps, tvt=tvt, jv=jv, C=C, n=n: e.scalar_tensor_tensor(
                        out=tvt[:, 0:n], in0=vps[:, 1:C - 1], scalar=self.vcol(f"fcw{li}_1", jv), in1=tvt[:, 0:n],
                        op0=ALU.mult, op1=ALU.add), reads=[("ps", pb + 1), ktv, "vecs"], writes=[ktv])
                    pr.op("dve", lambda e, vps=vps, tvt=tvt, jv=jv, C=C, n=n: e.scalar_tensor_tensor(
                        out=tvt[:, 0:n], in0=vps[:, 0:C - 2], scalar=self.vcol(f"fcw{li}_0", jv), in1=tvt[:, 0:n],
                        op0=ALU.mult, op1=ALU.add), reads=[("ps", pb + 1), ktv, "vecs"], writes=[ktv])
                    deferred_next.append(lambda ggt=ggt, tvt=tvt, j=j, n=n, kgg=kgg, ktv=ktv: pr.op(
                        "dve", lambda e: e.tensor_tensor(out=gvT[:, j, 0:n], in0=ggt[:, 0:n], in1=tvt[:, 0:n], op=ALU.mult),
                        reads=[kgg, ktv], writes=[("gvT", j)]))
                    for f_ in deferred:
                        f_()
                    deferred, deferred_next = deferred_next, []
            for f_ in deferred:
                f_()
            deferred = []
            kgv = [("gvT", j) for j in range(32)]
            sbank = 7
            sps = self.psb[sbank]
            for o2 in range(4):
                wd = wds[nwd % NWD]
                kwd = ("wd", li, nwd % NWD)
                nwd += 1
                self.pump_until(("wdn_b", li, o2))
                pr.dma("sp", wd[:], self.wdn_b[li, o2].rearrange("p (k c) -> p k c", k=32),
                       reads=[("wdn_b", li, o2)], writes=[kwd])
                for oo in range(2):
                    o = o2 * 2 + oo
                    fb = (npair % 3) * 2
                    npair += 1
                    fps = self.psb[fb]

                    def mmd(e, wd=wd, fps=fps, oo=oo, n=n):
                        for k in range(32):
                            i = e.matmul(fps[:, 0:n], wd[:, k, oo * 128:(oo + 1) * 128], gvT[:, k, 0:n],
                                         start=(k == 0), stop=(k == 31))
                        return i
                    pr.op("pe", mmd, reads=[kwd] + kgv, writes=[("ps", fb)])
                    for f_ in deferred:
                        f_()
                    deferred = []
                    pr.op("act", lambda e, fps=fps, o=o, n=n: e.activation(out=fT[:, o, 0:n], in_=fps[:, 0:n], func=AF.Identity,
                                                                            scale=self.vcol(f"nfpost{li}", o)),
                          reads=[("ps", fb), "vecs"], writes=[("fT", o)])
                    pr.op("act", lambda e, fps=fps, o=o, n=n: e.activation(out=sq[:, o, 0:n], in_=fps[:, 0:n], func=AF.Square),
                          reads=[("ps", fb)], writes=[(f"f{li}_sq", o)])
                    for f_ in deferred:
                        f_()
                    deferred = [lambda o=o, n=n: pr.op(
                        "pe", lambda e: e.matmul(sps[:, 0:n], self.ones_b[:], sq[:, o, 0:n], start=(o == 0), stop=(o == 7)),
                        reads=[(f"f{li}_sq", o), "ones_b"], writes=[("ps", sbank)])]
            for f_ in deferred:
                f_()
            deferred = []
            pr.op("act", lambda e, n=n: e.activation(out=rstd2[:, 0:n], in_=sps[:, 0:n], func=AF.Ln, scale=1.0 / D, bias=self.vcol("eps")),
                  reads=[("ps", sbank), "vecs"], writes=["rstd2"])
            pr.op("act", lambda e, n=n: e.activation(out=rstd2[:, 0:n], in_=rstd2[:, 0:n], func=AF.Exp, scale=-0.5),
                  reads=["rstd2"], writes=["rstd2"])
            for o in range(8):
                pr.op("dve", lambda e, o=o, n=n: e.tensor_tensor(out=fT[:, o, 0:n], in0=fT[:, o, 0:n], in1=rstd2[:, 0:n], op=ALU.mult),
                      reads=[("fT", o), "rstd2"], writes=[("fT", o)])
                pr.op("dve", lambda e, o=o, n=n, C=C: e.tensor_tensor(out=xnew[:, o, 0:n], in0=fT[:, o, 0:n], in1=xblk[:, o, 2:C], op=ALU.add),
                      reads=[("fT", o)] + kxr, writes=[("xnew", o)])
            pr.dma("pool", dst[:, :, t0:t0 + n], xnew[:, :, 0:n], reads=[("xnew", o) for o in range(8)],
                   writes=[("xT", 1 - self.cur)])
            self.pump_casts(6)
        self.cur = 1 - self.cur

    def phase_lru(self, li):
        pr, S, A = self.pr, self.S, self.A
        pr.fence()
        A.reset()
        j = li // 2
        T = 512
        src = self.xT[self.cur].rearrange("(c p) t -> p c t", p=128)
        dst = self.xT[1 - self.cur].rearrange("(c p) t -> p c t", p=128)
        win = A.alloc([128, 20, 1024], BF16)
        wgx = A.alloc([128, 10, 128], BF16)
        wga = A.alloc([128, 10, 128], BF16)
        wout = A.alloc([128, 10, 1024], BF16)
        for key in (("lwin", j), ("lwgx", j), ("lwga", j), ("lwout", j)):
            self.pump_until(key)
        pr.dma("sp", win, self.lwin_b[j].rearrange("(o p) c -> p o c", p=128), reads=[("lwin", j)], writes=["l_win"])
        pr.dma("sp", wgx, self.lwgx_b[j].rearrange("p (h c) -> p h c", h=10), reads=[("lwgx", j)], writes=["l_wgx"])
        pr.dma("sp", wga, self.lwga_b[j].rearrange("p (h c) -> p h c", h=10), reads=[("lwga", j)], writes=["l_wga"])
        pr.dma("sp", wout, self.lwout_b[j].rearrange("p (k c) -> p k c", k=10), reads=[("lwout", j)], writes=["l_wout"])
        xblk = A.alloc([128, 8, T], F32)
        sq = A.alloc([128, 8, T], BF16)
        hT = A.alloc([128, 8, T], BF16)
        rstd = A.alloc([128, T], F32)
        rstd2 = A.alloc([128, T], F32)
        hy = A.alloc([128, 10, T], BF16)
        mT = A.alloc([128, 8, T], F32)
        halo = A.alloc([128, 10, 3], F32)
        hstate = A.alloc([128, 10], F32)
        c8 = A.alloc([128, 10], F32)
        c16 = A.alloc([128, 10], F32)
        NB = 4
        ybr = [A.alloc([128, T], F32) for _ in range(NB)]
        XP = [A.alloc([128, T + 3], F32) for _ in range(NB)]
        tcv = [A.alloc([128, T], F32) for _ in range(NB)]
        xbb = [A.alloc([128, T], BF16) for _ in range(NB)]
        gx = [A.alloc([128, T], F32) for _ in range(2)]
        ga = [A.alloc([128, T], F32) for _ in range(2)]
        at = [A.alloc([128, T], F32) for _ in range(2)]
        mu = [A.alloc([128, T], F32) for _ in range(2)]
        bt = [A.alloc([128, T], F32) for _ in range(2)]
        hs = [A.alloc([128, T], F32) for _ in range(2)]
        lam = self.vcol(f"llam{j}", 0, 10)
        pr.op("act", lambda e: e.activation(out=c8, in_=lam, func=AF.Exp, scale=-1.0), reads=["vecs"], writes=["l_c8"], small=True)
        pr.op("act", lambda e: e.activation(out=c8, in_=c8, func=AF.Ln, bias=self.vcol("one")), reads=["l_c8", "vecs"], writes=["l_c8"], small=True)
        pr.op("dve", lambda e: e.tensor_scalar(out=c16, in0=c8, scalar1=-16.0, scalar2=None, op0=ALU.mult), reads=["l_c8"], writes=["l_c16"], small=True)
        pr.op("dve", lambda e: e.tensor_scalar(out=c8, in0=c8, scalar1=-8.0, scalar2=None, op0=ALU.mult), reads=["l_c8", "l_c16"], writes=["l_c8"], small=True)
        pr.op("pool", lambda e: e.memset(halo, 0.0), writes=["l_halo"])
        pr.op("pool", lambda e: e.memset(hstate, 0.0), writes=["l_hstate"])
        nbank = 0
        it = 0
        for wi in range(S // T):
            t0 = wi * T
            kx = ("l_xblk",)
            pr.dma("sp", xblk, src[:, :, t0:t0 + T], reads=[("xT", self.cur)], writes=[kx])
            self._prenorm_multi(xblk, [kx], T, f"nmpre{li}", hT, "l_hT", sq, "l_sq", rstd, "l_rstd", 6)
            khs = [("l_hT", c) for c in range(8)]
            def mmin(e, ps, oc):
                for k in range(8):
                    i = e.matmul(ps[:, 0:T], win[:, oc, k * 128:(k + 1) * 128], hT[:, k, :], start=(k == 0), stop=(k == 7))
                return i

            def lru_s1(c, p):
                nonlocal nbank
                yb, xb = nbank % 6, (nbank + 1) % 6
                nbank += 2
                yps, xps = self.psb[yb], self.psb[xb]
                ybt, XPt, tct, xbt = ybr[p], XP[p], tcv[p], xbb[p]
                K = lambda n: (n, p)
                pr.op("pe", lambda e: mmin(e, yps, c), reads=["l_win"] + khs, writes=[("ps", yb)])
                pr.op("pe", lambda e: mmin(e, xps, 10 + c), reads=["l_win"] + khs, writes=[("ps", xb)])
                pr.op("act", lambda e: e.activation(out=ybt, in_=yps[:, 0:T], func=AF.Gelu_apprx_tanh, bias=self.vcol(f"lbin{j}", c)),
                      reads=[("ps", yb), "vecs"], writes=[K("ybr")])
                pr.op("pool", lambda e: e.tensor_copy(out=XPt[:, 0:3], in_=halo[:, c, :]), reads=["l_halo"], writes=[K("XPh")])
                pr.op("act", lambda e: e.activation(out=XPt[:, 3:T + 3], in_=xps[:, 0:T], func=AF.Identity, bias=self.vcol(f"lbin{j}", 10 + c)),
                      reads=[("ps", xb), "vecs"], writes=[K("XP")])
                pr.op("pool", lambda e: e.tensor_copy(out=halo[:, c, :], in_=XPt[:, T:T + 3]), reads=[K("XP"), K("XPh")], writes=["l_halo"])
                pr.op("dve", lambda e: e.tensor_scalar(out=tct, in0=XPt[:, 3:T + 3], scalar1=self.vcol(f"lcw{j}_3", c),
                                                       scalar2=self.vcol(f"lcb{j}", c), op0=ALU.mult, op1=ALU.add),
                      reads=[K("XP"), "vecs"], writes=[K("tcv")])
                for tap in (2, 1, 0):
                    pr.op("dve", lambda e, tap=tap: e.scalar_tensor_tensor(
                        out=tct, in0=XPt[:, tap:tap + T], scalar=self.vcol(f"lcw{j}_{tap}", c), in1=tct, op0=ALU.mult, op1=ALU.add),
                        reads=[K("XP"), K("XPh"), K("tcv"), "vecs"], writes=[K("tcv")])
                pr.op("pool", lambda e: e.tensor_copy(out=xbt, in_=tct), reads=[K("tcv")], writes=[K("xbb")])

            def lru_s2(cs_, ps_):
                nonlocal nbank
                ctx = []
                for c, p in zip(cs_, ps_):
                    gxb, gab = nbank % 6, (nbank + 1) % 6
                    nbank += 2
                    q2 = c % 2
                    ctx.append(dict(c=c, p=p, q2=q2, gxb=gxb, gab=gab, gxps=self.psb[gxb], gaps=self.psb[gab],
                                    ybt=ybr[p], tct=tcv[p], xbt=xbb[p], gxt=gx[q2], gat=ga[q2], att=at[q2], mut=mu[q2],
                                    btt=bt[q2], hst=hs[q2]))
                for d_ in ctx:
                    c, p, q2 = d_["c"], d_["p"], d_["q2"]
                    pr.op("pe", lambda e, d_=d_: e.matmul(d_["gxps"][:, 0:T], wgx[:, d_["c"], :], d_["xbt"], start=True, stop=True),
                          reads=["l_wgx", ("xbb", p)], writes=[("ps", d_["gxb"])])
                    pr.op("pe", lambda e, d_=d_: e.matmul(d_["gaps"][:, 0:T], wga[:, d_["c"], :], d_["xbt"], start=True, stop=True),
                          reads=["l_wga", ("xbb", p)], writes=[("ps", d_["gab"])])
                for d_ in ctx:
                    c, p, q2 = d_["c"], d_["p"], d_["q2"]
                    pr.op("act", lambda e, d_=d_: e.activation(out=d_["gxt"], in_=d_["gxps"][:, 0:T], func=AF.Sigmoid, bias=self.vcol(f"lbgx{j}", d_["c"])),
                          reads=[("ps", d_["gxb"]), "vecs"], writes=[("gx", q2)])
                    pr.op("act", lambda e, d_=d_: e.activation(out=d_["gat"], in_=d_["gaps"][:, 0:T], func=AF.Sigmoid, bias=self.vcol(f"lbga{j}", d_["c"])),
                          reads=[("ps", d_["gab"]), "vecs"], writes=[("ga", q2)])
                for d_ in ctx:
                    c, p, q2 = d_["c"], d_["p"], d_["q2"]
                    pr.op("act", lambda e, d_=d_: e.activation(out=d_["att"], in_=d_["gat"], func=AF.Exp, scale=c8[:, d_["c"]:d_["c"] + 1]),
                          reads=[("ga", q2), "l_c8"], writes=[("at", q2)])
                    pr.op("dve", lambda e, d_=d_: e.tensor_tensor(out=d_["btt"], in0=d_["gxt"], in1=d_["tct"], op=ALU.mult),
                          reads=[("gx", q2), ("tcv", p)], writes=[("bt", q2)])
                    pr.op("dve", lambda e, d_=d_: e.tensor_tensor(out=d_["mut"], in0=d_["att"], in1=d_["att"], op=ALU.mult),
                          reads=[("at", q2)], writes=[("mu", q2)])
                    pr.op("dve", lambda e, d_=d_: e.tensor_scalar(out=d_["mut"], in0=d_["mut"], scalar1=1.0, scalar2=None, op0=ALU.min),
                          reads=[("mu", q2)], writes=[("mu", q2)])
                for d_ in ctx:
                    c, p, q2 = d_["c"], d_["p"], d_["q2"]
                    pr.op("act", lambda e, d_=d_: e.activation(out=d_["mut"], in_=d_["mut"], func=AF.Sqrt, scale=-1.0, bias=self.vcol("one")),
                          reads=[("mu", q2), "vecs"], writes=[("mu", q2)])
                for d_ in ctx:
                    c, p, q2 = d_["c"], d_["p"], d_["q2"]
                    pr.op("dve", lambda e, d_=d_: e.tensor_tensor(out=d_["btt"], in0=d_["btt"], in1=d_["mut"], op=ALU.mult),
                          reads=[("bt", q2), ("mu", q2)], writes=[("bt", q2)])
                    pr.op("dve", lambda e, d_=d_: e.tensor_tensor_scan(out=d_["hst"], data0=d_["att"], data1=d_["btt"],
                                                                       initial=hstate[:, d_["c"]:d_["c"] + 1], op0=ALU.mult, op1=ALU.add),
                          reads=[("at", q2), ("bt", q2), "l_hstate"], writes=[("hs", q2)])
                    pr.op("pool", lambda e, d_=d_: e.tensor_copy(out=hstate[:, d_["c"]:d_["c"] + 1], in_=d_["hst"][:, T - 1:T]),
                          reads=[("hs", q2)], writes=["l_hstate"])
                    pr.op("dve", lambda e, d_=d_: e.tensor_tensor(out=hy[:, d_["c"], :], in0=d_["hst"], in1=d_["ybt"], op=ALU.mult),
                          reads=[("hs", q2), ("ybr", p)], writes=[("l_hy", d_["c"])])

            ps_of = {}
            for pk in range(6):
                if pk < 5:
                    for c in (2 * pk, 2 * pk + 1):
                        ps_of[c] = it % NB
                        it += 1
                        lru_s1(c, ps_of[c])
                if pk >= 1:
                    cs_ = (2 * (pk - 1), 2 * (pk - 1) + 1)
                    lru_s2(cs_, [ps_of[c] for c in cs_])
            khy = [("l_hy", c) for c in range(10)]
            sbank = 7
            sps = self.psb[sbank]
            for o in range(8):
                mb = nbank % 6
                nbank += 1
                mps = self.psb[mb]

                def mmo(e, mps=mps, o=o):
                    for k in range(10):
                        i = e.matmul(mps[:, 0:T], wout[:, k, o * 128:(o + 1) * 128], hy[:, k, :], start=(k == 0), stop=(k == 9))
                    return i
                pr.op("pe", mmo, reads=["l_wout"] + khy, writes=[("ps", mb)])
                pr.op("act", lambda e, mps=mps, o=o: e.activation(out=mT[:, o, :], in_=mps[:, 0:T], func=AF.Identity, bias=self.vcol(f"lbout{j}", o)),
                      reads=[("ps", mb), "vecs"], writes=[("l_mT", o)])
                pr.op("act", lambda e, mps=mps, o=o: e.activation(out=sq[:, o, :], in_=mps[:, 0:T], func=AF.Square, bias=self.vcol(f"lbout{j}", o)),
                      reads=[("ps", mb), "vecs"], writes=[("l_sq", o)])
                pr.op("pe", lambda e, o=o: e.matmul(sps[:, 0:T], self.ones_b[:], sq[:, o, :], start=(o == 0), stop=(o == 7)),
                      reads=[("l_sq", o), "ones_b"], writes=[("ps", sbank)])
            self._resid(sps, sbank, rstd2, "l_rstd2", mT, "l_mT", f"nmpost{li}", xblk, [kx], 0, T, dst, t0)
            self.pump_casts(6)
        self.cur = 1 - self.cur

    def _resid(self, sps, sbank, rstd2, krs, mT, kmT, wname, xblk, kxr, xoff, n, dst, t0):
        pr = self.pr
        pr.op("act", lambda e: e.activation(out=rstd2[:, 0:n], in_=sps[:, 0:n], func=AF.Ln, scale=1.0 / D, bias=self.vcol("eps")),
              reads=[("ps", sbank), "vecs"], writes=[krs])
        pr.op("act", lambda e: e.activation(out=rstd2[:, 0:n], in_=rstd2[:, 0:n], func=AF.Exp, scale=-0.5),
              reads=[krs], writes=[krs])
        for o in range(8):
            pr.op("dve", lambda e, o=o: e.scalar_tensor_tensor(out=mT[:, o, 0:n], in0=mT[:, o, 0:n], scalar=self.vcol(wname, o),
                                                               in1=rstd2[:, 0:n], op0=ALU.mult, op1=ALU.mult),
                  reads=[(kmT, o), krs, "vecs"], writes=[(kmT, o)])
            pr.op("dve", lambda e, o=o: e.tensor_tensor(out=mT[:, o, 0:n], in0=mT[:, o, 0:n], in1=xblk[:, o, xoff:xoff + n], op=ALU.add),
                  reads=[(kmT, o)] + kxr, writes=[(kmT, o)])
        pr.dma("pool", dst[:, :, t0:t0 + n], mT[:, :, 0:n], reads=[(kmT, o) for o in range(8)], writes=[("xT", 1 - self.cur)])

    def phase_ssd(self, li):
        pr, S, A = self.pr, self.S, self.A
        pr.fence()
        A.reset()
        j = li // 2
        T = 512
        NQ = T // 128
        src = self.xT[self.cur].rearrange("(c p) t -> p c t", p=128)
        dst = self.xT[1 - self.cur].rearrange("(c p) t -> p c t", p=128)
        for b_ in range(10):
            self.pump_until(("swin", j, b_))
        self.pump_until(("swdt", j))
        for o in range(8):
            self.pump_until(("swout", j, o))
        wdt = A.alloc([128, 8, 32], BF16)
        pr.dma("sp", wdt, self.swdt_b[j].rearrange("p (k c) -> p k c", k=8), reads=[("swdt", j)], writes=["s_wdt"])
        normw = A.alloc([128, 2048], F32)
        pr.dma("sp", normw, self.snorm_d[j], writes=["s_normw"])
        sm = A.alloc([128, 96], F32)
        pr.dma("sp", sm, self.ssm_d[j], writes=["s_sm"])
        a_b = A.alloc([128, 32], F32)
        pr.op("act", lambda e: e.activation(out=a_b, in_=sm[:, 32:64], func=AF.Exp), reads=["s_sm"], writes=["s_ab"], small=True)
        pr.op("dve", lambda e: e.tensor_scalar(out=a_b, in0=a_b, scalar1=-1.0, scalar2=None, op0=ALU.mult), reads=["s_ab"], writes=["s_ab"], small=True)
        halo = A.alloc([128, 24, 3], F32)
        pr.op("pool", lambda e: e.memset(halo, 0.0), writes=["s_halo"])
        Sst = A.alloc([128, 2048], F32)
        Sbf = A.alloc([128, 2048], BF16)
        pr.op("pool", lambda e: e.memset(Sst, 0.0), writes=[("s_S", g) for g in range(4)])
        pr.op("pool", lambda e: e.memset(Sbf, 0.0), writes=[("s_Sbf", g) for g in range(4)])
        xblk = A.alloc([128, 8, T], F32)
        hT = A.alloc([128, 8, T], BF16)
        rstd = A.alloc([128, T], F32)
        rstd2 = rstd
        zs = A.alloc([128, NQ, 2048], BF16)
        mT = zs.bitcast(F32).rearrange("p q (o t) -> p (q o) t", o=2)
        xcT = A.alloc([128, 16, T], BF16)
        BT = A.alloc([128, 4, T], BF16)
        CT = A.alloc([128, 4, T], BF16)
        ynT = A.alloc([128, 16, T], BF16)
        sq = ynT[:, 0:8, :]
        sqp = [A.alloc([128, T], BF16) for _ in range(1)]
        NW = 2
        wsl = [A.alloc([128, 8, 512], BF16) for _ in range(NW)]
        wos = [A.alloc([128, 16, 128], BF16) for _ in range(2)]
        XP = [A.alloc([128, T + 3], F32) for _ in range(2)]
        tcv = [A.alloc([128, T], F32) for _ in range(2)]
        vdt = A.alloc([128, 32], F32)
        dts = A.alloc([128, 32], F32)
        das_ = [A.alloc([128, 32], F32) for _ in range(2)]
        css = A.alloc([128, 32], F32)
        ecs_ = [A.alloc([128, 32], F32) for _ in range(2)]
        dout = A.alloc([128, 32], F32)
        etot_ = [A.alloc([128, 32], F32) for _ in range(2)]
        xdt_ = [A.alloc([128, 2048], BF16) for _ in range(2)]
        xDb_ = [A.alloc([128, 2048], BF16) for _ in range(2)]
        xdd_ = [A.alloc([128, 2048], BF16) for _ in range(2)]
        Btm_ = [A.alloc([128, 512], BF16) for _ in range(2)]
        rhsM = [A.alloc([128, 8, 128], F32) for _ in range(2)]
        Eg = [A.alloc([128, 8, 128], BF16) for _ in range(2)]
        CBm = [A.alloc([128, 128], F32) for _ in range(2)]
        scT = [A.alloc([128, 8, 128], BF16) for _ in range(2)]
        yoffs = [A.alloc([128, 512], F32) for _ in range(1)]
        ysb = A.alloc([128, 2048], F32)
        ssq = A.alloc([128, 4], F32)
        gsc2 = yoffs
        rs4 = A.alloc([128, 4], F32)
        gn = A.alloc([128, 2048], BF16)
        self._nb = getattr(self, "_nb", 0)

        def bank():
            b = self._nb % 7
            self._nb += 1
            return b
        nws = 0
        nwo = 0
        nxp = 0
        ngr = 0
        for wi in range(S // T):
            t0 = wi * T
            kx = ("s_xblk",)
            pr.dma("sp", xblk, src[:, :, t0:t0 + T], reads=[("xT", self.cur)], writes=[kx])
            self._prenorm_multi(xblk, [kx], T, f"nmpre{li}", hT, "s_hT", sq, "s_ynT", rstd, "s_rstd", 7)
            khs = [("s_hT", c) for c in range(8)]
            sdef = []
            for blk in range(10):
                ws = wsl[nws % NW]
                kw = ("s_w", nws % NW)
                nws += 1
                pr.dma("sp", ws, self.swin_b[j, blk].rearrange("p (k c) -> p k c", k=8), reads=[("swin", j, blk)], writes=[kw])
                if blk < 4:
                    for q in range(NQ):
                        zb = bank()
                        zps = self.psb[zb]

                        def mmz(e, zps=zps, ws=ws, q=q):
                            for k in range(8):
                                i = e.matmul(zps[:, :], hT[:, k, q * 128:(q + 1) * 128], ws[:, k, :], start=(k == 0), stop=(k == 7))
                            return i
                        pr.op("pe", mmz, reads=[kw] + khs, writes=[("ps", zb)])
                        pr.op("act", lambda e, zps=zps, q=q, blk=blk: e.activation(out=zs[:, q, blk * 512:(blk + 1) * 512], in_=zps[:, :], func=AF.Silu),
                              reads=[("ps", zb)], writes=[("s_zs", q, blk)])
                else:
                    for i4 in range(4):
                        oc = (blk - 4) * 4 + i4
                        xb = bank()
                        xps = self.psb[xb]

                        def mmx(e, xps=xps, ws=ws, i4=i4):
                            for k in range(8):
                                i = e.matmul(xps[:, :], ws[:, k, i4 * 128:(i4 + 1) * 128], hT[:, k, :], start=(k == 0), stop=(k == 7))
                            return i
                        pr.op("pe", mmx, reads=[kw] + khs, writes=[("ps", xb)])
                        p = nxp % 2
                        nxp += 1
                        XPt, tct = XP[p], tcv[p]
                        pr.op("pool", lambda e, XPt=XPt, oc=oc: e.tensor_copy(out=XPt[:, 0:3], in_=halo[:, oc, :]), reads=["s_halo"], writes=[("s_XPh", p)])
                        pr.op("act", lambda e, XPt=XPt, xps=xps: e.activation(out=XPt[:, 3:T + 3], in_=xps[:, :], func=AF.Identity),
                              reads=[("ps", xb)], writes=[("s_XP", p)])
                        pr.op("pool", lambda e, XPt=XPt, oc=oc: e.tensor_copy(out=halo[:, oc, :], in_=XPt[:, T:T + 3]),
                              reads=[("s_XP", p), ("s_XPh", p)], writes=["s_halo"])
                        pr.op("dve", lambda e, XPt=XPt, tct=tct, oc=oc: e.tensor_scalar(out=tct, in0=XPt[:, 3:T + 3], scalar1=self.vcol(f"scw{j}_3", oc),
                                                                                 scalar2=self.vcol(f"scb{j}", oc), op0=ALU.mult, op1=ALU.add),
                              reads=[("s_XP", p), "vecs"], writes=[("s_tcv", p)])
                        for tap in (2, 1, 0):
                            pr.op("dve", lambda e, XPt=XPt, tct=tct, oc=oc, tap=tap: e.scalar_tensor_tensor(
                                out=tct, in0=XPt[:, tap:tap + T], scalar=self.vcol(f"scw{j}_{tap}", oc), in1=tct, op0=ALU.mult, op1=ALU.add),
                                reads=[("s_XP", p), ("s_XPh", p), ("s_tcv", p), "vecs"], writes=[("s_tcv", p)])
                        if oc < 16:
                            o_ap, ko = xcT[:, oc, :], ("s_xcT", oc)
                        elif oc < 20:
                            o_ap, ko = BT[:, oc - 16, :], ("s_BT", oc - 16)
                        else:
                            o_ap, ko = CT[:, oc - 20, :], ("s_CT", oc - 20)
                        for f_ in sdef:
                            f_()
                        sdef = [lambda o_ap=o_ap, tct=tct, p=p, ko=ko: pr.op(
                            "act", lambda e: e.activation(out=o_ap, in_=tct, func=AF.Silu), reads=[("s_tcv", p)], writes=[ko])]
            for f_ in sdef:
                f_()
            sdef = []
            if wi == 0:
                self.dump("hT", hT, khs)
                self.dump("zs0", zs[:, 0, :], [("s_zs", 0, b_) for b_ in range(4)])
                self.dump("xcT", xcT, [("s_xcT", c) for c in range(16)])
                self.dump("BT", BT, [("s_BT", c) for c in range(4)])
                self.dump("CT", CT, [("s_CT", c) for c in range(4)])
            def prep(q, par):
                qs = slice(q * 128, (q + 1) * 128)
                db = bank()
                dps = self.psb[db]

                def mmdt(e, dps=dps, q=q):
                    for k in range(8):
                        i = e.matmul(dps[:, 0:32], hT[:, k, q * 128:(q + 1) * 128], wdt[:, k, :], start=(k == 0), stop=(k == 7))
                    return i
                pr.op("pe", mmdt, reads=["s_wdt"] + khs, writes=[("ps", db)])
                pr.op("dve", lambda e, dps=dps: e.tensor_tensor(out=vdt, in0=dps[:, 0:32], in1=sm[:, 0:32], op=ALU.add),
                      reads=[("ps", db), "s_sm"], writes=["s_vdt"], small=True)
                pr.op("act", lambda e: e.activation(out=vdt, in_=vdt, func=AF.Exp), reads=["s_vdt"], writes=["s_vdt"], small=True)
                pr.op("act", lambda e: e.activation(out=dts, in_=vdt, func=AF.Ln, bias=self.vcol("one")), reads=["s_vdt", "vecs"], writes=["s_dts"], small=True)
                pr.op("dve", lambda e: e.tensor_tensor(out=das_[par], in0=dts, in1=a_b, op=ALU.mult), reads=["s_dts", "s_ab"], writes=[("s_das", par)], small=True)
                cb_ = bank()
                cps = self.psb[cb_]

                def mmcs(e, cps=cps):
                    e.matmul(cps[:, 0:32], self.triU[:], das_[par], start=True, stop=True)
                    return e.matmul(cps[:, 32:64], self.ones_f[:], das_[par], start=True, stop=True)
                pr.op("pe", mmcs, reads=[("s_das", par), "triU", "ones_f"], writes=[("ps", cb_)])
                pr.op("dve", lambda e, cps=cps: e.tensor_copy(out=css, in_=cps[:, 0:32]), reads=[("ps", cb_)], writes=["s_css"], small=True)
                pr.op("act", lambda e: e.activation(out=ecs_[par], in_=css, func=AF.Exp), reads=["s_css"], writes=[("s_ecs", par)], small=True)
                pr.op("dve", lambda e, cps=cps: e.tensor_tensor(out=dout, in0=cps[:, 32:64], in1=css, op=ALU.subtract),
                      reads=[("ps", cb_), "s_css"], writes=["s_dout"], small=True)
                pr.op("act", lambda e: e.activation(out=dout, in_=dout, func=AF.Exp), reads=["s_dout"], writes=["s_dout"], small=True)
                pr.op("act", lambda e, cps=cps: e.activation(out=etot_[par], in_=cps[:, 32:64], func=AF.Exp), reads=[("ps", cb_)], writes=[("s_etot", par)], small=True)
                for hb in range(2):
                    tb = bank()
                    tps = self.psb[tb].bitcast(BF16)

                    def trx(e, tps=tps, hb=hb, q=q):
                        for c8_ in range(8):
                            cx = hb * 8 + c8_
                            i = e.transpose(tps[:, c8_ * 128:(c8_ + 1) * 128], xcT[:, cx, q * 128:(q + 1) * 128], self.ident_b[:])
                        return i
                    pr.op("pe", trx, reads=[("s_xcT", hb * 8 + c) for c in range(8)] + ["ident_b"], writes=[("ps", tb)])
                    h0 = hb * 16
                    pr.op("dve", lambda e, tps=tps, hb=hb, h0=h0: e.tensor_tensor(
                        out=xdt_[par][:, hb * 1024:(hb + 1) * 1024].rearrange("p (h d) -> p h d", h=16),
                        in0=tps.rearrange("p (h d) -> p h d", h=16),
                        in1=dts[:, h0:h0 + 16].unsqueeze(2).to_broadcast([128, 16, 64]), op=ALU.mult),
                        reads=[("ps", tb), "s_dts"], writes=[("s_xdt", par, hb)])
                    pr.op("dve", lambda e, tps=tps, hb=hb, h0=h0: e.tensor_tensor(
                        out=xDb_[par][:, hb * 1024:(hb + 1) * 1024].rearrange("p (h d) -> p h d", h=16),
                        in0=tps.rearrange("p (h d) -> p h d", h=16),
                        in1=sm[:, 64 + h0:64 + h0 + 16].unsqueeze(2).to_broadcast([128, 16, 64]), op=ALU.mult),
                        reads=[("ps", tb), "s_sm"], writes=[("s_xDb", par, hb)])
                    pr.op("pool", lambda e, hb=hb, h0=h0: e.tensor_tensor(
                        out=xdd_[par][:, hb * 1024:(hb + 1) * 1024].rearrange("p (h d) -> p h d", h=16),
                        in0=xdt_[par][:, hb * 1024:(hb + 1) * 1024].rearrange("p (h d) -> p h d", h=16),
                        in1=dout[:, h0:h0 + 16].unsqueeze(2).to_broadcast([128, 16, 64]), op=ALU.mult),
                        reads=[("s_xdt", par, hb), "s_dout"], writes=[("s_xdd", par, hb)])
                bb = bank()
                bps = self.psb[bb].bitcast(BF16)

                def trb(e, bps=bps, q=q):
                    for g in range(4):
                        i = e.transpose(bps[:, g * 128:(g + 1) * 128], BT[:, g, q * 128:(q + 1) * 128], self.ident_b[:])
                    return i
                pr.op("pe", trb, reads=[("s_BT", g) for g in range(4)] + ["ident_b"], writes=[("ps", bb)])
                pr.op("act", lambda e, bps=bps: e.activation(out=Btm_[par], in_=bps[:, 0:512], func=AF.Identity), reads=[("ps", bb)], writes=[("s_Btm", par)])
            prep(0, 0)
            for q in range(NQ):
                qs = slice(q * 128, (q + 1) * 128)
                par = q % 2
                if q + 1 < NQ:
                    prep(q + 1, (q + 1) % 2)
                hbk_of = lambda g: g // 2

                def stageA(g, q=q, qs=qs, par=par):
                    pg = g % 2
                    gh = slice(g * 8, (g + 1) * 8)
                    rM, Et, CBt, sct = rhsM[pg], Eg[pg], CBm[pg], scT[pg]
                    pr.op("dve", lambda e: e.tensor_tensor(
                        out=rM, in0=self.triU[:].unsqueeze(1).to_broadcast([128, 8, 128]),
                        in1=das_[par][:, gh].unsqueeze(2).to_broadcast([128, 8, 128]), op=ALU.mult),
                        reads=[("s_das", par), "triU"], writes=[("s_rhsM", pg)])
                    d0, d1 = bank(), bank()

                    def mmD(e):
                        e.matmul(self.psb[d0][:, :], self.triSL[:], rM[:, 0:4, :], start=True, stop=True)
                        return e.matmul(self.psb[d1][:, :], self.triSL[:], rM[:, 4:8, :], start=True, stop=True)
                    pr.op("pe", mmD, reads=[("s_rhsM", pg), "triSL"], writes=[("ps", d0), ("ps", d1)])
                    pr.op("act", lambda e: e.activation(out=Et[:, 0:4, :], in_=self.psb[d0][:, :].rearrange("p (h l) -> p h l", h=4), func=AF.Exp),
                          reads=[("ps", d0)], writes=[("s_E", pg, 0)])
                    pr.op("act", lambda e: e.activation(out=Et[:, 4:8, :], in_=self.psb[d1][:, :].rearrange("p (h l) -> p h l", h=4), func=AF.Exp),
                          reads=[("ps", d1)], writes=[("s_E", pg, 1)])
                    cbb = bank()
                    cbps = self.psb[cbb]
                    pr.op("pe", lambda e: e.matmul(cbps[:, 0:128], BT[:, g, qs], CT[:, g, qs], start=True, stop=True),
                          reads=[("s_BT", g), ("s_CT", g)], writes=[("ps", cbb)])
                    pr.op("dve", lambda e: e.tensor_tensor(out=CBt, in0=cbps[:, 0:128], in1=self.triU[:], op=ALU.mult),
                          reads=[("ps", cbb), "triU"], writes=[("s_CBm", pg)], small=True)
                    pr.op("pool", lambda e: e.tensor_tensor(out=sct, in0=Et, in1=CBt.unsqueeze(1).to_broadcast([128, 8, 128]), op=ALU.mult),
                          reads=[("s_E", pg, 0), ("s_E", pg, 1), ("s_CBm", pg)], writes=[("s_scT", pg)])

                def stageB(g, q=q, qs=qs, par=par):
                    pg = g % 2
                    gh = slice(g * 8, (g + 1) * 8)
                    gc = slice(g * 512, (g + 1) * 512)
                    sct, yot = scT[pg], yoffs[0]
                    hbk = g // 2
                    ya, yo = bank(), bank()
                    yaps, yops = self.psb[ya], self.psb[yo]

                    def mmy(e):
                        e.matmul(yaps[:, :], self.ident_b[:], xDb_[par][:, gc], start=True, stop=False)
                        for hh in range(8):
                            c0 = g * 512 + hh * 64
                            i = e.matmul(yaps[:, hh * 64:(hh + 1) * 64], sct[:, hh, :], xdt_[par][:, c0:c0 + 64], start=False, stop=(hh == 7))
                        return i
                    pr.op("pe", mmy, reads=[("s_scT", pg), ("s_xdt", par, hbk), ("s_xDb", par, hbk), "ident_b"], writes=[("ps", ya)])
                    pr.op("pe", lambda e: e.matmul(yops[:, :], CT[:, g, qs], Sbf[:, gc], start=True, stop=True),
                          reads=[("s_CT", g), ("s_Sbf", g)], writes=[("ps", yo)])
                    pr.op("dve", lambda e: e.tensor_tensor(
                        out=yot.rearrange("p (h d) -> p h d", h=8), in0=yops[:, :].rearrange("p (h d) -> p h d", h=8),
                        in1=ecs_[par][:, gh].unsqueeze(2).to_broadcast([128, 8, 64]), op=ALU.mult),
                        reads=[("ps", yo), ("s_ecs", par)], writes=[("s_yoff", 0)])
                    pr.op("dve", lambda e: e.tensor_tensor(out=ysb[:, gc], in0=yaps[:, :], in1=yot, op=ALU.add),
                          reads=[("ps", ya), ("s_yoff", 0)], writes=[("s_ysb", g)])
                    sb_ = bank()
                    sps_ = self.psb[sb_]
                    pr.op("pe", lambda e: e.matmul(sps_[:, :], Btm_[par][:, g * 128:(g + 1) * 128], xdd_[par][:, gc], start=True, stop=True),
                          reads=[("s_Btm", par), ("s_xdd", par, hbk)], writes=[("ps", sb_)])
                    pr.op("dve", lambda e: e.tensor_tensor(
                        out=Sst[:, gc].rearrange("p (h d) -> p h d", h=8), in0=Sst[:, gc].rearrange("p (h d) -> p h d", h=8),
                        in1=etot_[par][:, gh].unsqueeze(2).to_broadcast([128, 8, 64]), op=ALU.mult),
                        reads=[("s_S", g), ("s_etot", par)], writes=[("s_S", g)])
                    pr.op("dve", lambda e: e.tensor_tensor(out=Sst[:, gc], in0=sps_[:, :], in1=Sst[:, gc], op=ALU.add),
                          reads=[("ps", sb_), ("s_S", g)], writes=[("s_S", g)])

                def stageC(g, q=q, par=par):
                    gc = slice(g * 512, (g + 1) * 512)
                    gst = gsc2[0]
                    pr.op("act", lambda e: e.activation(out=Sbf[:, gc], in_=Sst[:, gc], func=AF.Identity), reads=[("s_S", g)], writes=[("s_Sbf", g)])
                    pr.op("pool", lambda e: e.tensor_tensor(out=ysb[:, gc], in0=ysb[:, gc], in1=zs[:, q, gc], op=ALU.mult),
                          reads=[("s_ysb", g), ("s_zs", q, g)], writes=[("s_ysb", g)])
                    pr.op("act", lambda e: e.activation(out=gst, in_=ysb[:, gc], func=AF.Square),
                          reads=[("s_ysb", g)], writes=[("s_yoff", 0)])
                    pr.op("dve", lambda e: e.reduce_sum(out=ssq[:, g:g + 1], in_=gst, axis=mybir.AxisListType.X),
                          reads=[("s_yoff", 0)], writes=[("s_ssq", g)], small=True)

                stageA(0)
                stageA(1)
                stageB(0)
                stageA(2)
                stageB(1)
                stageA(3)
                stageB(2)
                stageC(0)
                stageB(3)
                stageC(1)
                stageC(2)
                stageC(3)
                kss = [("s_ssq", g) for g in range(4)]
                pr.op("dve", lambda e: e.tensor_scalar(out=rs4, in0=ssq, scalar1=1.0 / 512, scalar2=EPS, op0=ALU.mult, op1=ALU.add),
                      reads=kss, writes=["s_rs4"], force_same=True, small=True)
                pr.op("act", lambda e: e.activation(out=rs4, in_=rs4, func=AF.Ln), reads=["s_rs4"], writes=["s_rs4"], small=True)
                pr.op("act", lambda e: e.activation(out=rs4, in_=rs4, func=AF.Exp, scale=-0.5), reads=["s_rs4"], writes=["s_rs4"], force_same=True, small=True)
                for g in range(4):
                    gc = slice(g * 512, (g + 1) * 512)
                    pr.op("dve", lambda e, gc=gc, g=g: e.scalar_tensor_tensor(out=gn[:, gc], in0=ysb[:, gc], scalar=rs4[:, g:g + 1], in1=normw[:, gc],
                                                                              op0=ALU.mult, op1=ALU.mult),
                          reads=[("s_ysb", g), "s_rs4", "s_normw"], writes=[("s_gn", g)])
                for hb in range(2):
                    tb = bank()
                    tps = self.psb[tb].bitcast(BF16)

                    def trg(e, tps=tps, hb=hb):
                        for c8_ in range(8):
                            cx = hb * 8 + c8_
                            i = e.transpose(tps[:, c8_ * 128:(c8_ + 1) * 128], gn[:, cx * 128:(cx + 1) * 128], self.ident_b[:])
                        return i
                    pr.op("pe", trg, reads=[("s_gn", hb * 2), ("s_gn", hb * 2 + 1), "ident_b"], writes=[("ps", tb)])
                    pr.op("act", lambda e, tps=tps, hb=hb, qs=qs: e.activation(out=ynT[:, hb * 8:(hb + 1) * 8, qs], in_=tps.rearrange("p (c t) -> p c t", c=8), func=AF.Identity),
                          reads=[("ps", tb)], writes=[("s_ynT", hb * 8 + c) for c in range(8)])
            kyn = [("s_ynT", c) for c in range(16)]
            if wi == 0:
                self.dump("ynT", ynT, kyn)
            sbank = 7
            sps = self.psb[sbank]
            kzs_all = [("s_zs", q, b_) for q in range(NQ) for b_ in range(4)]
            for o in range(8):
                wo = wos[nwo % 2]
                kwo = ("s_wo", nwo % 2)
                nwo += 1
                pr.dma("sp", wo, self.swout_b[j, o].rearrange("p (k c) -> p k c", k=16), reads=[("swout", j, o)], writes=[kwo])
                mb = bank()
                mps = self.psb[mb]

                def mmo(e, mps=mps, wo=wo):
                    for k in range(16):
                        i = e.matmul(mps[:, :], wo[:, k, :], ynT[:, k, :], start=(k == 0), stop=(k == 15))
                    return i
                pr.op("pe", mmo, reads=[kwo] + kyn, writes=[("ps", mb)])
                sqt = sqp[0]
                pr.op("act", lambda e, mps=mps, o=o: e.activation(out=mT[:, o, :], in_=mps[:, :], func=AF.Identity),
                      reads=[("ps", mb)] + (kzs_all if o == 0 else []), writes=[("s_mT", o)] + (kzs_all if o == 0 else []))
                pr.op("act", lambda e, mps=mps, sqt=sqt: e.activation(out=sqt, in_=mps[:, :], func=AF.Square),
                      reads=[("ps", mb)], writes=[("s_sqp", 0)])
                pr.op("pe", lambda e, o=o, sqt=sqt: e.matmul(sps[:, :], self.ones_b[:], sqt, start=(o == 0), stop=(o == 7)),
                      reads=[("s_sqp", 0), "ones_b"], writes=[("ps", sbank)])
            self._resid(sps, sbank, rstd2, "s_rstd", mT, "s_mT", f"nmpost{li}", xblk, [kx], 0, T, dst, t0)
            for q in range(NQ):
                for b_ in range(4):
                    self.pr._record(self.pr.last_w[("xT", 1 - self.cur)], [], [("s_zs", q, b_)])
            self.pump_casts(6)
        self.cur = 1 - self.cur

    def _prenorm_multi(self, xblk, kxr, C, wname, hT, kh, sq, ksq, rstd, krstd, stat_bank):
        pr = self.pr
        ps = self.psb[stat_bank]
        for c in range(8):
            pr.op("act", lambda e, c=c: e.activation(out=sq[:, c, 0:C], in_=xblk[:, c, 0:C], func=AF.Square),
                  reads=kxr, writes=[(ksq, c)])

        def st(e):
            for c in range(8):
                i = e.matmul(ps[:, 0:C], self.ones_b[:], sq[:, c, 0:C], start=(c == 0), stop=(c == 7))
            return i
        pr.op("pe", st, reads=[(ksq, c) for c in range(8)] + ["ones_b"], writes=[("ps", stat_bank)])
        pr.op("act", lambda e: e.activation(out=rstd[:, 0:C], in_=ps[:, 0:C], func=AF.Ln, scale=1.0 / D, bias=self.vcol("eps")),
              reads=[("ps", stat_bank), "vecs"], writes=[krstd])
        pr.op("act", lambda e: e.activation(out=rstd[:, 0:C], in_=rstd[:, 0:C], func=AF.Exp, scale=-0.5),
              reads=[krstd], writes=[krstd])
        for c in range(8):
            pr.op("dve", lambda e, c=c: e.scalar_tensor_tensor(out=hT[:, c, 0:C], in0=xblk[:, c, 0:C], scalar=self.vcol(wname, c),
                                                               in1=rstd[:, 0:C], op0=ALU.mult, op1=ALU.mult),
                  reads=kxr + [krstd, "vecs"], writes=[(kh, c)])

    def build(self):
        for li in self.layers:
            if self.do_mixer and li % 2 == 0:
                j = li // 2
                for b_ in range(10):
                    self.add_cast(self.swin_b[j, b_], self.swin_f[j, b_], ("swin", j, b_))
                self.add_cast(self.swdt_b[j], self.swdt_f[j], ("swdt", j))
                for o in range(8):
                    self.add_cast(self.swout_b[j, o], self.swout_f[j, o], ("swout", j, o))
            if self.do_mixer and li % 2 == 1:
                j = li // 2
                self.add_cast(self.lwin_b[j], self.lwin_f[j], ("lwin", j))
                self.add_cast(self.lwgx_b[j], self.lwgx_f[j], ("lwgx", j))
                self.add_cast(self.lwga_b[j], self.lwga_f[j], ("lwga", j))
                self.add_cast(self.lwout_b[j], self.lwout_f[j], ("lwout", j))
            if self.do_ffn:
                for j2 in range(16):
                    self.add_cast(self.wup_b[li, j2], self.wup_f[li, j2], ("wup_b", li, j2))
                for o2 in range(4):
                    self.add_cast(self.wdn_b[li, o2], self.wdn_f[li, o2], ("wdn_b", li, o2))
        self.phase_in()
        for li in self.layers:
            if self.do_mixer:
                if li % 2 == 1:
                    self.phase_lru(li)
                else:
                    self.phase_ssd(li)
            if self.do_ffn:
                self.phase_ffn(li)
        self.phase_out()
        self.pr.finish_wait_all("sp")
        self.pr.emit()
        self.P.close()
        return self.nc


def pack_inputs(inp):
    vp = VecPack()
    vp.add_raw("eps", np.full((128, 1), EPS, np.float32))
    for li in range(DEPTH):
        vp.add(f"nmpre{li}", inp["norm_mix_pre"][li])
        vp.add(f"nmpost{li}", inp["norm_mix_post"][li])
        vp.add(f"nfpre{li}", inp["norm_ffn_pre"][li])
        vp.add(f"nfpost{li}", inp["norm_ffn_post"][li])
        for j in range(3):
            vp.add(f"fcw{li}_{j}", inp["ffn_conv_w"][li][j])
        vp.add(f"fcb{li}", inp["ffn_conv_b"][li])
    vp.add_raw("one", np.ones((128, 1), np.float32))
    for j in range(2):
        vp.add(f"lbin{j}", inp["lru_b_in"][j])
        for t in range(4):
            vp.add(f"lcw{j}_{t}", inp["lru_conv_w"][j][t])
        vp.add(f"lcb{j}", inp["lru_conv_b"][j])
        vp.add(f"lbgx{j}", inp["lru_b_gx"][j])
        vp.add(f"lbga{j}", inp["lru_b_ga"][j])
        vp.add(f"llam{j}", inp["lru_lambda"][j])
        vp.add(f"lbout{j}", inp["lru_b_out"][j])
    for j in range(2):
        for t in range(4):
            vp.add(f"scw{j}_{t}", inp["ssd_conv_w"][j][t])
        vp.add(f"scb{j}", inp["ssd_conv_b"][j])
    vecs = vp.build()
    swin = np.asarray(inp["ssd_w_in"], dtype=np.float32)
    sw = swin[:, :, :5120].reshape(2, 8, 128, 10, 512)
    swin_h = np.ascontiguousarray(sw.transpose(0, 3, 2, 1, 4)).reshape(2, 10, 128, 4096)
    sdt = swin[:, :, 5120:].reshape(2, 8, 128, 32)
    swdt_h = np.ascontiguousarray(sdt.transpose(0, 2, 1, 3)).reshape(2, 128, 256)
    swo = np.asarray(inp["ssd_w_out"], dtype=np.float32).reshape(2, 16, 128, 8, 128)
    swout_h = np.ascontiguousarray(swo.transpose(0, 3, 2, 1, 4)).reshape(2, 8, 128, 2048)
    snorm_h = np.ascontiguousarray(np.broadcast_to(np.asarray(inp["ssd_norm"], dtype=np.float32)[:, None, :], (2, 128, 2048)))
    ssm = np.concatenate([np.asarray(inp["ssd_dt_bias"], dtype=np.float32), np.asarray(inp["ssd_a_log"], dtype=np.float32),
                          np.asarray(inp["ssd_d"], dtype=np.float32)], axis=1)
    ssm_h = np.ascontiguousarray(np.broadcast_to(ssm[:, None, :], (2, 128, 96)))
    lwin = np.asarray(inp["lru_w_in"], dtype=np.float32)
    lw = lwin.reshape(2, 8, 128, 20, 128)
    lwin_h = np.ascontiguousarray(lw.transpose(0, 3, 2, 1, 4)).reshape(2, 2560, 1024)
    lwgx_h = np.ascontiguousarray(np.asarray(inp["lru_w_gx"], dtype=np.float32).transpose(0, 2, 1, 3)).reshape(2, 128, 1280)
    lwga_h = np.ascontiguousarray(np.asarray(inp["lru_w_ga"], dtype=np.float32).transpose(0, 2, 1, 3)).reshape(2, 128, 1280)
    lwo = np.asarray(inp["lru_w_out"], dtype=np.float32).reshape(2, 10, 128, 1024)
    lwout_h = np.ascontiguousarray(lwo.transpose(0, 2, 1, 3)).reshape(2, 128, 10240)
    wup = np.asarray(inp["ffn_w_up"], dtype=np.float32)
    g = wup[:, :, :DFF].reshape(DEPTH, 8, 128, 16, 256)
    v = wup[:, :, DFF:].reshape(DEPTH, 8, 128, 16, 256)
    gv = np.concatenate([g, v], axis=-1)
    wup_h = np.ascontiguousarray(gv.transpose(0, 3, 2, 1, 4)).reshape(DEPTH, 16, 128, 4096)
    wdn = np.asarray(inp["ffn_w_down"], dtype=np.float32)
    wd = wdn.reshape(DEPTH, 32, 128, 4, 256)
    wdn_h = np.ascontiguousarray(wd.transpose(0, 3, 2, 1, 4)).reshape(DEPTH, 4, 128, 8192)
    shared = {"vecs": vecs, "wup": wup_h, "wdn": wdn_h,
              "swin": swin_h, "swdt": swdt_h, "swout": swout_h, "snorm": snorm_h, "ssm": ssm_h,
              "lwin": lwin_h, "lwgx": lwgx_h, "lwga": lwga_h, "lwout": lwout_h}
    return shared, vp.index, vecs.shape[1]


_CACHE = {}


def run(inp, S=4096, layers=(0, 1, 2, 3), do_mixer=True, do_ffn=True, trace=False, debug=False):
    shared, vindex, nvec = pack_inputs(inp)
    x = np.asarray(inp["x"], dtype=np.float32)
    B = x.shape[0]
    b = Builder(S, vindex, nvec, layers=layers, do_mixer=do_mixer, do_ffn=do_ffn, debug=debug)
    nc = b.build()
    in_maps = []
    for i in range(B):
        m = dict(shared)
        m["x"] = np.ascontiguousarray(x[i, :S])
        in_maps.append(m)
    res = run_bass_kernel_spmd(nc, in_maps, core_ids=list(range(B)), trace=trace)
    y = np.stack([np.asarray(r["y"]) for r in res.results], axis=0)
    if debug:
        res.dbg = {k: np.asarray(res.results[0]["dbg_" + k]).astype(np.float32) for k in b.dbg}
    return y.astype(np.float32), res


def kernel(**inputs):
    y, _ = run(inputs)
    return y
```

```python
import numpy as np
import concourse.bass as bass
import concourse.mybir as mybir
from concourse.bass_utils import run_bass_kernel_spmd

F32 = mybir.dt.float32
BF16 = mybir.dt.bfloat16
AF = mybir.ActivationFunctionType
ALU = mybir.AluOpType

D = 1024
DEPTH = 4
DFF = 4096
EPS = 1e-6

ENGS = ("pe", "act", "dve", "pool", "sp")
EPOCH = 30000
NDSEM = 8


class Prog:
    def __init__(self, nc, same_engine_sync=False):
        self.nc = nc
        self.same_engine_sync = same_engine_sync
        self.ops = {e: [] for e in ENGS}
        self.cnt = {e: 0 for e in ENGS}
        self.epoch = {e: 0 for e in ENGS}
        self.sems = {e: [] for e in ENGS}
        self.dsems = {e: [] for e in ENGS}
        self.ndma = {e: 0 for e in ENGS}
        self.known = {e: {} for e in ENGS}
        self.last_w = {}
        self.readers = {}
        self._ctx = []
        self.fence_vals = {}
        self.small_toks = set()

    def _new_sem(self, name):
        cm = self.nc.semaphore(name)
        s = cm.__enter__()
        self._ctx.append(cm)
        return s

    def _eng_sem(self, e):
        ep = self.epoch[e]
        while len(self.sems[e]) <= ep:
            self.sems[e].append(self._new_sem(f"s_{e}_{len(self.sems[e])}"))
        return self.sems[e][ep]

    def _deps(self, e, reads, writes, force_same=False):
        toks = []
        for k in reads:
            t = self.last_w.get(k)
            if t is not None:
                toks.append(t)
        for k in writes:
            t = self.last_w.get(k)
            if t is not None:
                toks.append(t)
            toks.extend(self.readers.get(k, ()))
        waits = {}
        for (sem, val, te, isdma) in toks:
            if te == e and not isdma:
                if e == "pe" or not (self.same_engine_sync or force_same or (id(sem), val) in self.small_toks):
                    continue
            sid = id(sem)
            if self.known[e].get(sid, 0) >= val:
                continue
            if sid not in waits or waits[sid][1] < val:
                waits[sid] = (sem, val)
        for sid, (sem, val) in waits.items():
            self.known[e][sid] = val
        return list(waits.values())

    def _record(self, tok, reads, writes):
        for k in writes:
            self.last_w[k] = tok
            self.readers[k] = []
        for k in reads:
            if k in writes:
                continue
            lst = self.readers.setdefault(k, [])
            lst.append(tok)
            if len(lst) > 8:
                best = {}
                for t in lst:
                    sid = id(t[0])
                    if sid not in best or best[sid][1] < t[1]:
                        best[sid] = t
                self.readers[k] = list(best.values())

    def op(self, e, fn, reads=(), writes=(), force_same=False, small=False):
        waits = self._deps(e, reads, writes, force_same)
        if self.cnt[e] >= EPOCH:
            self.epoch[e] += 1
            self.cnt[e] = 0
        sem = self._eng_sem(e)
        self.cnt[e] += 1
        tok = (sem, self.cnt[e], e, False)
        if small:
            self.small_toks.add((id(sem), self.cnt[e]))
        self.ops[e].append((waits, fn, sem, 1))
        self._record(tok, reads, writes)
        return tok

    def dma(self, q, out, in_, reads=(), writes=(), nofence=False, **kw):
        if not self.dsems[q]:
            self.dsems[q] = [self._new_sem(f"d_{q}_{i}") for i in range(NDSEM)]
        i = self.ndma[q]
        self.ndma[q] += 1
        sem = self.dsems[q][i % NDSEM]
        waits = self._deps(q, reads, writes)
        prev = 16 * (i // NDSEM)
        if prev > 0 and self.known[q].get(id(sem), 0) < prev:
            waits.append((sem, prev))
            self.known[q][id(sem)] = prev
        tok = (sem, prev + 16, q, True)
        if not nofence:
            self.fence_vals.setdefault(q, {})[i % NDSEM] = (sem, prev + 16)

        def fn(eng, out=out, in_=in_, kw=kw):
            return eng.dma_start(out=out, in_=in_, **kw)

        self.ops[q].append((waits, fn, sem, 16))
        self._record(tok, reads, writes)
        return tok

    def fence(self):
        toks = []
        for f in ENGS:
            if self.sems[f] and self.cnt[f] > 0:
                toks.append((self.sems[f][self.epoch[f]], self.cnt[f]))
            for (sem, val) in self.fence_vals.get(f, {}).values():
                toks.append((sem, val))
        for e in ENGS:
            if not self.ops[e]:
                continue
            waits = []
            own = self.sems[e][self.epoch[e]] if self.sems[e] else None
            for (sem, val) in toks:
                if sem is own:
                    continue
                if self.known[e].get(id(sem), 0) >= val:
                    continue
                waits.append((sem, val))
                self.known[e][id(sem)] = val
            if waits:
                self.ops[e].append((waits, None, None, 0))

    def finish_wait_all(self, e="sp"):
        toks = {}
        for k, t in self.last_w.items():
            sid = id(t[0])
            if sid not in toks or toks[sid][1] < t[1]:
                toks[sid] = t
        waits = []
        for sid, (sem, val, te, isdma) in toks.items():
            if self.known[e].get(sid, 0) >= val:
                continue
            waits.append((sem, val))
        self.ops[e].append((waits, None, None, 0))

    def emit(self):
        nc = self.nc
        engmap = {"pe": "tensor", "act": "scalar", "dve": "vector", "pool": "gpsimd", "sp": "sync"}
        with nc.Block() as block:
            for e in ENGS:
                ops = self.ops[e]
                if not ops:
                    continue

                def body(eng, ops=ops):
                    for (waits, fn, sem, inc) in ops:
                        for (ws, wv) in waits:
                            eng.wait_ge(ws, wv)
                        if fn is None:
                            continue
                        ins = fn(eng)
                        ins.then_inc(sem, inc)

                getattr(block, engmap[e])(body)
        for cm in reversed(self._ctx):
            cm.__exit__(None, None, None)
        self._ctx = []


class Pools:
    def __init__(self, nc):
        self.nc = nc
        self._ctx = []

    def sb(self, name, shape, dt):
        cm = self.nc.sbuf_tensor(name, list(shape), dt)
        t = cm.__enter__()
        self._ctx.append(cm)
        return t

    def ps(self, name, shape, dt):
        cm = self.nc.psum_tensor(name, list(shape), dt)
        t = cm.__enter__()
        self._ctx.append(cm)
        return t

    def close(self):
        for cm in reversed(self._ctx):
            cm.__exit__(None, None, None)
        self._ctx = []


class Arena:
    def __init__(self, pools, nf32):
        self.t = pools.sb("arena", [128, nf32], F32)
        self.n = nf32
        self.off = 0

    def reset(self):
        self.off = 0

    def alloc(self, shape, dt):
        assert shape[0] == 128
        nel = 1
        for d_ in shape[1:]:
            nel *= d_
        nf = nel if dt == F32 else (nel + 1) // 2
        nf = (nf + 15) // 16 * 16
        assert self.off + nf <= self.n, ("arena overflow", self.off, nf, self.n)
        v = self.t[:, self.off:self.off + nf]
        self.off += nf
        if dt != F32:
            v = v.bitcast(dt)
        v = v[:, 0:nel]
        if len(shape) == 3:
            v = v.rearrange("p (a b) -> p a b", a=shape[1])
        return v


class VecPack:
    def __init__(self):
        self.cols = []
        self.index = {}
        self.n = 0

    def add(self, name, vec):
        vec = np.asarray(vec, dtype=np.float32).reshape(-1)
        assert vec.size % 128 == 0, (name, vec.size)
        nch = vec.size // 128
        self.index[name] = (self.n, nch)
        self.cols.append(vec.reshape(nch, 128).T)
        self.n += nch

    def add_raw(self, name, arr):
        arr = np.asarray(arr, dtype=np.float32)
        assert arr.shape[0] == 128
        self.index[name] = (self.n, arr.shape[1])
        self.cols.append(arr)
        self.n += arr.shape[1]

    def build(self):
        return np.ascontiguousarray(np.concatenate(self.cols, axis=1))


def windows(S, n, halo):
    out = []
    t = 0
    while t < S:
        m = min(n, S - t)
        out.append((t, m))
        t += m
    return out


class Builder:
    def __init__(self, S, vec_index, nvec, layers=(0, 1, 2, 3), do_mixer=True, do_ffn=True, debug=False):
        self.S = S
        self.debug = debug
        self.dbg = {}
        self.vi = vec_index
        self.layers = layers
        self.do_mixer = do_mixer
        self.do_ffn = do_ffn
        nc = self.nc = bass.Bass("TRN2", target_bir_lowering=False)
        self.P = Pools(nc)
        self.pr = Prog(nc)
        P, pr = self.P, self.pr
        self.x_in = nc.dram_tensor("x", [S, D], F32, kind="ExternalInput").ap()
        self.y_out = nc.dram_tensor("y", [S, D], F32, kind="ExternalOutput").ap()
        self.vec_d = nc.dram_tensor("vecs", [128, nvec], F32, kind="ExternalInput").ap()
        self.wup_f = nc.dram_tensor("wup", [DEPTH, 16, 128, 4096], F32, kind="ExternalInput").ap()
        self.wdn_f = nc.dram_tensor("wdn", [DEPTH, 4, 128, 8192], F32, kind="ExternalInput").ap()
        self.wup_b = nc.dram_tensor("wup_b", [DEPTH, 16, 128, 4096], BF16, kind="Internal").ap()
        self.wdn_b = nc.dram_tensor("wdn_b", [DEPTH, 4, 128, 8192], BF16, kind="Internal").ap()
        self.swin_f = nc.dram_tensor("swin", [2, 10, 128, 4096], F32, kind="ExternalInput").ap()
        self.swdt_f = nc.dram_tensor("swdt", [2, 128, 256], F32, kind="ExternalInput").ap()
        self.swout_f = nc.dram_tensor("swout", [2, 8, 128, 2048], F32, kind="ExternalInput").ap()
        self.swin_b = nc.dram_tensor("swin_b", [2, 10, 128, 4096], BF16, kind="Internal").ap()
        self.swdt_b = nc.dram_tensor("swdt_b", [2, 128, 256], BF16, kind="Internal").ap()
        self.swout_b = nc.dram_tensor("swout_b", [2, 8, 128, 2048], BF16, kind="Internal").ap()
        self.snorm_d = nc.dram_tensor("snorm", [2, 128, 2048], F32, kind="ExternalInput").ap()
        self.ssm_d = nc.dram_tensor("ssm", [2, 128, 96], F32, kind="ExternalInput").ap()
        self.lwin_f = nc.dram_tensor("lwin", [2, 2560, 1024], F32, kind="ExternalInput").ap()
        self.lwgx_f = nc.dram_tensor("lwgx", [2, 128, 1280], F32, kind="ExternalInput").ap()
        self.lwga_f = nc.dram_tensor("lwga", [2, 128, 1280], F32, kind="ExternalInput").ap()
        self.lwout_f = nc.dram_tensor("lwout", [2, 128, 10240], F32, kind="ExternalInput").ap()
        self.lwin_b = nc.dram_tensor("lwin_b", [2, 2560, 1024], BF16, kind="Internal").ap()
        self.lwgx_b = nc.dram_tensor("lwgx_b", [2, 128, 1280], BF16, kind="Internal").ap()
        self.lwga_b = nc.dram_tensor("lwga_b", [2, 128, 1280], BF16, kind="Internal").ap()
        self.lwout_b = nc.dram_tensor("lwout_b", [2, 128, 10240], BF16, kind="Internal").ap()
        self.xT = [nc.dram_tensor(f"xT{i}", [D, S], F32, kind="Internal").ap() for i in range(2)]
        self.cur = 0
        self.vecs = P.sb("vecs_sb", [128, nvec], F32)
        pr.dma("sp", self.vecs[:], self.vec_d, writes=["vecs"])
        self.ident = P.sb("ident", [128, 128], F32)
        self.ones_f = P.sb("ones_f", [128, 128], F32)
        self.ones_b = P.sb("ones_b", [128, 128], BF16)
        pr.op("pool", lambda e: e.memset(self.ones_f[:], 1.0), writes=["ones_f"])
        pr.op("pool", lambda e: e.memset(self.ones_b[:], 1.0), writes=["ones_b"])
        pr.op("pool", lambda e: e.affine_select(out=self.ident[:], in_=self.ones_f[:], pattern=[[-1, 128]],
                                                compare_op=ALU.is_equal, fill=0.0, base=0, channel_multiplier=1),
              reads=["ones_f"], writes=["ident"])
        self.ident_b = P.sb("ident_b", [128, 128], BF16)
        self.triU = P.sb("triU", [128, 128], F32)
        self.triSL = P.sb("triSL", [128, 128], F32)
        pr.op("pool", lambda e: e.tensor_copy(out=self.ident_b[:], in_=self.ident[:]), reads=["ident"], writes=["ident_b"])
        pr.op("pool", lambda e: e.affine_select(out=self.triU[:], in_=self.ones_f[:], pattern=[[1, 128]],
                                                compare_op=ALU.is_ge, fill=0.0, base=0, channel_multiplier=-1),
              reads=["ones_f"], writes=["triU"])
        pr.op("pool", lambda e: e.affine_select(out=self.triSL[:], in_=self.ones_f[:], pattern=[[-1, 128]],
                                                compare_op=ALU.is_gt, fill=0.0, base=0, channel_multiplier=1),
              reads=["ones_f"], writes=["triSL"])
        self.psall = P.ps("psall", [128, 4096], F32)
        self.psb = [self.psall[:, i * 512:(i + 1) * 512] for i in range(8)]
        self.A = Arena(P, 50176)
        self.cast_jobs = []
        self.cast_done = 0

    def dump(self, name, ap, reads):
        if not self.debug:
            return
        t = self.nc.dram_tensor("dbg_" + name, list(ap.shape), ap.dtype, kind="ExternalOutput").ap()
        self.dbg[name] = t
        self.pr.dma("pool", t, ap, reads=reads, writes=[("dbg", name)])

    def vcol(self, name, c=0, n=1):
        c0, nch = self.vi[name]
        assert c + n <= nch, (name, c, n, nch)
        return self.vecs[:, c0 + c:c0 + c + n]

    def add_cast(self, dst, src, key):
        self.cast_jobs.append((dst, src, key))

    def pump_casts(self, n):
        while n > 0 and self.cast_done < len(self.cast_jobs):
            dst, src, key = self.cast_jobs[self.cast_done]
            self.pr.dma("pool", dst, src, writes=[key], nofence=True, max_dma_last_dim=4096)
            self.cast_done += 1
            n -= 1

    def pump_until(self, key):
        while self.cast_done < len(self.cast_jobs) and key not in self.pr.last_w:
            self.pump_casts(1)

    def phase_in(self):
        pr, P, S = self.pr, self.P, self.S
        pr.fence()
        self.A.reset()
        dst = self.xT[self.cur].rearrange("(c p) t -> p c t", p=128)
        xin = [self.A.alloc([128, D], F32) for i in range(2)]
        xst = [self.A.alloc([128, 8, 512], F32) for i in range(2)]
        nblk = (S + 511) // 512
        for b in range(nblk):
            t0 = b * 512
            nt = min(512, S - t0) // 128
            st = xst[b % 2]
            for tt in range(nt):
                xi = xin[(b * 4 + tt) % 2]
                kxi = ("xin", (b * 4 + tt) % 2)
                pr.dma("sp", xi[:], self.x_in[t0 + tt * 128:t0 + (tt + 1) * 128, :], writes=[kxi])
                for half in range(2):
                    bank = (tt * 2 + half) % 8
                    ps = self.psb[bank]

                    def tr(e, xi=xi, ps=ps, half=half):
                        for q in range(4):
                            c = half * 4 + q
                            i = e.transpose(ps[:, q * 128:(q + 1) * 128], xi[:, c * 128:(c + 1) * 128], self.ident[:])
                        return i
                    pr.op("pe", tr, reads=[kxi, "ident"], writes=[("ps", bank)])
                    eng = "act" if half == 0 else "dve"

                    def ev(e, st=st, ps=ps, half=half, tt=tt, eng=eng):
                        o = st[:, half * 4:(half + 1) * 4, tt * 128:(tt + 1) * 128]
                        i_ = ps[:].rearrange("p (q t) -> p q t", q=4)
                        if eng == "act":
                            return e.activation(out=o, in_=i_, func=AF.Identity)
                        return e.tensor_copy(out=o, in_=i_)
                    pr.op(eng, ev, reads=[("ps", bank)], writes=[("xst", b % 2, half, tt)])
            rk = [("xst", b % 2, h, tt) for h in range(2) for tt in range(nt)]
            pr.dma("pool", dst[:, :, t0:t0 + nt * 128], st[:, :, 0:nt * 128], reads=rk, writes=[("xT", self.cur)])

    def phase_out(self):
        pr, P, S = self.pr, self.P, self.S
        pr.fence()
        self.A.reset()
        src = self.xT[self.cur].rearrange("(c p) t -> p c t", p=128)
        xld = [self.A.alloc([128, 8, 512], F32) for i in range(2)]
        yst = [self.A.alloc([128, D], F32) for i in range(2)]
        nblk = (S + 511) // 512
        for b in range(nblk):
            t0 = b * 512
            nt = min(512, S - t0) // 128
            xl = xld[b % 2]
            kx = ("xld", b % 2)
            pr.dma("sp", xl[:, :, 0:nt * 128], src[:, :, t0:t0 + nt * 128], reads=[("xT", self.cur)], writes=[kx])
            for tt in range(nt):
                ys = yst[(b * 4 + tt) % 2]
                ky = ("yst", (b * 4 + tt) % 2)
                for half in range(2):
                    bank = (tt * 2 + half) % 8
                    ps = self.psb[bank]

                    def tr(e, xl=xl, ps=ps, half=half, tt=tt):
                        for q in range(4):
                            c = half * 4 + q
                            i = e.transpose(ps[:, q * 128:(q + 1) * 128], xl[:, c, tt * 128:(tt + 1) * 128], self.ident[:])
                        return i
                    pr.op("pe", tr, reads=[kx, "ident"], writes=[("ps", bank)])
                    eng = "act" if half == 0 else "dve"

                    def ev(e, ys=ys, ps=ps, half=half, eng=eng):
                        o = ys[:, half * 512:(half + 1) * 512]
                        if eng == "act":
                            return e.activation(out=o, in_=ps[:], func=AF.Identity)
                        return e.tensor_copy(out=o, in_=ps[:])
                    pr.op(eng, ev, reads=[("ps", bank)], writes=[(ky, half)])
                pr.dma("pool", self.y_out[t0 + tt * 128:t0 + (tt + 1) * 128, :], ys[:],
                       reads=[(ky, 0), (ky, 1)], writes=["y"])

    def prenorm(self, xblk, kx, C, wname, hT, kh, sq, ksq, rstd, krstd, stat_bank):
        pr = self.pr
        ps = self.psb[stat_bank]
        for c in range(8):
            pr.op("act", lambda e, c=c: e.activation(out=sq[:, c, 0:C], in_=xblk[:, c, 0:C], func=AF.Square),
                  reads=[kx], writes=[(ksq, c)])

        def st(e):
            for c in range(8):
                i = e.matmul(ps[:, 0:C], self.ones_b[:], sq[:, c, 0:C], start=(c == 0), stop=(c == 7))
            return i
        pr.op("pe", st, reads=[(ksq, c) for c in range(8)] + ["ones_b"], writes=[("ps", stat_bank)])
        pr.op("act", lambda e: e.activation(out=rstd[:, 0:C], in_=ps[:, 0:C], func=AF.Ln, scale=1.0 / D, bias=self.vcol("eps")),
              reads=[("ps", stat_bank), "vecs"], writes=[krstd])
        pr.op("act", lambda e: e.activation(out=rstd[:, 0:C], in_=rstd[:, 0:C], func=AF.Exp, scale=-0.5),
              reads=[krstd], writes=[krstd])
        for c in range(8):
            pr.op("dve", lambda e, c=c: e.scalar_tensor_tensor(out=hT[:, c, 0:C], in0=xblk[:, c, 0:C], scalar=self.vcol(wname, c),
                                                               in1=rstd[:, 0:C], op0=ALU.mult, op1=ALU.mult),
                  reads=[kx, krstd, "vecs"], writes=[(kh, c)])

    def phase_ffn(self, li):
        pr, P, S = self.pr, self.P, self.S
        pr.fence()
        self.A.reset()
        src = self.xT[self.cur].rearrange("(c p) t -> p c t", p=128)
        dst = self.xT[1 - self.cur].rearrange("(c p) t -> p c t", p=128)
        xblk = self.A.alloc([128, 8, 512], F32)
        sq = self.A.alloc([128, 8, 512], BF16)
        hT = self.A.alloc([128, 8, 512], BF16)
        rstd = self.A.alloc([128, 512], F32)
        rstd2 = self.A.alloc([128, 512], F32)
        gvT = self.A.alloc([128, 32, 512], BF16)
        fT = self.A.alloc([128, 8, 512], F32)
        xnew = self.A.alloc([128, 8, 512], F32)
        NWU, NWD = 3, 2
        wus = [self.A.alloc([128, 8, 512], BF16) for i in range(NWU)]
        wds = [self.A.alloc([128, 32, 256], BF16) for i in range(NWD)]
        tg = [self.A.alloc([128, 512], F32) for i in range(4)]
        tv = [self.A.alloc([128, 512], F32) for i in range(4)]
        gg = [self.A.alloc([128, 512], F32) for i in range(4)]
        nwu = 0
        nwd = 0
        npair = 0
        wins = windows(S, 464, 2)
        for wi, (t0, n) in enumerate(wins):
            C = n + 2
            kx = (f"f{li}_xblk",)
            if t0 == 0:
                pr.op("pool", lambda e: e.memset(xblk[:, :, 0:2], 0.0), writes=[(kx, "halo")])
                pr.dma("sp", xblk[:, :, 2:C], src[:, :, 0:n], reads=[("xT", self.cur)], writes=[kx])
                kxr = [kx, (kx, "halo")]
            else:
                pr.dma("sp", xblk[:, :, 0:C], src[:, :, t0 - 2:t0 + n], reads=[("xT", self.cur)], writes=[kx, (kx, "halo")])
                kxr = [kx, (kx, "halo")]
            self._prenorm_multi(xblk, kxr, C, f"nfpre{li}", hT, f"f{li}_hT", sq, f"f{li}_sq", rstd, f"f{li}_rstd", 6)
            khs = [(f"f{li}_hT", c) for c in range(8)]
            deferred, deferred_next = [], []
            for j2 in range(16):
                ws = wus[nwu % NWU]
                kw = ("wu", li, nwu % NWU)
                nwu += 1
                self.pump_until(("wup_b", li, j2))
                pr.dma("sp", ws[:], self.wup_b[li, j2].rearrange("p (k c) -> p k c", k=8),
                       reads=[("wup_b", li, j2)], writes=[kw])
                for jj in range(2):
                    j = j2 * 2 + jj
                    pb = (npair % 4) * 2
                    npair += 1
                    gps, vps = self.psb[pb], self.psb[pb + 1]

                    def mmg(e, ws=ws, gps=gps, jj=jj, C=C, off=0):
                        for k in range(8):
                            i = e.matmul(gps[:, 0:C], ws[:, k, off + jj * 128:off + (jj + 1) * 128], hT[:, k, 0:C],
                                         start=(k == 0), stop=(k == 7))
                        return i
                    pr.op("pe", mmg, reads=[kw] + khs, writes=[("ps", pb)])
                    pr.op("pe", lambda e, ws=ws, vps=vps, jj=jj, C=C: mmg(e, ws, vps, jj, C, 256),
                          reads=[kw] + khs, writes=[("ps", pb + 1)])
                    par = j % 4
                    tgt, tvt, ggt = tg[par], tv[par], gg[par]
                    ktg, ktv, kgg = ("tg", par), ("tv", par), ("gg", par)
                    jv = 32 + j
                    pr.op("act", lambda e, gps=gps, tgt=tgt, j=j, C=C, n=n: e.activation(
                        out=tgt[:, 0:n], in_=gps[:, 2:C], func=AF.Identity,
                        scale=self.vcol(f"fcw{li}_2", j), bias=self.vcol(f"fcb{li}", j)),
                        reads=[("ps", pb), "vecs"], writes=[ktg])
                    pr.op("act", lambda e, vps=vps, tvt=tvt, jv=jv, C=C, n=n: e.activation(
                        out=tvt[:, 0:n], in_=vps[:, 2:C], func=AF.Identity,
                        scale=self.vcol(f"fcw{li}_2", jv), bias=self.vcol(f"fcb{li}", jv)),
                        reads=[("ps", pb + 1), "vecs"], writes=[ktv])
                    pr.op("dve", lambda e, gps=gps, tgt=tgt, j=j, C=C, n=n: e.scalar_tensor_tensor(
                        out=tgt[:, 0:n], in0=gps[:, 1:C - 1], scalar=self.vcol(f"fcw{li}_1", j), in1=tgt[:, 0:n],
                        op0=ALU.mult, op1=ALU.add), reads=[("ps", pb), ktg, "vecs"], writes=[ktg])
                    pr.op("dve", lambda e, gps=gps, tgt=tgt, j=j, C=C, n=n: e.scalar_tensor_tensor(
                        out=tgt[:, 0:n], in0=gps[:, 0:C - 2], scalar=self.vcol(f"fcw{li}_0", j), in1=tgt[:, 0:n],
                        op0=ALU.mult, op1=ALU.add), reads=[("ps", pb), ktg, "vecs"], writes=[ktg])
                    deferred_next.append(lambda tgt=tgt, ggt=ggt, n=n, ktg=ktg, kgg=kgg: pr.op(
                        "act", lambda e: e.activation(out=ggt[:, 0:n], in_=tgt[:, 0:n], func=AF.Gelu_apprx_tanh),
                        reads=[ktg], writes=[kgg]))
                    pr.op("dve", lambda e, vps=vps, tvt=tvt, jv=jv, C=C, n=n: e.scalar_tensor_tensor(
                        out=tvt[:, 0:n], in0=vps[:, 1:C - 1], scalar=self.vcol(f"fcw{li}_1", jv), in1=tvt[:, 0:n],
                        op0=ALU.mult, op1=ALU.add), reads=[("ps", pb + 1), ktv, "vecs"], writes=[ktv])
                    pr.op("dve", lambda e, vps=vps, tvt=tvt, jv=jv, C=C, n=n: e.scalar_tensor_tensor(
                        out=tvt[:, 0:n], in0=vps[:, 0:C - 2], scalar=self.vcol(f"fcw{li}_0", jv), in1=tvt[:, 0:n],
                        op0=ALU.mult, op1=ALU.add), reads=[("ps", pb + 1), ktv, "vecs"], writes=[ktv])
                    deferred_next.append(lambda ggt=ggt, tvt=tvt, j=j, n=n, kgg=kgg, ktv=ktv: pr.op(
                        "dve", lambda e: e.tensor_tensor(out=gvT[:, j, 0:n], in0=ggt[:, 0:n], in1=tvt[:, 0:n], op=ALU.mult),
                        reads=[kgg, ktv], writes=[("gvT", j)]))
                    for f_ in deferred:
                        f_()
                    deferred, deferred_next = deferred_next, []
            for f_ in deferred:
                f_()
            deferred = []
            kgv = [("gvT", j) for j in range(32)]
            sbank = 7
            sps = self.psb[sbank]
            for o2 in range(4):
                wd = wds[nwd % NWD]
                kwd = ("wd", li, nwd % NWD)
                nwd += 1
                self.pump_until(("wdn_b", li, o2))
                pr.dma("sp", wd[:], self.wdn_b[li, o2].rearrange("p (k c) -> p k c", k=32),
                       reads=[("wdn_b", li, o2)], writes=[kwd])
                for oo in range(2):
                    o = o2 * 2 + oo
                    fb = (npair % 3) * 2
                    npair += 1
                    fps = self.psb[fb]

                    def mmd(e, wd=wd, fps=fps, oo=oo, n=n):
                        for k in range(32):
                            i = e.matmul(fps[:, 0:n], wd[:, k, oo * 128:(oo + 1) * 128], gvT[:, k, 0:n],
                                         start=(k == 0), stop=(k == 31))
                        return i
                    pr.op("pe", mmd, reads=[kwd] + kgv, writes=[("ps", fb)])
                    for f_ in deferred:
                        f_()
                    deferred = []
                    pr.op("act", lambda e, fps=fps, o=o, n=n: e.activation(out=fT[:, o, 0:n], in_=fps[:, 0:n], func=AF.Identity,
                                                                            scale=self.vcol(f"nfpost{li}", o)),
                          reads=[("ps", fb), "vecs"], writes=[("fT", o)])
                    pr.op("act", lambda e, fps=fps, o=o, n=n: e.activation(out=sq[:, o, 0:n], in_=fps[:, 0:n], func=AF.Square),
                          reads=[("ps", fb)], writes=[(f"f{li}_sq", o)])
                    for f_ in deferred:
                        f_()
                    deferred = [lambda o=o, n=n: pr.op(
                        "pe", lambda e: e.matmul(sps[:, 0:n], self.ones_b[:], sq[:, o, 0:n], start=(o == 0), stop=(o == 7)),
                        reads=[(f"f{li}_sq", o), "ones_b"], writes=[("ps", sbank)])]
            for f_ in deferred:
                f_()
            deferred = []
            pr.op("act", lambda e, n=n: e.activation(out=rstd2[:, 0:n], in_=sps[:, 0:n], func=AF.Ln, scale=1.0 / D, bias=self.vcol("eps")),
                  reads=[("ps", sbank), "vecs"], writes=["rstd2"])
            pr.op("act", lambda e, n=n: e.activation(out=rstd2[:, 0:n], in_=rstd2[:, 0:n], func=AF.Exp, scale=-0.5),
                  reads=["rstd2"], writes=["rstd2"])
            for o in range(8):
                pr.op("dve", lambda e, o=o, n=n: e.tensor_tensor(out=fT[:, o, 0:n], in0=fT[:, o, 0:n], in1=rstd2[:, 0:n], op=ALU.mult),
                      reads=[("fT", o), "rstd2"], writes=[("fT", o)])
                pr.op("dve", lambda e, o=o, n=n, C=C: e.tensor_tensor(out=xnew[:, o, 0:n], in0=fT[:, o, 0:n], in1=xblk[:, o, 2:C], op=ALU.add),
                      reads=[("fT", o)] + kxr, writes=[("xnew", o)])
            pr.dma("pool", dst[:, :, t0:t0 + n], xnew[:, :, 0:n], reads=[("xnew", o) for o in range(8)],
                   writes=[("xT", 1 - self.cur)])
            self.pump_casts(6)
        self.cur = 1 - self.cur

    def phase_lru(self, li):
        pr, S, A = self.pr, self.S, self.A
        pr.fence()
        A.reset()
        j = li // 2
        T = 512
        src = self.xT[self.cur].rearrange("(c p) t -> p c t", p=128)
        dst = self.xT[1 - self.cur].rearrange("(c p) t -> p c t", p=128)
        win = A.alloc([128, 20, 1024], BF16)
        wgx = A.alloc([128, 10, 128], BF16)
        wga = A.alloc([128, 10, 128], BF16)
        wout = A.alloc([128, 10, 1024], BF16)
        for key in (("lwin", j), ("lwgx", j), ("lwga", j), ("lwout", j)):
            self.pump_until(key)
        pr.dma("sp", win, self.lwin_b[j].rearrange("(o p) c -> p o c", p=128), reads=[("lwin", j)], writes=["l_win"])
        pr.dma("sp", wgx, self.lwgx_b[j].rearrange("p (h c) -> p h c", h=10), reads=[("lwgx", j)], writes=["l_wgx"])
        pr.dma("sp", wga, self.lwga_b[j].rearrange("p (h c) -> p h c", h=10), reads=[("lwga", j)], writes=["l_wga"])
        pr.dma("sp", wout, self.lwout_b[j].rearrange("p (k c) -> p k c", k=10), reads=[("lwout", j)], writes=["l_wout"])
        xblk = A.alloc([128, 8, T], F32)
        sq = A.alloc([128, 8, T], BF16)
        hT = A.alloc([128, 8, T], BF16)
        rstd = A.alloc([128, T], F32)
        rstd2 = A.alloc([128, T], F32)
        hy = A.alloc([128, 10, T], BF16)
        mT = A.alloc([128, 8, T], F32)
        halo = A.alloc([128, 10, 3], F32)
        hstate = A.alloc([128, 10], F32)
        c8 = A.alloc([128, 10], F32)
        c16 = A.alloc([128, 10], F32)
        NB = 4
        ybr = [A.alloc([128, T], F32) for _ in range(NB)]
        XP = [A.alloc([128, T + 3], F32) for _ in range(NB)]
        tcv = [A.alloc([128, T], F32) for _ in range(NB)]
        xbb = [A.alloc([128, T], BF16) for _ in range(NB)]
        gx = [A.alloc([128, T], F32) for _ in range(2)]
        ga = [A.alloc([128, T], F32) for _ in range(2)]
        at = [A.alloc([128, T], F32) for _ in range(2)]
        mu = [A.alloc([128, T], F32) for _ in range(2)]
        bt = [A.alloc([128, T], F32) for _ in range(2)]
        hs = [A.alloc([128, T], F32) for _ in range(2)]
        lam = self.vcol(f"llam{j}", 0, 10)
        pr.op("act", lambda e: e.activation(out=c8, in_=lam, func=AF.Exp, scale=-1.0), reads=["vecs"], writes=["l_c8"], small=True)
        pr.op("act", lambda e: e.activation(out=c8, in_=c8, func=AF.Ln, bias=self.vcol("one")), reads=["l_c8", "vecs"], writes=["l_c8"], small=True)
        pr.op("dve", lambda e: e.tensor_scalar(out=c16, in0=c8, scalar1=-16.0, scalar2=None, op0=ALU.mult), reads=["l_c8"], writes=["l_c16"], small=True)
        pr.op("dve", lambda e: e.tensor_scalar(out=c8, in0=c8, scalar1=-8.0, scalar2=None, op0=ALU.mult), reads=["l_c8", "l_c16"], writes=["l_c8"], small=True)
        pr.op("pool", lambda e: e.memset(halo, 0.0), writes=["l_halo"])
        pr.op("pool", lambda e: e.memset(hstate, 0.0), writes=["l_hstate"])
        nbank = 0
        it = 0
        for wi in range(S // T):
            t0 = wi * T
            kx = ("l_xblk",)
            pr.dma("sp", xblk, src[:, :, t0:t0 + T], reads=[("xT", self.cur)], writes=[kx])
            self._prenorm_multi(xblk, [kx], T, f"nmpre{li}", hT, "l_hT", sq, "l_sq", rstd, "l_rstd", 6)
            khs = [("l_hT", c) for c in range(8)]
            def mmin(e, ps, oc):
                for k in range(8):
                    i = e.matmul(ps[:, 0:T], win[:, oc, k * 128:(k + 1) * 128], hT[:, k, :], start=(k == 0), stop=(k == 7))
                return i

            def lru_s1(c, p):
                nonlocal nbank
                yb, xb = nbank % 6, (nbank + 1) % 6
                nbank += 2
                yps, xps = self.psb[yb], self.psb[xb]
                ybt, XPt, tct, xbt = ybr[p], XP[p], tcv[p], xbb[p]
                K = lambda n: (n, p)
                pr.op("pe", lambda e: mmin(e, yps, c), reads=["l_win"] + khs, writes=[("ps", yb)])
                pr.op("pe", lambda e: mmin(e, xps, 10 + c), reads=["l_win"] + khs, writes=[("ps", xb)])
                pr.op("act", lambda e: e.activation(out=ybt, in_=yps[:, 0:T], func=AF.Gelu_apprx_tanh, bias=self.vcol(f"lbin{j}", c)),
                      reads=[("ps", yb), "vecs"], writes=[K("ybr")])
                pr.op("pool", lambda e: e.tensor_copy(out=XPt[:, 0:3], in_=halo[:, c, :]), reads=["l_halo"], writes=[K("XPh")])
                pr.op("act", lambda e: e.activation(out=XPt[:, 3:T + 3], in_=xps[:, 0:T], func=AF.Identity, bias=self.vcol(f"lbin{j}", 10 + c)),
                      reads=[("ps", xb), "vecs"], writes=[K("XP")])
                pr.op("pool", lambda e: e.tensor_copy(out=halo[:, c, :], in_=XPt[:, T:T + 3]), reads=[K("XP"), K("XPh")], writes=["l_halo"])
                pr.op("dve", lambda e: e.tensor_scalar(out=tct, in0=XPt[:, 3:T + 3], scalar1=self.vcol(f"lcw{j}_3", c),
                                                       scalar2=self.vcol(f"lcb{j}", c), op0=ALU.mult, op1=ALU.add),
                      reads=[K("XP"), "vecs"], writes=[K("tcv")])
                for tap in (2, 1, 0):
                    pr.op("dve", lambda e, tap=tap: e.scalar_tensor_tensor(
                        out=tct, in0=XPt[:, tap:tap + T], scalar=self.vcol(f"lcw{j}_{tap}", c), in1=tct, op0=ALU.mult, op1=ALU.add),
                        reads=[K("XP"), K("XPh"), K("tcv"), "vecs"], writes=[K("tcv")])
                pr.op("pool", lambda e: e.tensor_copy(out=xbt, in_=tct), reads=[K("tcv")], writes=[K("xbb")])

            def lru_s2(cs_, ps_):
                nonlocal nbank
                ctx = []
                for c, p in zip(cs_, ps_):
                    gxb, gab = nbank % 6, (nbank + 1) % 6
                    nbank += 2
                    q2 = c % 2
                    ctx.append(dict(c=c, p=p, q2=q2, gxb=gxb, gab=gab, gxps=self.psb[gxb], gaps=self.psb[gab],
                                    ybt=ybr[p], tct=tcv[p], xbt=xbb[p], gxt=gx[q2], gat=ga[q2], att=at[q2], mut=mu[q2],
                                    btt=bt[q2], hst=hs[q2]))
                for d_ in ctx:
                    c, p, q2 = d_["c"], d_["p"], d_["q2"]
                    pr.op("pe", lambda e, d_=d_: e.matmul(d_["gxps"][:, 0:T], wgx[:, d_["c"], :], d_["xbt"], start=True, stop=True),
                          reads=["l_wgx", ("xbb", p)], writes=[("ps", d_["gxb"])])
                    pr.op("pe", lambda e, d_=d_: e.matmul(d_["gaps"][:, 0:T], wga[:, d_["c"], :], d_["xbt"], start=True, stop=True),
                          reads=["l_wga", ("xbb", p)], writes=[("ps", d_["gab"])])
                for d_ in ctx:
                    c, p, q2 = d_["c"], d_["p"], d_["q2"]
                    pr.op("act", lambda e, d_=d_: e.activation(out=d_["gxt"], in_=d_["gxps"][:, 0:T], func=AF.Sigmoid, bias=self.vcol(f"lbgx{j}", d_["c"])),
                          reads=[("ps", d_["gxb"]), "vecs"], writes=[("gx", q2)])
                    pr.op("act", lambda e, d_=d_: e.activation(out=d_["gat"], in_=d_["gaps"][:, 0:T], func=AF.Sigmoid, bias=self.vcol(f"lbga{j}", d_["c"])),
                          reads=[("ps", d_["gab"]), "vecs"], writes=[("ga", q2)])
                for d_ in ctx:
                    c, p, q2 = d_["c"], d_["p"], d_["q2"]
                    pr.op("act", lambda e, d_=d_: e.activation(out=d_["att"], in_=d_["gat"], func=AF.Exp, scale=c8[:, d_["c"]:d_["c"] + 1]),
                          reads=[("ga", q2), "l_c8"], writes=[("at", q2)])
                    pr.op("dve", lambda e, d_=d_: e.tensor_tensor(out=d_["btt"], in0=d_["gxt"], in1=d_["tct"], op=ALU.mult),
                          reads=[("gx", q2), ("tcv", p)], writes=[("bt", q2)])
                    pr.op("dve", lambda e, d_=d_: e.tensor_tensor(out=d_["mut"], in0=d_["att"], in1=d_["att"], op=ALU.mult),
                          reads=[("at", q2)], writes=[("mu", q2)])
                    pr.op("dve", lambda e, d_=d_: e.tensor_scalar(out=d_["mut"], in0=d_["mut"], scalar1=1.0, scalar2=None, op0=ALU.min),
                          reads=[("mu", q2)], writes=[("mu", q2)])
                for d_ in ctx:
                    c, p, q2 = d_["c"], d_["p"], d_["q2"]
                    pr.op("act", lambda e, d_=d_: e.activation(out=d_["mut"], in_=d_["mut"], func=AF.Sqrt, scale=-1.0, bias=self.vcol("one")),
                          reads=[("mu", q2), "vecs"], writes=[("mu", q2)])
                for d_ in ctx:
                    c, p, q2 = d_["c"], d_["p"], d_["q2"]
                    pr.op("dve", lambda e, d_=d_: e.tensor_tensor(out=d_["btt"], in0=d_["btt"], in1=d_["mut"], op=ALU.mult),
                          reads=[("bt", q2), ("mu", q2)], writes=[("bt", q2)])
                    pr.op("dve", lambda e, d_=d_: e.tensor_tensor_scan(out=d_["hst"], data0=d_["att"], data1=d_["btt"],
                                                                       initial=hstate[:, d_["c"]:d_["c"] + 1], op0=ALU.mult, op1=ALU.add),
                          reads=[("at", q2), ("bt", q2), "l_hstate"], writes=[("hs", q2)])
                    pr.op("pool", lambda e, d_=d_: e.tensor_copy(out=hstate[:, d_["c"]:d_["c"] + 1], in_=d_["hst"][:, T - 1:T]),
                          reads=[("hs", q2)], writes=["l_hstate"])
                    pr.op("dve", lambda e, d_=d_: e.tensor_tensor(out=hy[:, d_["c"], :], in0=d_["hst"], in1=d_["ybt"], op=ALU.mult),
                          reads=[("hs", q2), ("ybr", p)], writes=[("l_hy", d_["c"])])

            ps_of = {}
            for pk in range(6):
                if pk < 5:
                    for c in (2 * pk, 2 * pk + 1):
                        ps_of[c] = it % NB
                        it += 1
                        lru_s1(c, ps_of[c])
                if pk >= 1:
                    cs_ = (2 * (pk - 1), 2 * (pk - 1) + 1)
                    lru_s2(cs_, [ps_of[c] for c in cs_])
            khy = [("l_hy", c) for c in range(10)]
            sbank = 7
            sps = self.psb[sbank]
            for o in range(8):
                mb = nbank % 6
                nbank += 1
                mps = self.psb[mb]

                def mmo(e, mps=mps, o=o):
                    for k in range(10):
                        i = e.matmul(mps[:, 0:T], wout[:, k, o * 128:(o + 1) * 128], hy[:, k, :], start=(k == 0), stop=(k == 9))
                    return i
                pr.op("pe", mmo, reads=["l_wout"] + khy, writes=[("ps", mb)])
                pr.op("act", lambda e, mps=mps, o=o: e.activation(out=mT[:, o, :], in_=mps[:, 0:T], func=AF.Identity, bias=self.vcol(f"lbout{j}", o)),
                      reads=[("ps", mb), "vecs"], writes=[("l_mT", o)])
                pr.op("act", lambda e, mps=mps, o=o: e.activation(out=sq[:, o, :], in_=mps[:, 0:T], func=AF.Square, bias=self.vcol(f"lbout{j}", o)),
                      reads=[("ps", mb), "vecs"], writes=[("l_sq", o)])
                pr.op("pe", lambda e, o=o: e.matmul(sps[:, 0:T], self.ones_b[:], sq[:, o, :], start=(o == 0), stop=(o == 7)),
                      reads=[("l_sq", o), "ones_b"], writes=[("ps", sbank)])
            self._resid(sps, sbank, rstd2, "l_rstd2", mT, "l_mT", f"nmpost{li}", xblk, [kx], 0, T, dst, t0)
            self.pump_casts(6)
        self.cur = 1 - self.cur

    def _resid(self, sps, sbank, rstd2, krs, mT, kmT, wname, xblk, kxr, xoff, n, dst, t0):
        pr = self.pr
        pr.op("act", lambda e: e.activation(out=rstd2[:, 0:n], in_=sps[:, 0:n], func=AF.Ln, scale=1.0 / D, bias=self.vcol("eps")),
              reads=[("ps", sbank), "vecs"], writes=[krs])
        pr.op("act", lambda e: e.activation(out=rstd2[:, 0:n], in_=rstd2[:, 0:n], func=AF.Exp, scale=-0.5),
              reads=[krs], writes=[krs])
        for o in range(8):
            pr.op("dve", lambda e, o=o: e.scalar_tensor_tensor(out=mT[:, o, 0:n], in0=mT[:, o, 0:n], scalar=self.vcol(wname, o),
                                                               in1=rstd2[:, 0:n], op0=ALU.mult, op1=ALU.mult),
                  reads=[(kmT, o), krs, "vecs"], writes=[(kmT, o)])
            pr.op("dve", lambda e, o=o: e.tensor_tensor(out=mT[:, o, 0:n], in0=mT[:, o, 0:n], in1=xblk[:, o, xoff:xoff + n], op=ALU.add),
                  reads=[(kmT, o)] + kxr, writes=[(kmT, o)])
        pr.dma("pool", dst[:, :, t0:t0 + n], mT[:, :, 0:n], reads=[(kmT, o) for o in range(8)], writes=[("xT", 1 - self.cur)])

    def phase_ssd(self, li):
        pr, S, A = self.pr, self.S, self.A
        pr.fence()
        A.reset()
        j = li // 2
        T = 512
        NQ = T // 128
        src = self.xT[self.cur].rearrange("(c p) t -> p c t", p=128)
        dst = self.xT[1 - self.cur].rearrange("(c p) t -> p c t", p=128)
        for b_ in range(10):
            self.pump_until(("swin", j, b_))
        self.pump_until(("swdt", j))
        for o in range(8):
            self.pump_until(("swout", j, o))
        wdt = A.alloc([128, 8, 32], BF16)
        pr.dma("sp", wdt, self.swdt_b[j].rearrange("p (k c) -> p k c", k=8), reads=[("swdt", j)], writes=["s_wdt"])
        normw = A.alloc([128, 2048], F32)
        pr.dma("sp", normw, self.snorm_d[j], writes=["s_normw"])
        sm = A.alloc([128, 96], F32)
        pr.dma("sp", sm, self.ssm_d[j], writes=["s_sm"])
        a_b = A.alloc([128, 32], F32)
        pr.op("act", lambda e: e.activation(out=a_b, in_=sm[:, 32:64], func=AF.Exp), reads=["s_sm"], writes=["s_ab"], small=True)
        pr.op("dve", lambda e: e.tensor_scalar(out=a_b, in0=a_b, scalar1=-1.0, scalar2=None, op0=ALU.mult), reads=["s_ab"], writes=["s_ab"], small=True)
        halo = A.alloc([128, 24, 3], F32)
        pr.op("pool", lambda e: e.memset(halo, 0.0), writes=["s_halo"])
        Sst = A.alloc([128, 2048], F32)
        Sbf = A.alloc([128, 2048], BF16)
        pr.op("pool", lambda e: e.memset(Sst, 0.0), writes=[("s_S", g) for g in range(4)])
        pr.op("pool", lambda e: e.memset(Sbf, 0.0), writes=[("s_Sbf", g) for g in range(4)])
        xblk = A.alloc([128, 8, T], F32)
        hT = A.alloc([128, 8, T], BF16)
        rstd = A.alloc([128, T], F32)
        rstd2 = rstd
        zs = A.alloc([128, NQ, 2048], BF16)
        mT = zs.bitcast(F32).rearrange("p q (o t) -> p (q o) t", o=2)
        xcT = A.alloc([128, 16, T], BF16)
        BT = A.alloc([128, 4, T], BF16)
        CT = A.alloc([128, 4, T], BF16)
        ynT = A.alloc([128, 16, T], BF16)
        sq = ynT[:, 0:8, :]
        sqp = [A.alloc([128, T], BF16) for _ in range(1)]
        NW = 2
        wsl = [A.alloc([128, 8, 512], BF16) for _ in range(NW)]
        wos = [A.alloc([128, 16, 128], BF16) for _ in range(2)]
        XP = [A.alloc([128, T + 3], F32) for _ in range(2)]
        tcv = [A.alloc([128, T], F32) for _ in range(2)]
        vdt = A.alloc([128, 32], F32)
        dts = A.alloc([128, 32], F32)
        das_ = [A.alloc([128, 32], F32) for _ in range(2)]
        css = A.alloc([128, 32], F32)
        ecs_ = [A.alloc([128, 32], F32) for _ in range(2)]
        dout = A.alloc([128, 32], F32)
        etot_ = [A.alloc([128, 32], F32) for _ in range(2)]
        xdt_ = [A.alloc([128, 2048], BF16) for _ in range(2)]
        xDb_ = [A.alloc([128, 2048], BF16) for _ in range(2)]
        xdd_ = [A.alloc([128, 2048], BF16) for _ in range(2)]
        Btm_ = [A.alloc([128, 512], BF16) for _ in range(2)]
        rhsM = [A.alloc([128, 8, 128], F32) for _ in range(2)]
        Eg = [A.alloc([128, 8, 128], BF16) for _ in range(2)]
        CBm = [A.alloc([128, 128], F32) for _ in range(2)]
        scT = [A.alloc([128, 8, 128], BF16) for _ in range(2)]
        yoffs = [A.alloc([128, 512], F32) for _ in range(1)]
        ysb = A.alloc([128, 2048], F32)
        ssq = A.alloc([128, 4], F32)
        gsc2 = yoffs
        rs4 = A.alloc([128, 4], F32)
        gn = A.alloc([128, 2048], BF16)
        self._nb = getattr(self, "_nb", 0)

        def bank():
            b = self._nb % 7
            self._nb += 1
            return b
        nws = 0
        nwo = 0
        nxp = 0
        ngr = 0
        for wi in range(S // T):
            t0 = wi * T
            kx = ("s_xblk",)
            pr.dma("sp", xblk, src[:, :, t0:t0 + T], reads=[("xT", self.cur)], writes=[kx])
            self._prenorm_multi(xblk, [kx], T, f"nmpre{li}", hT, "s_hT", sq, "s_ynT", rstd, "s_rstd", 7)
            khs = [("s_hT", c) for c in range(8)]
            sdef = []
            for blk in range(10):
                ws = wsl[nws % NW]
                kw = ("s_w", nws % NW)
                nws += 1
                pr.dma("sp", ws, self.swin_b[j, blk].rearrange("p (k c) -> p k c", k=8), reads=[("swin", j, blk)], writes=[kw])
                if blk < 4:
                    for q in range(NQ):
                        zb = bank()
                        zps = self.psb[zb]

                        def mmz(e, zps=zps, ws=ws, q=q):
                            for k in range(8):
                                i = e.matmul(zps[:, :], hT[:, k, q * 128:(q + 1) * 128], ws[:, k, :], start=(k == 0), stop=(k == 7))
                            return i
                        pr.op("pe", mmz, reads=[kw] + khs, writes=[("ps", zb)])
                        pr.op("act", lambda e, zps=zps, q=q, blk=blk: e.activation(out=zs[:, q, blk * 512:(blk + 1) * 512], in_=zps[:, :], func=AF.Silu),
                              reads=[("ps", zb)], writes=[("s_zs", q, blk)])
                else:
                    for i4 in range(4):
                        oc = (blk - 4) * 4 + i4
                        xb = bank()
                        xps = self.psb[xb]

                        def mmx(e, xps=xps, ws=ws, i4=i4):
                            for k in range(8):
                                i = e.matmul(xps[:, :], ws[:, k, i4 * 128:(i4 + 1) * 128], hT[:, k, :], start=(k == 0), stop=(k == 7))
                            return i
                        pr.op("pe", mmx, reads=[kw] + khs, writes=[("ps", xb)])
                        p = nxp % 2
                        nxp += 1
                        XPt, tct = XP[p], tcv[p]
                        pr.op("pool", lambda e, XPt=XPt, oc=oc: e.tensor_copy(out=XPt[:, 0:3], in_=halo[:, oc, :]), reads=["s_halo"], writes=[("s_XPh", p)])
                        pr.op("act", lambda e, XPt=XPt, xps=xps: e.activation(out=XPt[:, 3:T + 3], in_=xps[:, :], func=AF.Identity),
                              reads=[("ps", xb)], writes=[("s_XP", p)])
                        pr.op("pool", lambda e, XPt=XPt, oc=oc: e.tensor_copy(out=halo[:, oc, :], in_=XPt[:, T:T + 3]),
                              reads=[("s_XP", p), ("s_XPh", p)], writes=["s_halo"])
                        pr.op("act", lambda e, xps=xps, tct=tct, oc=oc: e.activation(out=tct, in_=xps[:, :], func=AF.Identity,
                                                                               scale=self.vcol(f"scw{j}_3", oc), bias=self.vcol(f"scb{j}", oc)),
                              reads=[("ps", xb), "vecs"], writes=[("s_tcv", p)])
                        for tap in (2, 1, 0):
                            pr.op("dve", lambda e, XPt=XPt, tct=tct, oc=oc, tap=tap: e.scalar_tensor_tensor(
                                out=tct, in0=XPt[:, tap:tap + T], scalar=self.vcol(f"scw{j}_{tap}", oc), in1=tct, op0=ALU.mult, op1=ALU.add),
                                reads=[("s_XP", p), ("s_XPh", p), ("s_tcv", p), "vecs"], writes=[("s_tcv", p)])
                        if oc < 16:
                            o_ap, ko = xcT[:, oc, :], ("s_xcT", oc)
                        elif oc < 20:
                            o_ap, ko = BT[:, oc - 16, :], ("s_BT", oc - 16)
                        else:
                            o_ap, ko = CT[:, oc - 20, :], ("s_CT", oc - 20)
                        for f_ in sdef:
                            f_()
                        sdef = [lambda o_ap=o_ap, tct=tct, p=p, ko=ko: pr.op(
                            "act", lambda e: e.activation(out=o_ap, in_=tct, func=AF.Silu), reads=[("s_tcv", p)], writes=[ko])]
            for f_ in sdef:
                f_()
            sdef = []
            if wi == 0:
                self.dump("hT", hT, khs)
                self.dump("zs0", zs[:, 0, :], [("s_zs", 0, b_) for b_ in range(4)])
                self.dump("xcT", xcT, [("s_xcT", c) for c in range(16)])
                self.dump("BT", BT, [("s_BT", c) for c in range(4)])
                self.dump("CT", CT, [("s_CT", c) for c in range(4)])
            def prep(q, par):
                qs = slice(q * 128, (q + 1) * 128)
                db = bank()
                dps = self.psb[db]

                def mmdt(e, dps=dps, q=q):
                    for k in range(8):
                        i = e.matmul(dps[:, 0:32], hT[:, k, q * 128:(q + 1) * 128], wdt[:, k, :], start=(k == 0), stop=(k == 7))
                    return i
                pr.op("pe", mmdt, reads=["s_wdt"] + khs, writes=[("ps", db)])
                pr.op("dve", lambda e, dps=dps: e.tensor_tensor(out=vdt, in0=dps[:, 0:32], in1=sm[:, 0:32], op=ALU.add),
                      reads=[("ps", db), "s_sm"], writes=["s_vdt"], small=True)
                pr.op("act", lambda e: e.activation(out=vdt, in_=vdt, func=AF.Exp), reads=["s_vdt"], writes=["s_vdt"], small=True)
                pr.op("act", lambda e: e.activation(out=dts, in_=vdt, func=AF.Ln, bias=self.vcol("one")), reads=["s_vdt", "vecs"], writes=["s_dts"], small=True)
                pr.op("dve", lambda e: e.tensor_tensor(out=das_[par], in0=dts, in1=a_b, op=ALU.mult), reads=["s_dts", "s_ab"], writes=[("s_das", par)], small=True)
                cb_ = bank()
                cps = self.psb[cb_]

                def mmcs(e, cps=cps):
                    e.matmul(cps[:, 0:32], self.triU[:], das_[par], start=True, stop=True)
                    return e.matmul(cps[:, 32:64], self.ones_f[:], das_[par], start=True, stop=True)
                pr.op("pe", mmcs, reads=[("s_das", par), "triU", "ones_f"], writes=[("ps", cb_)])
                pr.op("dve", lambda e, cps=cps: e.tensor_copy(out=css, in_=cps[:, 0:32]), reads=[("ps", cb_)], writes=["s_css"], small=True)
                pr.op("act", lambda e: e.activation(out=ecs_[par], in_=css, func=AF.Exp), reads=["s_css"], writes=[("s_ecs", par)], small=True)
                pr.op("dve", lambda e, cps=cps: e.tensor_tensor(out=dout, in0=cps[:, 32:64], in1=css, op=ALU.subtract),
                      reads=[("ps", cb_), "s_css"], writes=["s_dout"], small=True)
                pr.op("act", lambda e: e.activation(out=dout, in_=dout, func=AF.Exp), reads=["s_dout"], writes=["s_dout"], small=True)
                pr.op("act", lambda e, cps=cps: e.activation(out=etot_[par], in_=cps[:, 32:64], func=AF.Exp), reads=[("ps", cb_)], writes=[("s_etot", par)], small=True)
                for hb in range(2):
                    tb = bank()
                    tps = self.psb[tb].bitcast(BF16)

                    def trx(e, tps=tps, hb=hb, q=q):
                        for c8_ in range(8):
                            cx = hb * 8 + c8_
                            i = e.transpose(tps[:, c8_ * 128:(c8_ + 1) * 128], xcT[:, cx, q * 128:(q + 1) * 128], self.ident_b[:])
                        return i
                    pr.op("pe", trx, reads=[("s_xcT", hb * 8 + c) for c in range(8)] + ["ident_b"], writes=[("ps", tb)])
                    h0 = hb * 16
                    pr.op("dve", lambda e, tps=tps, hb=hb, h0=h0: e.tensor_tensor(
                        out=xdt_[par][:, hb * 1024:(hb + 1) * 1024].rearrange("p (h d) -> p h d", h=16),
                        in0=tps.rearrange("p (h d) -> p h d", h=16),
                        in1=dts[:, h0:h0 + 16].unsqueeze(2).to_broadcast([128, 16, 64]), op=ALU.mult),
                        reads=[("ps", tb), "s_dts"], writes=[("s_xdt", par, hb)])
                    pr.op("dve", lambda e, tps=tps, hb=hb, h0=h0: e.tensor_tensor(
                        out=xDb_[par][:, hb * 1024:(hb + 1) * 1024].rearrange("p (h d) -> p h d", h=16),
                        in0=tps.rearrange("p (h d) -> p h d", h=16),
                        in1=sm[:, 64 + h0:64 + h0 + 16].unsqueeze(2).to_broadcast([128, 16, 64]), op=ALU.mult),
                        reads=[("ps", tb), "s_sm"], writes=[("s_xDb", par, hb)])
                    pr.op("pool", lambda e, hb=hb, h0=h0: e.tensor_tensor(
                        out=xdd_[par][:, hb * 1024:(hb + 1) * 1024].rearrange("p (h d) -> p h d", h=16),
                        in0=xdt_[par][:, hb * 1024:(hb + 1) * 1024].rearrange("p (h d) -> p h d", h=16),
                        in1=dout[:, h0:h0 + 16].unsqueeze(2).to_broadcast([128, 16, 64]), op=ALU.mult),
                        reads=[("s_xdt", par, hb), "s_dout"], writes=[("s_xdd", par, hb)])
                bb = bank()
                bps = self.psb[bb].bitcast(BF16)

                def trb(e, bps=bps, q=q):
                    for g in range(4):
                        i = e.transpose(bps[:, g * 128:(g + 1) * 128], BT[:, g, q * 128:(q + 1) * 128], self.ident_b[:])
                    return i
                pr.op("pe", trb, reads=[("s_BT", g) for g in range(4)] + ["ident_b"], writes=[("ps", bb)])
                pr.op("act", lambda e, bps=bps: e.activation(out=Btm_[par], in_=bps[:, 0:512], func=AF.Identity), reads=[("ps", bb)], writes=[("s_Btm", par)])
            prep(0, 0)
            for q in range(NQ):
                qs = slice(q * 128, (q + 1) * 128)
                par = q % 2
                if q + 1 < NQ:
                    prep(q + 1, (q + 1) % 2)
                hbk_of = lambda g: g // 2

                def stageA(g, q=q, qs=qs, par=par):
                    pg = g % 2
                    gh = slice(g * 8, (g + 1) * 8)
                    rM, Et, CBt, sct = rhsM[pg], Eg[pg], CBm[pg], scT[pg]
                    pr.op("dve", lambda e: e.tensor_tensor(
                        out=rM, in0=self.triU[:].unsqueeze(1).to_broadcast([128, 8, 128]),
                        in1=das_[par][:, gh].unsqueeze(2).to_broadcast([128, 8, 128]), op=ALU.mult),
                        reads=[("s_das", par), "triU"], writes=[("s_rhsM", pg)])
                    d0, d1 = bank(), bank()

                    def mmD(e):
                        e.matmul(self.psb[d0][:, :], self.triSL[:], rM[:, 0:4, :], start=True, stop=True)
                        return e.matmul(self.psb[d1][:, :], self.triSL[:], rM[:, 4:8, :], start=True, stop=True)
                    pr.op("pe", mmD, reads=[("s_rhsM", pg), "triSL"], writes=[("ps", d0), ("ps", d1)])
                    pr.op("act", lambda e: e.activation(out=Et[:, 0:4, :], in_=self.psb[d0][:, :].rearrange("p (h l) -> p h l", h=4), func=AF.Exp),
                          reads=[("ps", d0)], writes=[("s_E", pg, 0)])
                    pr.op("act", lambda e: e.activation(out=Et[:, 4:8, :], in_=self.psb[d1][:, :].rearrange("p (h l) -> p h l", h=4), func=AF.Exp),
                          reads=[("ps", d1)], writes=[("s_E", pg, 1)])
                    cbb = bank()
                    cbps = self.psb[cbb]
                    pr.op("pe", lambda e: e.matmul(cbps[:, 0:128], BT[:, g, qs], CT[:, g, qs], start=True, stop=True),
                          reads=[("s_BT", g), ("s_CT", g)], writes=[("ps", cbb)])
                    pr.op("dve", lambda e: e.tensor_tensor(out=CBt, in0=cbps[:, 0:128], in1=self.triU[:], op=ALU.mult),
                          reads=[("ps", cbb), "triU"], writes=[("s_CBm", pg)], small=True)
                    pr.op("pool", lambda e: e.tensor_tensor(out=sct, in0=Et, in1=CBt.unsqueeze(1).to_broadcast([128, 8, 128]), op=ALU.mult),
                          reads=[("s_E", pg, 0), ("s_E", pg, 1), ("s_CBm", pg)], writes=[("s_scT", pg)])

                def stageB(g, q=q, qs=qs, par=par):
                    pg = g % 2
                    gh = slice(g * 8, (g + 1) * 8)
                    gc = slice(g * 512, (g + 1) * 512)
                    sct, yot = scT[pg], yoffs[0]
                    hbk = g // 2
                    ya, yo = bank(), bank()
                    yaps, yops = self.psb[ya], self.psb[yo]

                    def mmy(e):
                        e.matmul(yaps[:, :], self.ident_b[:], xDb_[par][:, gc], start=True, stop=False)
                        for hh in range(8):
                            c0 = g * 512 + hh * 64
                            i = e.matmul(yaps[:, hh * 64:(hh + 1) * 64], sct[:, hh, :], xdt_[par][:, c0:c0 + 64], start=False, stop=(hh == 7))
                        return i
                    pr.op("pe", mmy, reads=[("s_scT", pg), ("s_xdt", par, hbk), ("s_xDb", par, hbk), "ident_b"], writes=[("ps", ya)])
                    pr.op("pe", lambda e: e.matmul(yops[:, :], CT[:, g, qs], Sbf[:, gc], start=True, stop=True),
                          reads=[("s_CT", g), ("s_Sbf", g)], writes=[("ps", yo)])
                    pr.op("dve", lambda e: e.tensor_tensor(
                        out=yot.rearrange("p (h d) -> p h d", h=8), in0=yops[:, :].rearrange("p (h d) -> p h d", h=8),
                        in1=ecs_[par][:, gh].unsqueeze(2).to_broadcast([128, 8, 64]), op=ALU.mult),
                        reads=[("ps", yo), ("s_ecs", par)], writes=[("s_yoff", 0)])
                    pr.op("dve", lambda e: e.tensor_tensor(out=ysb[:, gc], in0=yaps[:, :], in1=yot, op=ALU.add),
                          reads=[("ps", ya), ("s_yoff", 0)], writes=[("s_ysb", g)])
                    sb_ = bank()
                    sps_ = self.psb[sb_]
                    pr.op("pe", lambda e: e.matmul(sps_[:, :], Btm_[par][:, g * 128:(g + 1) * 128], xdd_[par][:, gc], start=True, stop=True),
                          reads=[("s_Btm", par), ("s_xdd", par, hbk)], writes=[("ps", sb_)])
                    pr.op("dve", lambda e: e.tensor_tensor(
                        out=Sst[:, gc].rearrange("p (h d) -> p h d", h=8), in0=Sst[:, gc].rearrange("p (h d) -> p h d", h=8),
                        in1=etot_[par][:, gh].unsqueeze(2).to_broadcast([128, 8, 64]), op=ALU.mult),
                        reads=[("s_S", g), ("s_etot", par)], writes=[("s_S", g)])
                    pr.op("dve", lambda e: e.tensor_tensor(out=Sst[:, gc], in0=sps_[:, :], in1=Sst[:, gc], op=ALU.add),
                          reads=[("ps", sb_), ("s_S", g)], writes=[("s_S", g)])

                def stageC(g, q=q, par=par):
                    gc = slice(g * 512, (g + 1) * 512)
                    gst = gsc2[0]
                    pr.op("act", lambda e: e.activation(out=Sbf[:, gc], in_=Sst[:, gc], func=AF.Identity), reads=[("s_S", g)], writes=[("s_Sbf", g)])
                    pr.op("pool", lambda e: e.tensor_tensor(out=ysb[:, gc], in0=ysb[:, gc], in1=zs[:, q, gc], op=ALU.mult),
                          reads=[("s_ysb", g), ("s_zs", q, g)], writes=[("s_ysb", g)])
                    pr.op("act", lambda e: e.activation(out=gst, in_=ysb[:, gc], func=AF.Square),
                          reads=[("s_ysb", g)], writes=[("s_yoff", 0)])
                    pr.op("dve", lambda e: e.reduce_sum(out=ssq[:, g:g + 1], in_=gst, axis=mybir.AxisListType.X),
                          reads=[("s_yoff", 0)], writes=[("s_ssq", g)], small=True)

                stageA(0)
                stageA(1)
                stageB(0)
                stageA(2)
                stageB(1)
                stageA(3)
                stageB(2)
                stageC(0)
                stageB(3)
                stageC(1)
                stageC(2)
                stageC(3)
                kss = [("s_ssq", g) for g in range(4)]
                pr.op("dve", lambda e: e.tensor_scalar(out=rs4, in0=ssq, scalar1=1.0 / 512, scalar2=EPS, op0=ALU.mult, op1=ALU.add),
                      reads=kss, writes=["s_rs4"], force_same=True, small=True)
                pr.op("act", lambda e: e.activation(out=rs4, in_=rs4, func=AF.Ln), reads=["s_rs4"], writes=["s_rs4"], small=True)
                pr.op("act", lambda e: e.activation(out=rs4, in_=rs4, func=AF.Exp, scale=-0.5), reads=["s_rs4"], writes=["s_rs4"], force_same=True, small=True)
                for g in range(4):
                    gc = slice(g * 512, (g + 1) * 512)
                    pr.op("dve", lambda e, gc=gc, g=g: e.scalar_tensor_tensor(out=gn[:, gc], in0=ysb[:, gc], scalar=rs4[:, g:g + 1], in1=normw[:, gc],
                                                                              op0=ALU.mult, op1=ALU.mult),
                          reads=[("s_ysb", g), "s_rs4", "s_normw"], writes=[("s_gn", g)])
                for hb in range(2):
                    tb = bank()
                    tps = self.psb[tb].bitcast(BF16)

                    def trg(e, tps=tps, hb=hb):
                        for c8_ in range(8):
                            cx = hb * 8 + c8_
                            i = e.transpose(tps[:, c8_ * 128:(c8_ + 1) * 128], gn[:, cx * 128:(cx + 1) * 128], self.ident_b[:])
                        return i
                    pr.op("pe", trg, reads=[("s_gn", hb * 2), ("s_gn", hb * 2 + 1), "ident_b"], writes=[("ps", tb)])
                    pr.op("act", lambda e, tps=tps, hb=hb, qs=qs: e.activation(out=ynT[:, hb * 8:(hb + 1) * 8, qs], in_=tps.rearrange("p (c t) -> p c t", c=8), func=AF.Identity),
                          reads=[("ps", tb)], writes=[("s_ynT", hb * 8 + c) for c in range(8)])
            kyn = [("s_ynT", c) for c in range(16)]
            if wi == 0:
                self.dump("ynT", ynT, kyn)
            sbank = 7
            sps = self.psb[sbank]
            kzs_all = [("s_zs", q, b_) for q in range(NQ) for b_ in range(4)]
            for o in range(8):
                wo = wos[nwo % 2]
                kwo = ("s_wo", nwo % 2)
                nwo += 1
                pr.dma("sp", wo, self.swout_b[j, o].rearrange("p (k c) -> p k c", k=16), reads=[("swout", j, o)], writes=[kwo])
                mb = bank()
                mps = self.psb[mb]

                def mmo(e, mps=mps, wo=wo):
                    for k in range(16):
                        i = e.matmul(mps[:, :], wo[:, k, :], ynT[:, k, :], start=(k == 0), stop=(k == 15))
                    return i
                pr.op("pe", mmo, reads=[kwo] + kyn, writes=[("ps", mb)])
                sqt = sqp[0]
                pr.op("act", lambda e, mps=mps, o=o: e.activation(out=mT[:, o, :], in_=mps[:, :], func=AF.Identity),
                      reads=[("ps", mb)] + (kzs_all if o == 0 else []), writes=[("s_mT", o)] + (kzs_all if o == 0 else []))
                pr.op("act", lambda e, mps=mps, sqt=sqt: e.activation(out=sqt, in_=mps[:, :], func=AF.Square),
                      reads=[("ps", mb)], writes=[("s_sqp", 0)])
                pr.op("pe", lambda e, o=o, sqt=sqt: e.matmul(sps[:, :], self.ones_b[:], sqt, start=(o == 0), stop=(o == 7)),
                      reads=[("s_sqp", 0), "ones_b"], writes=[("ps", sbank)])
            self._resid(sps, sbank, rstd2, "s_rstd", mT, "s_mT", f"nmpost{li}", xblk, [kx], 0, T, dst, t0)
            for q in range(NQ):
                for b_ in range(4):
                    self.pr._record(self.pr.last_w[("xT", 1 - self.cur)], [], [("s_zs", q, b_)])
            self.pump_casts(6)
        self.cur = 1 - self.cur

    def _prenorm_multi(self, xblk, kxr, C, wname, hT, kh, sq, ksq, rstd, krstd, stat_bank):
        pr = self.pr
        ps = self.psb[stat_bank]
        for c in range(8):
            pr.op("act", lambda e, c=c: e.activation(out=sq[:, c, 0:C], in_=xblk[:, c, 0:C], func=AF.Square),
                  reads=kxr, writes=[(ksq, c)])

        def st(e):
            for c in range(8):
                i = e.matmul(ps[:, 0:C], self.ones_b[:], sq[:, c, 0:C], start=(c == 0), stop=(c == 7))
            return i
        pr.op("pe", st, reads=[(ksq, c) for c in range(8)] + ["ones_b"], writes=[("ps", stat_bank)])
        pr.op("act", lambda e: e.activation(out=rstd[:, 0:C], in_=ps[:, 0:C], func=AF.Ln, scale=1.0 / D, bias=self.vcol("eps")),
              reads=[("ps", stat_bank), "vecs"], writes=[krstd])
        pr.op("act", lambda e: e.activation(out=rstd[:, 0:C], in_=rstd[:, 0:C], func=AF.Exp, scale=-0.5),
              reads=[krstd], writes=[krstd])
        for c in range(8):
            pr.op("dve", lambda e, c=c: e.scalar_tensor_tensor(out=hT[:, c, 0:C], in0=xblk[:, c, 0:C], scalar=self.vcol(wname, c),
                                                               in1=rstd[:, 0:C], op0=ALU.mult, op1=ALU.mult),
                  reads=kxr + [krstd, "vecs"], writes=[(kh, c)])

    def build(self):
        for li in self.layers:
            if self.do_mixer and li % 2 == 0:
                j = li // 2
                for b_ in range(10):
                    self.add_cast(self.swin_b[j, b_], self.swin_f[j, b_], ("swin", j, b_))
                self.add_cast(self.swdt_b[j], self.swdt_f[j], ("swdt", j))
                for o in range(8):
                    self.add_cast(self.swout_b[j, o], self.swout_f[j, o], ("swout", j, o))
            if self.do_mixer and li % 2 == 1:
                j = li // 2
                self.add_cast(self.lwin_b[j], self.lwin_f[j], ("lwin", j))
                self.add_cast(self.lwgx_b[j], self.lwgx_f[j], ("lwgx", j))
                self.add_cast(self.lwga_b[j], self.lwga_f[j], ("lwga", j))
                self.add_cast(self.lwout_b[j], self.lwout_f[j], ("lwout", j))
            if self.do_ffn:
                for j2 in range(16):
                    self.add_cast(self.wup_b[li, j2], self.wup_f[li, j2], ("wup_b", li, j2))
                for o2 in range(4):
                    self.add_cast(self.wdn_b[li, o2], self.wdn_f[li, o2], ("wdn_b", li, o2))
        self.phase_in()
        for li in self.layers:
            if self.do_mixer:
                if li % 2 == 1:
                    self.phase_lru(li)
                else:
                    self.phase_ssd(li)
            if self.do_ffn:
                self.phase_ffn(li)
        self.phase_out()
        self.pr.finish_wait_all("sp")
        self.pr.emit()
        self.P.close()
        return self.nc


def pack_inputs(inp):
    vp = VecPack()
    vp.add_raw("eps", np.full((128, 1), EPS, np.float32))
    for li in range(DEPTH):
        vp.add(f"nmpre{li}", inp["norm_mix_pre"][li])
        vp.add(f"nmpost{li}", inp["norm_mix_post"][li])
        vp.add(f"nfpre{li}", inp["norm_ffn_pre"][li])
        vp.add(f"nfpost{li}", inp["norm_ffn_post"][li])
        for j in range(3):
            vp.add(f"fcw{li}_{j}", inp["ffn_conv_w"][li][j])
        vp.add(f"fcb{li}", inp["ffn_conv_b"][li])
    vp.add_raw("one", np.ones((128, 1), np.float32))
    for j in range(2):
        vp.add(f"lbin{j}", inp["lru_b_in"][j])
        for t in range(4):
            vp.add(f"lcw{j}_{t}", inp["lru_conv_w"][j][t])
        vp.add(f"lcb{j}", inp["lru_conv_b"][j])
        vp.add(f"lbgx{j}", inp["lru_b_gx"][j])
        vp.add(f"lbga{j}", inp["lru_b_ga"][j])
        vp.add(f"llam{j}", inp["lru_lambda"][j])
        vp.add(f"lbout{j}", inp["lru_b_out"][j])
    for j in range(2):
        for t in range(4):
            vp.add(f"scw{j}_{t}", inp["ssd_conv_w"][j][t])
        vp.add(f"scb{j}", inp["ssd_conv_b"][j])
    vecs = vp.build()
    swin = np.asarray(inp["ssd_w_in"], dtype=np.float32)
    sw = swin[:, :, :5120].reshape(2, 8, 128, 10, 512)
    swin_h = np.ascontiguousarray(sw.transpose(0, 3, 2, 1, 4)).reshape(2, 10, 128, 4096)
    sdt = swin[:, :, 5120:].reshape(2, 8, 128, 32)
    swdt_h = np.ascontiguousarray(sdt.transpose(0, 2, 1, 3)).reshape(2, 128, 256)
    swo = np.asarray(inp["ssd_w_out"], dtype=np.float32).reshape(2, 16, 128, 8, 128)
    swout_h = np.ascontiguousarray(swo.transpose(0, 3, 2, 1, 4)).reshape(2, 8, 128, 2048)
    snorm_h = np.ascontiguousarray(np.broadcast_to(np.asarray(inp["ssd_norm"], dtype=np.float32)[:, None, :], (2, 128, 2048)))
    ssm = np.concatenate([np.asarray(inp["ssd_dt_bias"], dtype=np.float32), np.asarray(inp["ssd_a_log"], dtype=np.float32),
                          np.asarray(inp["ssd_d"], dtype=np.float32)], axis=1)
    ssm_h = np.ascontiguousarray(np.broadcast_to(ssm[:, None, :], (2, 128, 96)))
    lwin = np.asarray(inp["lru_w_in"], dtype=np.float32)
    lw = lwin.reshape(2, 8, 128, 20, 128)
    lwin_h = np.ascontiguousarray(lw.transpose(0, 3, 2, 1, 4)).reshape(2, 2560, 1024)
    lwgx_h = np.ascontiguousarray(np.asarray(inp["lru_w_gx"], dtype=np.float32).transpose(0, 2, 1, 3)).reshape(2, 128, 1280)
    lwga_h = np.ascontiguousarray(np.asarray(inp["lru_w_ga"], dtype=np.float32).transpose(0, 2, 1, 3)).reshape(2, 128, 1280)
    lwo = np.asarray(inp["lru_w_out"], dtype=np.float32).reshape(2, 10, 128, 1024)
    lwout_h = np.ascontiguousarray(lwo.transpose(0, 2, 1, 3)).reshape(2, 128, 10240)
    wup = np.asarray(inp["ffn_w_up"], dtype=np.float32)
    g = wup[:, :, :DFF].reshape(DEPTH, 8, 128, 16, 256)
    v = wup[:, :, DFF:].reshape(DEPTH, 8, 128, 16, 256)
    gv = np.concatenate([g, v], axis=-1)
    wup_h = np.ascontiguousarray(gv.transpose(0, 3, 2, 1, 4)).reshape(DEPTH, 16, 128, 4096)
    wdn = np.asarray(inp["ffn_w_down"], dtype=np.float32)
    wd = wdn.reshape(DEPTH, 32, 128, 4, 256)
    wdn_h = np.ascontiguousarray(wd.transpose(0, 3, 2, 1, 4)).reshape(DEPTH, 4, 128, 8192)
    shared = {"vecs": vecs, "wup": wup_h, "wdn": wdn_h,
              "swin": swin_h, "swdt": swdt_h, "swout": swout_h, "snorm": snorm_h, "ssm": ssm_h,
              "lwin": lwin_h, "lwgx": lwgx_h, "lwga": lwga_h, "lwout": lwout_h}
    return shared, vp.index, vecs.shape[1]


_CACHE = {}


def run(inp, S=4096, layers=(0, 1, 2, 3), do_mixer=True, do_ffn=True, trace=False, debug=False):
    shared, vindex, nvec = pack_inputs(inp)
    x = np.asarray(inp["x"], dtype=np.float32)
    B = x.shape[0]
    b = Builder(S, vindex, nvec, layers=layers, do_mixer=do_mixer, do_ffn=do_ffn, debug=debug)
    nc = b.build()
    in_maps = []
    for i in range(B):
        m = dict(shared)
        m["x"] = np.ascontiguousarray(x[i, :S])
        in_maps.append(m)
    res = run_bass_kernel_spmd(nc, in_maps, core_ids=list(range(B)), trace=trace)
    y = np.stack([np.asarray(r["y"]) for r in res.results], axis=0)
    if debug:
        res.dbg = {k: np.asarray(res.results[0]["dbg_" + k]).astype(np.float32) for k in b.dbg}
    return y.astype(np.float32), res


def kernel(**inputs):
    y, _ = run(inputs)
    return y
```

```python
import numpy as np
import concourse.bass as bass
import concourse.mybir as mybir
from concourse.bass_utils import run_bass_kernel_spmd

F32 = mybir.dt.float32
BF16 = mybir.dt.bfloat16
AF = mybir.ActivationFunctionType
ALU = mybir.AluOpType

D = 1024
DEPTH = 4
DFF = 4096
EPS = 1e-6

ENGS = ("pe", "act", "dve", "pool", "sp")
EPOCH = 30000
NDSEM = 8


class Prog:
    def __init__(self, nc, same_engine_sync=False):
        self.nc = nc
        self.same_engine_sync = same_engine_sync
        self.ops = {e: [] for e in ENGS}
        self.cnt = {e: 0 for e in ENGS}
        self.epoch = {e: 0 for e in ENGS}
        self.sems = {e: [] for e in ENGS}
        self.dsems = {e: [] for e in ENGS}
        self.ndma = {e: 0 for e in ENGS}
        self.known = {e: {} for e in ENGS}
        self.last_w = {}
        self.readers = {}
        self._ctx = []
        self.fence_vals = {}
        self.small_toks = set()

    def _new_sem(self, name):
        cm = self.nc.semaphore(name)
        s = cm.__enter__()
        self._ctx.append(cm)
        return s

    def _eng_sem(self, e):
        ep = self.epoch[e]
        while len(self.sems[e]) <= ep:
            self.sems[e].append(self._new_sem(f"s_{e}_{len(self.sems[e])}"))
        return self.sems[e][ep]

    def _deps(self, e, reads, writes, force_same=False):
        toks = []
        for k in reads:
            t = self.last_w.get(k)
            if t is not None:
                toks.append(t)
        for k in writes:
            t = self.last_w.get(k)
            if t is not None:
                toks.append(t)
            toks.extend(self.readers.get(k, ()))
        waits = {}
        for (sem, val, te, isdma) in toks:
            if te == e and not isdma:
                if e == "pe" or not (self.same_engine_sync or force_same or (id(sem), val) in self.small_toks):
                    continue
            sid = id(sem)
            if self.known[e].get(sid, 0) >= val:
                continue
            if sid not in waits or waits[sid][1] < val:
                waits[sid] = (sem, val)
        for sid, (sem, val) in waits.items():
            self.known[e][sid] = val
        return list(waits.values())

    def _record(self, tok, reads, writes):
        for k in writes:
            self.last_w[k] = tok
            self.readers[k] = []
        for k in reads:
            if k in writes:
                continue
            lst = self.readers.setdefault(k, [])
            lst.append(tok)
            if len(lst) > 8:
                best = {}
                for t in lst:
                    sid = id(t[0])
                    if sid not in best or best[sid][1] < t[1]:
                        best[sid] = t
                self.readers[k] = list(best.values())

    def op(self, e, fn, reads=(), writes=(), force_same=False, small=False):
        waits = self._deps(e, reads, writes, force_same)
        if self.cnt[e] >= EPOCH:
            self.epoch[e] += 1
            self.cnt[e] = 0
        sem = self._eng_sem(e)
        self.cnt[e] += 1
        tok = (sem, self.cnt[e], e, False)
        if small:
            self.small_toks.add((id(sem), self.cnt[e]))
        self.ops[e].append((waits, fn, sem, 1))
        self._record(tok, reads, writes)
        return tok

    def dma(self, q, out, in_, reads=(), writes=(), nofence=False, **kw):
        if not self.dsems[q]:
            self.dsems[q] = [self._new_sem(f"d_{q}_{i}") for i in range(NDSEM)]
        i = self.ndma[q]
        self.ndma[q] += 1
        sem = self.dsems[q][i % NDSEM]
        waits = self._deps(q, reads, writes)
        prev = 16 * (i // NDSEM)
        if prev > 0 and self.known[q].get(id(sem), 0) < prev:
            waits.append((sem, prev))
            self.known[q][id(sem)] = prev
        tok = (sem, prev + 16, q, True)
        if not nofence:
            self.fence_vals.setdefault(q, {})[i % NDSEM] = (sem, prev + 16)

        def fn(eng, out=out, in_=in_, kw=kw):
            return eng.dma_start(out=out, in_=in_, **kw)

        self.ops[q].append((waits, fn, sem, 16))
        self._record(tok, reads, writes)
        return tok

    def fence(self):
        toks = []
        for f in ENGS:
            if self.sems[f] and self.cnt[f] > 0:
                toks.append((self.sems[f][self.epoch[f]], self.cnt[f]))
            for (sem, val) in self.fence_vals.get(f, {}).values():
                toks.append((sem, val))
        for e in ENGS:
            if not self.ops[e]:
                continue
            waits = []
            own = self.sems[e][self.epoch[e]] if self.sems[e] else None
            for (sem, val) in toks:
                if sem is own:
                    continue
                if self.known[e].get(id(sem), 0) >= val:
                    continue
                waits.append((sem, val))
                self.known[e][id(sem)] = val
            if waits:
                self.ops[e].append((waits, None, None, 0))

    def finish_wait_all(self, e="sp"):
        toks = {}
        for k, t in self.last_w.items():
            sid = id(t[0])
            if sid not in toks or toks[sid][1] < t[1]:
                toks[sid] = t
        waits = []
        for sid, (sem, val, te, isdma) in toks.items():
            if self.known[e].get(sid, 0) >= val:
                continue
            waits.append((sem, val))
        self.ops[e].append((waits, None, None, 0))

    def emit(self):
        nc = self.nc
        engmap = {"pe": "tensor", "act": "scalar", "dve": "vector", "pool": "gpsimd", "sp": "sync"}
        with nc.Block() as block:
            for e in ENGS:
                ops = self.ops[e]
                if not ops:
                    continue

                def body(eng, ops=ops):
                    for (waits, fn, sem, inc) in ops:
                        for (ws, wv) in waits:
                            eng.wait_ge(ws, wv)
                        if fn is None:
                            continue
                        ins = fn(eng)
                        ins.then_inc(sem, inc)

                getattr(block, engmap[e])(body)
        for cm in reversed(self._ctx):
            cm.__exit__(None, None, None)
        self._ctx = []


class Pools:
    def __init__(self, nc):
        self.nc = nc
        self._ctx = []

    def sb(self, name, shape, dt):
        cm = self.nc.sbuf_tensor(name, list(shape), dt)
        t = cm.__enter__()
        self._ctx.append(cm)
        return t

    def ps(self, name, shape, dt):
        cm = self.nc.psum_tensor(name, list(shape), dt)
        t = cm.__enter__()
        self._ctx.append(cm)
        return t

    def close(self):
        for cm in reversed(self._ctx):
            cm.__exit__(None, None, None)
        self._ctx = []


class Arena:
    def __init__(self, pools, nf32):
        self.t = pools.sb("arena", [128, nf32], F32)
        self.n = nf32
        self.off = 0

    def reset(self):
        self.off = 0

    def alloc(self, shape, dt):
        assert shape[0] == 128
        nel = 1
        for d_ in shape[1:]:
            nel *= d_
        nf = nel if dt == F32 else (nel + 1) // 2
        nf = (nf + 15) // 16 * 16
        assert self.off + nf <= self.n, ("arena overflow", self.off, nf, self.n)
        v = self.t[:, self.off:self.off + nf]
        self.off += nf
        if dt != F32:
            v = v.bitcast(dt)
        v = v[:, 0:nel]
        if len(shape) == 3:
            v = v.rearrange("p (a b) -> p a b", a=shape[1])
        return v


class VecPack:
    def __init__(self):
        self.cols = []
        self.index = {}
        self.n = 0

    def add(self, name, vec):
        vec = np.asarray(vec, dtype=np.float32).reshape(-1)
        assert vec.size % 128 == 0, (name, vec.size)
        nch = vec.size // 128
        self.index[name] = (self.n, nch)
        self.cols.append(vec.reshape(nch, 128).T)
        self.n += nch

    def add_raw(self, name, arr):
        arr = np.asarray(arr, dtype=np.float32)
        assert arr.shape[0] == 128
        self.index[name] = (self.n, arr.shape[1])
        self.cols.append(arr)
        self.n += arr.shape[1]

    def build(self):
        return np.ascontiguousarray(np.concatenate(self.cols, axis=1))


def windows(S, n, halo):
    out = []
    t = 0
    while t < S:
        m = min(n, S - t)
        out.append((t, m))
        t += m
    return out


class Builder:
    def __init__(self, S, vec_index, nvec, layers=(0, 1, 2, 3), do_mixer=True, do_ffn=True, debug=False):
        self.S = S
        self.debug = debug
        self.dbg = {}
        self.vi = vec_index
        self.layers = layers
        self.do_mixer = do_mixer
        self.do_ffn = do_ffn
        nc = self.nc = bass.Bass("TRN2", target_bir_lowering=False)
        self.P = Pools(nc)
        self.pr = Prog(nc)
        P, pr = self.P, self.pr
        self.x_in = nc.dram_tensor("x", [S, D], F32, kind="ExternalInput").ap()
        self.y_out = nc.dram_tensor("y", [S, D], F32, kind="ExternalOutput").ap()
        self.vec_d = nc.dram_tensor("vecs", [128, nvec], F32, kind="ExternalInput").ap()
        self.wup_f = nc.dram_tensor("wup", [DEPTH, 16, 128, 4096], F32, kind="ExternalInput").ap()
        self.wdn_f = nc.dram_tensor("wdn", [DEPTH, 4, 128, 8192], F32, kind="ExternalInput").ap()
        self.wup_b = nc.dram_tensor("wup_b", [DEPTH, 16, 128, 4096], BF16, kind="Internal").ap()
        self.wdn_b = nc.dram_tensor("wdn_b", [DEPTH, 4, 128, 8192], BF16, kind="Internal").ap()
        self.swin_f = nc.dram_tensor("swin", [2, 10, 128, 4096], F32, kind="ExternalInput").ap()
        self.swdt_f = nc.dram_tensor("swdt", [2, 128, 256], F32, kind="ExternalInput").ap()
        self.swout_f = nc.dram_tensor("swout", [2, 8, 128, 2048], F32, kind="ExternalInput").ap()
        self.swin_b = nc.dram_tensor("swin_b", [2, 10, 128, 4096], BF16, kind="Internal").ap()
        self.swdt_b = nc.dram_tensor("swdt_b", [2, 128, 256], BF16, kind="Internal").ap()
        self.swout_b = nc.dram_tensor("swout_b", [2, 8, 128, 2048], BF16, kind="Internal").ap()
        self.snorm_d = nc.dram_tensor("snorm", [2, 128, 2048], F32, kind="ExternalInput").ap()
        self.ssm_d = nc.dram_tensor("ssm", [2, 128, 96], F32, kind="ExternalInput").ap()
        self.lwin_f = nc.dram_tensor("lwin", [2, 2560, 1024], F32, kind="ExternalInput").ap()
        self.lwgx_f = nc.dram_tensor("lwgx", [2, 128, 1280], F32, kind="ExternalInput").ap()
        self.lwga_f = nc.dram_tensor("lwga", [2, 128, 1280], F32, kind="ExternalInput").ap()
        self.lwout_f = nc.dram_tensor("lwout", [2, 128, 10240], F32, kind="ExternalInput").ap()
        self.lwin_b = nc.dram_tensor("lwin_b", [2, 2560, 1024], BF16, kind="Internal").ap()
        self.lwgx_b = nc.dram_tensor("lwgx_b", [2, 128, 1280], BF16, kind="Internal").ap()
        self.lwga_b = nc.dram_tensor("lwga_b", [2, 128, 1280], BF16, kind="Internal").ap()
        self.lwout_b = nc.dram_tensor("lwout_b", [2, 128, 10240], BF16, kind="Internal").ap()
        self.xT = [nc.dram_tensor(f"xT{i}", [D, S], F32, kind="Internal").ap() for i in range(2)]
        self.cur = 0
        self.vecs = P.sb("vecs_sb", [128, nvec], F32)
        pr.dma("sp", self.vecs[:], self.vec_d, writes=["vecs"])
        self.ident = P.sb("ident", [128, 128], F32)
        self.ones_f = P.sb("ones_f", [128, 128], F32)
        self.ones_b = P.sb("ones_b", [128, 128], BF16)
        pr.op("pool", lambda e: e.memset(self.ones_f[:], 1.0), writes=["ones_f"])
        pr.op("pool", lambda e: e.memset(self.ones_b[:], 1.0), writes=["ones_b"])
        pr.op("pool", lambda e: e.affine_select(out=self.ident[:], in_=self.ones_f[:], pattern=[[-1, 128]],
                                                compare_op=ALU.is_equal, fill=0.0, base=0, channel_multiplier=1),
              reads=["ones_f"], writes=["ident"])
        self.ident_b = P.sb("ident_b", [128, 128], BF16)
        self.triU = P.sb("triU", [128, 128], F32)
        self.triSL = P.sb("triSL", [128, 128], F32)
        pr.op("pool", lambda e: e.tensor_copy(out=self.ident_b[:], in_=self.ident[:]), reads=["ident"], writes=["ident_b"])
        pr.op("pool", lambda e: e.affine_select(out=self.triU[:], in_=self.ones_f[:], pattern=[[1, 128]],
                                                compare_op=ALU.is_ge, fill=0.0, base=0, channel_multiplier=-1),
              reads=["ones_f"], writes=["triU"])
        pr.op("pool", lambda e: e.affine_select(out=self.triSL[:], in_=self.ones_f[:], pattern=[[-1, 128]],
                                                compare_op=ALU.is_gt, fill=0.0, base=0, channel_multiplier=1),
              reads=["ones_f"], writes=["triSL"])
        self.psall = P.ps("psall", [128, 4096], F32)
        self.psb = [self.psall[:, i * 512:(i + 1) * 512] for i in range(8)]
        self.A = Arena(P, 50176)
        self.cast_jobs = []
        self.cast_done = 0

    def dump(self, name, ap, reads):
        if not self.debug:
            return
        t = self.nc.dram_tensor("dbg_" + name, list(ap.shape), ap.dtype, kind="ExternalOutput").ap()
        self.dbg[name] = t
        self.pr.dma("pool", t, ap, reads=reads, writes=[("dbg", name)])

    def vcol(self, name, c=0, n=1):
        c0, nch = self.vi[name]
        assert c + n <= nch, (name, c, n, nch)
        return self.vecs[:, c0 + c:c0 + c + n]

    def add_cast(self, dst, src, key):
        self.cast_jobs.append((dst, src, key))

    def pump_casts(self, n):
        while n > 0 and self.cast_done < len(self.cast_jobs):
            dst, src, key = self.cast_jobs[self.cast_done]
            self.pr.dma("pool", dst, src, writes=[key], nofence=True, max_dma_last_dim=4096)
            self.cast_done += 1
            n -= 1

    def pump_until(self, key):
        while self.cast_done < len(self.cast_jobs) and key not in self.pr.last_w:
            self.pump_casts(1)

    def phase_in(self):
        pr, P, S = self.pr, self.P, self.S
        pr.fence()
        self.A.reset()
        dst = self.xT[self.cur].rearrange("(c p) t -> p c t", p=128)
        xin = [self.A.alloc([128, D], F32) for i in range(2)]
        xst = [self.A.alloc([128, 8, 512], F32) for i in range(2)]
        nblk = (S + 511) // 512
        for b in range(nblk):
            t0 = b * 512
            nt = min(512, S - t0) // 128
            st = xst[b % 2]
            for tt in range(nt):
                xi = xin[(b * 4 + tt) % 2]
                kxi = ("xin", (b * 4 + tt) % 2)
                pr.dma("sp", xi[:], self.x_in[t0 + tt * 128:t0 + (tt + 1) * 128, :], writes=[kxi])
                for half in range(2):
                    bank = (tt * 2 + half) % 8
                    ps = self.psb[bank]

                    def tr(e, xi=xi, ps=ps, half=half):
                        for q in range(4):
                            c = half * 4 + q
                            i = e.transpose(ps[:, q * 128:(q + 1) * 128], xi[:, c * 128:(c + 1) * 128], self.ident[:])
                        return i
                    pr.op("pe", tr, reads=[kxi, "ident"], writes=[("ps", bank)])
                    eng = "act" if half == 0 else "dve"

                    def ev(e, st=st, ps=ps, half=half, tt=tt, eng=eng):
                        o = st[:, half * 4:(half + 1) * 4, tt * 128:(tt + 1) * 128]
                        i_ = ps[:].rearrange("p (q t) -> p q t", q=4)
                        if eng == "act":
                            return e.activation(out=o, in_=i_, func=AF.Identity)
                        return e.tensor_copy(out=o, in_=i_)
                    pr.op(eng, ev, reads=[("ps", bank)], writes=[("xst", b % 2, half, tt)])
            rk = [("xst", b % 2, h, tt) for h in range(2) for tt in range(nt)]
            pr.dma("pool", dst[:, :, t0:t0 + nt * 128], st[:, :, 0:nt * 128], reads=rk, writes=[("xT", self.cur)])

    def phase_out(self):
        pr, P, S = self.pr, self.P, self.S
        pr.fence()
        self.A.reset()
        src = self.xT[self.cur].rearrange("(c p) t -> p c t", p=128)
        xld = [self.A.alloc([128, 8, 512], F32) for i in range(2)]
        yst = [self.A.alloc([128, D], F32) for i in range(2)]
        nblk = (S + 511) // 512
        for b in range(nblk):
            t0 = b * 512
            nt = min(512, S - t0) // 128
            xl = xld[b % 2]
            kx = ("xld", b % 2)
            pr.dma("sp", xl[:, :, 0:nt * 128], src[:, :, t0:t0 + nt * 128], reads=[("xT", self.cur)], writes=[kx])
            for tt in range(nt):
                ys = yst[(b * 4 + tt) % 2]
                ky = ("yst", (b * 4 + tt) % 2)
                for half in range(2):
                    bank = (tt * 2 + half) % 8
                    ps = self.psb[bank]

                    def tr(e, xl=xl, ps=ps, half=half, tt=tt):
                        for q in range(4):
                            c = half * 4 + q
                            i = e.transpose(ps[:, q * 128:(q + 1) * 128], xl[:, c, tt * 128:(tt + 1) * 128], self.ident[:])
                        return i
                    pr.op("pe", tr, reads=[kx, "ident"], writes=[("ps", bank)])
                    eng = "act" if half == 0 else "dve"

                    def ev(e, ys=ys, ps=ps, half=half, eng=eng):
                        o = ys[:, half * 512:(half + 1) * 512]
                        if eng == "act":
                            return e.activation(out=o, in_=ps[:], func=AF.Identity)
                        return e.tensor_copy(out=o, in_=ps[:])
                    pr.op(eng, ev, reads=[("ps", bank)], writes=[(ky, half)])
                pr.dma("pool", self.y_out[t0 + tt * 128:t0 + (tt + 1) * 128, :], ys[:],
                       reads=[(ky, 0), (ky, 1)], writes=["y"])

    def prenorm(self, xblk, kx, C, wname, hT, kh, sq, ksq, rstd, krstd, stat_bank):
        pr = self.pr
        ps = self.psb[stat_bank]
        for c in range(8):
            pr.op("act", lambda e, c=c: e.activation(out=sq[:, c, 0:C], in_=xblk[:, c, 0:C], func=AF.Square),
                  reads=[kx], writes=[(ksq, c)])

        def st(e):
            for c in range(8):
                i = e.matmul(ps[:, 0:C], self.ones_b[:], sq[:, c, 0:C], start=(c == 0), stop=(c == 7))
            return i
        pr.op("pe", st, reads=[(ksq, c) for c in range(8)] + ["ones_b"], writes=[("ps", stat_bank)])
        pr.op("act", lambda e: e.activation(out=rstd[:, 0:C], in_=ps[:, 0:C], func=AF.Ln, scale=1.0 / D, bias=self.vcol("eps")),
              reads=[("ps", stat_bank), "vecs"], writes=[krstd])
        pr.op("act", lambda e: e.activation(out=rstd[:, 0:C], in_=rstd[:, 0:C], func=AF.Exp, scale=-0.5),
              reads=[krstd], writes=[krstd])
        for c in range(8):
            pr.op("dve", lambda e, c=c: e.scalar_tensor_tensor(out=hT[:, c, 0:C], in0=xblk[:, c, 0:C], scalar=self.vcol(wname, c),
                                                               in1=rstd[:, 0:C], op0=ALU.mult, op1=ALU.mult),
                  reads=[kx, krstd, "vecs"], writes=[(kh, c)])

    def phase_ffn(self, li):
        pr, P, S = self.pr, self.P, self.S
        pr.fence()
        self.A.reset()
        src = self.xT[self.cur].rearrange("(c p) t -> p c t", p=128)
        dst = self.xT[1 - self.cur].rearrange("(c p) t -> p c t", p=128)
        xblks = [self.A.alloc([128, 8, 512], F32) for _ in range(2)]
        sq = self.A.alloc([128, 8, 512], BF16)
        sqps = [self.A.alloc([128, 512], BF16) for _ in range(2)]
        hT = self.A.alloc([128, 8, 512], BF16)
        rstd = self.A.alloc([128, 512], F32)
        rstd2 = self.A.alloc([128, 512], F32)
        gvT = self.A.alloc([128, 32, 512], BF16)
        fT = self.A.alloc([128, 8, 512], F32)
        NWU, NWD = 3, 2
        wus = [self.A.alloc([128, 8, 512], BF16) for i in range(NWU)]
        wds = [self.A.alloc([128, 32, 256], BF16) for i in range(NWD)]
        tg = [self.A.alloc([128, 512], F32) for i in range(4)]
        tv = [self.A.alloc([128, 512], F32) for i in range(4)]
        gg = [self.A.alloc([128, 512], F32) for i in range(4)]
        nwu = 0
        nwd = 0
        npair = 0
        wins = windows(S, 464, 2)
        def pre_p1(wi_):
            t0_, n_ = wins[wi_]
            C_ = n_ + 2
            xb = xblks[wi_ % 2]
            kx_ = (f"f{li}_xblk", wi_ % 2)
            if t0_ == 0:
                pr.op("pool", lambda e: e.memset(xb[:, :, 0:2], 0.0), writes=[(kx_, "halo")])
                pr.dma("sp", xb[:, :, 2:C_], src[:, :, 0:n_], reads=[("xT", self.cur)], writes=[kx_])
            else:
                pr.dma("sp", xb[:, :, 0:C_], src[:, :, t0_ - 2:t0_ + n_], reads=[("xT", self.cur)], writes=[kx_, (kx_, "halo")])
            for c in range(8):
                pr.op("act", lambda e, c=c: e.activation(out=sq[:, c, 0:C_], in_=xb[:, c, 0:C_], func=AF.Square),
                      reads=[kx_, (kx_, "halo")], writes=[(f"f{li}_sq", c)])

        def pre_p2(wi_):
            t0_, n_ = wins[wi_]
            C_ = n_ + 2
            xb = xblks[wi_ % 2]
            kx_ = (f"f{li}_xblk", wi_ % 2)
            ps = self.psb[6]

            def st(e):
                for c in range(8):
                    i = e.matmul(ps[:, 0:C_], self.ones_b[:], sq[:, c, 0:C_], start=(c == 0), stop=(c == 7))
                return i
            pr.op("pe", st, reads=[(f"f{li}_sq", c) for c in range(8)] + ["ones_b"], writes=[("ps", 6)])
            pr.op("act", lambda e: e.activation(out=rstd[:, 0:C_], in_=ps[:, 0:C_], func=AF.Ln, scale=1.0 / D, bias=self.vcol("eps")),
                  reads=[("ps", 6), "vecs"], writes=[f"f{li}_rstd"])
            pr.op("act", lambda e: e.activation(out=rstd[:, 0:C_], in_=rstd[:, 0:C_], func=AF.Exp, scale=-0.5),
                  reads=[f"f{li}_rstd"], writes=[f"f{li}_rstd"])
            for c in range(8):
                pr.op("dve", lambda e, c=c: e.scalar_tensor_tensor(out=hT[:, c, 0:C_], in0=xb[:, c, 0:C_], scalar=self.vcol(f"nfpre{li}", c),
                                                                   in1=rstd[:, 0:C_], op0=ALU.mult, op1=ALU.mult),
                      reads=[kx_, (kx_, "halo"), f"f{li}_rstd", "vecs"], writes=[(f"f{li}_hT", c)])

        pre_p1(0)
        pre_p2(0)
        for wi, (t0, n) in enumerate(wins):
            C = n + 2
            xblk = xblks[wi % 2]
            kx = (f"f{li}_xblk", wi % 2)
            kxr = [kx, (kx, "halo")]
            khs = [(f"f{li}_hT", c) for c in range(8)]
            deferred, deferred_next = [], []
            for j2 in range(16):
                ws = wus[nwu % NWU]
                kw = ("wu", li, nwu % NWU)
                nwu += 1
                self.pump_until(("wup_b", li, j2))
                pr.dma("sp", ws[:], self.wup_b[li, j2].rearrange("p (k c) -> p k c", k=8),
                       reads=[("wup_b", li, j2)], writes=[kw])
                for jj in range(2):
                    j = j2 * 2 + jj
                    pb = (npair % 4) * 2
                    npair += 1
                    gps, vps = self.psb[pb], self.psb[pb + 1]

                    def mmg(e, ws=ws, gps=gps, jj=jj, C=C, off=0):
                        for k in range(8):
                            i = e.matmul(gps[:, 0:C], ws[:, k, off + jj * 128:off + (jj + 1) * 128], hT[:, k, 0:C],
                                         start=(k == 0), stop=(k == 7))
                        return i
                    pr.op("pe", mmg, reads=[kw] + khs, writes=[("ps", pb)])
                    pr.op("pe", lambda e, ws=ws, vps=vps, jj=jj, C=C: mmg(e, ws, vps, jj, C, 256),
                          reads=[kw] + khs, writes=[("ps", pb + 1)])
                    par = j % 4
                    tgt, tvt, ggt = tg[par], tv[par], gg[par]
                    ktg, ktv, kgg = ("tg", par), ("tv", par), ("gg", par)
                    jv = 32 + j
                    pr.op("act", lambda e, gps=gps, tgt=tgt, j=j, C=C, n=n: e.activation(
                        out=tgt[:, 0:n], in_=gps[:, 2:C], func=AF.Identity,
                        scale=self.vcol(f"fcw{li}_2", j), bias=self.vcol(f"fcb{li}", j)),
                        reads=[("ps", pb), "vecs"], writes=[ktg])
                    pr.op("act", lambda e, vps=vps, tvt=tvt, jv=jv, C=C, n=n: e.activation(
                        out=tvt[:, 0:n], in_=vps[:, 2:C], func=AF.Identity,
                        scale=self.vcol(f"fcw{li}_2", jv), bias=self.vcol(f"fcb{li}", jv)),
                        reads=[("ps", pb + 1), "vecs"], writes=[ktv])
                    pr.op("dve", lambda e, gps=gps, tgt=tgt, j=j, C=C, n=n: e.scalar_tensor_tensor(
                        out=tgt[:, 0:n], in0=gps[:, 1:C - 1], scalar=self.vcol(f"fcw{li}_1", j), in1=tgt[:, 0:n],
                        op0=ALU.mult, op1=ALU.add), reads=[("ps", pb), ktg, "vecs"], writes=[ktg])
                    pr.op("dve", lambda e, gps=gps, tgt=tgt, j=j, C=C, n=n: e.scalar_tensor_tensor(
                        out=tgt[:, 0:n], in0=gps[:, 0:C - 2], scalar=self.vcol(f"fcw{li}_0", j), in1=tgt[:, 0:n],
                        op0=ALU.mult, op1=ALU.add), reads=[("ps", pb), ktg, "vecs"], writes=[ktg])
                    deferred_next.append(lambda tgt=tgt, ggt=ggt, n=n, ktg=ktg, kgg=kgg: pr.op(
                        "act", lambda e: e.activation(out=ggt[:, 0:n], in_=tgt[:, 0:n], func=AF.Gelu_apprx_tanh),
                        reads=[ktg], writes=[kgg]))
                    pr.op("dve", lambda e, vps=vps, tvt=tvt, jv=jv, C=C, n=n: e.scalar_tensor_tensor(
                        out=tvt[:, 0:n], in0=vps[:, 1:C - 1], scalar=self.vcol(f"fcw{li}_1", jv), in1=tvt[:, 0:n],
                        op0=ALU.mult, op1=ALU.add), reads=[("ps", pb + 1), ktv, "vecs"], writes=[ktv])
                    pr.op("dve", lambda e, vps=vps, tvt=tvt, jv=jv, C=C, n=n: e.scalar_tensor_tensor(
                        out=tvt[:, 0:n], in0=vps[:, 0:C - 2], scalar=self.vcol(f"fcw{li}_0", jv), in1=tvt[:, 0:n],
                        op0=ALU.mult, op1=ALU.add), reads=[("ps", pb + 1), ktv, "vecs"], writes=[ktv])
                    deferred_next.append(lambda ggt=ggt, tvt=tvt, j=j, n=n, kgg=kgg, ktv=ktv: pr.op(
                        "dve", lambda e: e.tensor_tensor(out=gvT[:, j, 0:n], in0=ggt[:, 0:n], in1=tvt[:, 0:n], op=ALU.mult),
                        reads=[kgg, ktv], writes=[("gvT", j)]))
                    for f_ in deferred:
                        f_()
                    deferred, deferred_next = deferred_next, []
            for f_ in deferred:
                f_()
            deferred = []
            if wi + 1 < len(wins):
                pre_p1(wi + 1)
            kgv = [("gvT", j) for j in range(32)]
            sbank = 7
            sps = self.psb[sbank]
            for o2 in range(4):
                wd = wds[nwd % NWD]
                kwd = ("wd", li, nwd % NWD)
                nwd += 1
                self.pump_until(("wdn_b", li, o2))
                pr.dma("sp", wd[:], self.wdn_b[li, o2].rearrange("p (k c) -> p k c", k=32),
                       reads=[("wdn_b", li, o2)], writes=[kwd])
                for oo in range(2):
                    o = o2 * 2 + oo
                    fb = (npair % 3) * 2
                    npair += 1
                    fps = self.psb[fb]

                    def mmd(e, wd=wd, fps=fps, oo=oo, n=n):
                        for k in range(32):
                            i = e.matmul(fps[:, 0:n], wd[:, k, oo * 128:(oo + 1) * 128], gvT[:, k, 0:n],
                                         start=(k == 0), stop=(k == 31))
                        return i
                    pr.op("pe", mmd, reads=[kwd] + kgv, writes=[("ps", fb)])
                    for f_ in deferred:
                        f_()
                    deferred = []
                    pr.op("act", lambda e, fps=fps, o=o, n=n: e.activation(out=fT[:, o, 0:n], in_=fps[:, 0:n], func=AF.Identity,
                                                                            scale=self.vcol(f"nfpost{li}", o)),
                          reads=[("ps", fb), "vecs"], writes=[("fT", o)])
                    sqt = sqps[o % 2]
                    pr.op("act", lambda e, fps=fps, sqt=sqt, n=n: e.activation(out=sqt[:, 0:n], in_=fps[:, 0:n], func=AF.Square),
                          reads=[("ps", fb)], writes=[(f"f{li}_sqp", o % 2)])
                    for f_ in deferred:
                        f_()
                    deferred = [lambda o=o, n=n, sqt=sqt: pr.op(
                        "pe", lambda e: e.matmul(sps[:, 0:n], self.ones_b[:], sqt[:, 0:n], start=(o == 0), stop=(o == 7)),
                        reads=[(f"f{li}_sqp", o % 2), "ones_b"], writes=[("ps", sbank)])]
                    if o == 1 and wi + 1 < len(wins):
                        pre_p2(wi + 1)
            for f_ in deferred:
                f_()
            deferred = []
            pr.op("act", lambda e, n=n: e.activation(out=rstd2[:, 0:n], in_=sps[:, 0:n], func=AF.Ln, scale=1.0 / D, bias=self.vcol("eps")),
                  reads=[("ps", sbank), "vecs"], writes=["rstd2"])
            pr.op("act", lambda e, n=n: e.activation(out=rstd2[:, 0:n], in_=rstd2[:, 0:n], func=AF.Exp, scale=-0.5),
                  reads=["rstd2"], writes=["rstd2"])
            for o in range(8):
                pr.op("dve", lambda e, o=o, n=n: e.tensor_tensor(out=fT[:, o, 0:n], in0=fT[:, o, 0:n], in1=rstd2[:, 0:n], op=ALU.mult),
                      reads=[("fT", o), "rstd2"], writes=[("fT", o)])
                pr.op("dve", lambda e, o=o, n=n, C=C, xblk=xblk: e.tensor_tensor(out=fT[:, o, 0:n], in0=fT[:, o, 0:n], in1=xblk[:, o, 2:C], op=ALU.add),
                      reads=[("fT", o)] + kxr, writes=[("fT", o)])
            pr.dma("pool", dst[:, :, t0:t0 + n], fT[:, :, 0:n], reads=[("fT", o) for o in range(8)],
                   writes=[("xT", 1 - self.cur)])
            self.pump_casts(6)
        self.cur = 1 - self.cur

    def phase_lru(self, li):
        pr, S, A = self.pr, self.S, self.A
        pr.fence()
        A.reset()
        j = li // 2
        T = 512
        src = self.xT[self.cur].rearrange("(c p) t -> p c t", p=128)
        dst = self.xT[1 - self.cur].rearrange("(c p) t -> p c t", p=128)
        win = A.alloc([128, 20, 1024], BF16)
        wgx = A.alloc([128, 10, 128], BF16)
        wga = A.alloc([128, 10, 128], BF16)
        wout = A.alloc([128, 10, 1024], BF16)
        for key in (("lwin", j), ("lwgx", j), ("lwga", j), ("lwout", j)):
            self.pump_until(key)
        pr.dma("sp", win, self.lwin_b[j].rearrange("(o p) c -> p o c", p=128), reads=[("lwin", j)], writes=["l_win"])
        pr.dma("sp", wgx, self.lwgx_b[j].rearrange("p (h c) -> p h c", h=10), reads=[("lwgx", j)], writes=["l_wgx"])
        pr.dma("sp", wga, self.lwga_b[j].rearrange("p (h c) -> p h c", h=10), reads=[("lwga", j)], writes=["l_wga"])
        pr.dma("sp", wout, self.lwout_b[j].rearrange("p (k c) -> p k c", k=10), reads=[("lwout", j)], writes=["l_wout"])
        xblk = A.alloc([128, 8, T], F32)
        sq = A.alloc([128, 8, T], BF16)
        hT = A.alloc([128, 8, T], BF16)
        rstd = A.alloc([128, T], F32)
        rstd2 = A.alloc([128, T], F32)
        hy = A.alloc([128, 10, T], BF16)
        mT = A.alloc([128, 8, T], F32)
        halo = A.alloc([128, 10, 3], F32)
        hstate = A.alloc([128, 10], F32)
        c8 = A.alloc([128, 10], F32)
        c16 = A.alloc([128, 10], F32)
        NB = 4
        ybr = [A.alloc([128, T], F32) for _ in range(NB)]
        XP = [A.alloc([128, T + 3], F32) for _ in range(NB)]
        tcv = [A.alloc([128, T], F32) for _ in range(NB)]
        xbb = [A.alloc([128, T], BF16) for _ in range(NB)]
        gx = [A.alloc([128, T], F32) for _ in range(2)]
        ga = [A.alloc([128, T], F32) for _ in range(2)]
        at = [A.alloc([128, T], F32) for _ in range(2)]
        mu = [A.alloc([128, T], F32) for _ in range(2)]
        bt = [A.alloc([128, T], F32) for _ in range(2)]
        hs = [A.alloc([128, T], F32) for _ in range(2)]
        lam = self.vcol(f"llam{j}", 0, 10)
        pr.op("act", lambda e: e.activation(out=c8, in_=lam, func=AF.Exp, scale=-1.0), reads=["vecs"], writes=["l_c8"], small=True)
        pr.op("act", lambda e: e.activation(out=c8, in_=c8, func=AF.Ln, bias=self.vcol("one")), reads=["l_c8", "vecs"], writes=["l_c8"], small=True)
        pr.op("dve", lambda e: e.tensor_scalar(out=c16, in0=c8, scalar1=-16.0, scalar2=None, op0=ALU.mult), reads=["l_c8"], writes=["l_c16"], small=True)
        pr.op("dve", lambda e: e.tensor_scalar(out=c8, in0=c8, scalar1=-8.0, scalar2=None, op0=ALU.mult), reads=["l_c8", "l_c16"], writes=["l_c8"], small=True)
        pr.op("pool", lambda e: e.memset(halo, 0.0), writes=["l_halo"])
        pr.op("pool", lambda e: e.memset(hstate, 0.0), writes=["l_hstate"])
        nbank = 0
        it = 0
        for wi in range(S // T):
            t0 = wi * T
            kx = ("l_xblk",)
            pr.dma("sp", xblk, src[:, :, t0:t0 + T], reads=[("xT", self.cur)], writes=[kx])
            self._prenorm_multi(xblk, [kx], T, f"nmpre{li}", hT, "l_hT", sq, "l_sq", rstd, "l_rstd", 6)
            khs = [("l_hT", c) for c in range(8)]
            def mmin(e, ps, oc):
                for k in range(8):
                    i = e.matmul(ps[:, 0:T], win[:, oc, k * 128:(k + 1) * 128], hT[:, k, :], start=(k == 0), stop=(k == 7))
                return i

            def lru_s1(c, p):
                nonlocal nbank
                yb, xb = nbank % 6, (nbank + 1) % 6
                nbank += 2
                yps, xps = self.psb[yb], self.psb[xb]
                ybt, XPt, tct, xbt = ybr[p], XP[p], tcv[p], xbb[p]
                K = lambda n: (n, p)
                pr.op("pe", lambda e: mmin(e, yps, c), reads=["l_win"] + khs, writes=[("ps", yb)])
                pr.op("pe", lambda e: mmin(e, xps, 10 + c), reads=["l_win"] + khs, writes=[("ps", xb)])
                pr.op("act", lambda e: e.activation(out=ybt, in_=yps[:, 0:T], func=AF.Gelu_apprx_tanh, bias=self.vcol(f"lbin{j}", c)),
                      reads=[("ps", yb), "vecs"], writes=[K("ybr")])
                pr.op("pool", lambda e: e.tensor_copy(out=XPt[:, 0:3], in_=halo[:, c, :]), reads=["l_halo"], writes=[K("XPh")])
                pr.op("act", lambda e: e.activation(out=XPt[:, 3:T + 3], in_=xps[:, 0:T], func=AF.Identity, bias=self.vcol(f"lbin{j}", 10 + c)),
                      reads=[("ps", xb), "vecs"], writes=[K("XP")])
                pr.op("pool", lambda e: e.tensor_copy(out=halo[:, c, :], in_=XPt[:, T:T + 3]), reads=[K("XP"), K("XPh")], writes=["l_halo"])
                pr.op("dve", lambda e: e.tensor_scalar(out=tct, in0=XPt[:, 3:T + 3], scalar1=self.vcol(f"lcw{j}_3", c),
                                                       scalar2=self.vcol(f"lcb{j}", c), op0=ALU.mult, op1=ALU.add),
                      reads=[K("XP"), "vecs"], writes=[K("tcv")])
                for tap in (2, 1, 0):
                    pr.op("dve", lambda e, tap=tap: e.scalar_tensor_tensor(
                        out=tct, in0=XPt[:, tap:tap + T], scalar=self.vcol(f"lcw{j}_{tap}", c), in1=tct, op0=ALU.mult, op1=ALU.add),
                        reads=[K("XP"), K("XPh"), K("tcv"), "vecs"], writes=[K("tcv")])
                pr.op("pool", lambda e: e.tensor_copy(out=xbt, in_=tct), reads=[K("tcv")], writes=[K("xbb")])

            def lru_s2(cs_, ps_):
                nonlocal nbank
                ctx = []
                for c, p in zip(cs_, ps_):
                    gxb, gab = nbank % 6, (nbank + 1) % 6
                    nbank += 2
                    q2 = c % 2
                    ctx.append(dict(c=c, p=p, q2=q2, gxb=gxb, gab=gab, gxps=self.psb[gxb], gaps=self.psb[gab],
                                    ybt=ybr[p], tct=tcv[p], xbt=xbb[p], gxt=gx[q2], gat=ga[q2], att=at[q2], mut=mu[q2],
                                    btt=bt[q2], hst=hs[q2]))
                for d_ in ctx:
                    c, p, q2 = d_["c"], d_["p"], d_["q2"]
                    pr.op("pe", lambda e, d_=d_: e.matmul(d_["gxps"][:, 0:T], wgx[:, d_["c"], :], d_["xbt"], start=True, stop=True),
                          reads=["l_wgx", ("xbb", p)], writes=[("ps", d_["gxb"])])
                    pr.op("pe", lambda e, d_=d_: e.matmul(d_["gaps"][:, 0:T], wga[:, d_["c"], :], d_["xbt"], start=True, stop=True),
                          reads=["l_wga", ("xbb", p)], writes=[("ps", d_["gab"])])
                for d_ in ctx:
                    c, p, q2 = d_["c"], d_["p"], d_["q2"]
                    pr.op("act", lambda e, d_=d_: e.activation(out=d_["gxt"], in_=d_["gxps"][:, 0:T], func=AF.Sigmoid, bias=self.vcol(f"lbgx{j}", d_["c"])),
                          reads=[("ps", d_["gxb"]), "vecs"], writes=[("gx", q2)])
                    pr.op("act", lambda e, d_=d_: e.activation(out=d_["gat"], in_=d_["gaps"][:, 0:T], func=AF.Sigmoid, bias=self.vcol(f"lbga{j}", d_["c"])),
                          reads=[("ps", d_["gab"]), "vecs"], writes=[("ga", q2)])
                for d_ in ctx:
                    c, p, q2 = d_["c"], d_["p"], d_["q2"]
                    pr.op("act", lambda e, d_=d_: e.activation(out=d_["att"], in_=d_["gat"], func=AF.Exp, scale=c8[:, d_["c"]:d_["c"] + 1]),
                          reads=[("ga", q2), "l_c8"], writes=[("at", q2)])
                    pr.op("dve", lambda e, d_=d_: e.tensor_tensor(out=d_["btt"], in0=d_["gxt"], in1=d_["tct"], op=ALU.mult),
                          reads=[("gx", q2), ("tcv", p)], writes=[("bt", q2)])
                    pr.op("dve", lambda e, d_=d_: e.tensor_tensor(out=d_["mut"], in0=d_["att"], in1=d_["att"], op=ALU.mult),
                          reads=[("at", q2)], writes=[("mu", q2)])
                    pr.op("dve", lambda e, d_=d_: e.tensor_scalar(out=d_["mut"], in0=d_["mut"], scalar1=1.0, scalar2=None, op0=ALU.min),
                          reads=[("mu", q2)], writes=[("mu", q2)])
                for d_ in ctx:
                    c, p, q2 = d_["c"], d_["p"], d_["q2"]
                    pr.op("act", lambda e, d_=d_: e.activation(out=d_["mut"], in_=d_["mut"], func=AF.Sqrt, scale=-1.0, bias=self.vcol("one")),
                          reads=[("mu", q2), "vecs"], writes=[("mu", q2)])
                for d_ in ctx:
                    c, p, q2 = d_["c"], d_["p"], d_["q2"]
                    pr.op("dve", lambda e, d_=d_: e.tensor_tensor(out=d_["btt"], in0=d_["btt"], in1=d_["mut"], op=ALU.mult),
                          reads=[("bt", q2), ("mu", q2)], writes=[("bt", q2)])
                    pr.op("dve", lambda e, d_=d_: e.tensor_tensor_scan(out=d_["hst"], data0=d_["att"], data1=d_["btt"],
                                                                       initial=hstate[:, d_["c"]:d_["c"] + 1], op0=ALU.mult, op1=ALU.add),
                          reads=[("at", q2), ("bt", q2), "l_hstate"], writes=[("hs", q2)])
                    pr.op("pool", lambda e, d_=d_: e.tensor_copy(out=hstate[:, d_["c"]:d_["c"] + 1], in_=d_["hst"][:, T - 1:T]),
                          reads=[("hs", q2)], writes=["l_hstate"])
                    pr.op("dve", lambda e, d_=d_: e.tensor_tensor(out=hy[:, d_["c"], :], in0=d_["hst"], in1=d_["ybt"], op=ALU.mult),
                          reads=[("hs", q2), ("ybr", p)], writes=[("l_hy", d_["c"])])

            ps_of = {}
            for pk in range(6):
                if pk < 5:
                    for c in (2 * pk, 2 * pk + 1):
                        ps_of[c] = it % NB
                        it += 1
                        lru_s1(c, ps_of[c])
                if pk >= 1:
                    cs_ = (2 * (pk - 1), 2 * (pk - 1) + 1)
                    lru_s2(cs_, [ps_of[c] for c in cs_])
            khy = [("l_hy", c) for c in range(10)]
            sbank = 7
            sps = self.psb[sbank]
            for o in range(8):
                mb = nbank % 6
                nbank += 1
                mps = self.psb[mb]

                def mmo(e, mps=mps, o=o):
                    for k in range(10):
                        i = e.matmul(mps[:, 0:T], wout[:, k, o * 128:(o + 1) * 128], hy[:, k, :], start=(k == 0), stop=(k == 9))
                    return i
                pr.op("pe", mmo, reads=["l_wout"] + khy, writes=[("ps", mb)])
                pr.op("act", lambda e, mps=mps, o=o: e.activation(out=mT[:, o, :], in_=mps[:, 0:T], func=AF.Identity, bias=self.vcol(f"lbout{j}", o)),
                      reads=[("ps", mb), "vecs"], writes=[("l_mT", o)])
                pr.op("act", lambda e, mps=mps, o=o: e.activation(out=sq[:, o, :], in_=mps[:, 0:T], func=AF.Square, bias=self.vcol(f"lbout{j}", o)),
                      reads=[("ps", mb), "vecs"], writes=[("l_sq", o)])
                pr.op("pe", lambda e, o=o: e.matmul(sps[:, 0:T], self.ones_b[:], sq[:, o, :], start=(o == 0), stop=(o == 7)),
                      reads=[("l_sq", o), "ones_b"], writes=[("ps", sbank)])
            self._resid(sps, sbank, rstd2, "l_rstd2", mT, "l_mT", f"nmpost{li}", xblk, [kx], 0, T, dst, t0)
            self.pump_casts(6)
        self.cur = 1 - self.cur

    def _resid(self, sps, sbank, rstd2, krs, mT, kmT, wname, xblk, kxr, xoff, n, dst, t0):
        pr = self.pr
        pr.op("act", lambda e: e.activation(out=rstd2[:, 0:n], in_=sps[:, 0:n], func=AF.Ln, scale=1.0 / D, bias=self.vcol("eps")),
              reads=[("ps", sbank), "vecs"], writes=[krs])
        pr.op("act", lambda e: e.activation(out=rstd2[:, 0:n], in_=rstd2[:, 0:n], func=AF.Exp, scale=-0.5),
              reads=[krs], writes=[krs])
        for o in range(8):
            pr.op("dve", lambda e, o=o: e.scalar_tensor_tensor(out=mT[:, o, 0:n], in0=mT[:, o, 0:n], scalar=self.vcol(wname, o),
                                                               in1=rstd2[:, 0:n], op0=ALU.mult, op1=ALU.mult),
                  reads=[(kmT, o), krs, "vecs"], writes=[(kmT, o)])
            pr.op("dve", lambda e, o=o: e.tensor_tensor(out=mT[:, o, 0:n], in0=mT[:, o, 0:n], in1=xblk[:, o, xoff:xoff + n], op=ALU.add),
                  reads=[(kmT, o)] + kxr, writes=[(kmT, o)])
        pr.dma("pool", dst[:, :, t0:t0 + n], mT[:, :, 0:n], reads=[(kmT, o) for o in range(8)], writes=[("xT", 1 - self.cur)])

    def phase_ssd(self, li):
        pr, S, A = self.pr, self.S, self.A
        pr.fence()
        A.reset()
        j = li // 2
        T = 512
        NQ = T // 128
        src = self.xT[self.cur].rearrange("(c p) t -> p c t", p=128)
        dst = self.xT[1 - self.cur].rearrange("(c p) t -> p c t", p=128)
        for b_ in range(10):
            self.pump_until(("swin", j, b_))
        self.pump_until(("swdt", j))
        for o in range(8):
            self.pump_until(("swout", j, o))
        wdt = A.alloc([128, 8, 32], BF16)
        pr.dma("sp", wdt, self.swdt_b[j].rearrange("p (k c) -> p k c", k=8), reads=[("swdt", j)], writes=["s_wdt"])
        normw = A.alloc([128, 2048], F32)
        pr.dma("sp", normw, self.snorm_d[j], writes=["s_normw"])
        sm = A.alloc([128, 96], F32)
        pr.dma("sp", sm, self.ssm_d[j], writes=["s_sm"])
        a_b = A.alloc([128, 32], F32)
        pr.op("act", lambda e: e.activation(out=a_b, in_=sm[:, 32:64], func=AF.Exp), reads=["s_sm"], writes=["s_ab"], small=True)
        pr.op("dve", lambda e: e.tensor_scalar(out=a_b, in0=a_b, scalar1=-1.0, scalar2=None, op0=ALU.mult), reads=["s_ab"], writes=["s_ab"], small=True)
        halo = A.alloc([128, 24, 3], F32)
        pr.op("pool", lambda e: e.memset(halo, 0.0), writes=["s_halo"])
        Sst = A.alloc([128, 2048], F32)
        Sbf = A.alloc([128, 2048], BF16)
        pr.op("pool", lambda e: e.memset(Sst, 0.0), writes=[("s_S", g) for g in range(4)])
        pr.op("pool", lambda e: e.memset(Sbf, 0.0), writes=[("s_Sbf", g) for g in range(4)])
        xblk = A.alloc([128, 8, T], F32)
        hT = A.alloc([128, 8, T], BF16)
        rstd = A.alloc([128, T], F32)
        rstd2 = rstd
        zs = A.alloc([128, NQ, 2048], BF16)
        mT = zs.bitcast(F32).rearrange("p q (o t) -> p (q o) t", o=2)
        xcT = A.alloc([128, 16, T], BF16)
        BT = A.alloc([128, 4, T], BF16)
        CT = A.alloc([128, 4, T], BF16)
        ynT = A.alloc([128, 16, T], BF16)
        sq = ynT[:, 0:8, :]
        sqp = [A.alloc([128, T], BF16) for _ in range(1)]
        NW = 2
        wsl = [A.alloc([128, 8, 512], BF16) for _ in range(NW)]
        wos = [A.alloc([128, 16, 128], BF16) for _ in range(2)]
        XP = [A.alloc([128, T + 3], F32) for _ in range(2)]
        tcv = [A.alloc([128, T], F32) for _ in range(2)]
        vdt = A.alloc([128, 32], F32)
        dts = A.alloc([128, 32], F32)
        das_ = [A.alloc([128, 32], F32) for _ in range(2)]
        css = A.alloc([128, 32], F32)
        ecs_ = [A.alloc([128, 32], F32) for _ in range(2)]
        dout = A.alloc([128, 32], F32)
        etot_ = [A.alloc([128, 32], F32) for _ in range(2)]
        xdt_ = [A.alloc([128, 2048], BF16) for _ in range(2)]
        xDb_ = [A.alloc([128, 2048], BF16) for _ in range(2)]
        xdd_ = [A.alloc([128, 2048], BF16) for _ in range(2)]
        Btm_ = [A.alloc([128, 512], BF16) for _ in range(2)]
        rhsM = [A.alloc([128, 8, 128], F32) for _ in range(2)]
        Eg = [A.alloc([128, 8, 128], BF16) for _ in range(2)]
        CBm = [A.alloc([128, 128], F32) for _ in range(2)]
        scT = [A.alloc([128, 8, 128], BF16) for _ in range(2)]
        yoffs = [A.alloc([128, 512], F32) for _ in range(1)]
        ysb = A.alloc([128, 2048], F32)
        ssq = A.alloc([128, 4], F32)
        gsc2 = yoffs
        rs4 = A.alloc([128, 4], F32)
        gn = A.alloc([128, 2048], BF16)
        self._nb = getattr(self, "_nb", 0)

        def bank():
            b = self._nb % 7
            self._nb += 1
            return b
        nws = 0
        nwo = 0
        nxp = 0
        ngr = 0
        for wi in range(S // T):
            t0 = wi * T
            kx = ("s_xblk",)
            pr.dma("sp", xblk, src[:, :, t0:t0 + T], reads=[("xT", self.cur)], writes=[kx])
            self._prenorm_multi(xblk, [kx], T, f"nmpre{li}", hT, "s_hT", sq, "s_ynT", rstd, "s_rstd", 7)
            khs = [("s_hT", c) for c in range(8)]
            sdef = []
            for blk in range(10):
                ws = wsl[nws % NW]
                kw = ("s_w", nws % NW)
                nws += 1
                pr.dma("sp", ws, self.swin_b[j, blk].rearrange("p (k c) -> p k c", k=8), reads=[("swin", j, blk)], writes=[kw])
                if blk < 4:
                    for q in range(NQ):
                        zb = bank()
                        zps = self.psb[zb]

                        def mmz(e, zps=zps, ws=ws, q=q):
                            for k in range(8):
                                i = e.matmul(zps[:, :], hT[:, k, q * 128:(q + 1) * 128], ws[:, k, :], start=(k == 0), stop=(k == 7))
                            return i
                        pr.op("pe", mmz, reads=[kw] + khs, writes=[("ps", zb)])
                        pr.op("act", lambda e, zps=zps, q=q, blk=blk: e.activation(out=zs[:, q, blk * 512:(blk + 1) * 512], in_=zps[:, :], func=AF.Silu),
                              reads=[("ps", zb)], writes=[("s_zs", q, blk)])
                else:
                    for i4 in range(4):
                        oc = (blk - 4) * 4 + i4
                        xb = bank()
                        xps = self.psb[xb]

                        def mmx(e, xps=xps, ws=ws, i4=i4):
                            for k in range(8):
                                i = e.matmul(xps[:, :], ws[:, k, i4 * 128:(i4 + 1) * 128], hT[:, k, :], start=(k == 0), stop=(k == 7))
                            return i
                        pr.op("pe", mmx, reads=[kw] + khs, writes=[("ps", xb)])
                        p = nxp % 2
                        nxp += 1
                        XPt, tct = XP[p], tcv[p]
                        pr.op("pool", lambda e, XPt=XPt, oc=oc: e.tensor_copy(out=XPt[:, 0:3], in_=halo[:, oc, :]), reads=["s_halo"], writes=[("s_XPh", p)])
                        pr.op("act", lambda e, XPt=XPt, xps=xps: e.activation(out=XPt[:, 3:T + 3], in_=xps[:, :], func=AF.Identity),
                              reads=[("ps", xb)], writes=[("s_XP", p)])
                        pr.op("pool", lambda e, XPt=XPt, oc=oc: e.tensor_copy(out=halo[:, oc, :], in_=XPt[:, T:T + 3]),
                              reads=[("s_XP", p), ("s_XPh", p)], writes=["s_halo"])
                        pr.op("dve", lambda e, XPt=XPt, tct=tct, oc=oc: e.tensor_scalar(out=tct, in0=XPt[:, 3:T + 3], scalar1=self.vcol(f"scw{j}_3", oc),
                                                                                 scalar2=self.vcol(f"scb{j}", oc), op0=ALU.mult, op1=ALU.add),
                              reads=[("s_XP", p), "vecs"], writes=[("s_tcv", p)])
                        for tap in (2, 1, 0):
                            pr.op("dve", lambda e, XPt=XPt, tct=tct, oc=oc, tap=tap: e.scalar_tensor_tensor(
                                out=tct, in0=XPt[:, tap:tap + T], scalar=self.vcol(f"scw{j}_{tap}", oc), in1=tct, op0=ALU.mult, op1=ALU.add),
                                reads=[("s_XP", p), ("s_XPh", p), ("s_tcv", p), "vecs"], writes=[("s_tcv", p)])
                        if oc < 16:
                            o_ap, ko = xcT[:, oc, :], ("s_xcT", oc)
                        elif oc < 20:
                            o_ap, ko = BT[:, oc - 16, :], ("s_BT", oc - 16)
                        else:
                            o_ap, ko = CT[:, oc - 20, :], ("s_CT", oc - 20)
                        for f_ in sdef:
                            f_()
                        sdef = [lambda o_ap=o_ap, tct=tct, p=p, ko=ko: pr.op(
                            "act", lambda e: e.activation(out=o_ap, in_=tct, func=AF.Silu), reads=[("s_tcv", p)], writes=[ko])]
            for f_ in sdef:
                f_()
            sdef = []
            if wi == 0:
                self.dump("hT", hT, khs)
                self.dump("zs0", zs[:, 0, :], [("s_zs", 0, b_) for b_ in range(4)])
                self.dump("xcT", xcT, [("s_xcT", c) for c in range(16)])
                self.dump("BT", BT, [("s_BT", c) for c in range(4)])
                self.dump("CT", CT, [("s_CT", c) for c in range(4)])
            def prep(q, par):
                qs = slice(q * 128, (q + 1) * 128)
                db = bank()
                dps = self.psb[db]

                def mmdt(e, dps=dps, q=q):
                    for k in range(8):
                        i = e.matmul(dps[:, 0:32], hT[:, k, q * 128:(q + 1) * 128], wdt[:, k, :], start=(k == 0), stop=(k == 7))
                    return i
                pr.op("pe", mmdt, reads=["s_wdt"] + khs, writes=[("ps", db)])
                pr.op("dve", lambda e, dps=dps: e.tensor_tensor(out=vdt, in0=dps[:, 0:32], in1=sm[:, 0:32], op=ALU.add),
                      reads=[("ps", db), "s_sm"], writes=["s_vdt"], small=True)
                pr.op("act", lambda e: e.activation(out=vdt, in_=vdt, func=AF.Exp), reads=["s_vdt"], writes=["s_vdt"], small=True)
                pr.op("act", lambda e: e.activation(out=dts, in_=vdt, func=AF.Ln, bias=self.vcol("one")), reads=["s_vdt", "vecs"], writes=["s_dts"], small=True)
                pr.op("dve", lambda e: e.tensor_tensor(out=das_[par], in0=dts, in1=a_b, op=ALU.mult), reads=["s_dts", "s_ab"], writes=[("s_das", par)], small=True)
                cb_ = bank()
                cps = self.psb[cb_]

                def mmcs(e, cps=cps):
                    e.matmul(cps[:, 0:32], self.triU[:], das_[par], start=True, stop=True)
                    return e.matmul(cps[:, 32:64], self.ones_f[:], das_[par], start=True, stop=True)
                pr.op("pe", mmcs, reads=[("s_das", par), "triU", "ones_f"], writes=[("ps", cb_)])
                pr.op("dve", lambda e, cps=cps: e.tensor_copy(out=css, in_=cps[:, 0:32]), reads=[("ps", cb_)], writes=["s_css"], small=True)
                pr.op("act", lambda e: e.activation(out=ecs_[par], in_=css, func=AF.Exp), reads=["s_css"], writes=[("s_ecs", par)], small=True)
                pr.op("dve", lambda e, cps=cps: e.tensor_tensor(out=dout, in0=cps[:, 32:64], in1=css, op=ALU.subtract),
                      reads=[("ps", cb_), "s_css"], writes=["s_dout"], small=True)
                pr.op("act", lambda e: e.activation(out=dout, in_=dout, func=AF.Exp), reads=["s_dout"], writes=["s_dout"], small=True)
                pr.op("act", lambda e, cps=cps: e.activation(out=etot_[par], in_=cps[:, 32:64], func=AF.Exp), reads=[("ps", cb_)], writes=[("s_etot", par)], small=True)
                for hb in range(2):
                    tb = bank()
                    tps = self.psb[tb].bitcast(BF16)

                    def trx(e, tps=tps, hb=hb, q=q):
                        for c8_ in range(8):
                            cx = hb * 8 + c8_
                            i = e.transpose(tps[:, c8_ * 128:(c8_ + 1) * 128], xcT[:, cx, q * 128:(q + 1) * 128], self.ident_b[:])
                        return i
                    pr.op("pe", trx, reads=[("s_xcT", hb * 8 + c) for c in range(8)] + ["ident_b"], writes=[("ps", tb)])
                    h0 = hb * 16
                    pr.op("dve", lambda e, tps=tps, hb=hb, h0=h0: e.tensor_tensor(
                        out=xdt_[par][:, hb * 1024:(hb + 1) * 1024].rearrange("p (h d) -> p h d", h=16),
                        in0=tps.rearrange("p (h d) -> p h d", h=16),
                        in1=dts[:, h0:h0 + 16].unsqueeze(2).to_broadcast([128, 16, 64]), op=ALU.mult),
                        reads=[("ps", tb), "s_dts"], writes=[("s_xdt", par, hb)])
                    pr.op("dve", lambda e, tps=tps, hb=hb, h0=h0: e.tensor_tensor(
                        out=xDb_[par][:, hb * 1024:(hb + 1) * 1024].rearrange("p (h d) -> p h d", h=16),
                        in0=tps.rearrange("p (h d) -> p h d", h=16),
                        in1=sm[:, 64 + h0:64 + h0 + 16].unsqueeze(2).to_broadcast([128, 16, 64]), op=ALU.mult),
                        reads=[("ps", tb), "s_sm"], writes=[("s_xDb", par, hb)])
                    pr.op("pool", lambda e, hb=hb, h0=h0: e.tensor_tensor(
                        out=xdd_[par][:, hb * 1024:(hb + 1) * 1024].rearrange("p (h d) -> p h d", h=16),
                        in0=xdt_[par][:, hb * 1024:(hb + 1) * 1024].rearrange("p (h d) -> p h d", h=16),
                        in1=dout[:, h0:h0 + 16].unsqueeze(2).to_broadcast([128, 16, 64]), op=ALU.mult),
                        reads=[("s_xdt", par, hb), "s_dout"], writes=[("s_xdd", par, hb)])
                bb = bank()
                bps = self.psb[bb].bitcast(BF16)

                def trb(e, bps=bps, q=q):
                    for g in range(4):
                        i = e.transpose(bps[:, g * 128:(g + 1) * 128], BT[:, g, q * 128:(q + 1) * 128], self.ident_b[:])
                    return i
                pr.op("pe", trb, reads=[("s_BT", g) for g in range(4)] + ["ident_b"], writes=[("ps", bb)])
                pr.op("act", lambda e, bps=bps: e.activation(out=Btm_[par], in_=bps[:, 0:512], func=AF.Identity), reads=[("ps", bb)], writes=[("s_Btm", par)])
            prep(0, 0)
            for q in range(NQ):
                qs = slice(q * 128, (q + 1) * 128)
                par = q % 2
                if q + 1 < NQ:
                    prep(q + 1, (q + 1) % 2)
                hbk_of = lambda g: g // 2

                def stageA(g, q=q, qs=qs, par=par):
                    pg = g % 2
                    gh = slice(g * 8, (g + 1) * 8)
                    rM, Et, CBt, sct = rhsM[pg], Eg[pg], CBm[pg], scT[pg]
                    pr.op("dve", lambda e: e.tensor_tensor(
                        out=rM, in0=self.triU[:].unsqueeze(1).to_broadcast([128, 8, 128]),
                        in1=das_[par][:, gh].unsqueeze(2).to_broadcast([128, 8, 128]), op=ALU.mult),
                        reads=[("s_das", par), "triU"], writes=[("s_rhsM", pg)])
                    d0, d1 = bank(), bank()

                    def mmD(e):
                        e.matmul(self.psb[d0][:, :], self.triSL[:], rM[:, 0:4, :], start=True, stop=True)
                        return e.matmul(self.psb[d1][:, :], self.triSL[:], rM[:, 4:8, :], start=True, stop=True)
                    pr.op("pe", mmD, reads=[("s_rhsM", pg), "triSL"], writes=[("ps", d0), ("ps", d1)])
                    pr.op("act", lambda e: e.activation(out=Et[:, 0:4, :], in_=self.psb[d0][:, :].rearrange("p (h l) -> p h l", h=4), func=AF.Exp),
                          reads=[("ps", d0)], writes=[("s_E", pg, 0)])
                    pr.op("act", lambda e: e.activation(out=Et[:, 4:8, :], in_=self.psb[d1][:, :].rearrange("p (h l) -> p h l", h=4), func=AF.Exp),
                          reads=[("ps", d1)], writes=[("s_E", pg, 1)])
                    cbb = bank()
                    cbps = self.psb[cbb]
                    pr.op("pe", lambda e: e.matmul(cbps[:, 0:128], BT[:, g, qs], CT[:, g, qs], start=True, stop=True),
                          reads=[("s_BT", g), ("s_CT", g)], writes=[("ps", cbb)])
                    pr.op("dve", lambda e: e.tensor_tensor(out=CBt, in0=cbps[:, 0:128], in1=self.triU[:], op=ALU.mult),
                          reads=[("ps", cbb), "triU"], writes=[("s_CBm", pg)], small=True)
                    pr.op("pool", lambda e: e.tensor_tensor(out=sct, in0=Et, in1=CBt.unsqueeze(1).to_broadcast([128, 8, 128]), op=ALU.mult),
                          reads=[("s_E", pg, 0), ("s_E", pg, 1), ("s_CBm", pg)], writes=[("s_scT", pg)])

                def stageB(g, q=q, qs=qs, par=par):
                    pg = g % 2
                    gh = slice(g * 8, (g + 1) * 8)
                    gc = slice(g * 512, (g + 1) * 512)
                    sct, yot = scT[pg], yoffs[0]
                    hbk = g // 2
                    ya, yo = bank(), bank()
                    yaps, yops = self.psb[ya], self.psb[yo]

                    def mmy(e):
                        e.matmul(yaps[:, :], self.ident_b[:], xDb_[par][:, gc], start=True, stop=False)
                        for hh in range(8):
                            c0 = g * 512 + hh * 64
                            i = e.matmul(yaps[:, hh * 64:(hh + 1) * 64], sct[:, hh, :], xdt_[par][:, c0:c0 + 64], start=False, stop=(hh == 7))
                        return i
                    pr.op("pe", mmy, reads=[("s_scT", pg), ("s_xdt", par, hbk), ("s_xDb", par, hbk), "ident_b"], writes=[("ps", ya)])
                    pr.op("pe", lambda e: e.matmul(yops[:, :], CT[:, g, qs], Sbf[:, gc], start=True, stop=True),
                          reads=[("s_CT", g), ("s_Sbf", g)], writes=[("ps", yo)])
                    pr.op("dve", lambda e: e.tensor_tensor(
                        out=yot.rearrange("p (h d) -> p h d", h=8), in0=yops[:, :].rearrange("p (h d) -> p h d", h=8),
                        in1=ecs_[par][:, gh].unsqueeze(2).to_broadcast([128, 8, 64]), op=ALU.mult),
                        reads=[("ps", yo), ("s_ecs", par)], writes=[("s_yoff", 0)])
                    pr.op("dve", lambda e: e.tensor_tensor(out=ysb[:, gc], in0=yaps[:, :], in1=yot, op=ALU.add),
                          reads=[("ps", ya), ("s_yoff", 0)], writes=[("s_ysb", g)])
                    sb_ = bank()
                    sps_ = self.psb[sb_]
                    pr.op("pe", lambda e: e.matmul(sps_[:, :], Btm_[par][:, g * 128:(g + 1) * 128], xdd_[par][:, gc], start=True, stop=True),
                          reads=[("s_Btm", par), ("s_xdd", par, hbk)], writes=[("ps", sb_)])
                    pr.op("dve", lambda e: e.tensor_tensor(
                        out=Sst[:, gc].rearrange("p (h d) -> p h d", h=8), in0=Sst[:, gc].rearrange("p (h d) -> p h d", h=8),
                        in1=etot_[par][:, gh].unsqueeze(2).to_broadcast([128, 8, 64]), op=ALU.mult),
                        reads=[("s_S", g), ("s_etot", par)], writes=[("s_S", g)])
                    pr.op("dve", lambda e: e.tensor_tensor(out=Sst[:, gc], in0=sps_[:, :], in1=Sst[:, gc], op=ALU.add),
                          reads=[("ps", sb_), ("s_S", g)], writes=[("s_S", g)])

                def stageC(g, q=q, par=par):
                    gc = slice(g * 512, (g + 1) * 512)
                    gst = gsc2[0]
                    pr.op("act", lambda e: e.activation(out=Sbf[:, gc], in_=Sst[:, gc], func=AF.Identity), reads=[("s_S", g)], writes=[("s_Sbf", g)])
                    pr.op("pool", lambda e: e.tensor_tensor(out=ysb[:, gc], in0=ysb[:, gc], in1=zs[:, q, gc], op=ALU.mult),
                          reads=[("s_ysb", g), ("s_zs", q, g)], writes=[("s_ysb", g)])
                    pr.op("act", lambda e: e.activation(out=gst, in_=ysb[:, gc], func=AF.Square),
                          reads=[("s_ysb", g)], writes=[("s_yoff", 0)])
                    pr.op("dve", lambda e: e.reduce_sum(out=ssq[:, g:g + 1], in_=gst, axis=mybir.AxisListType.X),
                          reads=[("s_yoff", 0)], writes=[("s_ssq", g)], small=True)

                stageA(0)
                stageA(1)
                stageB(0)
                stageA(2)
                stageB(1)
                stageA(3)
                stageB(2)
                stageC(0)
                stageB(3)
                stageC(1)
                stageC(2)
                stageC(3)
                kss = [("s_ssq", g) for g in range(4)]
                pr.op("dve", lambda e: e.tensor_scalar(out=rs4, in0=ssq, scalar1=1.0 / 512, scalar2=EPS, op0=ALU.mult, op1=ALU.add),
                      reads=kss, writes=["s_rs4"], force_same=True, small=True)
                pr.op("act", lambda e: e.activation(out=rs4, in_=rs4, func=AF.Ln), reads=["s_rs4"], writes=["s_rs4"], small=True)
                pr.op("act", lambda e: e.activation(out=rs4, in_=rs4, func=AF.Exp, scale=-0.5), reads=["s_rs4"], writes=["s_rs4"], force_same=True, small=True)
                for g in range(4):
                    gc = slice(g * 512, (g + 1) * 512)
                    pr.op("dve", lambda e, gc=gc, g=g: e.scalar_tensor_tensor(out=gn[:, gc], in0=ysb[:, gc], scalar=rs4[:, g:g + 1], in1=normw[:, gc],
                                                                              op0=ALU.mult, op1=ALU.mult),
                          reads=[("s_ysb", g), "s_rs4", "s_normw"], writes=[("s_gn", g)])
                for hb in range(2):
                    tb = bank()
                    tps = self.psb[tb].bitcast(BF16)

                    def trg(e, tps=tps, hb=hb):
                        for c8_ in range(8):
                            cx = hb * 8 + c8_
                            i = e.transpose(tps[:, c8_ * 128:(c8_ + 1) * 128], gn[:, cx * 128:(cx + 1) * 128], self.ident_b[:])
                        return i
                    pr.op("pe", trg, reads=[("s_gn", hb * 2), ("s_gn", hb * 2 + 1), "ident_b"], writes=[("ps", tb)])
                    pr.op("act", lambda e, tps=tps, hb=hb, qs=qs: e.activation(out=ynT[:, hb * 8:(hb + 1) * 8, qs], in_=tps.rearrange("p (c t) -> p c t", c=8), func=AF.Identity),
                          reads=[("ps", tb)], writes=[("s_ynT", hb * 8 + c) for c in range(8)])
            kyn = [("s_ynT", c) for c in range(16)]
            if wi == 0:
                self.dump("ynT", ynT, kyn)
            sbank = 7
            sps = self.psb[sbank]
            kzs_all = [("s_zs", q, b_) for q in range(NQ) for b_ in range(4)]
            for o in range(8):
                wo = wos[nwo % 2]
                kwo = ("s_wo", nwo % 2)
                nwo += 1
                pr.dma("sp", wo, self.swout_b[j, o].rearrange("p (k c) -> p k c", k=16), reads=[("swout", j, o)], writes=[kwo])
                mb = bank()
                mps = self.psb[mb]

                def mmo(e, mps=mps, wo=wo):
                    for k in range(16):
                        i = e.matmul(mps[:, :], wo[:, k, :], ynT[:, k, :], start=(k == 0), stop=(k == 15))
                    return i
                pr.op("pe", mmo, reads=[kwo] + kyn, writes=[("ps", mb)])
                sqt = sqp[0]
                pr.op("act", lambda e, mps=mps, o=o: e.activation(out=mT[:, o, :], in_=mps[:, :], func=AF.Identity),
                      reads=[("ps", mb)] + (kzs_all if o == 0 else []), writes=[("s_mT", o)] + (kzs_all if o == 0 else []))
                pr.op("act", lambda e, mps=mps, sqt=sqt: e.activation(out=sqt, in_=mps[:, :], func=AF.Square),
                      reads=[("ps", mb)], writes=[("s_sqp", 0)])
                pr.op("pe", lambda e, o=o, sqt=sqt: e.matmul(sps[:, :], self.ones_b[:], sqt, start=(o == 0), stop=(o == 7)),
                      reads=[("s_sqp", 0), "ones_b"], writes=[("ps", sbank)])
            self._resid(sps, sbank, rstd2, "s_rstd", mT, "s_mT", f"nmpost{li}", xblk, [kx], 0, T, dst, t0)
            for q in range(NQ):
                for b_ in range(4):
                    self.pr._record(self.pr.last_w[("xT", 1 - self.cur)], [], [("s_zs", q, b_)])
            self.pump_casts(6)
        self.cur = 1 - self.cur

    def _prenorm_multi(self, xblk, kxr, C, wname, hT, kh, sq, ksq, rstd, krstd, stat_bank):
        pr = self.pr
        ps = self.psb[stat_bank]
        for c in range(8):
            pr.op("act", lambda e, c=c: e.activation(out=sq[:, c, 0:C], in_=xblk[:, c, 0:C], func=AF.Square),
                  reads=kxr, writes=[(ksq, c)])

        def st(e):
            for c in range(8):
                i = e.matmul(ps[:, 0:C], self.ones_b[:], sq[:, c, 0:C], start=(c == 0), stop=(c == 7))
            return i
        pr.op("pe", st, reads=[(ksq, c) for c in range(8)] + ["ones_b"], writes=[("ps", stat_bank)])
        pr.op("act", lambda e: e.activation(out=rstd[:, 0:C], in_=ps[:, 0:C], func=AF.Ln, scale=1.0 / D, bias=self.vcol("eps")),
              reads=[("ps", stat_bank), "vecs"], writes=[krstd])
        pr.op("act", lambda e: e.activation(out=rstd[:, 0:C], in_=rstd[:, 0:C], func=AF.Exp, scale=-0.5),
              reads=[krstd], writes=[krstd])
        for c in range(8):
            pr.op("dve", lambda e, c=c: e.scalar_tensor_tensor(out=hT[:, c, 0:C], in0=xblk[:, c, 0:C], scalar=self.vcol(wname, c),
                                                               in1=rstd[:, 0:C], op0=ALU.mult, op1=ALU.mult),
                  reads=kxr + [krstd, "vecs"], writes=[(kh, c)])

    def build(self):
        for li in self.layers:
            if self.do_mixer and li % 2 == 0:
                j = li // 2
                for b_ in range(10):
                    self.add_cast(self.swin_b[j, b_], self.swin_f[j, b_], ("swin", j, b_))
                self.add_cast(self.swdt_b[j], self.swdt_f[j], ("swdt", j))
                for o in range(8):
                    self.add_cast(self.swout_b[j, o], self.swout_f[j, o], ("swout", j, o))
            if self.do_mixer and li % 2 == 1:
                j = li // 2
                self.add_cast(self.lwin_b[j], self.lwin_f[j], ("lwin", j))
                self.add_cast(self.lwgx_b[j], self.lwgx_f[j], ("lwgx", j))
                self.add_cast(self.lwga_b[j], self.lwga_f[j], ("lwga", j))
                self.add_cast(self.lwout_b[j], self.lwout_f[j], ("lwout", j))
            if self.do_ffn:
                for j2 in range(16):
                    self.add_cast(self.wup_b[li, j2], self.wup_f[li, j2], ("wup_b", li, j2))
                for o2 in range(4):
                    self.add_cast(self.wdn_b[li, o2], self.wdn_f[li, o2], ("wdn_b", li, o2))
        self.phase_in()
        for li in self.layers:
            if self.do_mixer:
                if li % 2 == 1:
                    self.phase_lru(li)
                else:
                    self.phase_ssd(li)
            if self.do_ffn:
                self.phase_ffn(li)
        self.phase_out()
        self.pr.finish_wait_all("sp")
        self.pr.emit()
        self.P.close()
        return self.nc


def pack_inputs(inp):
    vp = VecPack()
    vp.add_raw("eps", np.full((128, 1), EPS, np.float32))
    for li in range(DEPTH):
        vp.add(f"nmpre{li}", inp["norm_mix_pre"][li])
        vp.add(f"nmpost{li}", inp["norm_mix_post"][li])
        vp.add(f"nfpre{li}", inp["norm_ffn_pre"][li])
        vp.add(f"nfpost{li}", inp["norm_ffn_post"][li])
        for j in range(3):
            vp.add(f"fcw{li}_{j}", inp["ffn_conv_w"][li][j])
        vp.add(f"fcb{li}", inp["ffn_conv_b"][li])
    vp.add_raw("one", np.ones((128, 1), np.float32))
    for j in range(2):
        vp.add(f"lbin{j}", inp["lru_b_in"][j])
        for t in range(4):
            vp.add(f"lcw{j}_{t}", inp["lru_conv_w"][j][t])
        vp.add(f"lcb{j}", inp["lru_conv_b"][j])
        vp.add(f"lbgx{j}", inp["lru_b_gx"][j])
        vp.add(f"lbga{j}", inp["lru_b_ga"][j])
        vp.add(f"llam{j}", inp["lru_lambda"][j])
        vp.add(f"lbout{j}", inp["lru_b_out"][j])
    for j in range(2):
        for t in range(4):
            vp.add(f"scw{j}_{t}", inp["ssd_conv_w"][j][t])
        vp.add(f"scb{j}", inp["ssd_conv_b"][j])
    vecs = vp.build()
    swin = np.asarray(inp["ssd_w_in"], dtype=np.float32)
    sw = swin[:, :, :5120].reshape(2, 8, 128, 10, 512)
    swin_h = np.ascontiguousarray(sw.transpose(0, 3, 2, 1, 4)).reshape(2, 10, 128, 4096)
    sdt = swin[:, :, 5120:].reshape(2, 8, 128, 32)
    swdt_h = np.ascontiguousarray(sdt.transpose(0, 2, 1, 3)).reshape(2, 128, 256)
    swo = np.asarray(inp["ssd_w_out"], dtype=np.float32).reshape(2, 16, 128, 8, 128)
    swout_h = np.ascontiguousarray(swo.transpose(0, 3, 2, 1, 4)).reshape(2, 8, 128, 2048)
    snorm_h = np.ascontiguousarray(np.broadcast_to(np.asarray(inp["ssd_norm"], dtype=np.float32)[:, None, :], (2, 128, 2048)))
    ssm = np.concatenate([np.asarray(inp["ssd_dt_bias"], dtype=np.float32), np.asarray(inp["ssd_a_log"], dtype=np.float32),
                          np.asarray(inp["ssd_d"], dtype=np.float32)], axis=1)
    ssm_h = np.ascontiguousarray(np.broadcast_to(ssm[:, None, :], (2, 128, 96)))
    lwin = np.asarray(inp["lru_w_in"], dtype=np.float32)
    lw = lwin.reshape(2, 8, 128, 20, 128)
    lwin_h = np.ascontiguousarray(lw.transpose(0, 3, 2, 1, 4)).reshape(2, 2560, 1024)
    lwgx_h = np.ascontiguousarray(np.asarray(inp["lru_w_gx"], dtype=np.float32).transpose(0, 2, 1, 3)).reshape(2, 128, 1280)
    lwga_h = np.ascontiguousarray(np.asarray(inp["lru_w_ga"], dtype=np.float32).transpose(0, 2, 1, 3)).reshape(2, 128, 1280)
    lwo = np.asarray(inp["lru_w_out"], dtype=np.float32).reshape(2, 10, 128, 1024)
    lwout_h = np.ascontiguousarray(lwo.transpose(0, 2, 1, 3)).reshape(2, 128, 10240)
    wup = np.asarray(inp["ffn_w_up"], dtype=np.float32)
    g = wup[:, :, :DFF].reshape(DEPTH, 8, 128, 16, 256)
    v = wup[:, :, DFF:].reshape(DEPTH, 8, 128, 16, 256)
    gv = np.concatenate([g, v], axis=-1)
    wup_h = np.ascontiguousarray(gv.transpose(0, 3, 2, 1, 4)).reshape(DEPTH, 16, 128, 4096)
    wdn = np.asarray(inp["ffn_w_down"], dtype=np.float32)
    wd = wdn.reshape(DEPTH, 32, 128, 4, 256)
    wdn_h = np.ascontiguousarray(wd.transpose(0, 3, 2, 1, 4)).reshape(DEPTH, 4, 128, 8192)
    shared = {"vecs": vecs, "wup": wup_h, "wdn": wdn_h,
              "swin": swin_h, "swdt": swdt_h, "swout": swout_h, "snorm": snorm_h, "ssm": ssm_h,
              "lwin": lwin_h, "lwgx": lwgx_h, "lwga": lwga_h, "lwout": lwout_h}
    return shared, vp.index, vecs.shape[1]


_CACHE = {}


def run(inp, S=4096, layers=(0, 1, 2, 3), do_mixer=True, do_ffn=True, trace=False, debug=False):
    shared, vindex, nvec = pack_inputs(inp)
    x = np.asarray(inp["x"], dtype=np.float32)
    B = x.shape[0]
    b = Builder(S, vindex, nvec, layers=layers, do_mixer=do_mixer, do_ffn=do_ffn, debug=debug)
    nc = b.build()
    in_maps = []
    for i in range(B):
        m = dict(shared)
        m["x"] = np.ascontiguousarray(x[i, :S])
        in_maps.append(m)
    res = run_bass_kernel_spmd(nc, in_maps, core_ids=list(range(B)), trace=trace)
    y = np.stack([np.asarray(r["y"]) for r in res.results], axis=0)
    if debug:
        res.dbg = {k: np.asarray(res.results[0]["dbg_" + k]).astype(np.float32) for k in b.dbg}
    return y.astype(np.float32), res


def kernel(**inputs):
    y, _ = run(inputs)
    return y
```

```python
import numpy as np
import concourse.bass as bass
import concourse.mybir as mybir
from concourse.bass_utils import run_bass_kernel_spmd

F32 = mybir.dt.float32
BF16 = mybir.dt.bfloat16
AF = mybir.ActivationFunctionType
ALU = mybir.AluOpType

D = 1024
DEPTH = 4
DFF = 4096
EPS = 1e-6

ENGS = ("pe", "act", "dve", "pool", "sp")
EPOCH = 30000
NDSEM = 8


class Prog:
    def __init__(self, nc, same_engine_sync=False):
        self.nc = nc
        self.same_engine_sync = same_engine_sync
        self.ops = {e: [] for e in ENGS}
        self.cnt = {e: 0 for e in ENGS}
        self.epoch = {e: 0 for e in ENGS}
        self.sems = {e: [] for e in ENGS}
        self.dsems = {e: [] for e in ENGS}
        self.ndma = {e: 0 for e in ENGS}
        self.known = {e: {} for e in ENGS}
        self.last_w = {}
        self.readers = {}
        self._ctx = []
        self.fence_vals = {}
        self.small_toks = set()

    def _new_sem(self, name):
        cm = self.nc.semaphore(name)
        s = cm.__enter__()
        self._ctx.append(cm)
        return s

    def _eng_sem(self, e):
        ep = self.epoch[e]
        while len(self.sems[e]) <= ep:
            self.sems[e].append(self._new_sem(f"s_{e}_{len(self.sems[e])}"))
        return self.sems[e][ep]

    def _deps(self, e, reads, writes, force_same=False):
        toks = []
        for k in reads:
            t = self.last_w.get(k)
            if t is not None:
                toks.append(t)
        for k in writes:
            t = self.last_w.get(k)
            if t is not None:
                toks.append(t)
            toks.extend(self.readers.get(k, ()))
        waits = {}
        for (sem, val, te, isdma) in toks:
            if te == e and not isdma:
                if e == "pe" or not (self.same_engine_sync or force_same or (id(sem), val) in self.small_toks):
                    continue
            sid = id(sem)
            if self.known[e].get(sid, 0) >= val:
                continue
            if sid not in waits or waits[sid][1] < val:
                waits[sid] = (sem, val)
        for sid, (sem, val) in waits.items():
            self.known[e][sid] = val
        return list(waits.values())

    def _record(self, tok, reads, writes):
        for k in writes:
            self.last_w[k] = tok
            self.readers[k] = []
        for k in reads:
            if k in writes:
                continue
            lst = self.readers.setdefault(k, [])
            lst.append(tok)
            if len(lst) > 8:
                best = {}
                for t in lst:
                    sid = id(t[0])
                    if sid not in best or best[sid][1] < t[1]:
                        best[sid] = t
                self.readers[k] = list(best.values())

    def op(self, e, fn, reads=(), writes=(), force_same=False, small=False):
        waits = self._deps(e, reads, writes, force_same)
        if self.cnt[e] >= EPOCH:
            self.epoch[e] += 1
            self.cnt[e] = 0
        sem = self._eng_sem(e)
        self.cnt[e] += 1
        tok = (sem, self.cnt[e], e, False)
        if small:
            self.small_toks.add((id(sem), self.cnt[e]))
        self.ops[e].append((waits, fn, sem, 1))
        self._record(tok, reads, writes)
        return tok

    def dma(self, q, out, in_, reads=(), writes=(), nofence=False, **kw):
        if not self.dsems[q]:
            self.dsems[q] = [self._new_sem(f"d_{q}_{i}") for i in range(NDSEM)]
        i = self.ndma[q]
        self.ndma[q] += 1
        sem = self.dsems[q][i % NDSEM]
        waits = self._deps(q, reads, writes)
        prev = 16 * (i // NDSEM)
        if prev > 0 and self.known[q].get(id(sem), 0) < prev:
            waits.append((sem, prev))
            self.known[q][id(sem)] = prev
        tok = (sem, prev + 16, q, True)
        if not nofence:
            self.fence_vals.setdefault(q, {})[i % NDSEM] = (sem, prev + 16)

        def fn(eng, out=out, in_=in_, kw=kw):
            return eng.dma_start(out=out, in_=in_, **kw)

        self.ops[q].append((waits, fn, sem, 16))
        self._record(tok, reads, writes)
        return tok

    def fence(self):
        toks = []
        for f in ENGS:
            if self.sems[f] and self.cnt[f] > 0:
                toks.append((self.sems[f][self.epoch[f]], self.cnt[f]))
            for (sem, val) in self.fence_vals.get(f, {}).values():
                toks.append((sem, val))
        for e in ENGS:
            if not self.ops[e]:
                continue
            waits = []
            own = self.sems[e][self.epoch[e]] if self.sems[e] else None
            for (sem, val) in toks:
                if sem is own:
                    continue
                if self.known[e].get(id(sem), 0) >= val:
                    continue
                waits.append((sem, val))
                self.known[e][id(sem)] = val
            if waits:
                self.ops[e].append((waits, None, None, 0))

    def finish_wait_all(self, e="sp"):
        toks = {}
        for k, t in self.last_w.items():
            sid = id(t[0])
            if sid not in toks or toks[sid][1] < t[1]:
                toks[sid] = t
        waits = []
        for sid, (sem, val, te, isdma) in toks.items():
            if self.known[e].get(sid, 0) >= val:
                continue
            waits.append((sem, val))
        self.ops[e].append((waits, None, None, 0))

    def emit(self):
        nc = self.nc
        engmap = {"pe": "tensor", "act": "scalar", "dve": "vector", "pool": "gpsimd", "sp": "sync"}
        with nc.Block() as block:
            for e in ENGS:
                ops = self.ops[e]
                if not ops:
                    continue

                def body(eng, ops=ops):
                    for (waits, fn, sem, inc) in ops:
                        for (ws, wv) in waits:
                            eng.wait_ge(ws, wv)
                        if fn is None:
                            continue
                        ins = fn(eng)
                        ins.then_inc(sem, inc)

                getattr(block, engmap[e])(body)
        for cm in reversed(self._ctx):
            cm.__exit__(None, None, None)
        self._ctx = []


class Pools:
    def __init__(self, nc):
        self.nc = nc
        self._ctx = []

    def sb(self, name, shape, dt):
        cm = self.nc.sbuf_tensor(name, list(shape), dt)
        t = cm.__enter__()
        self._ctx.append(cm)
        return t

    def ps(self, name, shape, dt):
        cm = self.nc.psum_tensor(name, list(shape), dt)
        t = cm.__enter__()
        self._ctx.append(cm)
        return t

    def close(self):
        for cm in reversed(self._ctx):
            cm.__exit__(None, None, None)
        self._ctx = []


class Arena:
    def __init__(self, pools, nf32):
        self.t = pools.sb("arena", [128, nf32], F32)
        self.n = nf32
        self.off = 0

    def reset(self):
        self.off = 0

    def alloc(self, shape, dt):
        assert shape[0] == 128
        nel = 1
        for d_ in shape[1:]:
            nel *= d_
        nf = nel if dt == F32 else (nel + 1) // 2
        nf = (nf + 15) // 16 * 16
        assert self.off + nf <= self.n, ("arena overflow", self.off, nf, self.n)
        v = self.t[:, self.off:self.off + nf]
        self.off += nf
        if dt != F32:
            v = v.bitcast(dt)
        v = v[:, 0:nel]
        if len(shape) == 3:
            v = v.rearrange("p (a b) -> p a b", a=shape[1])
        return v


class VecPack:
    def __init__(self):
        self.cols = []
        self.index = {}
        self.n = 0

    def add(self, name, vec):
        vec = np.asarray(vec, dtype=np.float32).reshape(-1)
        assert vec.size % 128 == 0, (name, vec.size)
        nch = vec.size // 128
        self.index[name] = (self.n, nch)
        self.cols.append(vec.reshape(nch, 128).T)
        self.n += nch

    def add_raw(self, name, arr):
        arr = np.asarray(arr, dtype=np.float32)
        assert arr.shape[0] == 128
        self.index[name] = (self.n, arr.shape[1])
        self.cols.append(arr)
        self.n += arr.shape[1]

    def build(self):
        return np.ascontiguousarray(np.concatenate(self.cols, axis=1))


def windows(S, n, halo):
    out = []
    t = 0
    while t < S:
        m = min(n, S - t)
        out.append((t, m))
        t += m
    return out


class Builder:
    def __init__(self, S, vec_index, nvec, layers=(0, 1, 2, 3), do_mixer=True, do_ffn=True, debug=False):
        self.S = S
        self.debug = debug
        self.dbg = {}
        self.vi = vec_index
        self.layers = layers
        self.do_mixer = do_mixer
        self.do_ffn = do_ffn
        nc = self.nc = bass.Bass("TRN2", target_bir_lowering=False)
        self.P = Pools(nc)
        self.pr = Prog(nc)
        P, pr = self.P, self.pr
        self.x_in = nc.dram_tensor("x", [S, D], F32, kind="ExternalInput").ap()
        self.y_out = nc.dram_tensor("y", [S, D], F32, kind="ExternalOutput").ap()
        self.vec_d = nc.dram_tensor("vecs", [128, nvec], F32, kind="ExternalInput").ap()
        self.wup_f = nc.dram_tensor("wup", [DEPTH, 16, 128, 4096], F32, kind="ExternalInput").ap()
        self.wdn_f = nc.dram_tensor("wdn", [DEPTH, 4, 128, 8192], F32, kind="ExternalInput").ap()
        self.wup_b = nc.dram_tensor("wup_b", [DEPTH, 16, 128, 4096], BF16, kind="Internal").ap()
        self.wdn_b = nc.dram_tensor("wdn_b", [DEPTH, 4, 128, 8192], BF16, kind="Internal").ap()
        self.swin_f = nc.dram_tensor("swin", [2, 10, 128, 4096], F32, kind="ExternalInput").ap()
        self.swdt_f = nc.dram_tensor("swdt", [2, 128, 256], F32, kind="ExternalInput").ap()
        self.swout_f = nc.dram_tensor("swout", [2, 8, 128, 2048], F32, kind="ExternalInput").ap()
        self.swin_b = nc.dram_tensor("swin_b", [2, 10, 128, 4096], BF16, kind="Internal").ap()
        self.swdt_b = nc.dram_tensor("swdt_b", [2, 128, 256], BF16, kind="Internal").ap()
        self.swout_b = nc.dram_tensor("swout_b", [2, 8, 128, 2048], BF16, kind="Internal").ap()
        self.snorm_d = nc.dram_tensor("snorm", [2, 128, 2048], F32, kind="ExternalInput").ap()
        self.ssm_d = nc.dram_tensor("ssm", [2, 128, 96], F32, kind="ExternalInput").ap()
        self.lwin_f = nc.dram_tensor("lwin", [2, 2560, 1024], F32, kind="ExternalInput").ap()
        self.lwgx_f = nc.dram_tensor("lwgx", [2, 128, 1280], F32, kind="ExternalInput").ap()
        self.lwga_f = nc.dram_tensor("lwga", [2, 128, 1280], F32, kind="ExternalInput").ap()
        self.lwout_f = nc.dram_tensor("lwout", [2, 128, 10240], F32, kind="ExternalInput").ap()
        self.lwin_b = nc.dram_tensor("lwin_b", [2, 2560, 1024], BF16, kind="Internal").ap()
        self.lwgx_b = nc.dram_tensor("lwgx_b", [2, 128, 1280], BF16, kind="Internal").ap()
        self.lwga_b = nc.dram_tensor("lwga_b", [2, 128, 1280], BF16, kind="Internal").ap()
        self.lwout_b = nc.dram_tensor("lwout_b", [2, 128, 10240], BF16, kind="Internal").ap()
        self.xT = [nc.dram_tensor(f"xT{i}", [D, S], F32, kind="Internal").ap() for i in range(2)]
        self.cur = 0
        self.vecs = P.sb("vecs_sb", [128, nvec], F32)
        pr.dma("sp", self.vecs[:], self.vec_d, writes=["vecs"])
        self.ident = P.sb("ident", [128, 128], F32)
        self.ones_f = P.sb("ones_f", [128, 128], F32)
        self.ones_b = P.sb("ones_b", [128, 128], BF16)
        pr.op("pool", lambda e: e.memset(self.ones_f[:], 1.0), writes=["ones_f"])
        pr.op("pool", lambda e: e.memset(self.ones_b[:], 1.0), writes=["ones_b"])
        pr.op("pool", lambda e: e.affine_select(out=self.ident[:], in_=self.ones_f[:], pattern=[[-1, 128]],
                                                compare_op=ALU.is_equal, fill=0.0, base=0, channel_multiplier=1),
              reads=["ones_f"], writes=["ident"])
        self.ident_b = P.sb("ident_b", [128, 128], BF16)
        self.triU = P.sb("triU", [128, 128], F32)
        self.triSL = P.sb("triSL", [128, 128], F32)
        pr.op("pool", lambda e: e.tensor_copy(out=self.ident_b[:], in_=self.ident[:]), reads=["ident"], writes=["ident_b"])
        pr.op("pool", lambda e: e.affine_select(out=self.triU[:], in_=self.ones_f[:], pattern=[[1, 128]],
                                                compare_op=ALU.is_ge, fill=0.0, base=0, channel_multiplier=-1),
              reads=["ones_f"], writes=["triU"])
        pr.op("pool", lambda e: e.affine_select(out=self.triSL[:], in_=self.ones_f[:], pattern=[[-1, 128]],
                                                compare_op=ALU.is_gt, fill=0.0, base=0, channel_multiplier=1),
              reads=["ones_f"], writes=["triSL"])
        self.psall = P.ps("psall", [128, 4096], F32)
        self.psb = [self.psall[:, i * 512:(i + 1) * 512] for i in range(8)]
        self.A = Arena(P, 50176)
        self.cast_jobs = []
        self.cast_done = 0

    def dump(self, name, ap, reads):
        if not self.debug:
            return
        t = self.nc.dram_tensor("dbg_" + name, list(ap.shape), ap.dtype, kind="ExternalOutput").ap()
        self.dbg[name] = t
        self.pr.dma("pool", t, ap, reads=reads, writes=[("dbg", name)])

    def vcol(self, name, c=0, n=1):
        c0, nch = self.vi[name]
        assert c + n <= nch, (name, c, n, nch)
        return self.vecs[:, c0 + c:c0 + c + n]

    def add_cast(self, dst, src, key):
        self.cast_jobs.append((dst, src, key))

    def pump_casts(self, n):
        while n > 0 and self.cast_done < len(self.cast_jobs):
            dst, src, key = self.cast_jobs[self.cast_done]
            self.pr.dma("pool", dst, src, writes=[key], nofence=True, max_dma_last_dim=4096)
            self.cast_done += 1
            n -= 1

    def pump_until(self, key):
        while self.cast_done < len(self.cast_jobs) and key not in self.pr.last_w:
            self.pump_casts(1)

    def phase_in(self):
        pr, P, S = self.pr, self.P, self.S
        pr.fence()
        self.A.reset()
        dst = self.xT[self.cur].rearrange("(c p) t -> p c t", p=128)
        xin = [self.A.alloc([128, D], F32) for i in range(2)]
        xst = [self.A.alloc([128, 8, 512], F32) for i in range(2)]
        nblk = (S + 511) // 512
        for b in range(nblk):
            t0 = b * 512
            nt = min(512, S - t0) // 128
            st = xst[b % 2]
            for tt in range(nt):
                xi = xin[(b * 4 + tt) % 2]
                kxi = ("xin", (b * 4 + tt) % 2)
                pr.dma("sp", xi[:], self.x_in[t0 + tt * 128:t0 + (tt + 1) * 128, :], writes=[kxi])
                for half in range(2):
                    bank = (tt * 2 + half) % 8
                    ps = self.psb[bank]

                    def tr(e, xi=xi, ps=ps, half=half):
                        for q in range(4):
                            c = half * 4 + q
                            i = e.transpose(ps[:, q * 128:(q + 1) * 128], xi[:, c * 128:(c + 1) * 128], self.ident[:])
                        return i
                    pr.op("pe", tr, reads=[kxi, "ident"], writes=[("ps", bank)])
                    eng = "act" if half == 0 else "dve"

                    def ev(e, st=st, ps=ps, half=half, tt=tt, eng=eng):
                        o = st[:, half * 4:(half + 1) * 4, tt * 128:(tt + 1) * 128]
                        i_ = ps[:].rearrange("p (q t) -> p q t", q=4)
                        if eng == "act":
                            return e.activation(out=o, in_=i_, func=AF.Identity)
                        return e.tensor_copy(out=o, in_=i_)
                    pr.op(eng, ev, reads=[("ps", bank)], writes=[("xst", b % 2, half, tt)])
            rk = [("xst", b % 2, h, tt) for h in range(2) for tt in range(nt)]
            pr.dma("pool", dst[:, :, t0:t0 + nt * 128], st[:, :, 0:nt * 128], reads=rk, writes=[("xT", self.cur)])

    def phase_out(self):
        pr, P, S = self.pr, self.P, self.S
        pr.fence()
        self.A.reset()
        src = self.xT[self.cur].rearrange("(c p) t -> p c t", p=128)
        xld = [self.A.alloc([128, 8, 512], F32) for i in range(2)]
        yst = [self.A.alloc([128, D], F32) for i in range(2)]
        nblk = (S + 511) // 512
        for b in range(nblk):
            t0 = b * 512
            nt = min(512, S - t0) // 128
            xl = xld[b % 2]
            kx = ("xld", b % 2)
            pr.dma("sp", xl[:, :, 0:nt * 128], src[:, :, t0:t0 + nt * 128], reads=[("xT", self.cur)], writes=[kx])
            for tt in range(nt):
                ys = yst[(b * 4 + tt) % 2]
                ky = ("yst", (b * 4 + tt) % 2)
                for half in range(2):
                    bank = (tt * 2 + half) % 8
                    ps = self.psb[bank]

                    def tr(e, xl=xl, ps=ps, half=half, tt=tt):
                        for q in range(4):
                            c = half * 4 + q
                            i = e.transpose(ps[:, q * 128:(q + 1) * 128], xl[:, c, tt * 128:(tt + 1) * 128], self.ident[:])
                        return i
                    pr.op("pe", tr, reads=[kx, "ident"], writes=[("ps", bank)])
                    eng = "act" if half == 0 else "dve"

                    def ev(e, ys=ys, ps=ps, half=half, eng=eng):
                        o = ys[:, half * 512:(half + 1) * 512]
                        if eng == "act":
                            return e.activation(out=o, in_=ps[:], func=AF.Identity)
                        return e.tensor_copy(out=o, in_=ps[:])
                    pr.op(eng, ev, reads=[("ps", bank)], writes=[(ky, half)])
                pr.dma("pool", self.y_out[t0 + tt * 128:t0 + (tt + 1) * 128, :], ys[:],
                       reads=[(ky, 0), (ky, 1)], writes=["y"])

    def prenorm(self, xblk, kx, C, wname, hT, kh, sq, ksq, rstd, krstd, stat_bank):
        pr = self.pr
        ps = self.psb[stat_bank]
        for c in range(8):
            pr.op("act", lambda e, c=c: e.activation(out=sq[:, c, 0:C], in_=xblk[:, c, 0:C], func=AF.Square),
                  reads=[kx], writes=[(ksq, c)])

        def st(e):
            for c in range(8):
                i = e.matmul(ps[:, 0:C], self.ones_b[:], sq[:, c, 0:C], start=(c == 0), stop=(c == 7))
            return i
        pr.op("pe", st, reads=[(ksq, c) for c in range(8)] + ["ones_b"], writes=[("ps", stat_bank)])
        pr.op("act", lambda e: e.activation(out=rstd[:, 0:C], in_=ps[:, 0:C], func=AF.Ln, scale=1.0 / D, bias=self.vcol("eps")),
              reads=[("ps", stat_bank), "vecs"], writes=[krstd])
        pr.op("act", lambda e: e.activation(out=rstd[:, 0:C], in_=rstd[:, 0:C], func=AF.Exp, scale=-0.5),
              reads=[krstd], writes=[krstd])
        for c in range(8):
            pr.op("dve", lambda e, c=c: e.scalar_tensor_tensor(out=hT[:, c, 0:C], in0=xblk[:, c, 0:C], scalar=self.vcol(wname, c),
                                                               in1=rstd[:, 0:C], op0=ALU.mult, op1=ALU.mult),
                  reads=[kx, krstd, "vecs"], writes=[(kh, c)])

    def phase_ffn(self, li):
        pr, P, S = self.pr, self.P, self.S
        pr.fence()
        self.A.reset()
        src = self.xT[self.cur].rearrange("(c p) t -> p c t", p=128)
        dst = self.xT[1 - self.cur].rearrange("(c p) t -> p c t", p=128)
        xblks = [self.A.alloc([128, 8, 512], F32) for _ in range(2)]
        sq = self.A.alloc([128, 8, 512], BF16)
        sqps = [self.A.alloc([128, 512], BF16) for _ in range(2)]
        hT = self.A.alloc([128, 8, 512], BF16)
        rstd = self.A.alloc([128, 512], F32)
        rstd2 = self.A.alloc([128, 512], F32)
        gvT = self.A.alloc([128, 32, 512], BF16)
        fT = self.A.alloc([128, 8, 512], F32)
        NWU, NWD = 3, 2
        wus = [self.A.alloc([128, 8, 512], BF16) for i in range(NWU)]
        wds = [self.A.alloc([128, 32, 256], BF16) for i in range(NWD)]
        tg = [self.A.alloc([128, 512], F32) for i in range(4)]
        tv = [self.A.alloc([128, 512], F32) for i in range(4)]
        gg = [self.A.alloc([128, 512], F32) for i in range(4)]
        nwu = 0
        nwd = 0
        npair = 0
        wins = windows(S, 464, 2)
        def pre_p1(wi_):
            t0_, n_ = wins[wi_]
            C_ = n_ + 2
            xb = xblks[wi_ % 2]
            kx_ = (f"f{li}_xblk", wi_ % 2)
            if t0_ == 0:
                pr.op("pool", lambda e: e.memset(xb[:, :, 0:2], 0.0), writes=[(kx_, "halo")])
                pr.dma("sp", xb[:, :, 2:C_], src[:, :, 0:n_], reads=[("xT", self.cur)], writes=[kx_])
            else:
                pr.dma("sp", xb[:, :, 0:C_], src[:, :, t0_ - 2:t0_ + n_], reads=[("xT", self.cur)], writes=[kx_, (kx_, "halo")])
            for c in range(8):
                pr.op("act", lambda e, c=c: e.activation(out=sq[:, c, 0:C_], in_=xb[:, c, 0:C_], func=AF.Square),
                      reads=[kx_, (kx_, "halo")], writes=[(f"f{li}_sq", c)])

        def pre_p2(wi_):
            t0_, n_ = wins[wi_]
            C_ = n_ + 2
            xb = xblks[wi_ % 2]
            kx_ = (f"f{li}_xblk", wi_ % 2)
            ps = self.psb[6]

            def st(e):
                for c in range(8):
                    i = e.matmul(ps[:, 0:C_], self.ones_b[:], sq[:, c, 0:C_], start=(c == 0), stop=(c == 7))
                return i
            pr.op("pe", st, reads=[(f"f{li}_sq", c) for c in range(8)] + ["ones_b"], writes=[("ps", 6)])
            pr.op("act", lambda e: e.activation(out=rstd[:, 0:C_], in_=ps[:, 0:C_], func=AF.Ln, scale=1.0 / D, bias=self.vcol("eps")),
                  reads=[("ps", 6), "vecs"], writes=[f"f{li}_rstd"])
            pr.op("act", lambda e: e.activation(out=rstd[:, 0:C_], in_=rstd[:, 0:C_], func=AF.Exp, scale=-0.5),
                  reads=[f"f{li}_rstd"], writes=[f"f{li}_rstd"])
            for c in range(8):
                pr.op("dve", lambda e, c=c: e.scalar_tensor_tensor(out=hT[:, c, 0:C_], in0=xb[:, c, 0:C_], scalar=self.vcol(f"nfpre{li}", c),
                                                                   in1=rstd[:, 0:C_], op0=ALU.mult, op1=ALU.mult),
                      reads=[kx_, (kx_, "halo"), f"f{li}_rstd", "vecs"], writes=[(f"f{li}_hT", c)])

        pre_p1(0)
        pre_p2(0)
        for wi, (t0, n) in enumerate(wins):
            C = n + 2
            xblk = xblks[wi % 2]
            kx = (f"f{li}_xblk", wi % 2)
            kxr = [kx, (kx, "halo")]
            khs = [(f"f{li}_hT", c) for c in range(8)]
            deferred, deferred_next = [], []
            for j2 in range(16):
                ws = wus[nwu % NWU]
                kw = ("wu", li, nwu % NWU)
                nwu += 1
                self.pump_until(("wup_b", li, j2))
                pr.dma("sp", ws[:], self.wup_b[li, j2].rearrange("p (k c) -> p k c", k=8),
                       reads=[("wup_b", li, j2)], writes=[kw])
                for jj in range(2):
                    j = j2 * 2 + jj
                    pb = (npair % 4) * 2
                    npair += 1
                    gps, vps = self.psb[pb], self.psb[pb + 1]

                    def mmg(e, ws=ws, gps=gps, jj=jj, C=C, off=0):
                        for k in range(8):
                            i = e.matmul(gps[:, 0:C], ws[:, k, off + jj * 128:off + (jj + 1) * 128], hT[:, k, 0:C],
                                         start=(k == 0), stop=(k == 7))
                        return i
                    pr.op("pe", mmg, reads=[kw] + khs, writes=[("ps", pb)])
                    pr.op("pe", lambda e, ws=ws, vps=vps, jj=jj, C=C: mmg(e, ws, vps, jj, C, 256),
                          reads=[kw] + khs, writes=[("ps", pb + 1)])
                    par = j % 4
                    tgt, tvt, ggt = tg[par], tv[par], gg[par]
                    ktg, ktv, kgg = ("tg", par), ("tv", par), ("gg", par)
                    jv = 32 + j
                    pr.op("act", lambda e, gps=gps, tgt=tgt, j=j, C=C, n=n: e.activation(
                        out=tgt[:, 0:n], in_=gps[:, 2:C], func=AF.Identity,
                        scale=self.vcol(f"fcw{li}_2", j), bias=self.vcol(f"fcb{li}", j)),
                        reads=[("ps", pb), "vecs"], writes=[ktg])
                    pr.op("act", lambda e, vps=vps, tvt=tvt, jv=jv, C=C, n=n: e.activation(
                        out=tvt[:, 0:n], in_=vps[:, 2:C], func=AF.Identity,
                        scale=self.vcol(f"fcw{li}_2", jv), bias=self.vcol(f"fcb{li}", jv)),
                        reads=[("ps", pb + 1), "vecs"], writes=[ktv])
                    pr.op("dve", lambda e, gps=gps, tgt=tgt, j=j, C=C, n=n: e.scalar_tensor_tensor(
                        out=tgt[:, 0:n], in0=gps[:, 1:C - 1], scalar=self.vcol(f"fcw{li}_1", j), in1=tgt[:, 0:n],
                        op0=ALU.mult, op1=ALU.add), reads=[("ps", pb), ktg, "vecs"], writes=[ktg])
                    pr.op("dve", lambda e, gps=gps, tgt=tgt, j=j, C=C, n=n: e.scalar_tensor_tensor(
                        out=tgt[:, 0:n], in0=gps[:, 0:C - 2], scalar=self.vcol(f"fcw{li}_0", j), in1=tgt[:, 0:n],
                        op0=ALU.mult, op1=ALU.add), reads=[("ps", pb), ktg, "vecs"], writes=[ktg])
                    deferred_next.append(lambda tgt=tgt, ggt=ggt, n=n, ktg=ktg, kgg=kgg: pr.op(
                        "act", lambda e: e.activation(out=ggt[:, 0:n], in_=tgt[:, 0:n], func=AF.Gelu_apprx_tanh),
                        reads=[ktg], writes=[kgg]))
                    pr.op("dve", lambda e, vps=vps, tvt=tvt, jv=jv, C=C, n=n: e.scalar_tensor_tensor(
                        out=tvt[:, 0:n], in0=vps[:, 1:C - 1], scalar=self.vcol(f"fcw{li}_1", jv), in1=tvt[:, 0:n],
                        op0=ALU.mult, op1=ALU.add), reads=[("ps", pb + 1), ktv, "vecs"], writes=[ktv])
                    pr.op("dve", lambda e, vps=vps, tvt=tvt, jv=jv, C=C, n=n: e.scalar_tensor_tensor(
                        out=tvt[:, 0:n], in0=vps[:, 0:C - 2], scalar=self.vcol(f"fcw{li}_0", jv), in1=tvt[:, 0:n],
                        op0=ALU.mult, op1=ALU.add), reads=[("ps", pb + 1), ktv, "vecs"], writes=[ktv])
                    deferred_next.append(lambda ggt=ggt, tvt=tvt, j=j, n=n, kgg=kgg, ktv=ktv: pr.op(
                        "dve", lambda e: e.tensor_tensor(out=gvT[:, j, 0:n], in0=ggt[:, 0:n], in1=tvt[:, 0:n], op=ALU.mult),
                        reads=[kgg, ktv], writes=[("gvT", j)]))
                    for f_ in deferred:
                        f_()
                    deferred, deferred_next = deferred_next, []
            for f_ in deferred:
                f_()
            deferred = []
            if wi + 1 < len(wins):
                pre_p1(wi + 1)
            kgv = [("gvT", j) for j in range(32)]
            sbank = 7
            sps = self.psb[sbank]
            for o2 in range(4):
                wd = wds[nwd % NWD]
                kwd = ("wd", li, nwd % NWD)
                nwd += 1
                self.pump_until(("wdn_b", li, o2))
                pr.dma("sp", wd[:], self.wdn_b[li, o2].rearrange("p (k c) -> p k c", k=32),
                       reads=[("wdn_b", li, o2)], writes=[kwd])
                for oo in range(2):
                    o = o2 * 2 + oo
                    fb = (npair % 3) * 2
                    npair += 1
                    fps = self.psb[fb]

                    def mmd(e, wd=wd, fps=fps, oo=oo, n=n):
                        for k in range(32):
                            i = e.matmul(fps[:, 0:n], wd[:, k, oo * 128:(oo + 1) * 128], gvT[:, k, 0:n],
                                         start=(k == 0), stop=(k == 31))
                        return i
                    pr.op("pe", mmd, reads=[kwd] + kgv, writes=[("ps", fb)])
                    for f_ in deferred:
                        f_()
                    deferred = []
                    pr.op("act", lambda e, fps=fps, o=o, n=n: e.activation(out=fT[:, o, 0:n], in_=fps[:, 0:n], func=AF.Identity,
                                                                            scale=self.vcol(f"nfpost{li}", o)),
                          reads=[("ps", fb), "vecs"], writes=[("fT", o)])
                    sqt = sqps[o % 2]
                    pr.op("act", lambda e, fps=fps, sqt=sqt, n=n: e.activation(out=sqt[:, 0:n], in_=fps[:, 0:n], func=AF.Square),
                          reads=[("ps", fb)], writes=[(f"f{li}_sqp", o % 2)])
                    for f_ in deferred:
                        f_()
                    deferred = [lambda o=o, n=n, sqt=sqt: pr.op(
                        "pe", lambda e: e.matmul(sps[:, 0:n], self.ones_b[:], sqt[:, 0:n], start=(o == 0), stop=(o == 7)),
                        reads=[(f"f{li}_sqp", o % 2), "ones_b"], writes=[("ps", sbank)])]
                    if o == 1 and wi + 1 < len(wins):
                        pre_p2(wi + 1)
            for f_ in deferred:
                f_()
            deferred = []
            pr.op("act", lambda e, n=n: e.activation(out=rstd2[:, 0:n], in_=sps[:, 0:n], func=AF.Ln, scale=1.0 / D, bias=self.vcol("eps")),
                  reads=[("ps", sbank), "vecs"], writes=["rstd2"])
            pr.op("act", lambda e, n=n: e.activation(out=rstd2[:, 0:n], in_=rstd2[:, 0:n], func=AF.Exp, scale=-0.5),
                  reads=["rstd2"], writes=["rstd2"])
            for o in range(8):
                pr.op("dve", lambda e, o=o, n=n: e.tensor_tensor(out=fT[:, o, 0:n], in0=fT[:, o, 0:n], in1=rstd2[:, 0:n], op=ALU.mult),
                      reads=[("fT", o), "rstd2"], writes=[("fT", o)])
                pr.op("dve", lambda e, o=o, n=n, C=C, xblk=xblk: e.tensor_tensor(out=fT[:, o, 0:n], in0=fT[:, o, 0:n], in1=xblk[:, o, 2:C], op=ALU.add),
                      reads=[("fT", o)] + kxr, writes=[("fT", o)])
            pr.dma("pool", dst[:, :, t0:t0 + n], fT[:, :, 0:n], reads=[("fT", o) for o in range(8)],
                   writes=[("xT", 1 - self.cur)])
            self.pump_casts(6)
        self.cur = 1 - self.cur

    def phase_lru(self, li):
        pr, S, A = self.pr, self.S, self.A
        pr.fence()
        A.reset()
        j = li // 2
        T = 512
        src = self.xT[self.cur].rearrange("(c p) t -> p c t", p=128)
        dst = self.xT[1 - self.cur].rearrange("(c p) t -> p c t", p=128)
        win = A.alloc([128, 20, 1024], BF16)
        wgx = A.alloc([128, 10, 128], BF16)
        wga = A.alloc([128, 10, 128], BF16)
        wout = A.alloc([128, 10, 1024], BF16)
        for key in (("lwin", j), ("lwgx", j), ("lwga", j), ("lwout", j)):
            self.pump_until(key)
        pr.dma("sp", win, self.lwin_b[j].rearrange("(o p) c -> p o c", p=128), reads=[("lwin", j)], writes=["l_win"])
        pr.dma("sp", wgx, self.lwgx_b[j].rearrange("p (h c) -> p h c", h=10), reads=[("lwgx", j)], writes=["l_wgx"])
        pr.dma("sp", wga, self.lwga_b[j].rearrange("p (h c) -> p h c", h=10), reads=[("lwga", j)], writes=["l_wga"])
        pr.dma("sp", wout, self.lwout_b[j].rearrange("p (k c) -> p k c", k=10), reads=[("lwout", j)], writes=["l_wout"])
        xblks = [A.alloc([128, 8, T], F32) for _ in range(2)]
        sq = A.alloc([128, 8, T], BF16)
        hT = A.alloc([128, 8, T], BF16)
        rstd = A.alloc([128, T], F32)
        rstd2 = A.alloc([128, T], F32)
        hy = A.alloc([128, 10, T], BF16)
        mT = A.alloc([128, 8, T], F32)
        halo = A.alloc([128, 10, 3], F32)
        hstate = A.alloc([128, 10], F32)
        c8 = A.alloc([128, 10], F32)
        c16 = A.alloc([128, 10], F32)
        NB = 4
        ybr = [A.alloc([128, T], F32) for _ in range(NB)]
        XP = [A.alloc([128, T + 3], F32) for _ in range(NB)]
        tcv = [A.alloc([128, T], F32) for _ in range(NB)]
        xbb = [A.alloc([128, T], BF16) for _ in range(NB)]
        gx = [A.alloc([128, T], F32) for _ in range(2)]
        ga = [A.alloc([128, T], F32) for _ in range(2)]
        at = [A.alloc([128, T], F32) for _ in range(2)]
        mu = [A.alloc([128, T], F32) for _ in range(2)]
        bt = [A.alloc([128, T], F32) for _ in range(2)]
        hs = [A.alloc([128, T], F32) for _ in range(2)]
        lam = self.vcol(f"llam{j}", 0, 10)
        pr.op("act", lambda e: e.activation(out=c8, in_=lam, func=AF.Exp, scale=-1.0), reads=["vecs"], writes=["l_c8"], small=True)
        pr.op("act", lambda e: e.activation(out=c8, in_=c8, func=AF.Ln, bias=self.vcol("one")), reads=["l_c8", "vecs"], writes=["l_c8"], small=True)
        pr.op("dve", lambda e: e.tensor_scalar(out=c16, in0=c8, scalar1=-16.0, scalar2=None, op0=ALU.mult), reads=["l_c8"], writes=["l_c16"], small=True)
        pr.op("dve", lambda e: e.tensor_scalar(out=c8, in0=c8, scalar1=-8.0, scalar2=None, op0=ALU.mult), reads=["l_c8", "l_c16"], writes=["l_c8"], small=True)
        pr.op("pool", lambda e: e.memset(halo, 0.0), writes=["l_halo"])
        pr.op("pool", lambda e: e.memset(hstate, 0.0), writes=["l_hstate"])
        nbank = 0
        it = 0
        nwin = S // T

        def pre1(w_):
            xb = xblks[w_ % 2]
            kx_ = ("l_xblk", w_ % 2)
            pr.dma("sp", xb, src[:, :, w_ * T:(w_ + 1) * T], reads=[("xT", self.cur)], writes=[kx_])
            for c in range(8):
                pr.op("act", lambda e, c=c: e.activation(out=sq[:, c, :], in_=xb[:, c, :], func=AF.Square),
                      reads=[kx_], writes=[("l_sq", c)])

        def pre2(w_):
            xb = xblks[w_ % 2]
            kx_ = ("l_xblk", w_ % 2)
            ps = self.psb[6]

            def st(e):
                for c in range(8):
                    i = e.matmul(ps[:, 0:T], self.ones_b[:], sq[:, c, :], start=(c == 0), stop=(c == 7))
                return i
            pr.op("pe", st, reads=[("l_sq", c) for c in range(8)] + ["ones_b"], writes=[("ps", 6)])
            pr.op("act", lambda e: e.activation(out=rstd[:, 0:T], in_=ps[:, 0:T], func=AF.Ln, scale=1.0 / D, bias=self.vcol("eps")),
                  reads=[("ps", 6), "vecs"], writes=["l_rstd"])
            pr.op("act", lambda e: e.activation(out=rstd[:, 0:T], in_=rstd[:, 0:T], func=AF.Exp, scale=-0.5),
                  reads=["l_rstd"], writes=["l_rstd"])
            for c in range(8):
                pr.op("dve", lambda e, c=c: e.scalar_tensor_tensor(out=hT[:, c, :], in0=xb[:, c, :], scalar=self.vcol(f"nmpre{li}", c),
                                                                   in1=rstd[:, 0:T], op0=ALU.mult, op1=ALU.mult),
                      reads=[kx_, "l_rstd", "vecs"], writes=[("l_hT", c)])

        pre1(0)
        pre2(0)
        for wi in range(nwin):
            t0 = wi * T
            xblk = xblks[wi % 2]
            kx = ("l_xblk", wi % 2)
            khs = [("l_hT", c) for c in range(8)]
            def mmin(e, ps, oc):
                for k in range(8):
                    i = e.matmul(ps[:, 0:T], win[:, oc, k * 128:(k + 1) * 128], hT[:, k, :], start=(k == 0), stop=(k == 7))
                return i

            def lru_s1(c, p):
                nonlocal nbank
                yb, xb = nbank % 6, (nbank + 1) % 6
                nbank += 2
                yps, xps = self.psb[yb], self.psb[xb]
                ybt, XPt, tct, xbt = ybr[p], XP[p], tcv[p], xbb[p]
                K = lambda n: (n, p)
                pr.op("pe", lambda e: mmin(e, yps, c), reads=["l_win"] + khs, writes=[("ps", yb)])
                pr.op("pe", lambda e: mmin(e, xps, 10 + c), reads=["l_win"] + khs, writes=[("ps", xb)])
                pr.op("act", lambda e: e.activation(out=ybt, in_=yps[:, 0:T], func=AF.Gelu_apprx_tanh, bias=self.vcol(f"lbin{j}", c)),
                      reads=[("ps", yb), "vecs"], writes=[K("ybr")])
                pr.op("pool", lambda e: e.tensor_copy(out=XPt[:, 0:3], in_=halo[:, c, :]), reads=["l_halo"], writes=[K("XPh")])
                pr.op("act", lambda e: e.activation(out=XPt[:, 3:T + 3], in_=xps[:, 0:T], func=AF.Identity, bias=self.vcol(f"lbin{j}", 10 + c)),
                      reads=[("ps", xb), "vecs"], writes=[K("XP")])
                pr.op("pool", lambda e: e.tensor_copy(out=halo[:, c, :], in_=XPt[:, T:T + 3]), reads=[K("XP"), K("XPh")], writes=["l_halo"])
                pr.op("dve", lambda e: e.tensor_scalar(out=tct, in0=XPt[:, 3:T + 3], scalar1=self.vcol(f"lcw{j}_3", c),
                                                       scalar2=self.vcol(f"lcb{j}", c), op0=ALU.mult, op1=ALU.add),
                      reads=[K("XP"), "vecs"], writes=[K("tcv")])
                for tap in (2, 1, 0):
                    pr.op("dve", lambda e, tap=tap: e.scalar_tensor_tensor(
                        out=tct, in0=XPt[:, tap:tap + T], scalar=self.vcol(f"lcw{j}_{tap}", c), in1=tct, op0=ALU.mult, op1=ALU.add),
                        reads=[K("XP"), K("XPh"), K("tcv"), "vecs"], writes=[K("tcv")])
                pr.op("pool", lambda e: e.tensor_copy(out=xbt, in_=tct), reads=[K("tcv")], writes=[K("xbb")])

            def lru_s2(cs_, ps_):
                nonlocal nbank
                ctx = []
                for c, p in zip(cs_, ps_):
                    gxb, gab = nbank % 6, (nbank + 1) % 6
                    nbank += 2
                    q2 = c % 2
                    ctx.append(dict(c=c, p=p, q2=q2, gxb=gxb, gab=gab, gxps=self.psb[gxb], gaps=self.psb[gab],
                                    ybt=ybr[p], tct=tcv[p], xbt=xbb[p], gxt=gx[q2], gat=ga[q2], att=at[q2], mut=mu[q2],
                                    btt=bt[q2], hst=hs[q2]))
                for d_ in ctx:
                    c, p, q2 = d_["c"], d_["p"], d_["q2"]
                    pr.op("pe", lambda e, d_=d_: e.matmul(d_["gxps"][:, 0:T], wgx[:, d_["c"], :], d_["xbt"], start=True, stop=True),
                          reads=["l_wgx", ("xbb", p)], writes=[("ps", d_["gxb"])])
                    pr.op("pe", lambda e, d_=d_: e.matmul(d_["gaps"][:, 0:T], wga[:, d_["c"], :], d_["xbt"], start=True, stop=True),
                          reads=["l_wga", ("xbb", p)], writes=[("ps", d_["gab"])])
                for d_ in ctx:
                    c, p, q2 = d_["c"], d_["p"], d_["q2"]
                    pr.op("act", lambda e, d_=d_: e.activation(out=d_["gxt"], in_=d_["gxps"][:, 0:T], func=AF.Sigmoid, bias=self.vcol(f"lbgx{j}", d_["c"])),
                          reads=[("ps", d_["gxb"]), "vecs"], writes=[("gx", q2)])
                    pr.op("act", lambda e, d_=d_: e.activation(out=d_["gat"], in_=d_["gaps"][:, 0:T], func=AF.Sigmoid, bias=self.vcol(f"lbga{j}", d_["c"])),
                          reads=[("ps", d_["gab"]), "vecs"], writes=[("ga", q2)])
                for d_ in ctx:
                    c, p, q2 = d_["c"], d_["p"], d_["q2"]
                    pr.op("act", lambda e, d_=d_: e.activation(out=d_["att"], in_=d_["gat"], func=AF.Exp, scale=c8[:, d_["c"]:d_["c"] + 1]),
                          reads=[("ga", q2), "l_c8"], writes=[("at", q2)])
                    pr.op("dve", lambda e, d_=d_: e.tensor_tensor(out=d_["btt"], in0=d_["gxt"], in1=d_["tct"], op=ALU.mult),
                          reads=[("gx", q2), ("tcv", p)], writes=[("bt", q2)])
                    pr.op("dve", lambda e, d_=d_: e.tensor_tensor(out=d_["mut"], in0=d_["att"], in1=d_["att"], op=ALU.mult),
                          reads=[("at", q2)], writes=[("mu", q2)])
                    pr.op("dve", lambda e, d_=d_: e.tensor_scalar(out=d_["mut"], in0=d_["mut"], scalar1=1.0, scalar2=None, op0=ALU.min),
                          reads=[("mu", q2)], writes=[("mu", q2)])
                for d_ in ctx:
                    c, p, q2 = d_["c"], d_["p"], d_["q2"]
                    pr.op("act", lambda e, d_=d_: e.activation(out=d_["mut"], in_=d_["mut"], func=AF.Sqrt, scale=-1.0, bias=self.vcol("one")),
                          reads=[("mu", q2), "vecs"], writes=[("mu", q2)])
                for d_ in ctx:
                    c, p, q2 = d_["c"], d_["p"], d_["q2"]
                    pr.op("dve", lambda e, d_=d_: e.tensor_tensor(out=d_["btt"], in0=d_["btt"], in1=d_["mut"], op=ALU.mult),
                          reads=[("bt", q2), ("mu", q2)], writes=[("bt", q2)])
                    pr.op("dve", lambda e, d_=d_: e.tensor_tensor_scan(out=d_["hst"], data0=d_["att"], data1=d_["btt"],
                                                                       initial=hstate[:, d_["c"]:d_["c"] + 1], op0=ALU.mult, op1=ALU.add),
                          reads=[("at", q2), ("bt", q2), "l_hstate"], writes=[("hs", q2)])
                    pr.op("pool", lambda e, d_=d_: e.tensor_copy(out=hstate[:, d_["c"]:d_["c"] + 1], in_=d_["hst"][:, T - 1:T]),
                          reads=[("hs", q2)], writes=["l_hstate"])
                    pr.op("dve", lambda e, d_=d_: e.tensor_tensor(out=hy[:, d_["c"], :], in0=d_["hst"], in1=d_["ybt"], op=ALU.mult),
                          reads=[("hs", q2), ("ybr", p)], writes=[("l_hy", d_["c"])])

            ps_of = {}
            for pk in range(6):
                if pk < 5:
                    for c in (2 * pk, 2 * pk + 1):
                        ps_of[c] = it % NB
                        it += 1
                        lru_s1(c, ps_of[c])
                if pk == 5 and wi + 1 < nwin:
                    pre1(wi + 1)
                if pk >= 1:
                    cs_ = (2 * (pk - 1), 2 * (pk - 1) + 1)
                    lru_s2(cs_, [ps_of[c] for c in cs_])
            if wi + 1 < nwin:
                pre2(wi + 1)
            khy = [("l_hy", c) for c in range(10)]
            sbank = 7
            sps = self.psb[sbank]
            for o in range(8):
                mb = nbank % 6
                nbank += 1
                mps = self.psb[mb]

                def mmo(e, mps=mps, o=o):
                    for k in range(10):
                        i = e.matmul(mps[:, 0:T], wout[:, k, o * 128:(o + 1) * 128], hy[:, k, :], start=(k == 0), stop=(k == 9))
                    return i
                pr.op("pe", mmo, reads=["l_wout"] + khy, writes=[("ps", mb)])
                pr.op("act", lambda e, mps=mps, o=o: e.activation(out=mT[:, o, :], in_=mps[:, 0:T], func=AF.Identity, bias=self.vcol(f"lbout{j}", o)),
                      reads=[("ps", mb), "vecs"], writes=[("l_mT", o)])
                pr.op("act", lambda e, mps=mps, o=o: e.activation(out=sq[:, o, :], in_=mps[:, 0:T], func=AF.Square, bias=self.vcol(f"lbout{j}", o)),
                      reads=[("ps", mb), "vecs"], writes=[("l_sq", o)])
                pr.op("pe", lambda e, o=o: e.matmul(sps[:, 0:T], self.ones_b[:], sq[:, o, :], start=(o == 0), stop=(o == 7)),
                      reads=[("l_sq", o), "ones_b"], writes=[("ps", sbank)])
            self._resid(sps, sbank, rstd2, "l_rstd2", mT, "l_mT", f"nmpost{li}", xblk, [kx], 0, T, dst, t0)
            self.pump_casts(6)
        self.cur = 1 - self.cur

    def _resid(self, sps, sbank, rstd2, krs, mT, kmT, wname, xblk, kxr, xoff, n, dst, t0):
        pr = self.pr
        pr.op("act", lambda e: e.activation(out=rstd2[:, 0:n], in_=sps[:, 0:n], func=AF.Ln, scale=1.0 / D, bias=self.vcol("eps")),
              reads=[("ps", sbank), "vecs"], writes=[krs])
        pr.op("act", lambda e: e.activation(out=rstd2[:, 0:n], in_=rstd2[:, 0:n], func=AF.Exp, scale=-0.5),
              reads=[krs], writes=[krs])
        for o in range(8):
            pr.op("dve", lambda e, o=o: e.scalar_tensor_tensor(out=mT[:, o, 0:n], in0=mT[:, o, 0:n], scalar=self.vcol(wname, o),
                                                               in1=rstd2[:, 0:n], op0=ALU.mult, op1=ALU.mult),
                  reads=[(kmT, o), krs, "vecs"], writes=[(kmT, o)])
            pr.op("dve", lambda e, o=o: e.tensor_tensor(out=mT[:, o, 0:n], in0=mT[:, o, 0:n], in1=xblk[:, o, xoff:xoff + n], op=ALU.add),
                  reads=[(kmT, o)] + kxr, writes=[(kmT, o)])
        pr.dma("pool", dst[:, :, t0:t0 + n], mT[:, :, 0:n], reads=[(kmT, o) for o in range(8)], writes=[("xT", 1 - self.cur)])

    def phase_ssd(self, li):
        pr, S, A = self.pr, self.S, self.A
        pr.fence()
        A.reset()
        j = li // 2
        T = 512
        NQ = T // 128
        src = self.xT[self.cur].rearrange("(c p) t -> p c t", p=128)
        dst = self.xT[1 - self.cur].rearrange("(c p) t -> p c t", p=128)
        for b_ in range(10):
            self.pump_until(("swin", j, b_))
        self.pump_until(("swdt", j))
        for o in range(8):
            self.pump_until(("swout", j, o))
        wdt = A.alloc([128, 8, 32], BF16)
        pr.dma("sp", wdt, self.swdt_b[j].rearrange("p (k c) -> p k c", k=8), reads=[("swdt", j)], writes=["s_wdt"])
        normw = A.alloc([128, 2048], F32)
        pr.dma("sp", normw, self.snorm_d[j], writes=["s_normw"])
        sm = A.alloc([128, 96], F32)
        pr.dma("sp", sm, self.ssm_d[j], writes=["s_sm"])
        a_b = A.alloc([128, 32], F32)
        pr.op("act", lambda e: e.activation(out=a_b, in_=sm[:, 32:64], func=AF.Exp), reads=["s_sm"], writes=["s_ab"], small=True)
        pr.op("dve", lambda e: e.tensor_scalar(out=a_b, in0=a_b, scalar1=-1.0, scalar2=None, op0=ALU.mult), reads=["s_ab"], writes=["s_ab"], small=True)
        halo = A.alloc([128, 24, 3], F32)
        pr.op("pool", lambda e: e.memset(halo, 0.0), writes=["s_halo"])
        Sst = A.alloc([128, 2048], F32)
        Sbf = A.alloc([128, 2048], BF16)
        pr.op("pool", lambda e: e.memset(Sst, 0.0), writes=[("s_S", g) for g in range(4)])
        pr.op("pool", lambda e: e.memset(Sbf, 0.0), writes=[("s_Sbf", g) for g in range(4)])
        xblk = A.alloc([128, 8, T], F32)
        hT = A.alloc([128, 8, T], BF16)
        rstd = A.alloc([128, T], F32)
        rstd2 = rstd
        zs = A.alloc([128, NQ, 2048], BF16)
        mT = zs.bitcast(F32).rearrange("p q (o t) -> p (q o) t", o=2)
        xcT = A.alloc([128, 16, T], BF16)
        BT = A.alloc([128, 4, T], BF16)
        CT = A.alloc([128, 4, T], BF16)
        ynT = A.alloc([128, 16, T], BF16)
        sq = ynT[:, 0:8, :]
        sqp = [A.alloc([128, T], BF16) for _ in range(1)]
        NW = 2
        wsl = [A.alloc([128, 8, 512], BF16) for _ in range(NW)]
        wos = [A.alloc([128, 16, 128], BF16) for _ in range(2)]
        XP = [A.alloc([128, T + 3], F32) for _ in range(2)]
        tcv = [A.alloc([128, T], F32) for _ in range(2)]
        vdt = A.alloc([128, 32], F32)
        dts = A.alloc([128, 32], F32)
        das_ = [A.alloc([128, 32], F32) for _ in range(2)]
        css = A.alloc([128, 32], F32)
        ecs_ = [A.alloc([128, 32], F32) for _ in range(2)]
        dout = A.alloc([128, 32], F32)
        etot_ = [A.alloc([128, 32], F32) for _ in range(2)]
        xdt_ = [A.alloc([128, 2048], BF16) for _ in range(2)]
        xDb_ = [A.alloc([128, 2048], BF16) for _ in range(2)]
        xdd_ = [A.alloc([128, 2048], BF16) for _ in range(2)]
        Btm_ = [A.alloc([128, 512], BF16) for _ in range(2)]
        rhsM = [A.alloc([128, 8, 128], F32) for _ in range(2)]
        Eg = [A.alloc([128, 8, 128], BF16) for _ in range(2)]
        CBm = [A.alloc([128, 128], F32) for _ in range(2)]
        scT = [A.alloc([128, 8, 128], BF16) for _ in range(2)]
        yoffs = [A.alloc([128, 512], F32) for _ in range(1)]
        ysb = A.alloc([128, 2048], F32)
        ssq = A.alloc([128, 4], F32)
        gsc2 = yoffs
        rs4 = A.alloc([128, 4], F32)
        gn = A.alloc([128, 2048], BF16)
        self._nb = getattr(self, "_nb", 0)

        def bank():
            b = self._nb % 7
            self._nb += 1
            return b
        nws = 0
        nwo = 0
        nxp = 0
        ngr = 0
        for wi in range(S // T):
            t0 = wi * T
            kx = ("s_xblk",)
            pr.dma("sp", xblk, src[:, :, t0:t0 + T], reads=[("xT", self.cur)], writes=[kx])
            self._prenorm_multi(xblk, [kx], T, f"nmpre{li}", hT, "s_hT", sq, "s_ynT", rstd, "s_rstd", 7)
            khs = [("s_hT", c) for c in range(8)]
            sdef = []
            for blk in range(10):
                ws = wsl[nws % NW]
                kw = ("s_w", nws % NW)
                nws += 1
                pr.dma("sp", ws, self.swin_b[j, blk].rearrange("p (k c) -> p k c", k=8), reads=[("swin", j, blk)], writes=[kw])
                if blk < 4:
                    for q in range(NQ):
                        zb = bank()
                        zps = self.psb[zb]

                        def mmz(e, zps=zps, ws=ws, q=q):
                            for k in range(8):
                                i = e.matmul(zps[:, :], hT[:, k, q * 128:(q + 1) * 128], ws[:, k, :], start=(k == 0), stop=(k == 7))
                            return i
                        pr.op("pe", mmz, reads=[kw] + khs, writes=[("ps", zb)])
                        pr.op("act", lambda e, zps=zps, q=q, blk=blk: e.activation(out=zs[:, q, blk * 512:(blk + 1) * 512], in_=zps[:, :], func=AF.Silu),
                              reads=[("ps", zb)], writes=[("s_zs", q, blk)])
                else:
                    for i4 in range(4):
                        oc = (blk - 4) * 4 + i4
                        xb = bank()
                        xps = self.psb[xb]

                        def mmx(e, xps=xps, ws=ws, i4=i4):
                            for k in range(8):
                                i = e.matmul(xps[:, :], ws[:, k, i4 * 128:(i4 + 1) * 128], hT[:, k, :], start=(k == 0), stop=(k == 7))
                            return i
                        pr.op("pe", mmx, reads=[kw] + khs, writes=[("ps", xb)])
                        p = nxp % 2
                        nxp += 1
                        XPt, tct = XP[p], tcv[p]
                        pr.op("pool", lambda e, XPt=XPt, oc=oc: e.tensor_copy(out=XPt[:, 0:3], in_=halo[:, oc, :]), reads=["s_halo"], writes=[("s_XPh", p)])
                        pr.op("act", lambda e, XPt=XPt, xps=xps: e.activation(out=XPt[:, 3:T + 3], in_=xps[:, :], func=AF.Identity),
                              reads=[("ps", xb)], writes=[("s_XP", p)])
                        pr.op("pool", lambda e, XPt=XPt, oc=oc: e.tensor_copy(out=halo[:, oc, :], in_=XPt[:, T:T + 3]),
                              reads=[("s_XP", p), ("s_XPh", p)], writes=["s_halo"])
                        pr.op("dve", lambda e, XPt=XPt, tct=tct, oc=oc: e.tensor_scalar(out=tct, in0=XPt[:, 3:T + 3], scalar1=self.vcol(f"scw{j}_3", oc),
                                                                                 scalar2=self.vcol(f"scb{j}", oc), op0=ALU.mult, op1=ALU.add),
                              reads=[("s_XP", p), "vecs"], writes=[("s_tcv", p)])
                        for tap in (2, 1, 0):
                            pr.op("dve", lambda e, XPt=XPt, tct=tct, oc=oc, tap=tap: e.scalar_tensor_tensor(
                                out=tct, in0=XPt[:, tap:tap + T], scalar=self.vcol(f"scw{j}_{tap}", oc), in1=tct, op0=ALU.mult, op1=ALU.add),
                                reads=[("s_XP", p), ("s_XPh", p), ("s_tcv", p), "vecs"], writes=[("s_tcv", p)])
                        if oc < 16:
                            o_ap, ko = xcT[:, oc, :], ("s_xcT", oc)
                        elif oc < 20:
                            o_ap, ko = BT[:, oc - 16, :], ("s_BT", oc - 16)
                        else:
                            o_ap, ko = CT[:, oc - 20, :], ("s_CT", oc - 20)
                        for f_ in sdef:
                            f_()
                        sdef = [lambda o_ap=o_ap, tct=tct, p=p, ko=ko: pr.op(
                            "act", lambda e: e.activation(out=o_ap, in_=tct, func=AF.Silu), reads=[("s_tcv", p)], writes=[ko])]
            for f_ in sdef:
                f_()
            sdef = []
            if wi == 0:
                self.dump("hT", hT, khs)
                self.dump("zs0", zs[:, 0, :], [("s_zs", 0, b_) for b_ in range(4)])
                self.dump("xcT", xcT, [("s_xcT", c) for c in range(16)])
                self.dump("BT", BT, [("s_BT", c) for c in range(4)])
                self.dump("CT", CT, [("s_CT", c) for c in range(4)])
            def prep(q, par):
                qs = slice(q * 128, (q + 1) * 128)
                db = bank()
                dps = self.psb[db]

                def mmdt(e, dps=dps, q=q):
                    for k in range(8):
                        i = e.matmul(dps[:, 0:32], hT[:, k, q * 128:(q + 1) * 128], wdt[:, k, :], start=(k == 0), stop=(k == 7))
                    return i
                pr.op("pe", mmdt, reads=["s_wdt"] + khs, writes=[("ps", db)])
                pr.op("dve", lambda e, dps=dps: e.tensor_tensor(out=vdt, in0=dps[:, 0:32], in1=sm[:, 0:32], op=ALU.add),
                      reads=[("ps", db), "s_sm"], writes=["s_vdt"], small=True)
                pr.op("act", lambda e: e.activation(out=vdt, in_=vdt, func=AF.Exp), reads=["s_vdt"], writes=["s_vdt"], small=True)
                pr.op("act", lambda e: e.activation(out=dts, in_=vdt, func=AF.Ln, bias=self.vcol("one")), reads=["s_vdt", "vecs"], writes=["s_dts"], small=True)
                pr.op("dve", lambda e: e.tensor_tensor(out=das_[par], in0=dts, in1=a_b, op=ALU.mult), reads=["s_dts", "s_ab"], writes=[("s_das", par)], small=True)
                cb_ = bank()
                cps = self.psb[cb_]

                def mmcs(e, cps=cps):
                    e.matmul(cps[:, 0:32], self.triU[:], das_[par], start=True, stop=True)
                    return e.matmul(cps[:, 32:64], self.ones_f[:], das_[par], start=True, stop=True)
                pr.op("pe", mmcs, reads=[("s_das", par), "triU", "ones_f"], writes=[("ps", cb_)])
                pr.op("dve", lambda e, cps=cps: e.tensor_copy(out=css, in_=cps[:, 0:32]), reads=[("ps", cb_)], writes=["s_css"], small=True)
                pr.op("act", lambda e: e.activation(out=ecs_[par], in_=css, func=AF.Exp), reads=["s_css"], writes=[("s_ecs", par)], small=True)
                pr.op("dve", lambda e, cps=cps: e.tensor_tensor(out=dout, in0=cps[:, 32:64], in1=css, op=ALU.subtract),
                      reads=[("ps", cb_), "s_css"], writes=["s_dout"], small=True)
                pr.op("act", lambda e: e.activation(out=dout, in_=dout, func=AF.Exp), reads=["s_dout"], writes=["s_dout"], small=True)
                pr.op("act", lambda e, cps=cps: e.activation(out=etot_[par], in_=cps[:, 32:64], func=AF.Exp), reads=[("ps", cb_)], writes=[("s_etot", par)], small=True)
                for hb in range(2):
                    tb = bank()
                    tps = self.psb[tb].bitcast(BF16)

                    def trx(e, tps=tps, hb=hb, q=q):
                        for c8_ in range(8):
                            cx = hb * 8 + c8_
                            i = e.transpose(tps[:, c8_ * 128:(c8_ + 1) * 128], xcT[:, cx, q * 128:(q + 1) * 128], self.ident_b[:])
                        return i
                    pr.op("pe", trx, reads=[("s_xcT", hb * 8 + c) for c in range(8)] + ["ident_b"], writes=[("ps", tb)])
                    h0 = hb * 16
                    pr.op("dve", lambda e, tps=tps, hb=hb, h0=h0: e.tensor_tensor(
                        out=xdt_[par][:, hb * 1024:(hb + 1) * 1024].rearrange("p (h d) -> p h d", h=16),
                        in0=tps.rearrange("p (h d) -> p h d", h=16),
                        in1=dts[:, h0:h0 + 16].unsqueeze(2).to_broadcast([128, 16, 64]), op=ALU.mult),
                        reads=[("ps", tb), "s_dts"], writes=[("s_xdt", par, hb)])
                    pr.op("dve", lambda e, tps=tps, hb=hb, h0=h0: e.tensor_tensor(
                        out=xDb_[par][:, hb * 1024:(hb + 1) * 1024].rearrange("p (h d) -> p h d", h=16),
                        in0=tps.rearrange("p (h d) -> p h d", h=16),
                        in1=sm[:, 64 + h0:64 + h0 + 16].unsqueeze(2).to_broadcast([128, 16, 64]), op=ALU.mult),
                        reads=[("ps", tb), "s_sm"], writes=[("s_xDb", par, hb)])
                    pr.op("pool", lambda e, hb=hb, h0=h0: e.tensor_tensor(
                        out=xdd_[par][:, hb * 1024:(hb + 1) * 1024].rearrange("p (h d) -> p h d", h=16),
                        in0=xdt_[par][:, hb * 1024:(hb + 1) * 1024].rearrange("p (h d) -> p h d", h=16),
                        in1=dout[:, h0:h0 + 16].unsqueeze(2).to_broadcast([128, 16, 64]), op=ALU.mult),
                        reads=[("s_xdt", par, hb), "s_dout"], writes=[("s_xdd", par, hb)])
                bb = bank()
                bps = self.psb[bb].bitcast(BF16)

                def trb(e, bps=bps, q=q):
                    for g in range(4):
                        i = e.transpose(bps[:, g * 128:(g + 1) * 128], BT[:, g, q * 128:(q + 1) * 128], self.ident_b[:])
                    return i
                pr.op("pe", trb, reads=[("s_BT", g) for g in range(4)] + ["ident_b"], writes=[("ps", bb)])
                pr.op("act", lambda e, bps=bps: e.activation(out=Btm_[par], in_=bps[:, 0:512], func=AF.Identity), reads=[("ps", bb)], writes=[("s_Btm", par)])
            prep(0, 0)
            for q in range(NQ):
                qs = slice(q * 128, (q + 1) * 128)
                par = q % 2
                if q + 1 < NQ:
                    prep(q + 1, (q + 1) % 2)
                hbk_of = lambda g: g // 2

                def stageA(g, q=q, qs=qs, par=par):
                    pg = g % 2
                    gh = slice(g * 8, (g + 1) * 8)
                    rM, Et, CBt, sct = rhsM[pg], Eg[pg], CBm[pg], scT[pg]
                    pr.op("dve", lambda e: e.tensor_tensor(
                        out=rM, in0=self.triU[:].unsqueeze(1).to_broadcast([128, 8, 128]),
                        in1=das_[par][:, gh].unsqueeze(2).to_broadcast([128, 8, 128]), op=ALU.mult),
                        reads=[("s_das", par), "triU"], writes=[("s_rhsM", pg)])
                    d0, d1 = bank(), bank()

                    def mmD(e):
                        e.matmul(self.psb[d0][:, :], self.triSL[:], rM[:, 0:4, :], start=True, stop=True)
                        return e.matmul(self.psb[d1][:, :], self.triSL[:], rM[:, 4:8, :], start=True, stop=True)
                    pr.op("pe", mmD, reads=[("s_rhsM", pg), "triSL"], writes=[("ps", d0), ("ps", d1)])
                    pr.op("act", lambda e: e.activation(out=Et[:, 0:4, :], in_=self.psb[d0][:, :].rearrange("p (h l) -> p h l", h=4), func=AF.Exp),
                          reads=[("ps", d0)], writes=[("s_E", pg, 0)])
                    pr.op("act", lambda e: e.activation(out=Et[:, 4:8, :], in_=self.psb[d1][:, :].rearrange("p (h l) -> p h l", h=4), func=AF.Exp),
                          reads=[("ps", d1)], writes=[("s_E", pg, 1)])
                    cbb = bank()
                    cbps = self.psb[cbb]
                    pr.op("pe", lambda e: e.matmul(cbps[:, 0:128], BT[:, g, qs], CT[:, g, qs], start=True, stop=True),
                          reads=[("s_BT", g), ("s_CT", g)], writes=[("ps", cbb)])
                    pr.op("dve", lambda e: e.tensor_tensor(out=CBt, in0=cbps[:, 0:128], in1=self.triU[:], op=ALU.mult),
                          reads=[("ps", cbb), "triU"], writes=[("s_CBm", pg)], small=True)
                    pr.op("pool", lambda e: e.tensor_tensor(out=sct, in0=Et, in1=CBt.unsqueeze(1).to_broadcast([128, 8, 128]), op=ALU.mult),
                          reads=[("s_E", pg, 0), ("s_E", pg, 1), ("s_CBm", pg)], writes=[("s_scT", pg)])

                def stageB(g, q=q, qs=qs, par=par):
                    pg = g % 2
                    gh = slice(g * 8, (g + 1) * 8)
                    gc = slice(g * 512, (g + 1) * 512)
                    sct, yot = scT[pg], yoffs[0]
                    hbk = g // 2
                    ya, yo = bank(), bank()
                    yaps, yops = self.psb[ya], self.psb[yo]

                    def mmy(e):
                        e.matmul(yaps[:, :], self.ident_b[:], xDb_[par][:, gc], start=True, stop=False)
                        for hh in range(8):
                            c0 = g * 512 + hh * 64
                            i = e.matmul(yaps[:, hh * 64:(hh + 1) * 64], sct[:, hh, :], xdt_[par][:, c0:c0 + 64], start=False, stop=(hh == 7))
                        return i
                    pr.op("pe", mmy, reads=[("s_scT", pg), ("s_xdt", par, hbk), ("s_xDb", par, hbk), "ident_b"], writes=[("ps", ya)])
                    pr.op("pe", lambda e: e.matmul(yops[:, :], CT[:, g, qs], Sbf[:, gc], start=True, stop=True),
                          reads=[("s_CT", g), ("s_Sbf", g)], writes=[("ps", yo)])
                    pr.op("dve", lambda e: e.tensor_tensor(
                        out=yot.rearrange("p (h d) -> p h d", h=8), in0=yops[:, :].rearrange("p (h d) -> p h d", h=8),
                        in1=ecs_[par][:, gh].unsqueeze(2).to_broadcast([128, 8, 64]), op=ALU.mult),
                        reads=[("ps", yo), ("s_ecs", par)], writes=[("s_yoff", 0)])
                    pr.op("dve", lambda e: e.tensor_tensor(out=ysb[:, gc], in0=yaps[:, :], in1=yot, op=ALU.add),
                          reads=[("ps", ya), ("s_yoff", 0)], writes=[("s_ysb", g)])
                    sb_ = bank()
                    sps_ = self.psb[sb_]
                    pr.op("pe", lambda e: e.matmul(sps_[:, :], Btm_[par][:, g * 128:(g + 1) * 128], xdd_[par][:, gc], start=True, stop=True),
                          reads=[("s_Btm", par), ("s_xdd", par, hbk)], writes=[("ps", sb_)])
                    pr.op("dve", lambda e: e.tensor_tensor(
                        out=Sst[:, gc].rearrange("p (h d) -> p h d", h=8), in0=Sst[:, gc].rearrange("p (h d) -> p h d", h=8),
                        in1=etot_[par][:, gh].unsqueeze(2).to_broadcast([128, 8, 64]), op=ALU.mult),
                        reads=[("s_S", g), ("s_etot", par)], writes=[("s_S", g)])
                    pr.op("dve", lambda e: e.tensor_tensor(out=Sst[:, gc], in0=sps_[:, :], in1=Sst[:, gc], op=ALU.add),
                          reads=[("ps", sb_), ("s_S", g)], writes=[("s_S", g)])

                def stageC(g, q=q, par=par):
                    gc = slice(g * 512, (g + 1) * 512)
                    gst = gsc2[0]
                    pr.op("act", lambda e: e.activation(out=Sbf[:, gc], in_=Sst[:, gc], func=AF.Identity), reads=[("s_S", g)], writes=[("s_Sbf", g)])
                    pr.op("pool", lambda e: e.tensor_tensor(out=ysb[:, gc], in0=ysb[:, gc], in1=zs[:, q, gc], op=ALU.mult),
                          reads=[("s_ysb", g), ("s_zs", q, g)], writes=[("s_ysb", g)])
                    pr.op("act", lambda e: e.activation(out=gst, in_=ysb[:, gc], func=AF.Square),
                          reads=[("s_ysb", g)], writes=[("s_yoff", 0)])
                    pr.op("dve", lambda e: e.reduce_sum(out=ssq[:, g:g + 1], in_=gst, axis=mybir.AxisListType.X),
                          reads=[("s_yoff", 0)], writes=[("s_ssq", g)], small=True)

                stageA(0)
                stageA(1)
                stageB(0)
                stageA(2)
                stageB(1)
                stageA(3)
                stageB(2)
                stageC(0)
                stageB(3)
                stageC(1)
                stageC(2)
                stageC(3)
                kss = [("s_ssq", g) for g in range(4)]
                pr.op("dve", lambda e: e.tensor_scalar(out=rs4, in0=ssq, scalar1=1.0 / 512, scalar2=EPS, op0=ALU.mult, op1=ALU.add),
                      reads=kss, writes=["s_rs4"], force_same=True, small=True)
                pr.op("act", lambda e: e.activation(out=rs4, in_=rs4, func=AF.Ln), reads=["s_rs4"], writes=["s_rs4"], small=True)
                pr.op("act", lambda e: e.activation(out=rs4, in_=rs4, func=AF.Exp, scale=-0.5), reads=["s_rs4"], writes=["s_rs4"], force_same=True, small=True)
                for g in range(4):
                    gc = slice(g * 512, (g + 1) * 512)
                    pr.op("dve", lambda e, gc=gc, g=g: e.scalar_tensor_tensor(out=gn[:, gc], in0=ysb[:, gc], scalar=rs4[:, g:g + 1], in1=normw[:, gc],
                                                                              op0=ALU.mult, op1=ALU.mult),
                          reads=[("s_ysb", g), "s_rs4", "s_normw"], writes=[("s_gn", g)])
                for hb in range(2):
                    tb = bank()
                    tps = self.psb[tb].bitcast(BF16)

                    def trg(e, tps=tps, hb=hb):
                        for c8_ in range(8):
                            cx = hb * 8 + c8_
                            i = e.transpose(tps[:, c8_ * 128:(c8_ + 1) * 128], gn[:, cx * 128:(cx + 1) * 128], self.ident_b[:])
                        return i
                    pr.op("pe", trg, reads=[("s_gn", hb * 2), ("s_gn", hb * 2 + 1), "ident_b"], writes=[("ps", tb)])
                    pr.op("act", lambda e, tps=tps, hb=hb, qs=qs: e.activation(out=ynT[:, hb * 8:(hb + 1) * 8, qs], in_=tps.rearrange("p (c t) -> p c t", c=8), func=AF.Identity),
                          reads=[("ps", tb)], writes=[("s_ynT", hb * 8 + c) for c in range(8)])
            kyn = [("s_ynT", c) for c in range(16)]
            if wi == 0:
                self.dump("ynT", ynT, kyn)
            sbank = 7
            sps = self.psb[sbank]
            kzs_all = [("s_zs", q, b_) for q in range(NQ) for b_ in range(4)]
            for o in range(8):
                wo = wos[nwo % 2]
                kwo = ("s_wo", nwo % 2)
                nwo += 1
                pr.dma("sp", wo, self.swout_b[j, o].rearrange("p (k c) -> p k c", k=16), reads=[("swout", j, o)], writes=[kwo])
                mb = bank()
                mps = self.psb[mb]

                def mmo(e, mps=mps, wo=wo):
                    for k in range(16):
                        i = e.matmul(mps[:, :], wo[:, k, :], ynT[:, k, :], start=(k == 0), stop=(k == 15))
                    return i
                pr.op("pe", mmo, reads=[kwo] + kyn, writes=[("ps", mb)])
                sqt = sqp[0]
                pr.op("act", lambda e, mps=mps, o=o: e.activation(out=mT[:, o, :], in_=mps[:, :], func=AF.Identity),
                      reads=[("ps", mb)] + (kzs_all if o == 0 else []), writes=[("s_mT", o)] + (kzs_all if o == 0 else []))
                pr.op("act", lambda e, mps=mps, sqt=sqt: e.activation(out=sqt, in_=mps[:, :], func=AF.Square),
                      reads=[("ps", mb)], writes=[("s_sqp", 0)])
                pr.op("pe", lambda e, o=o, sqt=sqt: e.matmul(sps[:, :], self.ones_b[:], sqt, start=(o == 0), stop=(o == 7)),
                      reads=[("s_sqp", 0), "ones_b"], writes=[("ps", sbank)])
            self._resid(sps, sbank, rstd2, "s_rstd", mT, "s_mT", f"nmpost{li}", xblk, [kx], 0, T, dst, t0)
            for q in range(NQ):
                for b_ in range(4):
                    self.pr._record(self.pr.last_w[("xT", 1 - self.cur)], [], [("s_zs", q, b_)])
            self.pump_casts(6)
        self.cur = 1 - self.cur

    def _prenorm_multi(self, xblk, kxr, C, wname, hT, kh, sq, ksq, rstd, krstd, stat_bank):
        pr = self.pr
        ps = self.psb[stat_bank]
        for c in range(8):
            pr.op("act", lambda e, c=c: e.activation(out=sq[:, c, 0:C], in_=xblk[:, c, 0:C], func=AF.Square),
                  reads=kxr, writes=[(ksq, c)])

        def st(e):
            for c in range(8):
                i = e.matmul(ps[:, 0:C], self.ones_b[:], sq[:, c, 0:C], start=(c == 0), stop=(c == 7))
            return i
        pr.op("pe", st, reads=[(ksq, c) for c in range(8)] + ["ones_b"], writes=[("ps", stat_bank)])
        pr.op("act", lambda e: e.activation(out=rstd[:, 0:C], in_=ps[:, 0:C], func=AF.Ln, scale=1.0 / D, bias=self.vcol("eps")),
              reads=[("ps", stat_bank), "vecs"], writes=[krstd])
        pr.op("act", lambda e: e.activation(out=rstd[:, 0:C], in_=rstd[:, 0:C], func=AF.Exp, scale=-0.5),
              reads=[krstd], writes=[krstd])
        for c in range(8):
            pr.op("dve", lambda e, c=c: e.scalar_tensor_tensor(out=hT[:, c, 0:C], in0=xblk[:, c, 0:C], scalar=self.vcol(wname, c),
                                                               in1=rstd[:, 0:C], op0=ALU.mult, op1=ALU.mult),
                  reads=kxr + [krstd, "vecs"], writes=[(kh, c)])

    def build(self):
        for li in self.layers:
            if self.do_mixer and li % 2 == 0:
                j = li // 2
                for b_ in range(10):
                    self.add_cast(self.swin_b[j, b_], self.swin_f[j, b_], ("swin", j, b_))
                self.add_cast(self.swdt_b[j], self.swdt_f[j], ("swdt", j))
                for o in range(8):
                    self.add_cast(self.swout_b[j, o], self.swout_f[j, o], ("swout", j, o))
            if self.do_mixer and li % 2 == 1:
                j = li // 2
                self.add_cast(self.lwin_b[j], self.lwin_f[j], ("lwin", j))
                self.add_cast(self.lwgx_b[j], self.lwgx_f[j], ("lwgx", j))
                self.add_cast(self.lwga_b[j], self.lwga_f[j], ("lwga", j))
                self.add_cast(self.lwout_b[j], self.lwout_f[j], ("lwout", j))
            if self.do_ffn:
                for j2 in range(16):
                    self.add_cast(self.wup_b[li, j2], self.wup_f[li, j2], ("wup_b", li, j2))
                for o2 in range(4):
                    self.add_cast(self.wdn_b[li, o2], self.wdn_f[li, o2], ("wdn_b", li, o2))
        self.phase_in()
        for li in self.layers:
            if self.do_mixer:
                if li % 2 == 1:
                    self.phase_lru(li)
                else:
                    self.phase_ssd(li)
            if self.do_ffn:
                self.phase_ffn(li)
        self.phase_out()
        self.pr.finish_wait_all("sp")
        self.pr.emit()
        self.P.close()
        return self.nc


def pack_inputs(inp):
    vp = VecPack()
    vp.add_raw("eps", np.full((128, 1), EPS, np.float32))
    for li in range(DEPTH):
        vp.add(f"nmpre{li}", inp["norm_mix_pre"][li])
        vp.add(f"nmpost{li}", inp["norm_mix_post"][li])
        vp.add(f"nfpre{li}", inp["norm_ffn_pre"][li])
        vp.add(f"nfpost{li}", inp["norm_ffn_post"][li])
        for j in range(3):
            vp.add(f"fcw{li}_{j}", inp["ffn_conv_w"][li][j])
        vp.add(f"fcb{li}", inp["ffn_conv_b"][li])
    vp.add_raw("one", np.ones((128, 1), np.float32))
    for j in range(2):
        vp.add(f"lbin{j}", inp["lru_b_in"][j])
        for t in range(4):
            vp.add(f"lcw{j}_{t}", inp["lru_conv_w"][j][t])
        vp.add(f"lcb{j}", inp["lru_conv_b"][j])
        vp.add(f"lbgx{j}", inp["lru_b_gx"][j])
        vp.add(f"lbga{j}", inp["lru_b_ga"][j])
        vp.add(f"llam{j}", inp["lru_lambda"][j])
        vp.add(f"lbout{j}", inp["lru_b_out"][j])
    for j in range(2):
        for t in range(4):
            vp.add(f"scw{j}_{t}", inp["ssd_conv_w"][j][t])
        vp.add(f"scb{j}", inp["ssd_conv_b"][j])
    vecs = vp.build()
    swin = np.asarray(inp["ssd_w_in"], dtype=np.float32)
    sw = swin[:, :, :5120].reshape(2, 8, 128, 10, 512)
    swin_h = np.ascontiguousarray(sw.transpose(0, 3, 2, 1, 4)).reshape(2, 10, 128, 4096)
    sdt = swin[:, :, 5120:].reshape(2, 8, 128, 32)
    swdt_h = np.ascontiguousarray(sdt.transpose(0, 2, 1, 3)).reshape(2, 128, 256)
    swo = np.asarray(inp["ssd_w_out"], dtype=np.float32).reshape(2, 16, 128, 8, 128)
    swout_h = np.ascontiguousarray(swo.transpose(0, 3, 2, 1, 4)).reshape(2, 8, 128, 2048)
    snorm_h = np.ascontiguousarray(np.broadcast_to(np.asarray(inp["ssd_norm"], dtype=np.float32)[:, None, :], (2, 128, 2048)))
    ssm = np.concatenate([np.asarray(inp["ssd_dt_bias"], dtype=np.float32), np.asarray(inp["ssd_a_log"], dtype=np.float32),
                          np.asarray(inp["ssd_d"], dtype=np.float32)], axis=1)
    ssm_h = np.ascontiguousarray(np.broadcast_to(ssm[:, None, :], (2, 128, 96)))
    lwin = np.asarray(inp["lru_w_in"], dtype=np.float32)
    lw = lwin.reshape(2, 8, 128, 20, 128)
    lwin_h = np.ascontiguousarray(lw.transpose(0, 3, 2, 1, 4)).reshape(2, 2560, 1024)
    lwgx_h = np.ascontiguousarray(np.asarray(inp["lru_w_gx"], dtype=np.float32).transpose(0, 2, 1, 3)).reshape(2, 128, 1280)
    lwga_h = np.ascontiguousarray(np.asarray(inp["lru_w_ga"], dtype=np.float32).transpose(0, 2, 1, 3)).reshape(2, 128, 1280)
    lwo = np.asarray(inp["lru_w_out"], dtype=np.float32).reshape(2, 10, 128, 1024)
    lwout_h = np.ascontiguousarray(lwo.transpose(0, 2, 1, 3)).reshape(2, 128, 10240)
    wup = np.asarray(inp["ffn_w_up"], dtype=np.float32)
    g = wup[:, :, :DFF].reshape(DEPTH, 8, 128, 16, 256)
    v = wup[:, :, DFF:].reshape(DEPTH, 8, 128, 16, 256)
    gv = np.concatenate([g, v], axis=-1)
    wup_h = np.ascontiguousarray(gv.transpose(0, 3, 2, 1, 4)).reshape(DEPTH, 16, 128, 4096)
    wdn = np.asarray(inp["ffn_w_down"], dtype=np.float32)
    wd = wdn.reshape(DEPTH, 32, 128, 4, 256)
    wdn_h = np.ascontiguousarray(wd.transpose(0, 3, 2, 1, 4)).reshape(DEPTH, 4, 128, 8192)
    shared = {"vecs": vecs, "wup": wup_h, "wdn": wdn_h,
              "swin": swin_h, "swdt": swdt_h, "swout": swout_h, "snorm": snorm_h, "ssm": ssm_h,
              "lwin": lwin_h, "lwgx": lwgx_h, "lwga": lwga_h, "lwout": lwout_h}
    return shared, vp.index, vecs.shape[1]


_CACHE = {}


def run(inp, S=4096, layers=(0, 1, 2, 3), do_mixer=True, do_ffn=True, trace=False, debug=False):
    shared, vindex, nvec = pack_inputs(inp)
    x = np.asarray(inp["x"], dtype=np.float32)
    B = x.shape[0]
    b = Builder(S, vindex, nvec, layers=layers, do_mixer=do_mixer, do_ffn=do_ffn, debug=debug)
    nc = b.build()
    in_maps = []
    for i in range(B):
        m = dict(shared)
        m["x"] = np.ascontiguousarray(x[i, :S])
        in_maps.append(m)
    res = run_bass_kernel_spmd(nc, in_maps, core_ids=list(range(B)), trace=trace)
    y = np.stack([np.asarray(r["y"]) for r in res.results], axis=0)
    if debug:
        res.dbg = {k: np.asarray(res.results[0]["dbg_" + k]).astype(np.float32) for k in b.dbg}
    return y.astype(np.float32), res


def kernel(**inputs):
    y, _ = run(inputs)
    return y
```

```python
import numpy as np
import concourse.bass as bass
import concourse.mybir as mybir
from concourse.bass_utils import run_bass_kernel_spmd

F32 = mybir.dt.float32
BF16 = mybir.dt.bfloat16
AF = mybir.ActivationFunctionType
ALU = mybir.AluOpType

D = 1024
DEPTH = 4
DFF = 4096
EPS = 1e-6

ENGS = ("pe", "act", "dve", "pool", "sp")
EPOCH = 30000
NDSEM = 8


class Prog:
    def __init__(self, nc, same_engine_sync=False):
        self.nc = nc
        self.same_engine_sync = same_engine_sync
        self.ops = {e: [] for e in ENGS}
        self.cnt = {e: 0 for e in ENGS}
        self.epoch = {e: 0 for e in ENGS}
        self.sems = {e: [] for e in ENGS}
        self.dsems = {e: [] for e in ENGS}
        self.ndma = {e: 0 for e in ENGS}
        self.known = {e: {} for e in ENGS}
        self.last_w = {}
        self.readers = {}
        self._ctx = []
        self.fence_vals = {}
        self.small_toks = set()

    def _new_sem(self, name):
        cm = self.nc.semaphore(name)
        s = cm.__enter__()
        self._ctx.append(cm)
        return s

    def _eng_sem(self, e):
        ep = self.epoch[e]
        while len(self.sems[e]) <= ep:
            self.sems[e].append(self._new_sem(f"s_{e}_{len(self.sems[e])}"))
        return self.sems[e][ep]

    def _deps(self, e, reads, writes, force_same=False):
        toks = []
        for k in reads:
            t = self.last_w.get(k)
            if t is not None:
                toks.append(t)
        for k in writes:
            t = self.last_w.get(k)
            if t is not None:
                toks.append(t)
            toks.extend(self.readers.get(k, ()))
        waits = {}
        for (sem, val, te, isdma) in toks:
            if te == e and not isdma:
                if e == "pe" or not (self.same_engine_sync or force_same or (id(sem), val) in self.small_toks):
                    continue
            sid = id(sem)
            if self.known[e].get(sid, 0) >= val:
                continue
            if sid not in waits or waits[sid][1] < val:
                waits[sid] = (sem, val)
        for sid, (sem, val) in waits.items():
            self.known[e][sid] = val
        return list(waits.values())

    def _record(self, tok, reads, writes):
        for k in writes:
            self.last_w[k] = tok
            self.readers[k] = []
        for k in reads:
            if k in writes:
                continue
            lst = self.readers.setdefault(k, [])
            lst.append(tok)
            if len(lst) > 8:
                best = {}
                for t in lst:
                    sid = id(t[0])
                    if sid not in best or best[sid][1] < t[1]:
                        best[sid] = t
                self.readers[k] = list(best.values())

    def op(self, e, fn, reads=(), writes=(), force_same=False, small=False):
        waits = self._deps(e, reads, writes, force_same)
        if self.cnt[e] >= EPOCH:
            self.epoch[e] += 1
            self.cnt[e] = 0
        sem = self._eng_sem(e)
        self.cnt[e] += 1
        tok = (sem, self.cnt[e], e, False)
        if small:
            self.small_toks.add((id(sem), self.cnt[e]))
        self.ops[e].append((waits, fn, sem, 1))
        self._record(tok, reads, writes)
        return tok

    def dma(self, q, out, in_, reads=(), writes=(), nofence=False, **kw):
        if not self.dsems[q]:
            self.dsems[q] = [self._new_sem(f"d_{q}_{i}") for i in range(NDSEM)]
        i = self.ndma[q]
        self.ndma[q] += 1
        sem = self.dsems[q][i % NDSEM]
        waits = self._deps(q, reads, writes)
        prev = 16 * (i // NDSEM)
        if prev > 0 and self.known[q].get(id(sem), 0) < prev:
            waits.append((sem, prev))
            self.known[q][id(sem)] = prev
        tok = (sem, prev + 16, q, True)
        if not nofence:
            self.fence_vals.setdefault(q, {})[i % NDSEM] = (sem, prev + 16)

        def fn(eng, out=out, in_=in_, kw=kw):
            return eng.dma_start(out=out, in_=in_, **kw)

        self.ops[q].append((waits, fn, sem, 16))
        self._record(tok, reads, writes)
        return tok

    def fence(self):
        toks = []
        for f in ENGS:
            if self.sems[f] and self.cnt[f] > 0:
                toks.append((self.sems[f][self.epoch[f]], self.cnt[f]))
            for (sem, val) in self.fence_vals.get(f, {}).values():
                toks.append((sem, val))
        for e in ENGS:
            if not self.ops[e]:
                continue
            waits = []
            own = self.sems[e][self.epoch[e]] if self.sems[e] else None
            for (sem, val) in toks:
                if sem is own:
                    continue
                if self.known[e].get(id(sem), 0) >= val:
                    continue
                waits.append((sem, val))
                self.known[e][id(sem)] = val
            if waits:
                self.ops[e].append((waits, None, None, 0))

    def finish_wait_all(self, e="sp"):
        toks = {}
        for k, t in self.last_w.items():
            sid = id(t[0])
            if sid not in toks or toks[sid][1] < t[1]:
                toks[sid] = t
        waits = []
        for sid, (sem, val, te, isdma) in toks.items():
            if self.known[e].get(sid, 0) >= val:
                continue
            waits.append((sem, val))
        self.ops[e].append((waits, None, None, 0))

    def emit(self):
        nc = self.nc
        engmap = {"pe": "tensor", "act": "scalar", "dve": "vector", "pool": "gpsimd", "sp": "sync"}
        with nc.Block() as block:
            for e in ENGS:
                ops = self.ops[e]
                if not ops:
                    continue

                def body(eng, ops=ops):
                    for (waits, fn, sem, inc) in ops:
                        for (ws, wv) in waits:
                            eng.wait_ge(ws, wv)
                        if fn is None:
                            continue
                        ins = fn(eng)
                        ins.then_inc(sem, inc)

                getattr(block, engmap[e])(body)
        for cm in reversed(self._ctx):
            cm.__exit__(None, None, None)
        self._ctx = []


class Pools:
    def __init__(self, nc):
        self.nc = nc
        self._ctx = []

    def sb(self, name, shape, dt):
        cm = self.nc.sbuf_tensor(name, list(shape), dt)
        t = cm.__enter__()
        self._ctx.append(cm)
        return t

    def ps(self, name, shape, dt):
        cm = self.nc.psum_tensor(name, list(shape), dt)
        t = cm.__enter__()
        self._ctx.append(cm)
        return t

    def close(self):
        for cm in reversed(self._ctx):
            cm.__exit__(None, None, None)
        self._ctx = []


class Arena:
    def __init__(self, pools, nf32):
        self.t = pools.sb("arena", [128, nf32], F32)
        self.n = nf32
        self.off = 0

    def reset(self):
        self.off = 0

    def alloc(self, shape, dt):
        assert shape[0] == 128
        nel = 1
        for d_ in shape[1:]:
            nel *= d_
        nf = nel if dt == F32 else (nel + 1) // 2
        nf = (nf + 15) // 16 * 16
        assert self.off + nf <= self.n, ("arena overflow", self.off, nf, self.n)
        v = self.t[:, self.off:self.off + nf]
        self.off += nf
        if dt != F32:
            v = v.bitcast(dt)
        v = v[:, 0:nel]
        if len(shape) == 3:
            v = v.rearrange("p (a b) -> p a b", a=shape[1])
        return v


class VecPack:
    def __init__(self):
        self.cols = []
        self.index = {}
        self.n = 0

    def add(self, name, vec):
        vec = np.asarray(vec, dtype=np.float32).reshape(-1)
        assert vec.size % 128 == 0, (name, vec.size)
        nch = vec.size // 128
        self.index[name] = (self.n, nch)
        self.cols.append(vec.reshape(nch, 128).T)
        self.n += nch

    def add_raw(self, name, arr):
        arr = np.asarray(arr, dtype=np.float32)
        assert arr.shape[0] == 128
        self.index[name] = (self.n, arr.shape[1])
        self.cols.append(arr)
        self.n += arr.shape[1]

    def build(self):
        return np.ascontiguousarray(np.concatenate(self.cols, axis=1))


def windows(S, n, halo):
    out = []
    t = 0
    while t < S:
        m = min(n, S - t)
        out.append((t, m))
        t += m
    return out


class Builder:
    def __init__(self, S, vec_index, nvec, layers=(0, 1, 2, 3), do_mixer=True, do_ffn=True, debug=False):
        self.S = S
        self.debug = debug
        self.dbg = {}
        self.vi = vec_index
        self.layers = layers
        self.do_mixer = do_mixer
        self.do_ffn = do_ffn
        nc = self.nc = bass.Bass("TRN2", target_bir_lowering=False)
        self.P = Pools(nc)
        self.pr = Prog(nc)
        P, pr = self.P, self.pr
        self.x_in = nc.dram_tensor("x", [S, D], F32, kind="ExternalInput").ap()
        self.y_out = nc.dram_tensor("y", [S, D], F32, kind="ExternalOutput").ap()
        self.vec_d = nc.dram_tensor("vecs", [128, nvec], F32, kind="ExternalInput").ap()
        self.wup_f = nc.dram_tensor("wup", [DEPTH, 16, 128, 4096], F32, kind="ExternalInput").ap()
        self.wdn_f = nc.dram_tensor("wdn", [DEPTH, 4, 128, 8192], F32, kind="ExternalInput").ap()
        self.wup_b = nc.dram_tensor("wup_b", [DEPTH, 16, 128, 4096], BF16, kind="Internal").ap()
        self.wdn_b = nc.dram_tensor("wdn_b", [DEPTH, 4, 128, 8192], BF16, kind="Internal").ap()
        self.swin_f = nc.dram_tensor("swin", [2, 10, 128, 4096], F32, kind="ExternalInput").ap()
        self.swdt_f = nc.dram_tensor("swdt", [2, 128, 256], F32, kind="ExternalInput").ap()
        self.swout_f = nc.dram_tensor("swout", [2, 8, 128, 2048], F32, kind="ExternalInput").ap()
        self.swin_b = nc.dram_tensor("swin_b", [2, 10, 128, 4096], BF16, kind="Internal").ap()
        self.swdt_b = nc.dram_tensor("swdt_b", [2, 128, 256], BF16, kind="Internal").ap()
        self.swout_b = nc.dram_tensor("swout_b", [2, 8, 128, 2048], BF16, kind="Internal").ap()
        self.snorm_d = nc.dram_tensor("snorm", [2, 128, 2048], F32, kind="ExternalInput").ap()
        self.ssm_d = nc.dram_tensor("ssm", [2, 128, 96], F32, kind="ExternalInput").ap()
        self.lwin_f = nc.dram_tensor("lwin", [2, 2560, 1024], F32, kind="ExternalInput").ap()
        self.lwgx_f = nc.dram_tensor("lwgx", [2, 128, 1280], F32, kind="ExternalInput").ap()
        self.lwga_f = nc.dram_tensor("lwga", [2, 128, 1280], F32, kind="ExternalInput").ap()
        self.lwout_f = nc.dram_tensor("lwout", [2, 128, 10240], F32, kind="ExternalInput").ap()
        self.lwin_b = nc.dram_tensor("lwin_b", [2, 2560, 1024], BF16, kind="Internal").ap()
        self.lwgx_b = nc.dram_tensor("lwgx_b", [2, 128, 1280], BF16, kind="Internal").ap()
        self.lwga_b = nc.dram_tensor("lwga_b", [2, 128, 1280], BF16, kind="Internal").ap()
        self.lwout_b = nc.dram_tensor("lwout_b", [2, 128, 10240], BF16, kind="Internal").ap()
        self.xT = [nc.dram_tensor(f"xT{i}", [D, S], F32, kind="Internal").ap() for i in range(2)]
        self.cur = 0
        self.vecs = P.sb("vecs_sb", [128, nvec], F32)
        pr.dma("sp", self.vecs[:], self.vec_d, writes=["vecs"])
        self.ident = P.sb("ident", [128, 128], F32)
        self.ones_f = P.sb("ones_f", [128, 128], F32)
        self.ones_b = P.sb("ones_b", [128, 128], BF16)
        pr.op("pool", lambda e: e.memset(self.ones_f[:], 1.0), writes=["ones_f"])
        pr.op("pool", lambda e: e.memset(self.ones_b[:], 1.0), writes=["ones_b"])
        pr.op("pool", lambda e: e.affine_select(out=self.ident[:], in_=self.ones_f[:], pattern=[[-1, 128]],
                                                compare_op=ALU.is_equal, fill=0.0, base=0, channel_multiplier=1),
              reads=["ones_f"], writes=["ident"])
        self.ident_b = P.sb("ident_b", [128, 128], BF16)
        self.triU = P.sb("triU", [128, 128], F32)
        self.triSL = P.sb("triSL", [128, 128], F32)
        pr.op("pool", lambda e: e.tensor_copy(out=self.ident_b[:], in_=self.ident[:]), reads=["ident"], writes=["ident_b"])
        pr.op("pool", lambda e: e.affine_select(out=self.triU[:], in_=self.ones_f[:], pattern=[[1, 128]],
                                                compare_op=ALU.is_ge, fill=0.0, base=0, channel_multiplier=-1),
              reads=["ones_f"], writes=["triU"])
        pr.op("pool", lambda e: e.affine_select(out=self.triSL[:], in_=self.ones_f[:], pattern=[[-1, 128]],
                                                compare_op=ALU.is_gt, fill=0.0, base=0, channel_multiplier=1),
              reads=["ones_f"], writes=["triSL"])
        self.psall = P.ps("psall", [128, 4096], F32)
        self.psb = [self.psall[:, i * 512:(i + 1) * 512] for i in range(8)]
        self.A = Arena(P, 50176)
        self.cast_jobs = []
        self.cast_done = 0

    def dump(self, name, ap, reads):
        if not self.debug:
            return
        t = self.nc.dram_tensor("dbg_" + name, list(ap.shape), ap.dtype, kind="ExternalOutput").ap()
        self.dbg[name] = t
        self.pr.dma("pool", t, ap, reads=reads, writes=[("dbg", name)])

    def vcol(self, name, c=0, n=1):
        c0, nch = self.vi[name]
        assert c + n <= nch, (name, c, n, nch)
        return self.vecs[:, c0 + c:c0 + c + n]

    def add_cast(self, dst, src, key):
        self.cast_jobs.append((dst, src, key))

    def pump_casts(self, n):
        while n > 0 and self.cast_done < len(self.cast_jobs):
            dst, src, key = self.cast_jobs[self.cast_done]
            self.pr.dma("pool", dst, src, writes=[key], nofence=True, max_dma_last_dim=4096)
            self.cast_done += 1
            n -= 1

    def pump_until(self, key):
        while self.cast_done < len(self.cast_jobs) and key not in self.pr.last_w:
            self.pump_casts(1)

    def phase_in(self):
        pr, P, S = self.pr, self.P, self.S
        pr.fence()
        self.A.reset()
        dst = self.xT[self.cur].rearrange("(c p) t -> p c t", p=128)
        xin = [self.A.alloc([128, D], F32) for i in range(2)]
        xst = [self.A.alloc([128, 8, 512], F32) for i in range(2)]
        nblk = (S + 511) // 512
        for b in range(nblk):
            t0 = b * 512
            nt = min(512, S - t0) // 128
            st = xst[b % 2]
            for tt in range(nt):
                xi = xin[(b * 4 + tt) % 2]
                kxi = ("xin", (b * 4 + tt) % 2)
                pr.dma("sp", xi[:], self.x_in[t0 + tt * 128:t0 + (tt + 1) * 128, :], writes=[kxi])
                for half in range(2):
                    bank = (tt * 2 + half) % 8
                    ps = self.psb[bank]

                    def tr(e, xi=xi, ps=ps, half=half):
                        for q in range(4):
                            c = half * 4 + q
                            i = e.transpose(ps[:, q * 128:(q + 1) * 128], xi[:, c * 128:(c + 1) * 128], self.ident[:])
                        return i
                    pr.op("pe", tr, reads=[kxi, "ident"], writes=[("ps", bank)])
                    eng = "act" if half == 0 else "dve"

                    def ev(e, st=st, ps=ps, half=half, tt=tt, eng=eng):
                        o = st[:, half * 4:(half + 1) * 4, tt * 128:(tt + 1) * 128]
                        i_ = ps[:].rearrange("p (q t) -> p q t", q=4)
                        if eng == "act":
                            return e.activation(out=o, in_=i_, func=AF.Identity)
                        return e.tensor_copy(out=o, in_=i_)
                    pr.op(eng, ev, reads=[("ps", bank)], writes=[("xst", b % 2, half, tt)])
            rk = [("xst", b % 2, h, tt) for h in range(2) for tt in range(nt)]
            pr.dma("pool", dst[:, :, t0:t0 + nt * 128], st[:, :, 0:nt * 128], reads=rk, writes=[("xT", self.cur)])

    def phase_out(self):
        pr, P, S = self.pr, self.P, self.S
        pr.fence()
        self.A.reset()
        src = self.xT[self.cur].rearrange("(c p) t -> p c t", p=128)
        xld = [self.A.alloc([128, 8, 512], F32) for i in range(2)]
        yst = [self.A.alloc([128, D], F32) for i in range(2)]
        nblk = (S + 511) // 512
        for b in range(nblk):
            t0 = b * 512
            nt = min(512, S - t0) // 128
            xl = xld[b % 2]
            kx = ("xld", b % 2)
            pr.dma("sp", xl[:, :, 0:nt * 128], src[:, :, t0:t0 + nt * 128], reads=[("xT", self.cur)], writes=[kx])
            for tt in range(nt):
                ys = yst[(b * 4 + tt) % 2]
                ky = ("yst", (b * 4 + tt) % 2)
                for half in range(2):
                    bank = (tt * 2 + half) % 8
                    ps = self.psb[bank]

                    def tr(e, xl=xl, ps=ps, half=half, tt=tt):
                        for q in range(4):
                            c = half * 4 + q
                            i = e.transpose(ps[:, q * 128:(q + 1) * 128], xl[:, c, tt * 128:(tt + 1) * 128], self.ident[:])
                        return i
                    pr.op("pe", tr, reads=[kx, "ident"], writes=[("ps", bank)])
                    eng = "act" if half == 0 else "dve"

                    def ev(e, ys=ys, ps=ps, half=half, eng=eng):
                        o = ys[:, half * 512:(half + 1) * 512]
                        if eng == "act":
                            return e.activation(out=o, in_=ps[:], func=AF.Identity)
                        return e.tensor_copy(out=o, in_=ps[:])
                    pr.op(eng, ev, reads=[("ps", bank)], writes=[(ky, half)])
                pr.dma("pool", self.y_out[t0 + tt * 128:t0 + (tt + 1) * 128, :], ys[:],
                       reads=[(ky, 0), (ky, 1)], writes=["y"])

    def prenorm(self, xblk, kx, C, wname, hT, kh, sq, ksq, rstd, krstd, stat_bank):
        pr = self.pr
        ps = self.psb[stat_bank]
        for c in range(8):
            pr.op("act", lambda e, c=c: e.activation(out=sq[:, c, 0:C], in_=xblk[:, c, 0:C], func=AF.Square),
                  reads=[kx], writes=[(ksq, c)])

        def st(e):
            for c in range(8):
                i = e.matmul(ps[:, 0:C], self.ones_b[:], sq[:, c, 0:C], start=(c == 0), stop=(c == 7))
            return i
        pr.op("pe", st, reads=[(ksq, c) for c in range(8)] + ["ones_b"], writes=[("ps", stat_bank)])
        pr.op("act", lambda e: e.activation(out=rstd[:, 0:C], in_=ps[:, 0:C], func=AF.Ln, scale=1.0 / D, bias=self.vcol("eps")),
              reads=[("ps", stat_bank), "vecs"], writes=[krstd])
        pr.op("act", lambda e: e.activation(out=rstd[:, 0:C], in_=rstd[:, 0:C], func=AF.Exp, scale=-0.5),
              reads=[krstd], writes=[krstd])
        for c in range(8):
            pr.op("dve", lambda e, c=c: e.scalar_tensor_tensor(out=hT[:, c, 0:C], in0=xblk[:, c, 0:C], scalar=self.vcol(wname, c),
                                                               in1=rstd[:, 0:C], op0=ALU.mult, op1=ALU.mult),
                  reads=[kx, krstd, "vecs"], writes=[(kh, c)])

    def phase_ffn(self, li):
        pr, P, S = self.pr, self.P, self.S
        pr.fence()
        self.A.reset()
        src = self.xT[self.cur].rearrange("(c p) t -> p c t", p=128)
        dst = self.xT[1 - self.cur].rearrange("(c p) t -> p c t", p=128)
        xblks = [self.A.alloc([128, 8, 512], F32) for _ in range(2)]
        sq = self.A.alloc([128, 8, 512], BF16)
        sqps = [self.A.alloc([128, 512], BF16) for _ in range(2)]
        hT = self.A.alloc([128, 8, 512], BF16)
        rstd = self.A.alloc([128, 512], F32)
        rstd2 = self.A.alloc([128, 512], F32)
        gvT = self.A.alloc([128, 32, 512], BF16)
        fT = self.A.alloc([128, 8, 512], F32)
        NWU, NWD = 3, 2
        wus = [self.A.alloc([128, 8, 512], BF16) for i in range(NWU)]
        wds = [self.A.alloc([128, 32, 256], BF16) for i in range(NWD)]
        tg = [self.A.alloc([128, 512], F32) for i in range(4)]
        tv = [self.A.alloc([128, 512], F32) for i in range(4)]
        gg = [self.A.alloc([128, 512], F32) for i in range(4)]
        nwu = 0
        nwd = 0
        npair = 0
        wins = windows(S, 464, 2)
        def pre_p1(wi_):
            t0_, n_ = wins[wi_]
            C_ = n_ + 2
            xb = xblks[wi_ % 2]
            kx_ = (f"f{li}_xblk", wi_ % 2)
            if t0_ == 0:
                pr.op("pool", lambda e: e.memset(xb[:, :, 0:2], 0.0), writes=[(kx_, "halo")])
                pr.dma("sp", xb[:, :, 2:C_], src[:, :, 0:n_], reads=[("xT", self.cur)], writes=[kx_])
            else:
                pr.dma("sp", xb[:, :, 0:C_], src[:, :, t0_ - 2:t0_ + n_], reads=[("xT", self.cur)], writes=[kx_, (kx_, "halo")])
            for c in range(8):
                pr.op("act", lambda e, c=c: e.activation(out=sq[:, c, 0:C_], in_=xb[:, c, 0:C_], func=AF.Square),
                      reads=[kx_, (kx_, "halo")], writes=[(f"f{li}_sq", c)])

        def pre_p2(wi_):
            t0_, n_ = wins[wi_]
            C_ = n_ + 2
            xb = xblks[wi_ % 2]
            kx_ = (f"f{li}_xblk", wi_ % 2)
            ps = self.psb[6]

            def st(e):
                for c in range(8):
                    i = e.matmul(ps[:, 0:C_], self.ones_b[:], sq[:, c, 0:C_], start=(c == 0), stop=(c == 7))
                return i
            pr.op("pe", st, reads=[(f"f{li}_sq", c) for c in range(8)] + ["ones_b"], writes=[("ps", 6)])
            pr.op("act", lambda e: e.activation(out=rstd[:, 0:C_], in_=ps[:, 0:C_], func=AF.Ln, scale=1.0 / D, bias=self.vcol("eps")),
                  reads=[("ps", 6), "vecs"], writes=[f"f{li}_rstd"])
            pr.op("act", lambda e: e.activation(out=rstd[:, 0:C_], in_=rstd[:, 0:C_], func=AF.Exp, scale=-0.5),
                  reads=[f"f{li}_rstd"], writes=[f"f{li}_rstd"])
            for c in range(8):
                pr.op("dve", lambda e, c=c: e.scalar_tensor_tensor(out=hT[:, c, 0:C_], in0=xb[:, c, 0:C_], scalar=self.vcol(f"nfpre{li}", c),
                                                                   in1=rstd[:, 0:C_], op0=ALU.mult, op1=ALU.mult),
                      reads=[kx_, (kx_, "halo"), f"f{li}_rstd", "vecs"], writes=[(f"f{li}_hT", c)])

        pre_p1(0)
        pre_p2(0)
        for wi, (t0, n) in enumerate(wins):
            C = n + 2
            xblk = xblks[wi % 2]
            kx = (f"f{li}_xblk", wi % 2)
            kxr = [kx, (kx, "halo")]
            khs = [(f"f{li}_hT", c) for c in range(8)]
            deferred, deferred_next = [], []
            for j2 in range(16):
                ws = wus[nwu % NWU]
                kw = ("wu", li, nwu % NWU)
                nwu += 1
                self.pump_until(("wup_b", li, j2))
                pr.dma("sp", ws[:], self.wup_b[li, j2].rearrange("p (k c) -> p k c", k=8),
                       reads=[("wup_b", li, j2)], writes=[kw])
                for jj in range(2):
                    j = j2 * 2 + jj
                    pb = (npair % 4) * 2
                    npair += 1
                    gps, vps = self.psb[pb], self.psb[pb + 1]

                    def mmg(e, ws=ws, gps=gps, jj=jj, C=C, off=0):
                        for k in range(8):
                            i = e.matmul(gps[:, 0:C], ws[:, k, off + jj * 128:off + (jj + 1) * 128], hT[:, k, 0:C],
                                         start=(k == 0), stop=(k == 7))
                        return i
                    pr.op("pe", mmg, reads=[kw] + khs, writes=[("ps", pb)])
                    pr.op("pe", lambda e, ws=ws, vps=vps, jj=jj, C=C: mmg(e, ws, vps, jj, C, 256),
                          reads=[kw] + khs, writes=[("ps", pb + 1)])
                    par = j % 4
                    tgt, tvt, ggt = tg[par], tv[par], gg[par]
                    ktg, ktv, kgg = ("tg", par), ("tv", par), ("gg", par)
                    jv = 32 + j
                    pr.op("act", lambda e, gps=gps, tgt=tgt, j=j, C=C, n=n: e.activation(
                        out=tgt[:, 0:n], in_=gps[:, 2:C], func=AF.Identity,
                        scale=self.vcol(f"fcw{li}_2", j), bias=self.vcol(f"fcb{li}", j)),
                        reads=[("ps", pb), "vecs"], writes=[ktg])
                    pr.op("act", lambda e, vps=vps, tvt=tvt, jv=jv, C=C, n=n: e.activation(
                        out=tvt[:, 0:n], in_=vps[:, 2:C], func=AF.Identity,
                        scale=self.vcol(f"fcw{li}_2", jv), bias=self.vcol(f"fcb{li}", jv)),
                        reads=[("ps", pb + 1), "vecs"], writes=[ktv])
                    pr.op("dve", lambda e, gps=gps, tgt=tgt, j=j, C=C, n=n: e.scalar_tensor_tensor(
                        out=tgt[:, 0:n], in0=gps[:, 1:C - 1], scalar=self.vcol(f"fcw{li}_1", j), in1=tgt[:, 0:n],
                        op0=ALU.mult, op1=ALU.add), reads=[("ps", pb), ktg, "vecs"], writes=[ktg])
                    pr.op("dve", lambda e, gps=gps, tgt=tgt, j=j, C=C, n=n: e.scalar_tensor_tensor(
                        out=tgt[:, 0:n], in0=gps[:, 0:C - 2], scalar=self.vcol(f"fcw{li}_0", j), in1=tgt[:, 0:n],
                        op0=ALU.mult, op1=ALU.add), reads=[("ps", pb), ktg, "vecs"], writes=[ktg])
                    deferred_next.append(lambda tgt=tgt, ggt=ggt, n=n, ktg=ktg, kgg=kgg: pr.op(
                        "act", lambda e: e.activation(out=ggt[:, 0:n], in_=tgt[:, 0:n], func=AF.Gelu_apprx_tanh),
                        reads=[ktg], writes=[kgg]))
                    pr.op("dve", lambda e, vps=vps, tvt=tvt, jv=jv, C=C, n=n: e.scalar_tensor_tensor(
                        out=tvt[:, 0:n], in0=vps[:, 1:C - 1], scalar=self.vcol(f"fcw{li}_1", jv), in1=tvt[:, 0:n],
                        op0=ALU.mult, op1=ALU.add), reads=[("ps", pb + 1), ktv, "vecs"], writes=[ktv])
                    pr.op("dve", lambda e, vps=vps, tvt=tvt, jv=jv, C=C, n=n: e.scalar_tensor_tensor(
                        out=tvt[:, 0:n], in0=vps[:, 0:C - 2], scalar=self.vcol(f"fcw{li}_0", jv), in1=tvt[:, 0:n],
                        op0=ALU.mult, op1=ALU.add), reads=[("ps", pb + 1), ktv, "vecs"], writes=[ktv])
                    deferred_next.append(lambda ggt=ggt, tvt=tvt, j=j, n=n, kgg=kgg, ktv=ktv: pr.op(
                        "dve", lambda e: e.tensor_tensor(out=gvT[:, j, 0:n], in0=ggt[:, 0:n], in1=tvt[:, 0:n], op=ALU.mult),
                        reads=[kgg, ktv], writes=[("gvT", j)]))
                    for f_ in deferred:
                        f_()
                    deferred, deferred_next = deferred_next, []
            for f_ in deferred:
                f_()
            deferred = []
            if wi + 1 < len(wins):
                pre_p1(wi + 1)
            kgv = [("gvT", j) for j in range(32)]
            sbank = 7
            sps = self.psb[sbank]
            for o2 in range(4):
                wd = wds[nwd % NWD]
                kwd = ("wd", li, nwd % NWD)
                nwd += 1
                self.pump_until(("wdn_b", li, o2))
                pr.dma("sp", wd[:], self.wdn_b[li, o2].rearrange("p (k c) -> p k c", k=32),
                       reads=[("wdn_b", li, o2)], writes=[kwd])
                for oo in range(2):
                    o = o2 * 2 + oo
                    fb = (npair % 3) * 2
                    npair += 1
                    fps = self.psb[fb]

                    def mmd(e, wd=wd, fps=fps, oo=oo, n=n):
                        for k in range(32):
                            i = e.matmul(fps[:, 0:n], wd[:, k, oo * 128:(oo + 1) * 128], gvT[:, k, 0:n],
                                         start=(k == 0), stop=(k == 31))
                        return i
                    pr.op("pe", mmd, reads=[kwd] + kgv, writes=[("ps", fb)])
                    for f_ in deferred:
                        f_()
                    deferred = []
                    pr.op("act", lambda e, fps=fps, o=o, n=n: e.activation(out=fT[:, o, 0:n], in_=fps[:, 0:n], func=AF.Identity,
                                                                            scale=self.vcol(f"nfpost{li}", o)),
                          reads=[("ps", fb), "vecs"], writes=[("fT", o)])
                    sqt = sqps[o % 2]
                    pr.op("act", lambda e, fps=fps, sqt=sqt, n=n: e.activation(out=sqt[:, 0:n], in_=fps[:, 0:n], func=AF.Square),
                          reads=[("ps", fb)], writes=[(f"f{li}_sqp", o % 2)])
                    for f_ in deferred:
                        f_()
                    deferred = [lambda o=o, n=n, sqt=sqt: pr.op(
                        "pe", lambda e: e.matmul(sps[:, 0:n], self.ones_b[:], sqt[:, 0:n], start=(o == 0), stop=(o == 7)),
                        reads=[(f"f{li}_sqp", o % 2), "ones_b"], writes=[("ps", sbank)])]
                    if o == 1 and wi + 1 < len(wins):
                        pre_p2(wi + 1)
            for f_ in deferred:
                f_()
            deferred = []
            pr.op("act", lambda e, n=n: e.activation(out=rstd2[:, 0:n], in_=sps[:, 0:n], func=AF.Ln, scale=1.0 / D, bias=self.vcol("eps")),
                  reads=[("ps", sbank), "vecs"], writes=["rstd2"])
            pr.op("act", lambda e, n=n: e.activation(out=rstd2[:, 0:n], in_=rstd2[:, 0:n], func=AF.Exp, scale=-0.5),
                  reads=["rstd2"], writes=["rstd2"])
            for o in range(8):
                pr.op("dve", lambda e, o=o, n=n: e.tensor_tensor(out=fT[:, o, 0:n], in0=fT[:, o, 0:n], in1=rstd2[:, 0:n], op=ALU.mult),
                      reads=[("fT", o), "rstd2"], writes=[("fT", o)])
                pr.op("dve", lambda e, o=o, n=n, C=C, xblk=xblk: e.tensor_tensor(out=fT[:, o, 0:n], in0=fT[:, o, 0:n], in1=xblk[:, o, 2:C], op=ALU.add),
                      reads=[("fT", o)] + kxr, writes=[("fT", o)])
            pr.dma("pool", dst[:, :, t0:t0 + n], fT[:, :, 0:n], reads=[("fT", o) for o in range(8)],
                   writes=[("xT", 1 - self.cur)])
            self.pump_casts(6)
        self.cur = 1 - self.cur

    def phase_lru(self, li):
        pr, S, A = self.pr, self.S, self.A
        pr.fence()
        A.reset()
        j = li // 2
        T = 512
        src = self.xT[self.cur].rearrange("(c p) t -> p c t", p=128)
        dst = self.xT[1 - self.cur].rearrange("(c p) t -> p c t", p=128)
        win = A.alloc([128, 20, 1024], BF16)
        wgx = A.alloc([128, 10, 128], BF16)
        wga = A.alloc([128, 10, 128], BF16)
        wout = A.alloc([128, 10, 1024], BF16)
        for key in (("lwin", j), ("lwgx", j), ("lwga", j), ("lwout", j)):
            self.pump_until(key)
        pr.dma("sp", win, self.lwin_b[j].rearrange("(o p) c -> p o c", p=128), reads=[("lwin", j)], writes=["l_win"])
        pr.dma("sp", wgx, self.lwgx_b[j].rearrange("p (h c) -> p h c", h=10), reads=[("lwgx", j)], writes=["l_wgx"])
        pr.dma("sp", wga, self.lwga_b[j].rearrange("p (h c) -> p h c", h=10), reads=[("lwga", j)], writes=["l_wga"])
        pr.dma("sp", wout, self.lwout_b[j].rearrange("p (k c) -> p k c", k=10), reads=[("lwout", j)], writes=["l_wout"])
        xblks = [A.alloc([128, 8, T], F32) for _ in range(2)]
        sq = A.alloc([128, 8, T], BF16)
        hT = A.alloc([128, 8, T], BF16)
        rstd = A.alloc([128, T], F32)
        rstd2 = A.alloc([128, T], F32)
        hy = A.alloc([128, 10, T], BF16)
        mT = A.alloc([128, 8, T], F32)
        halo = A.alloc([128, 10, 3], F32)
        hstate = A.alloc([128, 10], F32)
        c8 = A.alloc([128, 10], F32)
        c16 = A.alloc([128, 10], F32)
        NB = 4
        ybr = [A.alloc([128, T], F32) for _ in range(NB)]
        XP = [A.alloc([128, T + 3], F32) for _ in range(NB)]
        tcv = [A.alloc([128, T], F32) for _ in range(NB)]
        xbb = [A.alloc([128, T], BF16) for _ in range(NB)]
        gx = [A.alloc([128, T], F32) for _ in range(2)]
        ga = [A.alloc([128, T], F32) for _ in range(2)]
        at = [A.alloc([128, T], F32) for _ in range(2)]
        mu = [A.alloc([128, T], F32) for _ in range(2)]
        bt = [A.alloc([128, T], F32) for _ in range(2)]
        hs = [A.alloc([128, T], F32) for _ in range(2)]
        lam = self.vcol(f"llam{j}", 0, 10)
        pr.op("act", lambda e: e.activation(out=c8, in_=lam, func=AF.Exp, scale=-1.0), reads=["vecs"], writes=["l_c8"], small=True)
        pr.op("act", lambda e: e.activation(out=c8, in_=c8, func=AF.Ln, bias=self.vcol("one")), reads=["l_c8", "vecs"], writes=["l_c8"], small=True)
        pr.op("dve", lambda e: e.tensor_scalar(out=c16, in0=c8, scalar1=-16.0, scalar2=None, op0=ALU.mult), reads=["l_c8"], writes=["l_c16"], small=True)
        pr.op("dve", lambda e: e.tensor_scalar(out=c8, in0=c8, scalar1=-8.0, scalar2=None, op0=ALU.mult), reads=["l_c8", "l_c16"], writes=["l_c8"], small=True)
        pr.op("pool", lambda e: e.memset(halo, 0.0), writes=["l_halo"])
        pr.op("pool", lambda e: e.memset(hstate, 0.0), writes=["l_hstate"])
        nbank = 0
        it = 0
        nwin = S // T

        def pre1(w_):
            xb = xblks[w_ % 2]
            kx_ = ("l_xblk", w_ % 2)
            pr.dma("sp", xb, src[:, :, w_ * T:(w_ + 1) * T], reads=[("xT", self.cur)], writes=[kx_])
            for c in range(8):
                pr.op("act", lambda e, c=c: e.activation(out=sq[:, c, :], in_=xb[:, c, :], func=AF.Square),
                      reads=[kx_], writes=[("l_sq", c)])

        def pre2(w_):
            xb = xblks[w_ % 2]
            kx_ = ("l_xblk", w_ % 2)
            ps = self.psb[6]

            def st(e):
                for c in range(8):
                    i = e.matmul(ps[:, 0:T], self.ones_b[:], sq[:, c, :], start=(c == 0), stop=(c == 7))
                return i
            pr.op("pe", st, reads=[("l_sq", c) for c in range(8)] + ["ones_b"], writes=[("ps", 6)])
            pr.op("act", lambda e: e.activation(out=rstd[:, 0:T], in_=ps[:, 0:T], func=AF.Ln, scale=1.0 / D, bias=self.vcol("eps")),
                  reads=[("ps", 6), "vecs"], writes=["l_rstd"])
            pr.op("act", lambda e: e.activation(out=rstd[:, 0:T], in_=rstd[:, 0:T], func=AF.Exp, scale=-0.5),
                  reads=["l_rstd"], writes=["l_rstd"])
            for c in range(8):
                pr.op("dve", lambda e, c=c: e.scalar_tensor_tensor(out=hT[:, c, :], in0=xb[:, c, :], scalar=self.vcol(f"nmpre{li}", c),
                                                                   in1=rstd[:, 0:T], op0=ALU.mult, op1=ALU.mult),
                      reads=[kx_, "l_rstd", "vecs"], writes=[("l_hT", c)])

        pre1(0)
        pre2(0)
        for wi in range(nwin):
            t0 = wi * T
            xblk = xblks[wi % 2]
            kx = ("l_xblk", wi % 2)
            khs = [("l_hT", c) for c in range(8)]
            def mmin(e, ps, oc):
                for k in range(8):
                    i = e.matmul(ps[:, 0:T], win[:, oc, k * 128:(k + 1) * 128], hT[:, k, :], start=(k == 0), stop=(k == 7))
                return i

            def lru_s1(c, p):
                nonlocal nbank
                yb, xb = nbank % 6, (nbank + 1) % 6
                nbank += 2
                yps, xps = self.psb[yb], self.psb[xb]
                ybt, XPt, tct, xbt = ybr[p], XP[p], tcv[p], xbb[p]
                K = lambda n: (n, p)
                pr.op("pe", lambda e: mmin(e, yps, c), reads=["l_win"] + khs, writes=[("ps", yb)])
                pr.op("pe", lambda e: mmin(e, xps, 10 + c), reads=["l_win"] + khs, writes=[("ps", xb)])
                pr.op("act", lambda e: e.activation(out=ybt, in_=yps[:, 0:T], func=AF.Gelu_apprx_tanh, bias=self.vcol(f"lbin{j}", c)),
                      reads=[("ps", yb), "vecs"], writes=[K("ybr")])
                pr.op("pool", lambda e: e.tensor_copy(out=XPt[:, 0:3], in_=halo[:, c, :]), reads=["l_halo"], writes=[K("XPh")])
                pr.op("act", lambda e: e.activation(out=XPt[:, 3:T + 3], in_=xps[:, 0:T], func=AF.Identity, bias=self.vcol(f"lbin{j}", 10 + c)),
                      reads=[("ps", xb), "vecs"], writes=[K("XP")])
                pr.op("pool", lambda e: e.tensor_copy(out=halo[:, c, :], in_=XPt[:, T:T + 3]), reads=[K("XP"), K("XPh")], writes=["l_halo"])
                pr.op("dve", lambda e: e.tensor_scalar(out=tct, in0=XPt[:, 3:T + 3], scalar1=self.vcol(f"lcw{j}_3", c),
                                                       scalar2=self.vcol(f"lcb{j}", c), op0=ALU.mult, op1=ALU.add),
                      reads=[K("XP"), "vecs"], writes=[K("tcv")])
                for tap in (2, 1, 0):
                    pr.op("dve", lambda e, tap=tap: e.scalar_tensor_tensor(
                        out=tct, in0=XPt[:, tap:tap + T], scalar=self.vcol(f"lcw{j}_{tap}", c), in1=tct, op0=ALU.mult, op1=ALU.add),
                        reads=[K("XP"), K("XPh"), K("tcv"), "vecs"], writes=[K("tcv")])
                pr.op("pool", lambda e: e.tensor_copy(out=xbt, in_=tct), reads=[K("tcv")], writes=[K("xbb")])

            def lru_s2(cs_, ps_):
                nonlocal nbank
                ctx = []
                for c, p in zip(cs_, ps_):
                    gxb, gab = nbank % 6, (nbank + 1) % 6
                    nbank += 2
                    q2 = c % 2
                    ctx.append(dict(c=c, p=p, q2=q2, gxb=gxb, gab=gab, gxps=self.psb[gxb], gaps=self.psb[gab],
                                    ybt=ybr[p], tct=tcv[p], xbt=xbb[p], gxt=gx[q2], gat=ga[q2], att=at[q2], mut=mu[q2],
                                    btt=bt[q2], hst=hs[q2]))
                for d_ in ctx:
                    c, p, q2 = d_["c"], d_["p"], d_["q2"]
                    pr.op("pe", lambda e, d_=d_: e.matmul(d_["gxps"][:, 0:T], wgx[:, d_["c"], :], d_["xbt"], start=True, stop=True),
                          reads=["l_wgx", ("xbb", p)], writes=[("ps", d_["gxb"])])
                    pr.op("pe", lambda e, d_=d_: e.matmul(d_["gaps"][:, 0:T], wga[:, d_["c"], :], d_["xbt"], start=True, stop=True),
                          reads=["l_wga", ("xbb", p)], writes=[("ps", d_["gab"])])
                for d_ in ctx:
                    c, p, q2 = d_["c"], d_["p"], d_["q2"]
                    pr.op("act", lambda e, d_=d_: e.activation(out=d_["gxt"], in_=d_["gxps"][:, 0:T], func=AF.Sigmoid, bias=self.vcol(f"lbgx{j}", d_["c"])),
                          reads=[("ps", d_["gxb"]), "vecs"], writes=[("gx", q2)])
                    pr.op("act", lambda e, d_=d_: e.activation(out=d_["gat"], in_=d_["gaps"][:, 0:T], func=AF.Sigmoid, bias=self.vcol(f"lbga{j}", d_["c"])),
                          reads=[("ps", d_["gab"]), "vecs"], writes=[("ga", q2)])
                for d_ in ctx:
                    c, p, q2 = d_["c"], d_["p"], d_["q2"]
                    pr.op("act", lambda e, d_=d_: e.activation(out=d_["att"], in_=d_["gat"], func=AF.Exp, scale=c8[:, d_["c"]:d_["c"] + 1]),
                          reads=[("ga", q2), "l_c8"], writes=[("at", q2)])
                    pr.op("dve", lambda e, d_=d_: e.tensor_tensor(out=d_["btt"], in0=d_["gxt"], in1=d_["tct"], op=ALU.mult),
                          reads=[("gx", q2), ("tcv", p)], writes=[("bt", q2)])
                    pr.op("dve", lambda e, d_=d_: e.tensor_tensor(out=d_["mut"], in0=d_["att"], in1=d_["att"], op=ALU.mult),
                          reads=[("at", q2)], writes=[("mu", q2)])
                    pr.op("dve", lambda e, d_=d_: e.tensor_scalar(out=d_["mut"], in0=d_["mut"], scalar1=1.0, scalar2=None, op0=ALU.min),
                          reads=[("mu", q2)], writes=[("mu", q2)])
                for d_ in ctx:
                    c, p, q2 = d_["c"], d_["p"], d_["q2"]
                    pr.op("act", lambda e, d_=d_: e.activation(out=d_["mut"], in_=d_["mut"], func=AF.Sqrt, scale=-1.0, bias=self.vcol("one")),
                          reads=[("mu", q2), "vecs"], writes=[("mu", q2)])
                for d_ in ctx:
                    c, p, q2 = d_["c"], d_["p"], d_["q2"]
                    pr.op("dve", lambda e, d_=d_: e.tensor_tensor(out=d_["btt"], in0=d_["btt"], in1=d_["mut"], op=ALU.mult),
                          reads=[("bt", q2), ("mu", q2)], writes=[("bt", q2)])
                    pr.op("dve", lambda e, d_=d_: e.tensor_tensor_scan(out=d_["hst"], data0=d_["att"], data1=d_["btt"],
                                                                       initial=hstate[:, d_["c"]:d_["c"] + 1], op0=ALU.mult, op1=ALU.add),
                          reads=[("at", q2), ("bt", q2), "l_hstate"], writes=[("hs", q2)])
                    pr.op("pool", lambda e, d_=d_: e.tensor_copy(out=hstate[:, d_["c"]:d_["c"] + 1], in_=d_["hst"][:, T - 1:T]),
                          reads=[("hs", q2)], writes=["l_hstate"])
                    pr.op("dve", lambda e, d_=d_: e.tensor_tensor(out=hy[:, d_["c"], :], in0=d_["hst"], in1=d_["ybt"], op=ALU.mult),
                          reads=[("hs", q2), ("ybr", p)], writes=[("l_hy", d_["c"])])

            ps_of = {}
            for pk in range(6):
                if pk < 5:
                    for c in (2 * pk, 2 * pk + 1):
                        ps_of[c] = it % NB
                        it += 1
                        lru_s1(c, ps_of[c])
                if pk == 5 and wi + 1 < nwin:
                    pre1(wi + 1)
                if pk >= 1:
                    cs_ = (2 * (pk - 1), 2 * (pk - 1) + 1)
                    lru_s2(cs_, [ps_of[c] for c in cs_])
            if wi + 1 < nwin:
                pre2(wi + 1)
            khy = [("l_hy", c) for c in range(10)]
            sbank = 7
            sps = self.psb[sbank]
            for o in range(8):
                mb = nbank % 6
                nbank += 1
                mps = self.psb[mb]

                def mmo(e, mps=mps, o=o):
                    for k in range(10):
                        i = e.matmul(mps[:, 0:T], wout[:, k, o * 128:(o + 1) * 128], hy[:, k, :], start=(k == 0), stop=(k == 9))
                    return i
                pr.op("pe", mmo, reads=["l_wout"] + khy, writes=[("ps", mb)])
                pr.op("act", lambda e, mps=mps, o=o: e.activation(out=mT[:, o, :], in_=mps[:, 0:T], func=AF.Identity, bias=self.vcol(f"lbout{j}", o)),
                      reads=[("ps", mb), "vecs"], writes=[("l_mT", o)])
                pr.op("act", lambda e, mps=mps, o=o: e.activation(out=sq[:, o, :], in_=mps[:, 0:T], func=AF.Square, bias=self.vcol(f"lbout{j}", o)),
                      reads=[("ps", mb), "vecs"], writes=[("l_sq", o)])
                pr.op("pe", lambda e, o=o: e.matmul(sps[:, 0:T], self.ones_b[:], sq[:, o, :], start=(o == 0), stop=(o == 7)),
                      reads=[("l_sq", o), "ones_b"], writes=[("ps", sbank)])
            self._resid(sps, sbank, rstd2, "l_rstd2", mT, "l_mT", f"nmpost{li}", xblk, [kx], 0, T, dst, t0)
            self.pump_casts(6)
        self.cur = 1 - self.cur

    def _resid(self, sps, sbank, rstd2, krs, mT, kmT, wname, xblk, kxr, xoff, n, dst, t0):
        pr = self.pr
        pr.op("act", lambda e: e.activation(out=rstd2[:, 0:n], in_=sps[:, 0:n], func=AF.Ln, scale=1.0 / D, bias=self.vcol("eps")),
              reads=[("ps", sbank), "vecs"], writes=[krs])
        pr.op("act", lambda e: e.activation(out=rstd2[:, 0:n], in_=rstd2[:, 0:n], func=AF.Exp, scale=-0.5),
              reads=[krs], writes=[krs])
        for o in range(8):
            pr.op("dve", lambda e, o=o: e.scalar_tensor_tensor(out=mT[:, o, 0:n], in0=mT[:, o, 0:n], scalar=self.vcol(wname, o),
                                                               in1=rstd2[:, 0:n], op0=ALU.mult, op1=ALU.mult),
                  reads=[(kmT, o), krs, "vecs"], writes=[(kmT, o)])
            pr.op("dve", lambda e, o=o: e.tensor_tensor(out=mT[:, o, 0:n], in0=mT[:, o, 0:n], in1=xblk[:, o, xoff:xoff + n], op=ALU.add),
                  reads=[(kmT, o)] + kxr, writes=[(kmT, o)])
        pr.dma("pool", dst[:, :, t0:t0 + n], mT[:, :, 0:n], reads=[(kmT, o) for o in range(8)], writes=[("xT", 1 - self.cur)])

    def phase_ssd(self, li):
        pr, S, A = self.pr, self.S, self.A
        pr.fence()
        A.reset()
        j = li // 2
        T = 512
        NQ = T // 128
        src = self.xT[self.cur].rearrange("(c p) t -> p c t", p=128)
        dst = self.xT[1 - self.cur].rearrange("(c p) t -> p c t", p=128)
        for b_ in range(10):
            self.pump_until(("swin", j, b_))
        self.pump_until(("swdt", j))
        for o in range(8):
            self.pump_until(("swout", j, o))
        wdt = A.alloc([128, 8, 32], BF16)
        pr.dma("sp", wdt, self.swdt_b[j].rearrange("p (k c) -> p k c", k=8), reads=[("swdt", j)], writes=["s_wdt"])
        normw = A.alloc([128, 2048], F32)
        pr.dma("sp", normw, self.snorm_d[j], writes=["s_normw"])
        sm = A.alloc([128, 96], F32)
        pr.dma("sp", sm, self.ssm_d[j], writes=["s_sm"])
        a_b = A.alloc([128, 32], F32)
        pr.op("act", lambda e: e.activation(out=a_b, in_=sm[:, 32:64], func=AF.Exp), reads=["s_sm"], writes=["s_ab"], small=True)
        pr.op("dve", lambda e: e.tensor_scalar(out=a_b, in0=a_b, scalar1=-1.0, scalar2=None, op0=ALU.mult), reads=["s_ab"], writes=["s_ab"], small=True)
        halo = A.alloc([128, 24, 3], F32)
        pr.op("pool", lambda e: e.memset(halo, 0.0), writes=["s_halo"])
        Sst = A.alloc([128, 2048], F32)
        Sbf = A.alloc([128, 2048], BF16)
        pr.op("pool", lambda e: e.memset(Sst, 0.0), writes=[("s_S", g) for g in range(4)])
        pr.op("pool", lambda e: e.memset(Sbf, 0.0), writes=[("s_Sbf", g) for g in range(4)])
        xblk = A.alloc([128, 8, T], F32)
        hT = A.alloc([128, 8, T], BF16)
        rstd = A.alloc([128, T], F32)
        rstd2 = rstd
        zs = A.alloc([128, NQ, 2048], BF16)
        mT = zs.bitcast(F32).rearrange("p q (o t) -> p (q o) t", o=2)
        xcT = A.alloc([128, 16, T], BF16)
        BT = A.alloc([128, 4, T], BF16)
        CT = A.alloc([128, 4, T], BF16)
        ynT = A.alloc([128, 16, T], BF16)
        sq = ynT[:, 0:8, :]
        sqp = [A.alloc([128, T], BF16) for _ in range(1)]
        NW = 2
        wsl = [A.alloc([128, 8, 512], BF16) for _ in range(NW)]
        wos = [A.alloc([128, 16, 128], BF16) for _ in range(2)]
        XP = [A.alloc([128, T + 3], F32) for _ in range(2)]
        tcv = [A.alloc([128, T], F32) for _ in range(2)]
        vdt = A.alloc([128, 32], F32)
        dts = A.alloc([128, 32], F32)
        das_ = [A.alloc([128, 32], F32) for _ in range(2)]
        css = A.alloc([128, 32], F32)
        ecs_ = [A.alloc([128, 32], F32) for _ in range(2)]
        dout = A.alloc([128, 32], F32)
        etot_ = [A.alloc([128, 32], F32) for _ in range(2)]
        xdt_ = [A.alloc([128, 2048], BF16) for _ in range(2)]
        xDb_ = [A.alloc([128, 2048], BF16) for _ in range(2)]
        xdd_ = [A.alloc([128, 2048], BF16) for _ in range(2)]
        Btm_ = [A.alloc([128, 512], BF16) for _ in range(2)]
        rhsM = [A.alloc([128, 8, 128], F32) for _ in range(2)]
        Eg = [A.alloc([128, 8, 128], BF16) for _ in range(2)]
        CBm = [A.alloc([128, 128], F32) for _ in range(2)]
        scT = [A.alloc([128, 8, 128], BF16) for _ in range(2)]
        yoffs = [A.alloc([128, 512], F32) for _ in range(1)]
        ysb = A.alloc([128, 2048], F32)
        ssq = A.alloc([128, 4], F32)
        gsc2 = yoffs
        rs4 = A.alloc([128, 4], F32)
        gn = A.alloc([128, 2048], BF16)
        self._nb = getattr(self, "_nb", 0)

        def bank():
            b = self._nb % 7
            self._nb += 1
            return b
        nws = 0
        nwo = 0
        nxp = 0
        ngr = 0
        for wi in range(S // T):
            t0 = wi * T
            kx = ("s_xblk",)
            pr.dma("sp", xblk, src[:, :, t0:t0 + T], reads=[("xT", self.cur)], writes=[kx])
            self._prenorm_multi(xblk, [kx], T, f"nmpre{li}", hT, "s_hT", sq, "s_ynT", rstd, "s_rstd", 7)
            khs = [("s_hT", c) for c in range(8)]
            sdef = []
            for blk in range(10):
                ws = wsl[nws % NW]
                kw = ("s_w", nws % NW)
                nws += 1
                pr.dma("sp", ws, self.swin_b[j, blk].rearrange("p (k c) -> p k c", k=8), reads=[("swin", j, blk)], writes=[kw])
                if blk < 4:
                    for q in range(NQ):
                        zb = bank()
                        zps = self.psb[zb]

                        def mmz(e, zps=zps, ws=ws, q=q):
                            for k in range(8):
                                i = e.matmul(zps[:, :], hT[:, k, q * 128:(q + 1) * 128], ws[:, k, :], start=(k == 0), stop=(k == 7))
                            return i
                        pr.op("pe", mmz, reads=[kw] + khs, writes=[("ps", zb)])
                        pr.op("act", lambda e, zps=zps, q=q, blk=blk: e.activation(out=zs[:, q, blk * 512:(blk + 1) * 512], in_=zps[:, :], func=AF.Silu),
                              reads=[("ps", zb)], writes=[("s_zs", q, blk)])
                else:
                    for i4 in range(4):
                        oc = (blk - 4) * 4 + i4
                        xb = bank()
                        xps = self.psb[xb]

                        def mmx(e, xps=xps, ws=ws, i4=i4):
                            for k in range(8):
                                i = e.matmul(xps[:, :], ws[:, k, i4 * 128:(i4 + 1) * 128], hT[:, k, :], start=(k == 0), stop=(k == 7))
                            return i
                        pr.op("pe", mmx, reads=[kw] + khs, writes=[("ps", xb)])
                        p = nxp % 2
                        nxp += 1
                        XPt, tct = XP[p], tcv[p]
                        pr.op("pool", lambda e, XPt=XPt, oc=oc: e.tensor_copy(out=XPt[:, 0:3], in_=halo[:, oc, :]), reads=["s_halo"], writes=[("s_XPh", p)])
                        pr.op("act", lambda e, XPt=XPt, xps=xps: e.activation(out=XPt[:, 3:T + 3], in_=xps[:, :], func=AF.Identity),
                              reads=[("ps", xb)], writes=[("s_XP", p)])
                        pr.op("pool", lambda e, XPt=XPt, oc=oc: e.tensor_copy(out=halo[:, oc, :], in_=XPt[:, T:T + 3]),
                              reads=[("s_XP", p), ("s_XPh", p)], writes=["s_halo"])
                        pr.op("dve", lambda e, XPt=XPt, tct=tct, oc=oc: e.tensor_scalar(out=tct, in0=XPt[:, 3:T + 3], scalar1=self.vcol(f"scw{j}_3", oc),
                                                                                 scalar2=self.vcol(f"scb{j}", oc), op0=ALU.mult, op1=ALU.add),
                              reads=[("s_XP", p), "vecs"], writes=[("s_tcv", p)])
                        for tap in (2, 1, 0):
                            pr.op("dve", lambda e, XPt=XPt, tct=tct, oc=oc, tap=tap: e.scalar_tensor_tensor(
                                out=tct, in0=XPt[:, tap:tap + T], scalar=self.vcol(f"scw{j}_{tap}", oc), in1=tct, op0=ALU.mult, op1=ALU.add),
                                reads=[("s_XP", p), ("s_XPh", p), ("s_tcv", p), "vecs"], writes=[("s_tcv", p)])
                        if oc < 16:
                            o_ap, ko = xcT[:, oc, :], ("s_xcT", oc)
                        elif oc < 20:
                            o_ap, ko = BT[:, oc - 16, :], ("s_BT", oc - 16)
                        else:
                            o_ap, ko = CT[:, oc - 20, :], ("s_CT", oc - 20)
                        for f_ in sdef:
                            f_()
                        sdef = [lambda o_ap=o_ap, tct=tct, p=p, ko=ko: pr.op(
                            "act", lambda e: e.activation(out=o_ap, in_=tct, func=AF.Silu), reads=[("s_tcv", p)], writes=[ko])]
            for f_ in sdef:
                f_()
            sdef = []
            if wi == 0:
                self.dump("hT", hT, khs)
                self.dump("zs0", zs[:, 0, :], [("s_zs", 0, b_) for b_ in range(4)])
                self.dump("xcT", xcT, [("s_xcT", c) for c in range(16)])
                self.dump("BT", BT, [("s_BT", c) for c in range(4)])
                self.dump("CT", CT, [("s_CT", c) for c in range(4)])
            def prep(q, par):
                qs = slice(q * 128, (q + 1) * 128)
                db = bank()
                dps = self.psb[db]

                def mmdt(e, dps=dps, q=q):
                    for k in range(8):
                        i = e.matmul(dps[:, 0:32], hT[:, k, q * 128:(q + 1) * 128], wdt[:, k, :], start=(k == 0), stop=(k == 7))
                    return i
                pr.op("pe", mmdt, reads=["s_wdt"] + khs, writes=[("ps", db)])
                pr.op("dve", lambda e, dps=dps: e.tensor_tensor(out=vdt, in0=dps[:, 0:32], in1=sm[:, 0:32], op=ALU.add),
                      reads=[("ps", db), "s_sm"], writes=["s_vdt"], small=True)
                pr.op("act", lambda e: e.activation(out=vdt, in_=vdt, func=AF.Exp), reads=["s_vdt"], writes=["s_vdt"], small=True)
                pr.op("act", lambda e: e.activation(out=dts, in_=vdt, func=AF.Ln, bias=self.vcol("one")), reads=["s_vdt", "vecs"], writes=["s_dts"], small=True)
                pr.op("dve", lambda e: e.tensor_tensor(out=das_[par], in0=dts, in1=a_b, op=ALU.mult), reads=["s_dts", "s_ab"], writes=[("s_das", par)], small=True)
                cb_ = bank()
                cps = self.psb[cb_]

                def mmcs(e, cps=cps):
                    e.matmul(cps[:, 0:32], self.triU[:], das_[par], start=True, stop=True)
                    return e.matmul(cps[:, 32:64], self.ones_f[:], das_[par], start=True, stop=True)
                pr.op("pe", mmcs, reads=[("s_das", par), "triU", "ones_f"], writes=[("ps", cb_)])
                pr.op("dve", lambda e, cps=cps: e.tensor_copy(out=css, in_=cps[:, 0:32]), reads=[("ps", cb_)], writes=["s_css"], small=True)
                pr.op("act", lambda e: e.activation(out=ecs_[par], in_=css, func=AF.Exp), reads=["s_css"], writes=[("s_ecs", par)], small=True)
                pr.op("dve", lambda e, cps=cps: e.tensor_tensor(out=dout, in0=cps[:, 32:64], in1=css, op=ALU.subtract),
                      reads=[("ps", cb_), "s_css"], writes=["s_dout"], small=True)
                pr.op("act", lambda e: e.activation(out=dout, in_=dout, func=AF.Exp), reads=["s_dout"], writes=["s_dout"], small=True)
                pr.op("act", lambda e, cps=cps: e.activation(out=etot_[par], in_=cps[:, 32:64], func=AF.Exp), reads=[("ps", cb_)], writes=[("s_etot", par)], small=True)
                for hb in range(2):
                    tb = bank()
                    tps = self.psb[tb].bitcast(BF16)

                    def trx(e, tps=tps, hb=hb, q=q):
                        for c8_ in range(8):
                            cx = hb * 8 + c8_
                            i = e.transpose(tps[:, c8_ * 128:(c8_ + 1) * 128], xcT[:, cx, q * 128:(q + 1) * 128], self.ident_b[:])
                        return i
                    pr.op("pe", trx, reads=[("s_xcT", hb * 8 + c) for c in range(8)] + ["ident_b"], writes=[("ps", tb)])
                    h0 = hb * 16
                    pr.op("dve", lambda e, tps=tps, hb=hb, h0=h0: e.tensor_tensor(
                        out=xdt_[par][:, hb * 1024:(hb + 1) * 1024].rearrange("p (h d) -> p h d", h=16),
                        in0=tps.rearrange("p (h d) -> p h d", h=16),
                        in1=dts[:, h0:h0 + 16].unsqueeze(2).to_broadcast([128, 16, 64]), op=ALU.mult),
                        reads=[("ps", tb), "s_dts"], writes=[("s_xdt", par, hb)])
                    pr.op("dve", lambda e, tps=tps, hb=hb, h0=h0: e.tensor_tensor(
                        out=xDb_[par][:, hb * 1024:(hb + 1) * 1024].rearrange("p (h d) -> p h d", h=16),
                        in0=tps.rearrange("p (h d) -> p h d", h=16),
                        in1=sm[:, 64 + h0:64 + h0 + 16].unsqueeze(2).to_broadcast([128, 16, 64]), op=ALU.mult),
                        reads=[("ps", tb), "s_sm"], writes=[("s_xDb", par, hb)])
                    pr.op("pool", lambda e, hb=hb, h0=h0: e.tensor_tensor(
                        out=xdd_[par][:, hb * 1024:(hb + 1) * 1024].rearrange("p (h d) -> p h d", h=16),
                        in0=xdt_[par][:, hb * 1024:(hb + 1) * 1024].rearrange("p (h d) -> p h d", h=16),
                        in1=dout[:, h0:h0 + 16].unsqueeze(2).to_broadcast([128, 16, 64]), op=ALU.mult),
                        reads=[("s_xdt", par, hb), "s_dout"], writes=[("s_xdd", par, hb)])
                bb = bank()
                bps = self.psb[bb].bitcast(BF16)

                def trb(e, bps=bps, q=q):
                    for g in range(4):
                        i = e.transpose(bps[:, g * 128:(g + 1) * 128], BT[:, g, q * 128:(q + 1) * 128], self.ident_b[:])
                    return i
                pr.op("pe", trb, reads=[("s_BT", g) for g in range(4)] + ["ident_b"], writes=[("ps", bb)])
                pr.op("act", lambda e, bps=bps: e.activation(out=Btm_[par], in_=bps[:, 0:512], func=AF.Identity), reads=[("ps", bb)], writes=[("s_Btm", par)])
            prep(0, 0)
            tail_def = []
            for q in range(NQ):
                qs = slice(q * 128, (q + 1) * 128)
                par = q % 2
                if q + 1 < NQ:
                    prep(q + 1, (q + 1) % 2)
                hbk_of = lambda g: g // 2

                def stageA(g, q=q, qs=qs, par=par):
                    pg = g % 2
                    gh = slice(g * 8, (g + 1) * 8)
                    rM, Et, CBt, sct = rhsM[pg], Eg[pg], CBm[pg], scT[pg]
                    pr.op("dve", lambda e: e.tensor_tensor(
                        out=rM, in0=self.triU[:].unsqueeze(1).to_broadcast([128, 8, 128]),
                        in1=das_[par][:, gh].unsqueeze(2).to_broadcast([128, 8, 128]), op=ALU.mult),
                        reads=[("s_das", par), "triU"], writes=[("s_rhsM", pg)])
                    d0, d1 = bank(), bank()

                    def mmD(e):
                        e.matmul(self.psb[d0][:, :], self.triSL[:], rM[:, 0:4, :], start=True, stop=True)
                        return e.matmul(self.psb[d1][:, :], self.triSL[:], rM[:, 4:8, :], start=True, stop=True)
                    pr.op("pe", mmD, reads=[("s_rhsM", pg), "triSL"], writes=[("ps", d0), ("ps", d1)])
                    pr.op("act", lambda e: e.activation(out=Et[:, 0:4, :], in_=self.psb[d0][:, :].rearrange("p (h l) -> p h l", h=4), func=AF.Exp),
                          reads=[("ps", d0)], writes=[("s_E", pg, 0)])
                    pr.op("act", lambda e: e.activation(out=Et[:, 4:8, :], in_=self.psb[d1][:, :].rearrange("p (h l) -> p h l", h=4), func=AF.Exp),
                          reads=[("ps", d1)], writes=[("s_E", pg, 1)])
                    cbb = bank()
                    cbps = self.psb[cbb]
                    pr.op("pe", lambda e: e.matmul(cbps[:, 0:128], BT[:, g, qs], CT[:, g, qs], start=True, stop=True),
                          reads=[("s_BT", g), ("s_CT", g)], writes=[("ps", cbb)])
                    pr.op("dve", lambda e: e.tensor_tensor(out=CBt, in0=cbps[:, 0:128], in1=self.triU[:], op=ALU.mult),
                          reads=[("ps", cbb), "triU"], writes=[("s_CBm", pg)], small=True)
                    pr.op("pool", lambda e: e.tensor_tensor(out=sct, in0=Et, in1=CBt.unsqueeze(1).to_broadcast([128, 8, 128]), op=ALU.mult),
                          reads=[("s_E", pg, 0), ("s_E", pg, 1), ("s_CBm", pg)], writes=[("s_scT", pg)])

                def stageB(g, q=q, qs=qs, par=par):
                    pg = g % 2
                    gh = slice(g * 8, (g + 1) * 8)
                    gc = slice(g * 512, (g + 1) * 512)
                    sct, yot = scT[pg], yoffs[0]
                    hbk = g // 2
                    ya, yo = bank(), bank()
                    yaps, yops = self.psb[ya], self.psb[yo]

                    def mmy(e):
                        e.matmul(yaps[:, :], self.ident_b[:], xDb_[par][:, gc], start=True, stop=False)
                        for hh in range(8):
                            c0 = g * 512 + hh * 64
                            i = e.matmul(yaps[:, hh * 64:(hh + 1) * 64], sct[:, hh, :], xdt_[par][:, c0:c0 + 64], start=False, stop=(hh == 7))
                        return i
                    pr.op("pe", mmy, reads=[("s_scT", pg), ("s_xdt", par, hbk), ("s_xDb", par, hbk), "ident_b"], writes=[("ps", ya)])
                    pr.op("pe", lambda e: e.matmul(yops[:, :], CT[:, g, qs], Sbf[:, gc], start=True, stop=True),
                          reads=[("s_CT", g), ("s_Sbf", g)], writes=[("ps", yo)])
                    pr.op("dve", lambda e: e.tensor_tensor(
                        out=yot.rearrange("p (h d) -> p h d", h=8), in0=yops[:, :].rearrange("p (h d) -> p h d", h=8),
                        in1=ecs_[par][:, gh].unsqueeze(2).to_broadcast([128, 8, 64]), op=ALU.mult),
                        reads=[("ps", yo), ("s_ecs", par)], writes=[("s_yoff", 0)])
                    pr.op("dve", lambda e: e.tensor_tensor(out=ysb[:, gc], in0=yaps[:, :], in1=yot, op=ALU.add),
                          reads=[("ps", ya), ("s_yoff", 0)], writes=[("s_ysb", g)])
                    sb_ = bank()
                    sps_ = self.psb[sb_]
                    pr.op("pe", lambda e: e.matmul(sps_[:, :], Btm_[par][:, g * 128:(g + 1) * 128], xdd_[par][:, gc], start=True, stop=True),
                          reads=[("s_Btm", par), ("s_xdd", par, hbk)], writes=[("ps", sb_)])
                    pr.op("dve", lambda e: e.tensor_tensor(
                        out=Sst[:, gc].rearrange("p (h d) -> p h d", h=8), in0=Sst[:, gc].rearrange("p (h d) -> p h d", h=8),
                        in1=etot_[par][:, gh].unsqueeze(2).to_broadcast([128, 8, 64]), op=ALU.mult),
                        reads=[("s_S", g), ("s_etot", par)], writes=[("s_S", g)])
                    pr.op("dve", lambda e: e.tensor_tensor(out=Sst[:, gc], in0=sps_[:, :], in1=Sst[:, gc], op=ALU.add),
                          reads=[("ps", sb_), ("s_S", g)], writes=[("s_S", g)])

                def stageC(g, q=q, par=par):
                    gc = slice(g * 512, (g + 1) * 512)
                    gst = gsc2[0]
                    pr.op("act", lambda e: e.activation(out=Sbf[:, gc], in_=Sst[:, gc], func=AF.Identity), reads=[("s_S", g)], writes=[("s_Sbf", g)])
                    pr.op("pool", lambda e: e.tensor_tensor(out=ysb[:, gc], in0=ysb[:, gc], in1=zs[:, q, gc], op=ALU.mult),
                          reads=[("s_ysb", g), ("s_zs", q, g)], writes=[("s_ysb", g)])
                    pr.op("act", lambda e: e.activation(out=gst, in_=ysb[:, gc], func=AF.Square),
                          reads=[("s_ysb", g)], writes=[("s_yoff", 0)])
                    pr.op("dve", lambda e: e.reduce_sum(out=ssq[:, g:g + 1], in_=gst, axis=mybir.AxisListType.X),
                          reads=[("s_yoff", 0)], writes=[("s_ssq", g)], small=True)

                stageA(0)
                stageA(1)
                for f_ in tail_def:
                    f_()
                tail_def = []
                stageB(0)
                stageA(2)
                stageB(1)
                stageA(3)
                stageB(2)
                stageC(0)
                stageB(3)
                stageC(1)
                stageC(2)
                stageC(3)
                kss = [("s_ssq", g) for g in range(4)]
                pr.op("dve", lambda e: e.tensor_scalar(out=rs4, in0=ssq, scalar1=1.0 / 512, scalar2=EPS, op0=ALU.mult, op1=ALU.add),
                      reads=kss, writes=["s_rs4"], force_same=True, small=True)
                pr.op("act", lambda e: e.activation(out=rs4, in_=rs4, func=AF.Ln), reads=["s_rs4"], writes=["s_rs4"], small=True)
                pr.op("act", lambda e: e.activation(out=rs4, in_=rs4, func=AF.Exp, scale=-0.5), reads=["s_rs4"], writes=["s_rs4"], force_same=True, small=True)
                for g in range(4):
                    gc = slice(g * 512, (g + 1) * 512)
                    pr.op("dve", lambda e, gc=gc, g=g: e.scalar_tensor_tensor(out=gn[:, gc], in0=ysb[:, gc], scalar=rs4[:, g:g + 1], in1=normw[:, gc],
                                                                              op0=ALU.mult, op1=ALU.mult),
                          reads=[("s_ysb", g), "s_rs4", "s_normw"], writes=[("s_gn", g)])
                def gn_tail(qs=qs):
                    for hb in range(2):
                        tb = bank()
                        tps = self.psb[tb].bitcast(BF16)

                        def trg(e, tps=tps, hb=hb):
                            for c8_ in range(8):
                                cx = hb * 8 + c8_
                                i = e.transpose(tps[:, c8_ * 128:(c8_ + 1) * 128], gn[:, cx * 128:(cx + 1) * 128], self.ident_b[:])
                            return i
                        pr.op("pe", trg, reads=[("s_gn", hb * 2), ("s_gn", hb * 2 + 1), "ident_b"], writes=[("ps", tb)])
                        pr.op("act", lambda e, tps=tps, hb=hb, qs=qs: e.activation(out=ynT[:, hb * 8:(hb + 1) * 8, qs], in_=tps.rearrange("p (c t) -> p c t", c=8), func=AF.Identity),
                              reads=[("ps", tb)], writes=[("s_ynT", hb * 8 + c) for c in range(8)])
                tail_def.append(gn_tail)
            for f_ in tail_def:
                f_()
            tail_def = []
            kyn = [("s_ynT", c) for c in range(16)]
            if wi == 0:
                self.dump("ynT", ynT, kyn)
            sbank = 7
            sps = self.psb[sbank]
            kzs_all = [("s_zs", q, b_) for q in range(NQ) for b_ in range(4)]
            for o in range(8):
                wo = wos[nwo % 2]
                kwo = ("s_wo", nwo % 2)
                nwo += 1
                pr.dma("sp", wo, self.swout_b[j, o].rearrange("p (k c) -> p k c", k=16), reads=[("swout", j, o)], writes=[kwo])
                mb = bank()
                mps = self.psb[mb]

                def mmo(e, mps=mps, wo=wo):
                    for k in range(16):
                        i = e.matmul(mps[:, :], wo[:, k, :], ynT[:, k, :], start=(k == 0), stop=(k == 15))
                    return i
                pr.op("pe", mmo, reads=[kwo] + kyn, writes=[("ps", mb)])
                sqt = sqp[0]
                pr.op("act", lambda e, mps=mps, o=o: e.activation(out=mT[:, o, :], in_=mps[:, :], func=AF.Identity),
                      reads=[("ps", mb)] + (kzs_all if o == 0 else []), writes=[("s_mT", o)] + (kzs_all if o == 0 else []))
                pr.op("act", lambda e, mps=mps, sqt=sqt: e.activation(out=sqt, in_=mps[:, :], func=AF.Square),
                      reads=[("ps", mb)], writes=[("s_sqp", 0)])
                pr.op("pe", lambda e, o=o, sqt=sqt: e.matmul(sps[:, :], self.ones_b[:], sqt, start=(o == 0), stop=(o == 7)),
                      reads=[("s_sqp", 0), "ones_b"], writes=[("ps", sbank)])
            self._resid(sps, sbank, rstd2, "s_rstd", mT, "s_mT", f"nmpost{li}", xblk, [kx], 0, T, dst, t0)
            for q in range(NQ):
                for b_ in range(4):
                    self.pr._record(self.pr.last_w[("xT", 1 - self.cur)], [], [("s_zs", q, b_)])
            self.pump_casts(6)
        self.cur = 1 - self.cur

    def _prenorm_multi(self, xblk, kxr, C, wname, hT, kh, sq, ksq, rstd, krstd, stat_bank):
        pr = self.pr
        ps = self.psb[stat_bank]
        for c in range(8):
            pr.op("act", lambda e, c=c: e.activation(out=sq[:, c, 0:C], in_=xblk[:, c, 0:C], func=AF.Square),
                  reads=kxr, writes=[(ksq, c)])

        def st(e):
            for c in range(8):
                i = e.matmul(ps[:, 0:C], self.ones_b[:], sq[:, c, 0:C], start=(c == 0), stop=(c == 7))
            return i
        pr.op("pe", st, reads=[(ksq, c) for c in range(8)] + ["ones_b"], writes=[("ps", stat_bank)])
        pr.op("act", lambda e: e.activation(out=rstd[:, 0:C], in_=ps[:, 0:C], func=AF.Ln, scale=1.0 / D, bias=self.vcol("eps")),
              reads=[("ps", stat_bank), "vecs"], writes=[krstd])
        pr.op("act", lambda e: e.activation(out=rstd[:, 0:C], in_=rstd[:, 0:C], func=AF.Exp, scale=-0.5),
              reads=[krstd], writes=[krstd])
        for c in range(8):
            pr.op("dve", lambda e, c=c: e.scalar_tensor_tensor(out=hT[:, c, 0:C], in0=xblk[:, c, 0:C], scalar=self.vcol(wname, c),
                                                               in1=rstd[:, 0:C], op0=ALU.mult, op1=ALU.mult),
                  reads=kxr + [krstd, "vecs"], writes=[(kh, c)])

    def build(self):
        for li in self.layers:
            if self.do_mixer and li % 2 == 0:
                j = li // 2
                self.add_cast(self.swdt_b[j], self.swdt_f[j], ("swdt", j))
                for b_ in range(10):
                    self.add_cast(self.swin_b[j, b_], self.swin_f[j, b_], ("swin", j, b_))
                for o in range(8):
                    self.add_cast(self.swout_b[j, o], self.swout_f[j, o], ("swout", j, o))
            if self.do_mixer and li % 2 == 1:
                j = li // 2
                self.add_cast(self.lwin_b[j], self.lwin_f[j], ("lwin", j))
                self.add_cast(self.lwgx_b[j], self.lwgx_f[j], ("lwgx", j))
                self.add_cast(self.lwga_b[j], self.lwga_f[j], ("lwga", j))
                self.add_cast(self.lwout_b[j], self.lwout_f[j], ("lwout", j))
            if self.do_ffn:
                for j2 in range(16):
                    self.add_cast(self.wup_b[li, j2], self.wup_f[li, j2], ("wup_b", li, j2))
                for o2 in range(4):
                    self.add_cast(self.wdn_b[li, o2], self.wdn_f[li, o2], ("wdn_b", li, o2))
        self.pump_casts(4)
        self.phase_in()
        for li in self.layers:
            if self.do_mixer:
                if li % 2 == 1:
                    self.phase_lru(li)
                else:
                    self.phase_ssd(li)
            if self.do_ffn:
                self.phase_ffn(li)
        self.phase_out()
        self.pr.finish_wait_all("sp")
        self.pr.emit()
        self.P.close()
        return self.nc


def pack_inputs(inp):
    vp = VecPack()
    vp.add_raw("eps", np.full((128, 1), EPS, np.float32))
    for li in range(DEPTH):
        vp.add(f"nmpre{li}", inp["norm_mix_pre"][li])
        vp.add(f"nmpost{li}", inp["norm_mix_post"][li])
        vp.add(f"nfpre{li}", inp["norm_ffn_pre"][li])
        vp.add(f"nfpost{li}", inp["norm_ffn_post"][li])
        for j in range(3):
            vp.add(f"fcw{li}_{j}", inp["ffn_conv_w"][li][j])
        vp.add(f"fcb{li}", inp["ffn_conv_b"][li])
    vp.add_raw("one", np.ones((128, 1), np.float32))
    for j in range(2):
        vp.add(f"lbin{j}", inp["lru_b_in"][j])
        for t in range(4):
            vp.add(f"lcw{j}_{t}", inp["lru_conv_w"][j][t])
        vp.add(f"lcb{j}", inp["lru_conv_b"][j])
        vp.add(f"lbgx{j}", inp["lru_b_gx"][j])
        vp.add(f"lbga{j}", inp["lru_b_ga"][j])
        vp.add(f"llam{j}", inp["lru_lambda"][j])
        vp.add(f"lbout{j}", inp["lru_b_out"][j])
    for j in range(2):
        for t in range(4):
            vp.add(f"scw{j}_{t}", inp["ssd_conv_w"][j][t])
        vp.add(f"scb{j}", inp["ssd_conv_b"][j])
    vecs = vp.build()
    swin = np.asarray(inp["ssd_w_in"], dtype=np.float32)
    sw = swin[:, :, :5120].reshape(2, 8, 128, 10, 512)
    swin_h = np.ascontiguousarray(sw.transpose(0, 3, 2, 1, 4)).reshape(2, 10, 128, 4096)
    sdt = swin[:, :, 5120:].reshape(2, 8, 128, 32)
    swdt_h = np.ascontiguousarray(sdt.transpose(0, 2, 1, 3)).reshape(2, 128, 256)
    swo = np.asarray(inp["ssd_w_out"], dtype=np.float32).reshape(2, 16, 128, 8, 128)
    swout_h = np.ascontiguousarray(swo.transpose(0, 3, 2, 1, 4)).reshape(2, 8, 128, 2048)
    snorm_h = np.ascontiguousarray(np.broadcast_to(np.asarray(inp["ssd_norm"], dtype=np.float32)[:, None, :], (2, 128, 2048)))
    ssm = np.concatenate([np.asarray(inp["ssd_dt_bias"], dtype=np.float32), np.asarray(inp["ssd_a_log"], dtype=np.float32),
                          np.asarray(inp["ssd_d"], dtype=np.float32)], axis=1)
    ssm_h = np.ascontiguousarray(np.broadcast_to(ssm[:, None, :], (2, 128, 96)))
    lwin = np.asarray(inp["lru_w_in"], dtype=np.float32)
    lw = lwin.reshape(2, 8, 128, 20, 128)
    lwin_h = np.ascontiguousarray(lw.transpose(0, 3, 2, 1, 4)).reshape(2, 2560, 1024)
    lwgx_h = np.ascontiguousarray(np.asarray(inp["lru_w_gx"], dtype=np.float32).transpose(0, 2, 1, 3)).reshape(2, 128, 1280)
    lwga_h = np.ascontiguousarray(np.asarray(inp["lru_w_ga"], dtype=np.float32).transpose(0, 2, 1, 3)).reshape(2, 128, 1280)
    lwo = np.asarray(inp["lru_w_out"], dtype=np.float32).reshape(2, 10, 128, 1024)
    lwout_h = np.ascontiguousarray(lwo.transpose(0, 2, 1, 3)).reshape(2, 128, 10240)
    wup = np.asarray(inp["ffn_w_up"], dtype=np.float32)
    g = wup[:, :, :DFF].reshape(DEPTH, 8, 128, 16, 256)
    v = wup[:, :, DFF:].reshape(DEPTH, 8, 128, 16, 256)
    gv = np.concatenate([g, v], axis=-1)
    wup_h = np.ascontiguousarray(gv.transpose(0, 3, 2, 1, 4)).reshape(DEPTH, 16, 128, 4096)
    wdn = np.asarray(inp["ffn_w_down"], dtype=np.float32)
    wd = wdn.reshape(DEPTH, 32, 128, 4, 256)
    wdn_h = np.ascontiguousarray(wd.transpose(0, 3, 2, 1, 4)).reshape(DEPTH, 4, 128, 8192)
    shared = {"vecs": vecs, "wup": wup_h, "wdn": wdn_h,
              "swin": swin_h, "swdt": swdt_h, "swout": swout_h, "snorm": snorm_h, "ssm": ssm_h,
              "lwin": lwin_h, "lwgx": lwgx_h, "lwga": lwga_h, "lwout": lwout_h}
    return shared, vp.index, vecs.shape[1]


_CACHE = {}


def run(inp, S=4096, layers=(0, 1, 2, 3), do_mixer=True, do_ffn=True, trace=False, debug=False):
    shared, vindex, nvec = pack_inputs(inp)
    x = np.asarray(inp["x"], dtype=np.float32)
    B = x.shape[0]
    b = Builder(S, vindex, nvec, layers=layers, do_mixer=do_mixer, do_ffn=do_ffn, debug=debug)
    nc = b.build()
    in_maps = []
    for i in range(B):
        m = dict(shared)
        m["x"] = np.ascontiguousarray(x[i, :S])
        in_maps.append(m)
    res = run_bass_kernel_spmd(nc, in_maps, core_ids=list(range(B)), trace=trace)
    y = np.stack([np.asarray(r["y"]) for r in res.results], axis=0)
    if debug:
        res.dbg = {k: np.asarray(res.results[0]["dbg_" + k]).astype(np.float32) for k in b.dbg}
    return y.astype(np.float32), res


def kernel(**inputs):
    y, _ = run(inputs)
    return y
```
